# Optimizing a Trainium2 kernel written in Bass

```python
import math
import jax, jax.numpy as jnp
from jax import lax
import numpy as np

D_MODEL = 1024
BATCH = 8
SEQ = 2048
DEPTH = 2
DEC_BATCH = 16
DEC_SEQ = 4096
PAST_LEN = 128

H_A = 8
HD_QK = 64
HD_V = 128
W_A = H_A * HD_V
Q_BLOCK = 128
ROPE_THETA = 10000.0
SUBLN_EPS = 1e-5
H_B = 8
HD_K = 128
HD_VB = 128
W_B = H_B * HD_VB
CONV_W = 4
CHUNK = 64
EPS = 1e-6
SPLIT_SIZES = (
    H_A * 2 * HD_QK,
    H_A * 2 * HD_QK,
    W_A,
    W_A,
    2 * H_B * HD_K + W_B,
    W_B,
    2 * H_B,
    2 * H_B,
    2 * D_MODEL,
)
IN_WIDTH = sum(SPLIT_SIZES)
SPLIT_POINTS = [int(v) for v in np.cumsum(SPLIT_SIZES)[:-1]]

kernel_name = 'hybrid_diffattn_gdn_encoder'


def rmsnorm(x, g, eps=EPS):
    xf = x.astype(jnp.float32)
    y = xf * lax.rsqrt(jnp.mean(xf * xf, -1, keepdims=True) + eps)
    return (y * g.astype(jnp.float32)).astype(x.dtype)


def l2norm(x):
    return x * lax.rsqrt(jnp.sum(x * x, -1, keepdims=True) + EPS)


def rope_tables(S, dtype):
    inv = ROPE_THETA ** (-jnp.arange(0, HD_QK, 2, dtype=jnp.float32) / HD_QK)
    ang = jnp.arange(S, dtype=jnp.float32)[:, None] * inv[None, :]
    ang = jnp.concatenate([ang, ang], -1)
    return jnp.cos(ang).astype(dtype), jnp.sin(ang).astype(dtype)


def apply_rope(x, cos, sin):
    c = cos[:, None, None, :]
    s = sin[:, None, None, :]
    x1, x2 = jnp.split(x, 2, -1)
    return x * c + jnp.concatenate([-x2, x1], -1) * s


def diff_attention(q, k, v, lam):
    B, H, _, S, Dq = q.shape
    nb = S // Q_BLOCK
    qb = jnp.moveaxis(q.reshape(B, H, 2, nb, Q_BLOCK, Dq), 3, 0)
    scale = Dq ** -0.5

    def one(qi):
        s = jnp.einsum('bhcqd,bhckd->bhcqk', qi, k, preferred_element_type=jnp.float32) * scale
        p = jax.nn.softmax(s, axis=-1)
        w = p[:, :, 0] - lam * p[:, :, 1]
        return jnp.einsum('bhqk,bhkv->bhqv', w.astype(v.dtype), v)

    o = lax.map(one, qb)
    return o.transpose(1, 0, 3, 2, 4).reshape(B, S, H, v.shape[-1])


def diff_branch(q, k, v, z, lam_qk, gain, lam_init, cos, sin):
    B, S, _ = q.shape
    q = apply_rope(q.reshape(B, S, H_A, 2, HD_QK), cos, sin).transpose(0, 2, 3, 1, 4)
    k = apply_rope(k.reshape(B, S, H_A, 2, HD_QK), cos, sin).transpose(0, 2, 3, 1, 4)
    v = v.reshape(B, S, H_A, HD_V).transpose(0, 2, 1, 3)
    lq = lam_qk.astype(jnp.float32)
    lam = jnp.exp(jnp.sum(lq[0] * lq[1])) - jnp.exp(jnp.sum(lq[2] * lq[3])) + lam_init
    o = diff_attention(q, k, v, lam)
    o = rmsnorm(o, gain, SUBLN_EPS) * (1.0 - lam_init)
    return o.reshape(B, S, W_A) * jax.nn.silu(z)


def gdn_direction(q, k, v, g, beta):
    B, H, L, DK = q.shape
    DV = v.shape[-1]
    N = L // CHUNK
    q = q.reshape(B, H, N, CHUNK, DK)
    k = k.reshape(B, H, N, CHUNK, DK)
    v = v.reshape(B, H, N, CHUNK, DV)
    g = g.reshape(B, H, N, CHUNK)
    beta = beta.reshape(B, H, N, CHUNK)
    gc = jnp.cumsum(g, -1)
    idx = jnp.arange(CHUNK)
    tril = idx[:, None] >= idx[None, :]
    strict = idx[:, None] > idx[None, :]
    decay = jnp.exp(jnp.where(tril, gc[..., :, None] - gc[..., None, :], -jnp.inf))
    kb = k * beta[..., None]
    a = jnp.where(strict, jnp.einsum('bhncd,bhnjd->bhncj', kb, k) * decay, 0.0)
    eye = jnp.eye(CHUNK, dtype=a.dtype)
    t = lax.linalg.triangular_solve(a + eye, jnp.broadcast_to(eye, a.shape), left_side=True, lower=True, unit_diagonal=True)
    u = jnp.einsum('bhncj,bhnjv->bhncv', t, v * beta[..., None])
    w = jnp.einsum('bhncj,bhnjd->bhncd', t, kb * jnp.exp(gc)[..., None])
    attn = jnp.where(tril, jnp.einsum('bhncd,bhnjd->bhncj', q, k) * decay, 0.0)
    qg = q * jnp.exp(gc)[..., None]
    kd = k * jnp.exp(gc[..., -1:] - gc)[..., None]
    gl = jnp.exp(gc[..., -1])
    xs = (jnp.moveaxis(u, 2, 0), jnp.moveaxis(w, 2, 0), jnp.moveaxis(attn, 2, 0),
          jnp.moveaxis(qg, 2, 0), jnp.moveaxis(kd, 2, 0), jnp.moveaxis(gl, 2, 0))

    def step(state, inp):
        u_i, w_i, attn_i, qg_i, kd_i, gl_i = inp
        v_new = u_i - jnp.einsum('bhcd,bhdv->bhcv', w_i, state)
        o = jnp.einsum('bhcd,bhdv->bhcv', qg_i, state) + jnp.einsum('bhcj,bhjv->bhcv', attn_i, v_new)
        state = state * gl_i[..., None, None] + jnp.einsum('bhcd,bhcv->bhdv', kd_i, v_new)
        return state, o

    s0 = jnp.zeros((B, H, DK, DV), q.dtype)
    _, o = lax.scan(step, s0, xs)
    return jnp.moveaxis(o, 0, 2).reshape(B, H, L, DV)


def gdn_branch(qkv, z, a, b, conv_w, a_log, dt_bias, gain):
    B, S, C = qkv.shape
    left = (CONV_W - 1) // 2
    qkv = lax.conv_general_dilated(qkv, conv_w[:, None, :].astype(qkv.dtype), (1,), [(left, CONV_W - 1 - left)],
                                   dimension_numbers=('NWC', 'WIO', 'NWC'), feature_group_count=C)
    qkv = jax.nn.silu(qkv).astype(jnp.float32)
    q, k, v = jnp.split(qkv, [H_B * HD_K, 2 * H_B * HD_K], -1)
    q = l2norm(q.reshape(B, S, H_B, HD_K).transpose(0, 2, 1, 3)) * (HD_K ** -0.5)
    k = l2norm(k.reshape(B, S, H_B, HD_K).transpose(0, 2, 1, 3))
    v = v.reshape(B, S, H_B, HD_VB).transpose(0, 2, 1, 3)
    a = a.astype(jnp.float32).reshape(B, S, 2, H_B).transpose(2, 0, 3, 1)
    b = b.astype(jnp.float32).reshape(B, S, 2, H_B).transpose(2, 0, 3, 1)
    g = -jnp.exp(a_log.astype(jnp.float32))[:, None, :, None] * jax.nn.softplus(a + dt_bias.astype(jnp.float32)[:, None, :, None])
    beta = jax.nn.sigmoid(b)
    o_f = gdn_direction(q, k, v, g[0], beta[0])
    fl = lambda t: jnp.flip(t, 2)
    o_b = fl(gdn_direction(fl(q), fl(k), fl(v), fl(g[1]), fl(beta[1])))
    o = (o_f + o_b).transpose(0, 2, 1, 3).astype(z.dtype)
    o = rmsnorm(o, gain)
    return o.reshape(B, S, W_B) * jax.nn.silu(z)


def trunk(x, norm_g, w_in, conv_w, lam_qk, diff_norm_g, a_log, dt_bias, gdn_norm_g, w_branch, w_out, final_g):
    B, S, D = x.shape
    cos, sin = rope_tables(S, x.dtype)
    for l in range(DEPTH):
        h = rmsnorm(x, norm_g[l])
        proj = h @ w_in[l]
        q_a, k_a, v_a, z_a, qkv_b, z_b, a_b, b_b, gate = jnp.split(proj, SPLIT_POINTS, -1)
        lam_init = 0.8 - 0.6 * math.exp(-0.3 * l)
        y_a = diff_branch(q_a, k_a, v_a, z_a, lam_qk[l], diff_norm_g[l], lam_init, cos, sin)
        y_b = gdn_branch(qkv_b, z_b, a_b, b_b, conv_w[l], a_log[l], dt_bias[l], gdn_norm_g[l])
        gates = jax.nn.sigmoid(gate.astype(jnp.float32)).astype(x.dtype).reshape(B, S, 2, D)
        merged = gates[:, :, 0] * (y_a @ w_branch[l, 0]) + gates[:, :, 1] * (y_b @ w_branch[l, 1])
        x = x + merged @ w_out[l]
    return rmsnorm(x, final_g)


def setup_inputs(seed: int = 0) -> dict:
    key = jax.random.key(seed)
    ks = jax.random.split(key, 16)
    f32 = jnp.float32
    x_prompt = jax.random.normal(ks[0], (BATCH, SEQ, D_MODEL), f32)
    x_sample = jax.random.normal(ks[1], (DEC_BATCH, DEC_SEQ, D_MODEL), f32)
    norm_g = 1.0 + 0.02 * jax.random.normal(ks[2], (DEPTH, D_MODEL), f32)
    w_in = jax.random.normal(ks[3], (DEPTH, D_MODEL, IN_WIDTH), f32) * D_MODEL ** -0.5
    conv_w = jax.random.normal(ks[4], (DEPTH, CONV_W, 2 * H_B * HD_K + W_B), f32) * CONV_W ** -0.5
    lam_qk = 0.1 * jax.random.normal(ks[5], (DEPTH, 4, HD_QK), f32)
    diff_norm_g = 1.0 + 0.02 * jax.random.normal(ks[6], (DEPTH, HD_V), f32)
    a_log = jnp.log(jax.random.uniform(ks[7], (DEPTH, 2, H_B), f32, 1.0, 16.0))
    dt = jnp.exp(jax.random.uniform(ks[8], (DEPTH, 2, H_B), f32, math.log(0.001), math.log(0.1)))
    dt_bias = dt + jnp.log(-jnp.expm1(-dt))
    gdn_norm_g = 1.0 + 0.02 * jax.random.normal(ks[9], (DEPTH, HD_VB), f32)
    w_branch = jax.random.normal(ks[10], (DEPTH, 2, W_A, D_MODEL), f32) * W_A ** -0.5
    w_out = jax.random.normal(ks[11], (DEPTH, D_MODEL, D_MODEL), f32) * D_MODEL ** -0.5
    final_g = 1.0 + 0.02 * jax.random.normal(ks[12], (D_MODEL,), f32)
    return {'x_prompt': x_prompt, 'x_sample': x_sample, 'norm_g': norm_g, 'w_in': w_in, 'conv_w': conv_w,
            'lam_qk': lam_qk, 'diff_norm_g': diff_norm_g, 'a_log': a_log, 'dt_bias': dt_bias,
            'gdn_norm_g': gdn_norm_g, 'w_branch': w_branch, 'w_out': w_out, 'final_g': final_g}


def reference(x_prompt, x_sample, norm_g, w_in, conv_w, lam_qk, diff_norm_g, a_log, dt_bias, gdn_norm_g, w_branch, w_out, final_g):
    y_prompt = trunk(x_prompt, norm_g, w_in, conv_w, lam_qk, diff_norm_g, a_log, dt_bias, gdn_norm_g, w_branch, w_out, final_g)
    y_sample = trunk(x_sample, norm_g, w_in, conv_w, lam_qk, diff_norm_g, a_log, dt_bias, gdn_norm_g, w_branch, w_out, final_g)
    return (y_prompt, y_sample)
```

```python
import math
import os
DBG = int(os.environ.get('MKDBG', '99'))
import contextlib
import numpy as np
import ml_dtypes
import concourse.bass as bass
import concourse.mybir as mybir
from concourse.bass_utils import run_bass_kernel_spmd

F32 = mybir.dt.float32
BF16 = mybir.dt.bfloat16
AF = mybir.ActivationFunctionType
ALU = mybir.AluOpType
AX = mybir.AxisListType

D = 1024
DEPTH = 2
INW = 10272
EPS = 1e-6
SUBLN_EPS = 1e-5
SEQS_FULL = (2048, 4096, 4096)


class Res:
    __slots__ = ("name", "last_w", "readers", "excl")

    def __init__(self, name="", excl=False):
        self.name = name
        self.excl = excl
        self.last_w = None
        self.readers = []


class Op:
    __slots__ = ("eng", "fn", "deps", "needed", "semval", "dma", "dsem")

    def __init__(self, eng, fn, dma):
        self.eng = eng
        self.fn = fn
        self.deps = []
        self.needed = False
        self.semval = None
        self.dma = dma
        self.dsem = None


class Sched:
    ENG = ("tensor", "vector", "scalar", "gpsimd", "sync")
    DQ = ("sync", "gpsimd")
    NDMA = 12

    def __init__(self, nc, same_engine_sync=True):
        self.nc = nc
        self.ops = []
        self.same_engine_sync = same_engine_sync
        self.last = {e: None for e in self.ENG}
        self.lastd = {e: [] for e in self.DQ}

    def op(self, eng, fn, reads=(), writes=(), dma=False):
        o = Op(eng, fn, dma)
        deps = set()
        ex = [r for r in reads if r.excl]
        if ex:
            reads = [r for r in reads if not r.excl]
            writes = list(writes) + [r for r in ex if r not in writes]
        for r in reads:
            if r.last_w is not None:
                deps.add(r.last_w)
        for w in writes:
            if w.last_w is not None:
                deps.add(w.last_w)
            for rd in w.readers:
                deps.add(rd)
        o.deps = list(deps)
        for r in reads:
            r.readers.append(o)
        for w in writes:
            w.last_w = o
            w.readers = []
        self.ops.append(o)
        if dma:
            self.lastd[eng].append(o)
            if len(self.lastd[eng]) > self.NDMA:
                self.lastd[eng].pop(0)
        else:
            self.last[eng] = o
        return o

    def barrier(self):
        deps = [o for o in self.last.values() if o is not None]
        for q in self.DQ:
            deps += self.lastd[q]
        for e in self.ENG:
            o = Op(e, None, False)
            o.deps = list(deps)
            self.ops.append(o)

    def emit(self):
        nc = self.nc
        engs = {"tensor": nc.tensor, "vector": nc.vector, "scalar": nc.scalar,
                "gpsimd": nc.gpsimd, "sync": nc.sync}
        ses = self.same_engine_sync
        for o in self.ops:
            for d in o.deps:
                if d.dma:
                    continue
                if d.eng == o.eng and not o.dma and (o.eng == "tensor" or not ses) and o.fn is not None:
                    continue
                d.needed = True
        tl = {e: nc.alloc_semaphore(name=f"tl_{e}") for e in self.ENG}
        cnt = {e: 0 for e in self.ENG}
        dsems = {e: [nc.alloc_semaphore(name=f"d_{e}_{k}") for k in range(self.NDMA)] for e in self.DQ}
        dcount = {e: [0] * self.NDMA for e in dsems}
        dnext = {e: 0 for e in dsems}
        waited = {e: {} for e in self.ENG}
        nwait = 0
        for o in self.ops:
            E = engs[o.eng]
            w = waited[o.eng]
            reqs = {}
            for d in o.deps:
                if d.dma:
                    key = ("d", d.eng, d.dsem[0])
                    sem, val = d.dsem[1], d.dsem[2]
                else:
                    if d.semval is None:
                        continue
                    key = ("t", d.eng)
                    sem, val = tl[d.eng], d.semval
                if reqs.get(key, (None, -1))[1] < val:
                    reqs[key] = (sem, val)
            if o.dma:
                k = dnext[o.eng]
                dnext[o.eng] = (k + 1) % self.NDMA
                prev = dcount[o.eng][k]
                if prev > 0:
                    key = ("d", o.eng, k)
                    if reqs.get(key, (None, -1))[1] < prev:
                        reqs[key] = (dsems[o.eng][k], prev)
                dcount[o.eng][k] = prev + 16
                o.dsem = (k, dsems[o.eng][k], prev + 16)
            for key, (sem, val) in reqs.items():
                if w.get(key, 0) < val:
                    E.wait_ge(sem, val)
                    w[key] = val
                    nwait += 1
            if o.fn is None:
                continue
            ins = o.fn(E)
            if o.dma:
                ins.then_inc(o.dsem[1], 16)
            elif o.needed:
                cnt[o.eng] += 1
                o.semval = cnt[o.eng]
                ins.then_inc(tl[o.eng], 1)
        for e in dsems:
            for k in range(self.NDMA):
                if dcount[e][k] > 0:
                    nc.sync.wait_ge(dsems[e][k], dcount[e][k])
        print(f"[sched] ops={len(self.ops)} waits={nwait} incs={cnt}", flush=True)


class Ring:
    def __init__(self, nc, name, shape, dtype, n):
        self.t = Ring.stack.enter_context(nc.sbuf_tensor(name, [shape[0], n] + list(shape[1:]), dtype))
        self.res = [Res(f"{name}{i}") for i in range(n)]
        self.n = n
        self.i = 0

    def next(self):
        k = self.i % self.n
        self.i += 1
        return self.t[:, k], self.res[k]


def build(seqs, debug=False, same_engine_sync=True, phases="ABCD", alim=99, nlayers=DEPTH, pool="gpsimd"):
    NT = sum(seqs)
    nc = bass.Bass("TRN2", target_bir_lowering=False)
    S = Sched(nc, same_engine_sync=same_engine_sync)

    def dram(name, shape, dt, kind):
        return nc.dram_tensor(name, shape, dt, kind=kind).ap()

    okind = "ExternalOutput" if debug else "Internal"
    xin = dram("xin", [NT, D], F32, "ExternalInput")
    w_in = dram("w_in", [DEPTH, D, INW], F32, "ExternalInput")
    w_br = dram("w_branch", [DEPTH, 2, D, D], F32, "ExternalInput")
    w_out = dram("w_out", [DEPTH, D, D], F32, "ExternalInput")
    norm_g = dram("norm_g", [DEPTH, D], F32, "ExternalInput")
    final_g = dram("final_g", [1, D], F32, "ExternalInput")
    convw = dram("convw", [DEPTH, 128, 24, 4], F32, "ExternalInput")
    lam_qk = dram("lam_qk", [DEPTH, 1, 256], F32, "ExternalInput")
    dng = dram("dng", [DEPTH, 128, 1], F32, "ExternalInput")
    gng = dram("gng", [DEPTH, 1, 128], F32, "ExternalInput")
    a_log = dram("a_log", [DEPTH, 1, 16], F32, "ExternalInput")
    dt_bias = dram("dt_bias", [DEPTH, 1, 16], F32, "ExternalInput")
    cosT_d = dram("cosT", [128, 4096], F32, "ExternalInput")
    sinT_d = dram("sinT", [128, 4096], F32, "ExternalInput")
    cf_d = dram("constf", [128, 6, 128], F32, "ExternalInput")
    cb_d = dram("constb", [128, 3, 128], BF16, "ExternalInput")
    yout = dram("yout", [NT, D], F32, "ExternalOutput")
    x1 = dram("x1", [NT, D], F32, okind)
    qkT = dram("qkT", [2048, NT], BF16, okind)
    va = dram("va", [NT, D], BF16, okind)
    zaT = dram("zaT", [D, NT], BF16, okind)
    qbT = dram("qbT", [D, NT], BF16, okind)
    kbT = dram("kbT", [D, NT], BF16, okind)
    ktm = dram("ktm", [NT, D], BF16, okind)
    vtm = dram("vtm", [NT, D], BF16, okind)
    zb = dram("zb", [NT, D], BF16, okind)
    ab = dram("ab", [NT, 32], F32, okind)
    gT = dram("gT", [2048, NT], BF16, okind)
    yaT = dram("yaT", [D, NT], BF16, okind)
    ybT = dram("ybT", [D, NT], BF16, okind)
    R = {n: Res(n) for n in "x1 qkT va zaT qbT kbT ktm vtm zb ab gT yaT ybT yout".split()}

    cf = nc.alloc_sbuf_tensor("cf", [128, 6, 128], F32)
    cb = nc.alloc_sbuf_tensor("cb", [128, 3, 128], BF16)
    Rc = Res("consts")
    S.op("sync", lambda e: e.dma_start(out=cf[:], in_=cf_d[:, :, :]), writes=[Rc], dma=True)
    S.op("sync", lambda e: e.dma_start(out=cb[:], in_=cb_d[:, :, :]), writes=[Rc], dma=True)
    ident_f, ones_f = cf[:, 0, :], cf[:, 1, :]
    ident_b, ones_b, perm_b = cb[:, 0, :], cb[:, 1, :], cb[:, 2, :]
    MI = {0: cf[0:64, 2, 0:64], 1: cf[0:64, 4, 0:64]}
    MS = {0: cf[0:64, 3, 0:64], 1: cf[0:64, 5, 0:64]}

    PS = [nc.alloc_psum_tensor(f"ps{b}", [128, 512], F32) for b in range(8)]
    RPS = [Res(f"ps{b}", excl=True) for b in range(8)]

    DMAQ = ["sync", "gpsimd"]
    dq = [0]

    def dma(out, in_, reads=(), writes=(), q=None):
        if q is None:
            q = DMAQ[dq[0] % 2]
            dq[0] += 1
        return S.op(q, lambda e: e.dma_start(out=out, in_=in_), reads=reads, writes=writes, dma=True)

    def mm(out, lhsT, rhs, start, stop, reads, writes, **kw):
        return S.op("tensor", lambda e: e.matmul(out, lhsT=lhsT, rhs=rhs, start=start, stop=stop, **kw),
                    reads=reads, writes=writes)

    def tr(out, in_, ident, reads, writes):
        return S.op("tensor", lambda e: e.transpose(out, in_, ident), reads=list(reads) + [Rc], writes=writes)

    def act(out, in_, func, reads, writes, eng="scalar", **kw):
        return S.op("scalar", lambda e: e.activation(out=out, in_=in_, func=func, **kw), reads=reads, writes=writes)

    def tt(eng, out, in0, in1, op, reads, writes):
        eng = pool if eng == "gpsimd" else eng
        return S.op(eng, lambda e: e.tensor_tensor(out=out, in0=in0, in1=in1, op=op), reads=reads, writes=writes)

    def ts(eng, out, in0, s1, op0, reads, writes, s2=None, op1=None):
        eng = pool if eng == "gpsimd" else eng
        if op1 is None:
            return S.op(eng, lambda e: e.tensor_scalar(out=out, in0=in0, scalar1=s1, scalar2=None, op0=op0),
                        reads=reads, writes=writes)
        return S.op(eng, lambda e: e.tensor_scalar(out=out, in0=in0, scalar1=s1, scalar2=s2, op0=op0, op1=op1),
                    reads=reads, writes=writes)

    def stt(eng, out, in0, scalar, in1, op0, op1, reads, writes):
        eng = "vector"
        return S.op(eng, lambda e: e.scalar_tensor_tensor(out=out, in0=in0, scalar=scalar, in1=in1, op0=op0, op1=op1),
                    reads=reads, writes=writes)

    def cp(eng, out, in_, reads, writes):
        eng = pool if eng == "gpsimd" else eng
        if eng == "scalar":
            return act(out, in_, AF.Copy, reads, writes)
        return S.op(eng, lambda e: e.tensor_copy(out=out, in_=in_), reads=reads, writes=writes)

    seq_off = [sum(seqs[:i]) for i in range(len(seqs))]

    for l in range(nlayers):
        lam_init = 0.8 - 0.6 * math.exp(-0.3 * l)
        xsrc = xin if l == 0 else x1
        Rxsrc = [] if l == 0 else [R["x1"]]
        if "A" in phases:
          with contextlib.ExitStack() as st:
            Ring.stack = st

            def sb(name, shape, dt):
                return st.enter_context(nc.sbuf_tensor(f"A{l}_{name}", shape, dt))
            SMAX = max(seqs)
            hT = sb("hT", [128, 8, SMAX], BF16)
            RhT = Res("hT")
            gtile = sb("gtile", [128, D], F32)
            Rg = Res("gtile")
            dma(gtile[:], norm_g[l:l + 1, :].broadcast_to([128, D]), writes=[Rg])
            cwt = sb("cwt", [128, 24, 4], F32)
            Rcw = Res("cw")
            dma(cwt[:], convw[l], writes=[Rcw])
            xt_ring = Ring(nc, f"A{l}_xt", [128, D], F32, 2)
            hb_ring = Ring(nc, f"A{l}_hb", [128, D], BF16, 2)
            sq_junk = sb("sqj", [128, D], BF16)
            Rsqj = Res("sqj")
            ss_ring = Ring(nc, f"A{l}_ss", [128, 2], F32, 2)
            wst_ring = Ring(nc, f"A{l}_wst", [128, 8, 512], F32, 2)
            wsec = sb("wsec", [128, 8, 2048], BF16)
            Rw = Res("wsec")
            ev_ring = Ring(nc, f"A{l}_ev", [128, 512], BF16, 4)
            evf_ring = Ring(nc, f"A{l}_evf", [128, 512], F32, 3)
            cs_ring = Ring(nc, f"A{l}_cs", [128, 2, 512], F32, 2)
            xc = sb("xc", [128, SMAX + 4], F32)
            Rxc = Res("xc")
            xacc = sb("xacc", [128, SMAX], F32)
            Rxa = Res("xacc")
            ab_ring = Ring(nc, f"A{l}_ab", [128, 32], F32, 2)
            wv = w_in[l].rearrange("(c p) n -> p c n", p=128)

            def load_w(col0, ncols):
                nb = (ncols + 511) // 512
                for b in range(nb):
                    n = min(512, ncols - b * 512)
                    wst, Rwst = wst_ring.next()
                    dma(wst[:, :, 0:n], wv[:, :, col0 + b * 512: col0 + b * 512 + n], writes=[Rwst])
                    cp("gpsimd", wsec[:, :, b * 512: b * 512 + n], wst[:, :, 0:n], reads=[Rwst], writes=[Rw])

            psi = [0]

            def nextps(k=4):
                b = psi[0] % k
                psi[0] += 1
                return PS[b], RPS[b]

            for si, Sq in enumerate(seqs):
                t0 = seq_off[si]
                NTL = Sq // 128
                NB = Sq // 512
                for m in range(NTL):
                    xt, Rxt = xt_ring.next()
                    dma(xt, xsrc[t0 + m * 128: t0 + (m + 1) * 128, :], reads=Rxsrc, writes=[Rxt])
                    ss, Rss = ss_ring.next()
                    act(sq_junk[:], xt, AF.Square, reads=[Rxt], writes=[Rsqj, Rss], accum_out=ss[:, 0:1])
                    act(ss[:, 1:2], ss[:, 0:1], AF.Sqrt, reads=[Rss], writes=[Rss], scale=1.0 / D, bias=EPS)
                    S.op("vector", lambda e, ss=ss: e.reciprocal(out=ss[:, 1:2], in_=ss[:, 1:2]), reads=[Rss], writes=[Rss])
                    hb, Rhb = hb_ring.next()
                    stt("vector", hb, xt, ss[:, 1:2], gtile[:], ALU.mult, ALU.mult, reads=[Rxt, Rss, Rg], writes=[Rhb])
                    ps, Rps = nextps()
                    psb = ps[:, :].bitcast(BF16)
                    for c in range(8):
                        tr(psb[:, c * 128:(c + 1) * 128], hb[:, c * 128:(c + 1) * 128], ident_b, reads=[Rhb], writes=[Rps])
                    cp("vector", hT[:, :, m * 128:(m + 1) * 128], psb[:, 0:1024].rearrange("p (c t) -> p c t", c=8),
                       reads=[Rps], writes=[RhT])

                def fm_matmuls(ch, tb):
                    ps, Rps = nextps()
                    for c in range(8):
                        mm(ps[:, :], wsec[:, c, ch * 128:(ch + 1) * 128], hT[:, c, tb * 512:(tb + 1) * 512],
                           c == 0, c == 7, reads=[Rw, RhT], writes=[Rps])
                    return ps, Rps

                def tm_matmuls(m, cb0, n=512):
                    ps, Rps = nextps()
                    for c in range(8):
                        mm(ps[:, 0:n], hT[:, c, m * 128:(m + 1) * 128], wsec[:, c, cb0:cb0 + n],
                           c == 0, c == 7, reads=[Rw, RhT], writes=[Rps])
                    return ps, Rps

                if alim >= 2:
                    load_w(0, 2048)
                for tb in range(NB if (alim >= 2 and DBG >= 1) else 0):
                    cs, Rcs = cs_ring.next()
                    dma(cs[:, 0, :], cosT_d[:, tb * 512:(tb + 1) * 512], writes=[Rcs])
                    dma(cs[:, 1, :], sinT_d[:, tb * 512:(tb + 1) * 512], writes=[Rcs])
                    for ch in range(16 if DBG >= 2 else 0):
                        ps, Rps = fm_matmuls(ch, tb)
                        if DBG < 3:
                            continue
                        qb_, Rqb = ev_ring.next()
                        cp("scalar", qb_, ps[:, :], reads=[Rps], writes=[Rqb])
                        if DBG < 4:
                            continue
                        t1, Rt1 = evf_ring.next()
                        if os.environ.get('MKX') == '1':
                            tt("vector", t1, ps[:, :], gtile[:, 0:512], ALU.mult, reads=[Rps, Rg], writes=[Rt1])
                        elif os.environ.get('MKX') == '3':
                            tt("vector", t1, ps[:, :], cs[:, 0, :], ALU.mult, reads=[Rps, Rcs, Rqb], writes=[Rt1])
                        elif os.environ.get('MKX') == '2':
                            cp("vector", t1, ps[:, :], reads=[Rps], writes=[Rt1])
                            tt("vector", t1, t1, cs[:, 0, :], ALU.mult, reads=[Rt1, Rcs], writes=[Rt1])
                        else:
                            tt("vector", t1, ps[:, :], cs[:, 0, :], ALU.mult, reads=[Rps, Rcs], writes=[Rt1])
                        if DBG < 5:
                            continue
                        ps2, Rps2 = PS[4 + (psi[0] % 2)], RPS[4 + (psi[0] % 2)]
                        mm(ps2[:, :], perm_b, qb_, True, True, reads=[Rqb, Rc], writes=[Rps2])
                        t2, Rt2 = evf_ring.next()
                        tt("vector", t2, ps2[:, :], cs[:, 1, :], ALU.mult, reads=[Rps2, Rcs], writes=[Rt2])
                        ob, Rob = ev_ring.next()
                        tt("gpsimd", ob, t1, t2, ALU.add, reads=[Rt1, Rt2], writes=[Rob])
                        dma(qkT[ch * 128:(ch + 1) * 128, t0 + tb * 512: t0 + (tb + 1) * 512], ob, reads=[Rob], writes=[R["qkT"]])
                if alim < 3:
                    continue
                load_w(2048, 2048)
                for m in range(NTL):
                    for cb0 in (0, 512):
                        ps, Rps = tm_matmuls(m, cb0)
                        ob, Rob = ev_ring.next()
                        cp("scalar", ob, ps[:, :], reads=[Rps], writes=[Rob])
                        dma(va[t0 + m * 128: t0 + (m + 1) * 128, cb0:cb0 + 512], ob, reads=[Rob], writes=[R["va"]])
                for tb in range(NB):
                    for ch in range(8):
                        ps, Rps = fm_matmuls(8 + ch, tb)
                        ob, Rob = ev_ring.next()
                        act(ob, ps[:, :], AF.Silu, reads=[Rps], writes=[Rob])
                        dma(zaT[ch * 128:(ch + 1) * 128, t0 + tb * 512: t0 + (tb + 1) * 512], ob, reads=[Rob], writes=[R["zaT"]])
                if alim < 4:
                    continue
                for part in range(3):
                    load_w(4096 + part * 1024, 1024)
                    for ch in range(8):
                        gch = part * 8 + ch
                        S.op("gpsimd", lambda e: e.memset(xc[:, 0:1], 0.0), writes=[Rxc])
                        S.op("gpsimd", lambda e, Sq=Sq: e.memset(xc[:, Sq + 1:Sq + 4], 0.0), writes=[Rxc])
                        for tb in range(NB):
                            ps, Rps = fm_matmuls(ch, tb)
                            cp("scalar", xc[:, 1 + tb * 512: 1 + (tb + 1) * 512], ps[:, :], reads=[Rps], writes=[Rxc])
                        ts("vector", xacc[:, 0:Sq], xc[:, 0:Sq], cwt[:, gch, 0:1], ALU.mult, reads=[Rxc, Rcw], writes=[Rxa])
                        for j in range(1, 4):
                            stt("vector", xacc[:, 0:Sq], xc[:, j:j + Sq], cwt[:, gch, j:j + 1], xacc[:, 0:Sq], ALU.mult, ALU.add,
                                reads=[Rxc, Rxa, Rcw], writes=[Rxa])
                        act(xacc[:, 0:Sq], xacc[:, 0:Sq], AF.Silu, reads=[Rxa], writes=[Rxa])
                        for tb in range(NB):
                            sl = slice(tb * 512, (tb + 1) * 512)
                            ob, Rob = ev_ring.next()
                            if part < 2:
                                sq, Rsq = evf_ring.next()
                                tt("gpsimd", sq, xacc[:, sl], xacc[:, sl], ALU.mult, reads=[Rxa], writes=[Rsq])
                                ps, Rps = nextps()
                                mm(ps[:, :], ones_f, sq, True, True, reads=[Rsq, Rc], writes=[Rps])
                                rn, Rrn = evf_ring.next()
                                act(rn, ps[:, :], AF.Sqrt, reads=[Rps], writes=[Rrn], bias=(128.0 * EPS if part == 0 else EPS),
                                    scale=(128.0 if part == 0 else 1.0))
                                S.op("vector", lambda e, rn=rn: e.reciprocal(out=rn, in_=rn), reads=[Rrn], writes=[Rrn])
                                tt("vector", ob, xacc[:, sl], rn, ALU.mult, reads=[Rxa, Rrn], writes=[Rob])
                                dst = qbT if part == 0 else kbT
                                dma(dst[ch * 128:(ch + 1) * 128, t0 + tb * 512: t0 + (tb + 1) * 512], ob, reads=[Rob],
                                    writes=[R["qbT" if part == 0 else "kbT"]])
                            else:
                                cp("vector", ob, xacc[:, sl], reads=[Rxa], writes=[Rob])
                            if part >= 1:
                                ps, Rps = nextps()
                                psb = ps[:, :].bitcast(BF16)
                                for q4 in range(4):
                                    tr(psb[:, q4 * 128:(q4 + 1) * 128], ob[:, q4 * 128:(q4 + 1) * 128], ident_b, reads=[Rob], writes=[Rps])
                                o2, Ro2 = ev_ring.next()
                                cp("vector", o2, psb[:, 0:512], reads=[Rps], writes=[Ro2])
                                dst = ktm if part == 1 else vtm
                                dma(dst[t0 + tb * 512: t0 + (tb + 1) * 512, ch * 128:(ch + 1) * 128].rearrange("(q p) f -> p q f", p=128),
                                    o2.rearrange("p (q f) -> p q f", q=4), reads=[Ro2], writes=[R["ktm" if part == 1 else "vtm"]])
                if alim < 5:
                    continue
                load_w(7168, 1056)
                for m in range(NTL):
                    for cb0 in (0, 512):
                        ps, Rps = tm_matmuls(m, cb0)
                        ob, Rob = ev_ring.next()
                        act(ob, ps[:, :], AF.Silu, reads=[Rps], writes=[Rob])
                        dma(zb[t0 + m * 128: t0 + (m + 1) * 128, cb0:cb0 + 512], ob, reads=[Rob], writes=[R["zb"]])
                    ps, Rps = tm_matmuls(m, 1024, n=32)
                    abt, Rab = ab_ring.next()
                    cp("vector", abt, ps[:, 0:32], reads=[Rps], writes=[Rab])
                    dma(ab[t0 + m * 128: t0 + (m + 1) * 128, :], abt, reads=[Rab], writes=[R["ab"]])
                if alim < 6:
                    continue
                load_w(8224, 2048)
                for tb in range(NB):
                    for ch in range(16):
                        ps, Rps = fm_matmuls(ch, tb)
                        ob, Rob = ev_ring.next()
                        act(ob, ps[:, :], AF.Sigmoid, reads=[Rps], writes=[Rob])
                        dma(gT[ch * 128:(ch + 1) * 128, t0 + tb * 512: t0 + (tb + 1) * 512], ob, reads=[Rob], writes=[R["gT"]])
            S.barrier()

        if "B" in phases:
          with contextlib.ExitStack() as st:
            Ring.stack = st

            def sb(name, shape, dt):
                return st.enter_context(nc.sbuf_tensor(f"B{l}_{name}", shape, dt))
            SMAX = max(seqs)
            lq = sb("lq", [128, 256], F32)
            lsc = sb("lsc", [128, 8], F32)
            Rl = Res("lam")
            dma(lq[:], lam_qk[l].broadcast_to([128, 256]), writes=[Rl])
            tt("vector", lq[:, 0:64], lq[:, 0:64], lq[:, 64:128], ALU.mult, reads=[Rl], writes=[Rl])
            tt("vector", lq[:, 128:192], lq[:, 128:192], lq[:, 192:256], ALU.mult, reads=[Rl], writes=[Rl])
            S.op("vector", lambda e: e.tensor_reduce(out=lsc[:, 0:1], in_=lq[:, 0:64], axis=AX.X, op=ALU.add), reads=[Rl], writes=[Rl])
            S.op("vector", lambda e: e.tensor_reduce(out=lsc[:, 1:2], in_=lq[:, 128:192], axis=AX.X, op=ALU.add), reads=[Rl], writes=[Rl])
            act(lsc[:, 2:4], lsc[:, 0:2], AF.Exp, reads=[Rl], writes=[Rl])
            tt("vector", lsc[:, 4:5], lsc[:, 3:4], lsc[:, 2:3], ALU.subtract, reads=[Rl], writes=[Rl])
            ts("vector", lsc[:, 5:6], lsc[:, 4:5], -lam_init, ALU.add, reads=[Rl], writes=[Rl])
            neglam = lsc[:, 5:6]
            dgt = sb("dgt", [128, 2], F32)
            dma(dgt[:, 0:1], dng[l], writes=[Rl])
            ts("vector", dgt[:, 1:2], dgt[:, 0:1], (1.0 - lam_init), ALU.mult, reads=[Rl], writes=[Rl])
            QT = sb("QT", [128, SMAX], BF16)
            KT = sb("KT", [128, SMAX], BF16)
            V = sb("V", [128, SMAX // 128, 128], BF16)
            RQ, RK, RV = Res("QT"), Res("KT"), Res("V")
            pt_ring = Ring(nc, f"B{l}_pt", [128, 512], BF16, 4)
            f_ring = Ring(nc, f"B{l}_f", [128, 512], F32, 6)
            z_ring = Ring(nc, f"B{l}_z", [128, 512], BF16, 2)
            y_ring = Ring(nc, f"B{l}_y", [128, 512], BF16, 2)
            scale = 64 ** -0.5
            spi = 0
            for si, Sq in enumerate(seqs):
                t0 = seq_off[si]
                NKT = Sq // 128
                NB = Sq // 512
                for h in range(8):
                    dma(QT[:, 0:Sq], qkT[h * 128:(h + 1) * 128, t0:t0 + Sq], reads=[R["qkT"]], writes=[RQ])
                    dma(KT[:, 0:Sq], qkT[1024 + h * 128: 1024 + (h + 1) * 128, t0:t0 + Sq], reads=[R["qkT"]], writes=[RK])
                    dma(V[:, 0:NKT, :], va[t0:t0 + Sq, h * 128:(h + 1) * 128].rearrange("(k p) f -> p k f", p=128),
                        reads=[R["va"]], writes=[RV])
                    for qb_ in range(NB):
                        qs = slice(qb_ * 512, (qb_ + 1) * 512)
                        zt, Rz = z_ring.next()
                        dma(zt, zaT[h * 128:(h + 1) * 128, t0 + qb_ * 512: t0 + (qb_ + 1) * 512], reads=[R["zaT"]], writes=[Rz])
                        for kt in range(NKT):
                            for c in range(2):
                                sp, Rsp = PS[spi % 2], RPS[spi % 2]
                                spi += 1
                                mm(sp[:, :], KT[c * 64:(c + 1) * 64, kt * 128:(kt + 1) * 128], QT[c * 64:(c + 1) * 64, qs],
                                   True, True, reads=[RK, RQ], writes=[Rsp])
                                pt, Rpt = pt_ring.next()
                                act(pt, sp[:, :], AF.Exp, reads=[Rsp], writes=[Rpt], scale=scale)
                                mm(PS[2 + 2 * c][:, :], V[:, kt, :], pt, kt == 0, kt == NKT - 1, reads=[RV, Rpt], writes=[RPS[2 + 2 * c]])
                                mm(PS[3 + 2 * c][:, :], ones_b, pt, kt == 0, kt == NKT - 1, reads=[Rc, Rpt], writes=[RPS[3 + 2 * c]])
                        r0, Rr0 = f_ring.next()
                        S.op("vector", lambda e, r0=r0: e.reciprocal(out=r0, in_=PS[3][:, :]), reads=[RPS[3]], writes=[Rr0])
                        t0_, Rt0 = f_ring.next()
                        tt("vector", t0_, PS[2][:, :], r0, ALU.mult, reads=[RPS[2], Rr0], writes=[Rt0])
                        r1, Rr1 = f_ring.next()
                        S.op("vector", lambda e, r1=r1: e.reciprocal(out=r1, in_=PS[5][:, :]), reads=[RPS[5]], writes=[Rr1])
                        t1_, Rt1 = f_ring.next()
                        tt("vector", t1_, PS[4][:, :], r1, ALU.mult, reads=[RPS[4], Rr1], writes=[Rt1])
                        o_, Ro = f_ring.next()
                        stt("gpsimd", o_, t1_, neglam, t0_, ALU.mult, ALU.add, reads=[Rt1, Rt0, Rl], writes=[Ro])
                        sq_, Rsq = f_ring.next()
                        tt("gpsimd", sq_, o_, o_, ALU.mult, reads=[Ro], writes=[Rsq])
                        mm(PS[6][:, :], ones_f, sq_, True, True, reads=[Rsq, Rc], writes=[RPS[6]])
                        rn, Rrn = f_ring.next()
                        act(rn, PS[6][:, :], AF.Sqrt, reads=[RPS[6]], writes=[Rrn], scale=1.0 / 128, bias=SUBLN_EPS)
                        S.op("vector", lambda e, rn=rn: e.reciprocal(out=rn, in_=rn), reads=[Rrn], writes=[Rrn])
                        stt("vector", o_, o_, dgt[:, 1:2], rn, ALU.mult, ALU.mult, reads=[Ro, Rrn, Rl], writes=[Ro])
                        yt, Ry = y_ring.next()
                        tt("gpsimd", yt, o_, zt, ALU.mult, reads=[Ro, Rz], writes=[Ry])
                        dma(yaT[h * 128:(h + 1) * 128, t0 + qb_ * 512: t0 + (qb_ + 1) * 512], yt, reads=[Ry], writes=[R["yaT"]])
            S.barrier()

        if "C" in phases:
          with contextlib.ExitStack() as st:
            Ring.stack = st

            def sb(name, shape, dt):
                return st.enter_context(nc.sbuf_tensor(f"C{l}_{name}", shape, dt))
            SMAX = max(seqs)
            NMAX = SMAX // 64
            Rp = Res("gparams")
            alb = sb("alb", [64, 16], F32)
            dtb = sb("dtb", [64, 16], F32)
            dma(alb[:], a_log[l].broadcast_to([64, 16]), writes=[Rp])
            dma(dtb[:], dt_bias[l].broadcast_to([64, 16]), writes=[Rp])
            act(alb[:], alb[:], AF.Exp, reads=[Rp], writes=[Rp])
            ts("vector", alb[:], alb[:], -1.0, ALU.mult, reads=[Rp], writes=[Rp])
            gnb = sb("gnb", [64, 128], F32)
            dma(gnb[:], gng[l].broadcast_to([64, 128]), writes=[Rp])
            ABt = sb("ABt", [64, NMAX, 32], F32)
            G = sb("G", [64, NMAX, 16], F32)
            NBt = sb("NBt", [64, NMAX, 16], F32)
            RG = Res("G")
            O2 = [sb("Of", [64, NMAX, 128], F32), sb("Ob", [64, NMAX, 128], F32)]
            RO = [Res("Of"), Res("Ob")]
            Sst = [sb("Sf", [128, 128], F32), sb("Sb", [128, 128], F32)]
            Sbf = [sb("Sfb", [128, 128], BF16), sb("Sbb", [128, 128], BF16)]
            RS = [Res("Sf"), Res("Sb")]
            RSb = [Res("Sfb"), Res("Sbb")]
            vn_ring = [Ring(nc, f"C{l}_vn{d}", [64, 128], BF16, 2) for d in range(2)]

            def gb(name, shape, dt):
                return [Ring(nc, f"C{l}_{name}{d}", shape, dt, 1) for d in range(2)]
            W8 = 8 * 64
            qT_r, kT_r = gb("qT", [128, W8], BF16), gb("kT", [128, W8], BF16)
            kt_r, vt_r = gb("kt", [64, 8, 128], BF16), gb("vt", [64, 8, 128], BF16)
            GU_r = gb("GU", [64, W8], F32)
            GCc_r = gb("GCc", [64, 8, 4], F32)
            GCr_r = gb("GCr", [128, W8], F32)
            Dm_r = gb("Dm", [64, W8], F32)
            Ei_r = gb("Ei", [64, W8], F32)
            at_r = gb("at", [64, W8], BF16)
            P_r = [gb("Pa", [64, W8], F32), gb("Pb", [64, W8], F32)]
            PT_r = [gb("PTa", [64, W8], F32), gb("PTb", [64, W8], F32)]
            Y_r = gb("Y", [64, W8], F32)
            TT_r = gb("TT", [64, W8], BF16)
            KG_r, KD_r = gb("KG", [64, 8, 128], BF16), gb("KD", [64, 8, 128], BF16)
            Ub_r = gb("Ub", [64, 8, 128], F32)
            WT_r, QG_r = gb("WT", [128, W8], BF16), gb("QG", [128, W8], BF16)
            gl_r = gb("gl", [128, 8], F32)
            yb_r = Ring(nc, f"C{l}_yb", [64, 8, 128], F32, 2)
            ybb_r = Ring(nc, f"C{l}_ybb", [64, 8, 128], BF16, 2)
            zb_r = Ring(nc, f"C{l}_zb", [64, 8, 128], BF16, 2)
            rs_r = Ring(nc, f"C{l}_rs", [64, 8, 2], F32, 2)
            yo_r = Ring(nc, f"C{l}_yo", [128, W8], BF16, 2)

            def bc3(ap2, n):
                return ap2.unsqueeze(2).broadcast_to([ap2.shape[0], 8, n])

            def v3(ap, inner):
                return ap.rearrange("p (n i) -> p n i", n=8)

            for si, Sq in enumerate(seqs):
                t0 = seq_off[si]
                N = Sq // 64
                NG = N // 8
                dma(ABt[:, 0:N, :], ab[t0:t0 + Sq, :].rearrange("(n t) c -> t n c", t=64), reads=[R["ab"]], writes=[RG], q="sync")
                al_b = alb[:, :].unsqueeze(1).broadcast_to([64, N, 16])
                dt_b = dtb[:, :].unsqueeze(1).broadcast_to([64, N, 16])
                tt("vector", G[:, 0:N, :], ABt[:, 0:N, 0:16], dt_b, ALU.add, reads=[RG, Rp], writes=[RG])
                act(G[:, 0:N, :], G[:, 0:N, :], AF.Exp, reads=[RG], writes=[RG])
                act(G[:, 0:N, :], G[:, 0:N, :], AF.Ln, reads=[RG], writes=[RG], bias=1.0)
                tt("vector", G[:, 0:N, :], G[:, 0:N, :], al_b, ALU.mult, reads=[RG, Rp], writes=[RG])
                act(NBt[:, 0:N, :], ABt[:, 0:N, 16:32], AF.Sigmoid, reads=[RG], writes=[RG])
                ts("vector", NBt[:, 0:N, :], NBt[:, 0:N, :], -1.0, ALU.mult, reads=[RG], writes=[RG])
                for h in range(8):
                    for d in range(2):
                        S.op("gpsimd", lambda e, d=d: e.memset(Sst[d][:], 0.0), writes=[RS[d]])
                        S.op("gpsimd", lambda e, d=d: e.memset(Sbf[d][:], 0.0), writes=[RSb[d]])
                    for gi in range(NG):
                        prep = {}
                        for d in range(2):
                            g = gi if d == 0 else NG - 1 - gi
                            col = d * 8 + h
                            n0 = g * 8
                            tok = t0 + g * 512
                            last = 63 if d == 0 else 0
                            qT, RqT = qT_r[d].next()
                            kT, RkT = kT_r[d].next()
                            kt_, Rkt = kt_r[d].next()
                            vt_, Rvt = vt_r[d].next()
                            dma(qT, qbT[h * 128:(h + 1) * 128, tok:tok + 512], reads=[R["qbT"]], writes=[RqT])
                            dma(kT, kbT[h * 128:(h + 1) * 128, tok:tok + 512], reads=[R["kbT"]], writes=[RkT])
                            dma(kt_, ktm[tok:tok + 512, h * 128:(h + 1) * 128].rearrange("(n t) f -> t n f", t=64), reads=[R["ktm"]], writes=[Rkt])
                            dma(vt_, vtm[tok:tok + 512, h * 128:(h + 1) * 128].rearrange("(n t) f -> t n f", t=64), reads=[R["vtm"]], writes=[Rvt])
                            Gc = G[:, n0:n0 + 8, col]
                            NBc = NBt[:, n0:n0 + 8, col]
                            GCc, RGCc = GCc_r[d].next()
                            mm(PS[0][0:64, 0:8], MI[d], Gc, True, True, reads=[RG, Rc], writes=[RPS[0]])
                            cp("vector", GCc[:, :, 0], PS[0][0:64, 0:8], reads=[RPS[0]], writes=[RGCc])
                            GU, RGU = GU_r[d].next()
                            tt("gpsimd", v3(GU, 64), bc3(Gc, 64), MI[d].unsqueeze(1).broadcast_to([64, 8, 64]), ALU.mult,
                               reads=[RG, Rc], writes=[RGU])
                            mm(PS[1][:, :], ones_f[0:64, :], GU, True, True, reads=[RGU, Rc], writes=[RPS[1]])
                            GCr, RGCr = GCr_r[d].next()
                            cp("scalar", GCr, PS[1][:, :], reads=[RPS[1]], writes=[RGCr])
                            Dm, RDm = Dm_r[d].next()
                            tt("vector", v3(Dm, 64), v3(GCr[0:64, :], 64), bc3(GCc[:, :, 0], 64), ALU.subtract, reads=[RGCr, RGCc], writes=[RDm])
                            ts("gpsimd", Dm, Dm, 0.0, ALU.min, reads=[RDm], writes=[RDm])
                            act(Dm, Dm, AF.Exp, reads=[RDm], writes=[RDm])
                            Ei, REi = Ei_r[d].next()
                            tt("gpsimd", v3(Ei, 64), v3(Dm, 64), MI[d].unsqueeze(1).broadcast_to([64, 8, 64]), ALU.mult, reads=[RDm, Rc], writes=[REi])
                            act(GCr, GCr, AF.Exp, reads=[RGCr], writes=[RGCr])
                            gl, Rgl = gl_r[d].next()
                            cp("vector", gl, v3(GCr, 64)[:, :, last], reads=[RGCr], writes=[Rgl])
                            act(GCc[:, :, 1], GCc[:, :, 0], AF.Exp, reads=[RGCc], writes=[RGCc])
                            act(GCc[:, :, 2], GCc[:, :, 0], AF.Identity, reads=[RGCc], writes=[RGCc], scale=-1.0)
                            for n in range(8):
                                cs_ = slice(n * 64, (n + 1) * 64)
                                mm(PS[2][0:64, cs_], kT[:, cs_], kT[:, cs_], True, True, reads=[RkT], writes=[RPS[2]])
                            for n in range(8):
                                cs_ = slice(n * 64, (n + 1) * 64)
                                mm(PS[3][0:64, cs_], kT[:, cs_], qT[:, cs_], True, True, reads=[RkT, RqT], writes=[RPS[3]])
                            at_, Rat = at_r[d].next()
                            tt("vector", at_, PS[3][0:64, :], Ei, ALU.mult, reads=[RPS[3], REi], writes=[Rat])
                            P0, RP0 = P_r[0][d].next()
                            tt("vector", P0, PS[2][0:64, :], Ei, ALU.mult, reads=[RPS[2], REi], writes=[RP0])
                            tt("gpsimd", v3(P0, 64), v3(P0, 64), MS[d].unsqueeze(1).broadcast_to([64, 8, 64]), ALU.mult, reads=[RP0, Rc], writes=[RP0])
                            tt("gpsimd", v3(P0, 64), v3(P0, 64), bc3(NBc, 64), ALU.mult, reads=[RP0, RG], writes=[RP0])
                            prep[d] = dict(g=g, n0=n0, col=col, last=last, qT=(qT, RqT), kT=(kT, RkT), kt=(kt_, Rkt), vt=(vt_, Rvt),
                                           GCc=(GCc, RGCc), GCr=(GCr, RGCr), gl=(gl, Rgl), at=(at_, Rat), P0=(P0, RP0), NBc=NBc)
                        cur = {}
                        for d in range(2):
                            P0, RP0 = prep[d]["P0"]
                            pb = 4 * d
                            for n in range(8):
                                cs_ = slice(n * 64, (n + 1) * 64)
                                tr(PS[pb][0:64, cs_], P0[:, cs_], ident_f[0:64, 0:64], reads=[RP0], writes=[RPS[pb]])
                            PT0, RPT0 = PT_r[0][d].next()
                            cp("scalar", PT0, PS[pb][0:64, :], reads=[RPS[pb]], writes=[RPT0])
                            Y, RY = Y_r[d].next()
                            tt("gpsimd", v3(Y, 64), v3(P0, 64), ident_f[0:64, 0:64].unsqueeze(1).broadcast_to([64, 8, 64]), ALU.add, reads=[RP0, Rc], writes=[RY])
                            cur[d] = [P0, RP0, PT0, RPT0, Y, RY]
                        for k in range(0, 6):
                            for d in range(2):
                                P, RP, PT, RPT, Y, RY = cur[d]
                                pb = 4 * d
                                if k < 5:
                                    for n in range(8):
                                        cs_ = slice(n * 64, (n + 1) * 64)
                                        mm(PS[pb + 1][0:64, cs_], PT[:, cs_], P[:, cs_], True, True, reads=[RP, RPT], writes=[RPS[pb + 1]])
                                    for n in range(8):
                                        cs_ = slice(n * 64, (n + 1) * 64)
                                        mm(PS[pb + 2][0:64, cs_], P[:, cs_], PT[:, cs_], True, True, reads=[RP, RPT], writes=[RPS[pb + 2]])
                                if k >= 1:
                                    for n in range(8):
                                        cs_ = slice(n * 64, (n + 1) * 64)
                                        mm(PS[pb + 3][0:64, cs_], PT[:, cs_], Y[:, cs_], True, True, reads=[RY, RPT], writes=[RPS[pb + 3]])
                                    tt("vector", Y, Y, PS[pb + 3][0:64, :], ALU.add, reads=[RY, RPS[pb + 3]], writes=[RY])
                                if k < 5:
                                    Pn, RPn = P_r[(k + 1) % 2][d].next()
                                    PTn, RPTn = PT_r[(k + 1) % 2][d].next()
                                    cp("scalar", Pn, PS[pb + 1][0:64, :], reads=[RPS[pb + 1]], writes=[RPn])
                                    cp("vector", PTn, PS[pb + 2][0:64, :], reads=[RPS[pb + 2]], writes=[RPTn])
                                    cur[d] = [Pn, RPn, PTn, RPTn, Y, RY]
                        for d in range(2):
                            pr = prep[d]
                            pb = 4 * d
                            Y, RY = cur[d][4], cur[d][5]
                            TT, RTT = TT_r[d].next()
                            cp("gpsimd", TT, Y, reads=[RY], writes=[RTT])
                            kt_, Rkt = pr["kt"]
                            vt_, Rvt = pr["vt"]
                            GCc, RGCc = pr["GCc"]
                            GCr, RGCr = pr["GCr"]
                            gl, Rgl = pr["gl"]
                            qT, RqT = pr["qT"]
                            KG, RKG = KG_r[d].next()
                            tt("gpsimd", KG, kt_, bc3(GCc[:, :, 1], 128), ALU.mult, reads=[Rkt, RGCc], writes=[RKG])
                            pr["KG"] = (KG, RKG)
                            pr["TT"] = (TT, RTT)
                        prep_done = prep
                        for d in range(2):
                            pr = prep_done[d]
                            pb = 4 * d
                            TT, RTT = pr["TT"]
                            KG, RKG = pr["KG"]
                            kt_, Rkt = pr["kt"]
                            vt_, Rvt = pr["vt"]
                            GCc, RGCc = pr["GCc"]
                            GCr, RGCr = pr["GCr"]
                            gl, Rgl = pr["gl"]
                            qT, RqT = pr["qT"]
                            last = pr["last"]
                            NBc = pr["NBc"]
                            KD, RKD = KD_r[d].next()
                            mm(PS[pb][0:64, 0:8], ones_f[0:64, 0:64], G[:, pr["n0"]:pr["n0"] + 8, pr["col"]], True, True, reads=[RG, Rc], writes=[RPS[pb]])
                            tt("vector", GCc[:, :, 3], PS[pb][0:64, 0:8], GCc[:, :, 0], ALU.subtract, reads=[RPS[pb], RGCc], writes=[RGCc])
                            act(GCc[:, :, 3], GCc[:, :, 3], AF.Exp, reads=[RGCc], writes=[RGCc])
                            tt("gpsimd", KD, kt_, bc3(GCc[:, :, 3], 128), ALU.mult, reads=[Rkt, RGCc], writes=[RKD])
                            Ub, RUb = Ub_r[d].next()
                            for half in range(2):
                                for n in range(4):
                                    nn = half * 4 + n
                                    mm(PS[pb + 1][0:64, n * 128:(n + 1) * 128], TT[:, nn * 64:(nn + 1) * 64], vt_[:, nn, :], True, True,
                                       reads=[RTT, Rvt], writes=[RPS[pb + 1]])
                                tt("vector", Ub[:, half * 4:(half + 1) * 4, :], PS[pb + 1][0:64, :].rearrange("p (n f) -> p n f", n=4),
                                   NBc[:, half * 4:(half + 1) * 4].unsqueeze(2).broadcast_to([64, 4, 128]), ALU.mult,
                                   reads=[RPS[pb + 1], RG], writes=[RUb])
                            WT, RWT = WT_r[d].next()
                            for n in range(8):
                                cs_ = slice(n * 64, (n + 1) * 64)
                                mm(PS[pb + 2][:, cs_], KG[:, n, :], TT[:, cs_], True, True, reads=[RKG, RTT], writes=[RPS[pb + 2]])
                            cp("scalar", WT, PS[pb + 2][:, :], reads=[RPS[pb + 2]], writes=[RWT])
                            QG, RQG = QG_r[d].next()
                            tt("vector", QG, qT, GCr, ALU.mult, reads=[RqT, RGCr], writes=[RQG])
                            pr.update(KD=(KD, RKD), Ub=(Ub, RUb), WT=(WT, RWT), QG=(QG, RQG))
                        for step in range(8):
                            for d in range(2):
                                pr = prep_done[d]
                                pb = 4 * d
                                n = step if d == 0 else 7 - step
                                ng = pr["n0"] + n
                                cs_ = slice(n * 64, (n + 1) * 64)
                                WT, RWT = pr["WT"]
                                QG, RQG = pr["QG"]
                                KD, RKD = pr["KD"]
                                Ub, RUb = pr["Ub"]
                                at_, Rat = pr["at"]
                                gl, Rgl = pr["gl"]
                                NBc = pr["NBc"]
                                psn = PS[pb + 3]
                                Rpsn = RPS[pb + 3]
                                mm(psn[0:64, 0:128], WT[:, cs_], Sbf[d][:], True, True, reads=[RWT, RSb[d]], writes=[Rpsn])
                                vn, Rvn = vn_ring[d].next()
                                stt("vector", vn, psn[0:64, 0:128], NBc[:, n:n + 1], Ub[:, n, :], ALU.mult, ALU.subtract,
                                    reads=[Rpsn, RUb, RG], writes=[Rvn])
                                mm(psn[0:64, 128:256], QG[:, cs_], Sbf[d][:], True, False, reads=[RQG, RSb[d]], writes=[Rpsn])
                                mm(psn[0:64, 128:256], at_[:, cs_], vn, False, True, reads=[Rat, Rvn], writes=[Rpsn])
                                mm(psn[:, 256:384], KD[:, n, :], vn, True, True, reads=[RKD, Rvn], writes=[Rpsn])
                                cp("scalar", O2[d][:, ng, :], psn[0:64, 128:256], reads=[Rpsn], writes=[RO[d]])
                                stt("vector", Sst[d][:], Sst[d][:], gl[:, n:n + 1], psn[:, 256:384], ALU.mult, ALU.add,
                                    reads=[RS[d], Rgl, Rpsn], writes=[RS[d]])
                                cp("gpsimd", Sbf[d][:], Sst[d][:], reads=[RS[d]], writes=[RSb[d]])
                    for g in range(NG):
                        n0 = g * 8
                        tok = t0 + g * 512
                        yb_, Ryb = yb_r.next()
                        tt("vector", yb_, O2[0][:, n0:n0 + 8, :], O2[1][:, n0:n0 + 8, :], ALU.add, reads=[RO[0], RO[1]], writes=[Ryb])
                        ybb, Rybb = ybb_r.next()
                        rs, Rrs = rs_r.next()
                        sqf, Rsqf = Ub_r[0].next()
                        tt("gpsimd", sqf, yb_, yb_, ALU.mult, reads=[Ryb], writes=[Rsqf])
                        S.op("vector", lambda e, rs=rs, sqf=sqf: e.tensor_reduce(out=rs[:, :, 0], in_=sqf, axis=AX.X, op=ALU.add),
                             reads=[Rsqf], writes=[Rrs])
                        act(rs[:, :, 1], rs[:, :, 0], AF.Sqrt, reads=[Rrs], writes=[Rrs], scale=1.0 / 128, bias=EPS)
                        S.op("vector", lambda e, rs=rs: e.reciprocal(out=rs[:, :, 1], in_=rs[:, :, 1]), reads=[Rrs], writes=[Rrs])
                        tt("vector", yb_, yb_, bc3(rs[:, :, 1], 128), ALU.mult, reads=[Ryb, Rrs], writes=[Ryb])
                        tt("gpsimd", yb_, yb_, gnb[:, :].unsqueeze(1).broadcast_to([64, 8, 128]), ALU.mult, reads=[Ryb, Rp], writes=[Ryb])
                        zt, Rz = zb_r.next()
                        dma(zt, zb[tok:tok + 512, h * 128:(h + 1) * 128].rearrange("(n t) f -> t n f", t=64), reads=[R["zb"]], writes=[Rz])
                        tt("vector", ybb, yb_, zt, ALU.mult, reads=[Ryb, Rz], writes=[Rybb])
                        psb = PS[0][:, :].bitcast(BF16)
                        for n in range(8):
                            tr(psb[:, n * 64:(n + 1) * 64], ybb[:, n, :], ident_b[0:64, 0:64], reads=[Rybb], writes=[RPS[0]])
                        yo, Ryo = yo_r.next()
                        cp("scalar", yo, psb[:, 0:512], reads=[RPS[0]], writes=[Ryo])
                        dma(ybT[h * 128:(h + 1) * 128, tok:tok + 512], yo, reads=[Ryo], writes=[R["ybT"]])
            S.barrier()

        if "D" in phases:
          with contextlib.ExitStack() as st:
            Ring.stack = st

            def sb(name, shape, dt):
                return st.enter_context(nc.sbuf_tensor(f"D{l}_{name}", shape, dt))
            Wm = [sb(f"W{i}", [128, 8, D], BF16) for i in range(3)]
            RW = Res("Wm")
            wst_ring = Ring(nc, f"D{l}_wst", [128, 8, 512], F32, 2)
            srcs = [w_br[l, 0], w_br[l, 1], w_out[l]]
            for i in range(3):
                wv = srcs[i].rearrange("(c p) n -> p c n", p=128)
                for b in range(2):
                    wst, Rwst = wst_ring.next()
                    dma(wst, wv[:, :, b * 512:(b + 1) * 512], writes=[Rwst])
                    cp("gpsimd", Wm[i][:, :, b * 512:(b + 1) * 512], wst, reads=[Rwst], writes=[RW])
            fgt = sb("fgt", [128, D], F32)
            dma(fgt[:], final_g[0:1, :].broadcast_to([128, D]), writes=[RW])
            ya_r = Ring(nc, f"D{l}_ya", [128, 8, 512], BF16, 2)
            yb_r2 = Ring(nc, f"D{l}_yb", [128, 8, 512], BF16, 2)
            g_r = Ring(nc, f"D{l}_g", [128, 16, 512], BF16, 2)
            mT_r = Ring(nc, f"D{l}_mT", [128, 8, 512], BF16, 2)
            f_r = Ring(nc, f"D{l}_f", [128, 512], F32, 4)
            x_r = Ring(nc, f"D{l}_x", [128, D], F32, 2)
            xo_r = Ring(nc, f"D{l}_xo", [128, D], F32, 2)
            ss_r = Ring(nc, f"D{l}_ss", [128, 2], F32, 2)
            junk = sb("junk", [128, D], BF16)
            Rj = Res("junk")
            pi = 0
            for tb in range(NT // 512):
                tok = tb * 512
                yat, Rya = ya_r.next()
                ybt, Ryb = yb_r2.next()
                gt, Rgt = g_r.next()
                dma(yat, yaT[:, tok:tok + 512].rearrange("(c p) t -> p c t", p=128), reads=[R["yaT"]], writes=[Rya])
                dma(ybt, ybT[:, tok:tok + 512].rearrange("(c p) t -> p c t", p=128), reads=[R["ybT"]], writes=[Ryb])
                dma(gt, gT[:, tok:tok + 512].rearrange("(c p) t -> p c t", p=128), reads=[R["gT"]], writes=[Rgt])
                mT, RmT = mT_r.next()
                for dm in range(8):
                    pa, Rpa = PS[pi % 2], RPS[pi % 2]
                    pb_, Rpb = PS[2 + pi % 2], RPS[2 + pi % 2]
                    pi += 1
                    for c in range(8):
                        mm(pa[:, :], Wm[0][:, c, dm * 128:(dm + 1) * 128], yat[:, c, :], c == 0, c == 7, reads=[RW, Rya], writes=[Rpa])
                    for c in range(8):
                        mm(pb_[:, :], Wm[1][:, c, dm * 128:(dm + 1) * 128], ybt[:, c, :], c == 0, c == 7, reads=[RW, Ryb], writes=[Rpb])
                    f0, Rf0 = f_r.next()
                    f1, Rf1 = f_r.next()
                    tt("vector", f0, pa[:, :], gt[:, dm, :], ALU.mult, reads=[Rpa, Rgt], writes=[Rf0])
                    tt("vector", f1, pb_[:, :], gt[:, 8 + dm, :], ALU.mult, reads=[Rpb, Rgt], writes=[Rf1])
                    tt("gpsimd", mT[:, dm, :], f0, f1, ALU.add, reads=[Rf0, Rf1], writes=[RmT])
                for m in range(4):
                    xt, Rxt = x_r.next()
                    dma(xt, xsrc[tok + m * 128: tok + (m + 1) * 128, :], reads=Rxsrc, writes=[Rxt])
                    xo, Rxo = xo_r.next()
                    for cb0 in (0, 512):
                        po, Rpo = PS[4 + pi % 2], RPS[4 + pi % 2]
                        pi += 1
                        for c in range(8):
                            mm(po[:, :], mT[:, c, m * 128:(m + 1) * 128], Wm[2][:, c, cb0:cb0 + 512], c == 0, c == 7, reads=[RW, RmT], writes=[Rpo])
                        tt("vector", xo[:, cb0:cb0 + 512], po[:, :], xt[:, cb0:cb0 + 512], ALU.add, reads=[Rpo, Rxt], writes=[Rxo])
                    if l < DEPTH - 1:
                        dma(x1[tok + m * 128: tok + (m + 1) * 128, :], xo, reads=[Rxo], writes=[R["x1"]])
                    else:
                        ss, Rss = ss_r.next()
                        act(junk[:], xo, AF.Square, reads=[Rxo], writes=[Rj, Rss], accum_out=ss[:, 0:1])
                        act(ss[:, 1:2], ss[:, 0:1], AF.Sqrt, reads=[Rss], writes=[Rss], scale=1.0 / D, bias=EPS)
                        S.op("vector", lambda e, ss=ss: e.reciprocal(out=ss[:, 1:2], in_=ss[:, 1:2]), reads=[Rss], writes=[Rss])
                        stt("vector", xo, xo, ss[:, 1:2], fgt[:], ALU.mult, ALU.mult, reads=[Rxo, Rss, RW], writes=[Rxo])
                        dma(yout[tok + m * 128: tok + (m + 1) * 128, :], xo, reads=[Rxo], writes=[R["yout"]])
            S.barrier()
    S.emit()
    return nc


def make_consts():
    inv = (10000.0 ** (-np.arange(0, 64, 2, dtype=np.float32) / 64)).astype(np.float32)
    pos = np.arange(4096, dtype=np.float32)
    ang = pos[:, None] * inv[None, :]
    ang = np.concatenate([ang, ang], -1)
    cos = np.cos(ang).astype(np.float32).T
    sin = np.sin(ang).astype(np.float32).T
    sgn = np.concatenate([-np.ones(32, np.float32), np.ones(32, np.float32)])[:, None]
    cosT = np.concatenate([cos, cos], 0)
    sinT = np.concatenate([sin * sgn, sin * sgn], 0)
    idx = np.arange(64)
    cf = np.zeros((128, 6, 128), np.float32)
    cf[:, 0, :] = np.eye(128, dtype=np.float32)
    cf[:, 1, :] = 1.0
    cf[:64, 2, :64] = (idx[:, None] <= idx[None, :])
    cf[:64, 3, :64] = (idx[:, None] < idx[None, :])
    cf[:64, 4, :64] = (idx[:, None] >= idx[None, :])
    cf[:64, 5, :64] = (idx[:, None] > idx[None, :])
    cbm = np.zeros((128, 3, 128), np.float32)
    cbm[:, 0, :] = np.eye(128)
    cbm[:, 1, :] = 1.0
    perm = np.zeros((128, 128), np.float32)
    for f in range(128):
        base = (f // 64) * 64
        dd = f % 64
        k = base + (dd + 32) % 64
        perm[k, f] = 1.0
    cbm[:, 2, :] = perm
    return np.ascontiguousarray(cosT), np.ascontiguousarray(sinT), cf, cbm.astype(ml_dtypes.bfloat16)


def shared_inputs(norm_g, w_in, conv_w, lam_qk, diff_norm_g, a_log, dt_bias, gdn_norm_g, w_branch, w_out, final_g):
    cosT, sinT, cf, cbm = make_consts()
    f = lambda a: np.ascontiguousarray(np.asarray(a, dtype=np.float32))
    convw = f(conv_w).reshape(DEPTH, 4, 24, 128).transpose(0, 3, 2, 1)
    return {
        "w_in": f(w_in), "w_branch": f(w_branch), "w_out": f(w_out), "norm_g": f(norm_g),
        "final_g": f(final_g).reshape(1, D), "convw": np.ascontiguousarray(convw),
        "lam_qk": f(lam_qk).reshape(DEPTH, 1, 256), "dng": f(diff_norm_g).reshape(DEPTH, 128, 1),
        "gng": f(gdn_norm_g).reshape(DEPTH, 1, 128), "a_log": f(a_log).reshape(DEPTH, 1, 16),
        "dt_bias": f(dt_bias).reshape(DEPTH, 1, 16), "cosT": cosT, "sinT": sinT, "constf": cf, "constb": cbm,
    }


_NC_CACHE = {}


def kernel(x_prompt, x_sample, norm_g, w_in, conv_w, lam_qk, diff_norm_g, a_log, dt_bias, gdn_norm_g, w_branch, w_out, final_g):
    x_prompt = np.asarray(x_prompt, dtype=np.float32)
    x_sample = np.asarray(x_sample, dtype=np.float32)
    seqs = SEQS_FULL
    key = tuple(seqs)
    if key not in _NC_CACHE:
        _NC_CACHE[key] = build(seqs)
    nc = _NC_CACHE[key]
    sh = shared_inputs(norm_g, w_in, conv_w, lam_qk, diff_norm_g, a_log, dt_bias, gdn_norm_g, w_branch, w_out, final_g)
    in_maps = []
    for c in range(8):
        xin = np.concatenate([x_prompt[c], x_sample[2 * c], x_sample[2 * c + 1]], axis=0)
        m = dict(sh)
        m["xin"] = np.ascontiguousarray(xin)
        in_maps.append(m)
    res = run_bass_kernel_spmd(nc, in_maps, core_ids=list(range(8)))
    y_prompt = np.empty_like(x_prompt)
    y_sample = np.empty_like(x_sample)
    for c in range(8):
        y = res.results[c]["yout"]
        y_prompt[c] = y[0:2048]
        y_sample[2 * c] = y[2048:2048 + 4096]
        y_sample[2 * c + 1] = y[2048 + 4096:]
    return (y_prompt, y_sample)
```

```python
import math
import os
DBG = int(os.environ.get('MKDBG', '99'))
import contextlib
import numpy as np
import ml_dtypes
import concourse.bass as bass
import concourse.mybir as mybir
from concourse.bass_utils import run_bass_kernel_spmd

F32 = mybir.dt.float32
BF16 = mybir.dt.bfloat16
AF = mybir.ActivationFunctionType
ALU = mybir.AluOpType
AX = mybir.AxisListType

D = 1024
DEPTH = 2
INW = 10272
EPS = 1e-6
SUBLN_EPS = 1e-5
SEQS_FULL = (2048, 4096, 4096)


class Res:
    __slots__ = ("name", "last_w", "readers", "excl")

    def __init__(self, name="", excl=False):
        self.name = name
        self.excl = excl
        self.last_w = None
        self.readers = []


class Op:
    __slots__ = ("eng", "fn", "deps", "needed", "semval", "dma", "dsem")

    def __init__(self, eng, fn, dma):
        self.eng = eng
        self.fn = fn
        self.deps = []
        self.needed = False
        self.semval = None
        self.dma = dma
        self.dsem = None


class Sched:
    ENG = ("tensor", "vector", "scalar", "gpsimd", "sync")
    DQ = ("sync", "gpsimd")
    NDMA = 12

    def __init__(self, nc, same_engine_sync=True):
        self.nc = nc
        self.ops = []
        self.same_engine_sync = same_engine_sync
        self.last = {e: None for e in self.ENG}
        self.lastd = {e: [] for e in self.DQ}

    def op(self, eng, fn, reads=(), writes=(), dma=False):
        o = Op(eng, fn, dma)
        deps = set()
        ex = [r for r in reads if r.excl]
        if ex:
            reads = [r for r in reads if not r.excl]
            writes = list(writes) + [r for r in ex if r not in writes]
        for r in reads:
            if r.last_w is not None:
                deps.add(r.last_w)
        for w in writes:
            if w.last_w is not None:
                deps.add(w.last_w)
            for rd in w.readers:
                deps.add(rd)
        o.deps = list(deps)
        for r in reads:
            r.readers.append(o)
        for w in writes:
            w.last_w = o
            w.readers = []
        self.ops.append(o)
        if dma:
            self.lastd[eng].append(o)
            if len(self.lastd[eng]) > self.NDMA:
                self.lastd[eng].pop(0)
        else:
            self.last[eng] = o
        return o

    def barrier(self):
        deps = [o for o in self.last.values() if o is not None]
        for q in self.DQ:
            deps += self.lastd[q]
        for e in self.ENG:
            o = Op(e, None, False)
            o.deps = list(deps)
            self.ops.append(o)

    def emit(self):
        nc = self.nc
        engs = {"tensor": nc.tensor, "vector": nc.vector, "scalar": nc.scalar,
                "gpsimd": nc.gpsimd, "sync": nc.sync}
        ses = self.same_engine_sync
        for o in self.ops:
            for d in o.deps:
                if d.dma:
                    continue
                if d.eng == o.eng and not o.dma and (o.eng == "tensor" or not ses) and o.fn is not None:
                    continue
                d.needed = True
        tl = {e: nc.alloc_semaphore(name=f"tl_{e}") for e in self.ENG}
        cnt = {e: 0 for e in self.ENG}
        dsems = {e: [nc.alloc_semaphore(name=f"d_{e}_{k}") for k in range(self.NDMA)] for e in self.DQ}
        dcount = {e: [0] * self.NDMA for e in dsems}
        dnext = {e: 0 for e in dsems}
        waited = {e: {} for e in self.ENG}
        nwait = 0
        for o in self.ops:
            E = engs[o.eng]
            w = waited[o.eng]
            reqs = {}
            for d in o.deps:
                if d.dma:
                    key = ("d", d.eng, d.dsem[0])
                    sem, val = d.dsem[1], d.dsem[2]
                else:
                    if d.semval is None:
                        continue
                    if d.eng == "tensor" and o.eng == "tensor" and not o.dma:
                        continue
                    key = ("t", d.eng)
                    sem, val = tl[d.eng], d.semval
                if reqs.get(key, (None, -1))[1] < val:
                    reqs[key] = (sem, val)
            if o.dma:
                k = dnext[o.eng]
                dnext[o.eng] = (k + 1) % self.NDMA
                prev = dcount[o.eng][k]
                if prev > 0:
                    key = ("d", o.eng, k)
                    if reqs.get(key, (None, -1))[1] < prev:
                        reqs[key] = (dsems[o.eng][k], prev)
                dcount[o.eng][k] = prev + 16
                o.dsem = (k, dsems[o.eng][k], prev + 16)
            for key, (sem, val) in reqs.items():
                if w.get(key, 0) < val:
                    E.wait_ge(sem, val)
                    w[key] = val
                    nwait += 1
            if o.fn is None:
                continue
            ins = o.fn(E)
            if o.dma:
                ins.then_inc(o.dsem[1], 16)
            elif o.needed:
                cnt[o.eng] += 1
                o.semval = cnt[o.eng]
                ins.then_inc(tl[o.eng], 1)
        for e in dsems:
            for k in range(self.NDMA):
                if dcount[e][k] > 0:
                    nc.sync.wait_ge(dsems[e][k], dcount[e][k])
        print(f"[sched] ops={len(self.ops)} waits={nwait} incs={cnt}", flush=True)


class Ring:
    def __init__(self, nc, name, shape, dtype, n):
        self.t = Ring.stack.enter_context(nc.sbuf_tensor(name, [shape[0], n] + list(shape[1:]), dtype))
        self.res = [Res(f"{name}{i}") for i in range(n)]
        self.n = n
        self.i = 0

    def next(self):
        k = self.i % self.n
        self.i += 1
        return self.t[:, k], self.res[k]


def build(seqs, debug=False, same_engine_sync=True, phases="ABCD", alim=99, nlayers=DEPTH, pool="gpsimd"):
    NT = sum(seqs)
    nc = bass.Bass("TRN2", target_bir_lowering=False)
    S = Sched(nc, same_engine_sync=same_engine_sync)

    def dram(name, shape, dt, kind):
        return nc.dram_tensor(name, shape, dt, kind=kind).ap()

    okind = "ExternalOutput" if debug else "Internal"
    xin = dram("xin", [NT, D], F32, "ExternalInput")
    w_in = dram("w_in", [DEPTH, D, INW], F32, "ExternalInput")
    w_br = dram("w_branch", [DEPTH, 2, D, D], F32, "ExternalInput")
    w_out = dram("w_out", [DEPTH, D, D], F32, "ExternalInput")
    norm_g = dram("norm_g", [DEPTH, D], F32, "ExternalInput")
    final_g = dram("final_g", [1, D], F32, "ExternalInput")
    convw = dram("convw", [DEPTH, 128, 24, 4], F32, "ExternalInput")
    lam_qk = dram("lam_qk", [DEPTH, 1, 256], F32, "ExternalInput")
    dng = dram("dng", [DEPTH, 128, 1], F32, "ExternalInput")
    gng = dram("gng", [DEPTH, 1, 128], F32, "ExternalInput")
    a_log = dram("a_log", [DEPTH, 1, 16], F32, "ExternalInput")
    dt_bias = dram("dt_bias", [DEPTH, 1, 16], F32, "ExternalInput")
    cosT_d = dram("cosT", [128, 4096], F32, "ExternalInput")
    sinT_d = dram("sinT", [128, 4096], F32, "ExternalInput")
    cf_d = dram("constf", [128, 6, 128], F32, "ExternalInput")
    cb_d = dram("constb", [128, 3, 128], BF16, "ExternalInput")
    yout = dram("yout", [NT, D], F32, "ExternalOutput")
    x1 = dram("x1", [NT, D], F32, okind)
    qkT = dram("qkT", [2048, NT], BF16, okind)
    va = dram("va", [NT, D], BF16, okind)
    zaT = dram("zaT", [D, NT], BF16, okind)
    qbT = dram("qbT", [D, NT], BF16, okind)
    kbT = dram("kbT", [D, NT], BF16, okind)
    ktm = dram("ktm", [NT, D], BF16, okind)
    vtm = dram("vtm", [NT, D], BF16, okind)
    zb = dram("zb", [NT, D], BF16, okind)
    ab = dram("ab", [NT, 32], F32, okind)
    gT = dram("gT", [2048, NT], BF16, okind)
    yaT = dram("yaT", [D, NT], BF16, okind)
    ybT = dram("ybT", [D, NT], BF16, okind)
    R = {n: Res(n) for n in "x1 qkT va zaT qbT kbT ktm vtm zb ab gT yaT ybT yout".split()}

    cf = nc.alloc_sbuf_tensor("cf", [128, 6, 128], F32)
    cb = nc.alloc_sbuf_tensor("cb", [128, 3, 128], BF16)
    Rc = Res("consts")
    S.op("sync", lambda e: e.dma_start(out=cf[:], in_=cf_d[:, :, :]), writes=[Rc], dma=True)
    S.op("sync", lambda e: e.dma_start(out=cb[:], in_=cb_d[:, :, :]), writes=[Rc], dma=True)
    ident_f, ones_f = cf[:, 0, :], cf[:, 1, :]
    ident_b, ones_b, perm_b = cb[:, 0, :], cb[:, 1, :], cb[:, 2, :]
    MI = {0: cf[0:64, 2, 0:64], 1: cf[0:64, 4, 0:64]}
    MS = {0: cf[0:64, 3, 0:64], 1: cf[0:64, 5, 0:64]}

    PS = [nc.alloc_psum_tensor(f"ps{b}", [128, 512], F32) for b in range(8)]
    RPS = [Res(f"ps{b}", excl=True) for b in range(8)]

    DMAQ = ["sync", "gpsimd"]
    dq = [0]

    def dma(out, in_, reads=(), writes=(), q=None):
        if q is None:
            q = DMAQ[dq[0] % 2]
            dq[0] += 1
        return S.op(q, lambda e: e.dma_start(out=out, in_=in_), reads=reads, writes=writes, dma=True)

    def mm(out, lhsT, rhs, start, stop, reads, writes, **kw):
        return S.op("tensor", lambda e: e.matmul(out, lhsT=lhsT, rhs=rhs, start=start, stop=stop, **kw),
                    reads=reads, writes=writes)

    def tr(out, in_, ident, reads, writes):
        return S.op("tensor", lambda e: e.transpose(out, in_, ident), reads=list(reads) + [Rc], writes=writes)

    def act(out, in_, func, reads, writes, eng="scalar", **kw):
        return S.op("scalar", lambda e: e.activation(out=out, in_=in_, func=func, **kw), reads=reads, writes=writes)

    def tt(eng, out, in0, in1, op, reads, writes):
        eng = pool if eng == "gpsimd" else eng
        return S.op(eng, lambda e: e.tensor_tensor(out=out, in0=in0, in1=in1, op=op), reads=reads, writes=writes)

    def ts(eng, out, in0, s1, op0, reads, writes, s2=None, op1=None):
        eng = pool if eng == "gpsimd" else eng
        if op1 is None:
            return S.op(eng, lambda e: e.tensor_scalar(out=out, in0=in0, scalar1=s1, scalar2=None, op0=op0),
                        reads=reads, writes=writes)
        return S.op(eng, lambda e: e.tensor_scalar(out=out, in0=in0, scalar1=s1, scalar2=s2, op0=op0, op1=op1),
                    reads=reads, writes=writes)

    def stt(eng, out, in0, scalar, in1, op0, op1, reads, writes):
        eng = "vector"
        return S.op(eng, lambda e: e.scalar_tensor_tensor(out=out, in0=in0, scalar=scalar, in1=in1, op0=op0, op1=op1),
                    reads=reads, writes=writes)

    def cp(eng, out, in_, reads, writes):
        eng = pool if eng == "gpsimd" else eng
        if eng == "scalar":
            return act(out, in_, AF.Copy, reads, writes)
        return S.op(eng, lambda e: e.tensor_copy(out=out, in_=in_), reads=reads, writes=writes)

    seq_off = [sum(seqs[:i]) for i in range(len(seqs))]

    for l in range(nlayers):
        lam_init = 0.8 - 0.6 * math.exp(-0.3 * l)
        xsrc = xin if l == 0 else x1
        Rxsrc = [] if l == 0 else [R["x1"]]
        if "A" in phases:
          with contextlib.ExitStack() as st:
            Ring.stack = st

            def sb(name, shape, dt):
                return st.enter_context(nc.sbuf_tensor(f"A{l}_{name}", shape, dt))
            SMAX = max(seqs)
            hT = sb("hT", [128, 8, SMAX], BF16)
            RhT = Res("hT")
            gtile = sb("gtile", [128, D], F32)
            Rg = Res("gtile")
            dma(gtile[:], norm_g[l:l + 1, :].broadcast_to([128, D]), writes=[Rg])
            cwt = sb("cwt", [128, 24, 4], F32)
            Rcw = Res("cw")
            dma(cwt[:], convw[l], writes=[Rcw])
            xt_ring = Ring(nc, f"A{l}_xt", [128, D], F32, 2)
            hb_ring = Ring(nc, f"A{l}_hb", [128, D], BF16, 2)
            sq_junk = sb("sqj", [128, D], BF16)
            Rsqj = Res("sqj")
            ss_ring = Ring(nc, f"A{l}_ss", [128, 2], F32, 2)
            wst_ring = Ring(nc, f"A{l}_wst", [128, 8, 512], F32, 2)
            wsec = sb("wsec", [128, 8, 2048], BF16)
            Rw = Res("wsec")
            ev_ring = Ring(nc, f"A{l}_ev", [128, 512], BF16, 4)
            evf_ring = Ring(nc, f"A{l}_evf", [128, 512], F32, 3)
            cs_ring = Ring(nc, f"A{l}_cs", [128, 2, 512], F32, 2)
            xc = sb("xc", [128, SMAX + 4], F32)
            Rxc = Res("xc")
            xacc = sb("xacc", [128, SMAX], F32)
            Rxa = Res("xacc")
            ab_ring = Ring(nc, f"A{l}_ab", [128, 32], F32, 2)
            wv = w_in[l].rearrange("(c p) n -> p c n", p=128)

            def load_w(col0, ncols):
                nb = (ncols + 511) // 512
                for b in range(nb):
                    n = min(512, ncols - b * 512)
                    wst, Rwst = wst_ring.next()
                    dma(wst[:, :, 0:n], wv[:, :, col0 + b * 512: col0 + b * 512 + n], writes=[Rwst])
                    cp("gpsimd", wsec[:, :, b * 512: b * 512 + n], wst[:, :, 0:n], reads=[Rwst], writes=[Rw])

            psi = [0]

            def nextps(k=4):
                b = psi[0] % k
                psi[0] += 1
                return PS[b], RPS[b]

            for si, Sq in enumerate(seqs):
                t0 = seq_off[si]
                NTL = Sq // 128
                NB = Sq // 512
                for m in range(NTL):
                    xt, Rxt = xt_ring.next()
                    dma(xt, xsrc[t0 + m * 128: t0 + (m + 1) * 128, :], reads=Rxsrc, writes=[Rxt])
                    ss, Rss = ss_ring.next()
                    act(sq_junk[:], xt, AF.Square, reads=[Rxt], writes=[Rsqj, Rss], accum_out=ss[:, 0:1])
                    act(ss[:, 1:2], ss[:, 0:1], AF.Sqrt, reads=[Rss], writes=[Rss], scale=1.0 / D, bias=EPS)
                    S.op("vector", lambda e, ss=ss: e.reciprocal(out=ss[:, 1:2], in_=ss[:, 1:2]), reads=[Rss], writes=[Rss])
                    hb, Rhb = hb_ring.next()
                    stt("vector", hb, xt, ss[:, 1:2], gtile[:], ALU.mult, ALU.mult, reads=[Rxt, Rss, Rg], writes=[Rhb])
                    ps, Rps = nextps()
                    psb = ps[:, :].bitcast(BF16)
                    for c in range(8):
                        tr(psb[:, c * 128:(c + 1) * 128], hb[:, c * 128:(c + 1) * 128], ident_b, reads=[Rhb], writes=[Rps])
                    cp("vector", hT[:, :, m * 128:(m + 1) * 128], psb[:, 0:1024].rearrange("p (c t) -> p c t", c=8),
                       reads=[Rps], writes=[RhT])

                def fm_matmuls(ch, tb):
                    ps, Rps = nextps()
                    for c in range(8):
                        mm(ps[:, :], wsec[:, c, ch * 128:(ch + 1) * 128], hT[:, c, tb * 512:(tb + 1) * 512],
                           c == 0, c == 7, reads=[Rw, RhT], writes=[Rps])
                    return ps, Rps

                def tm_matmuls(m, cb0, n=512):
                    ps, Rps = nextps()
                    for c in range(8):
                        mm(ps[:, 0:n], hT[:, c, m * 128:(m + 1) * 128], wsec[:, c, cb0:cb0 + n],
                           c == 0, c == 7, reads=[Rw, RhT], writes=[Rps])
                    return ps, Rps

                if alim >= 2:
                    load_w(0, 2048)
                for tb in range(NB if (alim >= 2 and DBG >= 1) else 0):
                    cs, Rcs = cs_ring.next()
                    dma(cs[:, 0, :], cosT_d[:, tb * 512:(tb + 1) * 512], writes=[Rcs])
                    dma(cs[:, 1, :], sinT_d[:, tb * 512:(tb + 1) * 512], writes=[Rcs])
                    for ch in range(16 if DBG >= 2 else 0):
                        ps, Rps = fm_matmuls(ch, tb)
                        if DBG < 3:
                            continue
                        qb_, Rqb = ev_ring.next()
                        cp("scalar", qb_, ps[:, :], reads=[Rps], writes=[Rqb])
                        if DBG < 4:
                            continue
                        t1, Rt1 = evf_ring.next()
                        if os.environ.get('MKX') == '1':
                            tt("vector", t1, ps[:, :], gtile[:, 0:512], ALU.mult, reads=[Rps, Rg], writes=[Rt1])
                        elif os.environ.get('MKX') == '3':
                            tt("vector", t1, ps[:, :], cs[:, 0, :], ALU.mult, reads=[Rps, Rcs, Rqb], writes=[Rt1])
                        elif os.environ.get('MKX') == '2':
                            cp("vector", t1, ps[:, :], reads=[Rps], writes=[Rt1])
                            tt("vector", t1, t1, cs[:, 0, :], ALU.mult, reads=[Rt1, Rcs], writes=[Rt1])
                        else:
                            tt("vector", t1, ps[:, :], cs[:, 0, :], ALU.mult, reads=[Rps, Rcs], writes=[Rt1])
                        if DBG < 5:
                            continue
                        ps2, Rps2 = PS[4 + (psi[0] % 2)], RPS[4 + (psi[0] % 2)]
                        mm(ps2[:, :], perm_b, qb_, True, True, reads=[Rqb, Rc], writes=[Rps2])
                        t2, Rt2 = evf_ring.next()
                        tt("vector", t2, ps2[:, :], cs[:, 1, :], ALU.mult, reads=[Rps2, Rcs], writes=[Rt2])
                        ob, Rob = ev_ring.next()
                        tt("gpsimd", ob, t1, t2, ALU.add, reads=[Rt1, Rt2], writes=[Rob])
                        dma(qkT[ch * 128:(ch + 1) * 128, t0 + tb * 512: t0 + (tb + 1) * 512], ob, reads=[Rob], writes=[R["qkT"]])
                if alim < 3:
                    continue
                load_w(2048, 2048)
                for m in range(NTL):
                    for cb0 in (0, 512):
                        ps, Rps = tm_matmuls(m, cb0)
                        ob, Rob = ev_ring.next()
                        cp("scalar", ob, ps[:, :], reads=[Rps], writes=[Rob])
                        dma(va[t0 + m * 128: t0 + (m + 1) * 128, cb0:cb0 + 512], ob, reads=[Rob], writes=[R["va"]])
                for tb in range(NB):
                    for ch in range(8):
                        ps, Rps = fm_matmuls(8 + ch, tb)
                        ob, Rob = ev_ring.next()
                        act(ob, ps[:, :], AF.Silu, reads=[Rps], writes=[Rob])
                        dma(zaT[ch * 128:(ch + 1) * 128, t0 + tb * 512: t0 + (tb + 1) * 512], ob, reads=[Rob], writes=[R["zaT"]])
                if alim < 4:
                    continue
                for part in range(3):
                    load_w(4096 + part * 1024, 1024)
                    for ch in range(8):
                        gch = part * 8 + ch
                        S.op("gpsimd", lambda e: e.memset(xc[:, 0:1], 0.0), writes=[Rxc])
                        S.op("gpsimd", lambda e, Sq=Sq: e.memset(xc[:, Sq + 1:Sq + 4], 0.0), writes=[Rxc])
                        for tb in range(NB):
                            ps, Rps = fm_matmuls(ch, tb)
                            cp("scalar", xc[:, 1 + tb * 512: 1 + (tb + 1) * 512], ps[:, :], reads=[Rps], writes=[Rxc])
                        ts("vector", xacc[:, 0:Sq], xc[:, 0:Sq], cwt[:, gch, 0:1], ALU.mult, reads=[Rxc, Rcw], writes=[Rxa])
                        for j in range(1, 4):
                            stt("vector", xacc[:, 0:Sq], xc[:, j:j + Sq], cwt[:, gch, j:j + 1], xacc[:, 0:Sq], ALU.mult, ALU.add,
                                reads=[Rxc, Rxa, Rcw], writes=[Rxa])
                        act(xacc[:, 0:Sq], xacc[:, 0:Sq], AF.Silu, reads=[Rxa], writes=[Rxa])
                        for tb in range(NB):
                            sl = slice(tb * 512, (tb + 1) * 512)
                            ob, Rob = ev_ring.next()
                            if part < 2:
                                sq, Rsq = evf_ring.next()
                                tt("gpsimd", sq, xacc[:, sl], xacc[:, sl], ALU.mult, reads=[Rxa], writes=[Rsq])
                                ps, Rps = nextps()
                                mm(ps[:, :], ones_f, sq, True, True, reads=[Rsq, Rc], writes=[Rps])
                                rn, Rrn = evf_ring.next()
                                act(rn, ps[:, :], AF.Sqrt, reads=[Rps], writes=[Rrn], bias=(128.0 * EPS if part == 0 else EPS),
                                    scale=(128.0 if part == 0 else 1.0))
                                S.op("vector", lambda e, rn=rn: e.reciprocal(out=rn, in_=rn), reads=[Rrn], writes=[Rrn])
                                tt("vector", ob, xacc[:, sl], rn, ALU.mult, reads=[Rxa, Rrn], writes=[Rob])
                                dst = qbT if part == 0 else kbT
                                dma(dst[ch * 128:(ch + 1) * 128, t0 + tb * 512: t0 + (tb + 1) * 512], ob, reads=[Rob],
                                    writes=[R["qbT" if part == 0 else "kbT"]])
                            else:
                                cp("vector", ob, xacc[:, sl], reads=[Rxa], writes=[Rob])
                            if part >= 1:
                                ps, Rps = nextps()
                                psb = ps[:, :].bitcast(BF16)
                                for q4 in range(4):
                                    tr(psb[:, q4 * 128:(q4 + 1) * 128], ob[:, q4 * 128:(q4 + 1) * 128], ident_b, reads=[Rob], writes=[Rps])
                                o2, Ro2 = ev_ring.next()
                                cp("vector", o2, psb[:, 0:512], reads=[Rps], writes=[Ro2])
                                dst = ktm if part == 1 else vtm
                                dma(dst[t0 + tb * 512: t0 + (tb + 1) * 512, ch * 128:(ch + 1) * 128].rearrange("(q p) f -> p q f", p=128),
                                    o2.rearrange("p (q f) -> p q f", q=4), reads=[Ro2], writes=[R["ktm" if part == 1 else "vtm"]])
                if alim < 5:
                    continue
                load_w(7168, 1056)
                for m in range(NTL):
                    for cb0 in (0, 512):
                        ps, Rps = tm_matmuls(m, cb0)
                        ob, Rob = ev_ring.next()
                        act(ob, ps[:, :], AF.Silu, reads=[Rps], writes=[Rob])
                        dma(zb[t0 + m * 128: t0 + (m + 1) * 128, cb0:cb0 + 512], ob, reads=[Rob], writes=[R["zb"]])
                    ps, Rps = tm_matmuls(m, 1024, n=32)
                    abt, Rab = ab_ring.next()
                    cp("vector", abt, ps[:, 0:32], reads=[Rps], writes=[Rab])
                    dma(ab[t0 + m * 128: t0 + (m + 1) * 128, :], abt, reads=[Rab], writes=[R["ab"]])
                if alim < 6:
                    continue
                load_w(8224, 2048)
                for tb in range(NB):
                    for ch in range(16):
                        ps, Rps = fm_matmuls(ch, tb)
                        ob, Rob = ev_ring.next()
                        act(ob, ps[:, :], AF.Sigmoid, reads=[Rps], writes=[Rob])
                        dma(gT[ch * 128:(ch + 1) * 128, t0 + tb * 512: t0 + (tb + 1) * 512], ob, reads=[Rob], writes=[R["gT"]])
            S.barrier()

        if "B" in phases:
          with contextlib.ExitStack() as st:
            Ring.stack = st

            def sb(name, shape, dt):
                return st.enter_context(nc.sbuf_tensor(f"B{l}_{name}", shape, dt))
            SMAX = max(seqs)
            lq = sb("lq", [128, 256], F32)
            lsc = sb("lsc", [128, 8], F32)
            Rl = Res("lam")
            dma(lq[:], lam_qk[l].broadcast_to([128, 256]), writes=[Rl])
            tt("vector", lq[:, 0:64], lq[:, 0:64], lq[:, 64:128], ALU.mult, reads=[Rl], writes=[Rl])
            tt("vector", lq[:, 128:192], lq[:, 128:192], lq[:, 192:256], ALU.mult, reads=[Rl], writes=[Rl])
            S.op("vector", lambda e: e.tensor_reduce(out=lsc[:, 0:1], in_=lq[:, 0:64], axis=AX.X, op=ALU.add), reads=[Rl], writes=[Rl])
            S.op("vector", lambda e: e.tensor_reduce(out=lsc[:, 1:2], in_=lq[:, 128:192], axis=AX.X, op=ALU.add), reads=[Rl], writes=[Rl])
            act(lsc[:, 2:4], lsc[:, 0:2], AF.Exp, reads=[Rl], writes=[Rl])
            tt("vector", lsc[:, 4:5], lsc[:, 3:4], lsc[:, 2:3], ALU.subtract, reads=[Rl], writes=[Rl])
            ts("vector", lsc[:, 5:6], lsc[:, 4:5], -lam_init, ALU.add, reads=[Rl], writes=[Rl])
            neglam = lsc[:, 5:6]
            dgt = sb("dgt", [128, 2], F32)
            dma(dgt[:, 0:1], dng[l], writes=[Rl])
            ts("vector", dgt[:, 1:2], dgt[:, 0:1], (1.0 - lam_init), ALU.mult, reads=[Rl], writes=[Rl])
            QTz = [sb("QT0", [128, SMAX], BF16), sb("QT1", [128, SMAX], BF16)]
            S.op("gpsimd", lambda e: e.memset(QTz[0][64:128, :], 0.0), writes=[])
            S.op("gpsimd", lambda e: e.memset(QTz[1][0:64, :], 0.0), writes=[])
            KT = sb("KT", [128, SMAX], BF16)
            V = sb("V", [128, SMAX // 128, 128], BF16)
            RQ, RK, RV = Res("QT"), Res("KT"), Res("V")
            pt_ring = Ring(nc, f"B{l}_pt", [128, 512], BF16, 6)
            f_ring = Ring(nc, f"B{l}_f", [128, 512], F32, 6)
            z_ring = Ring(nc, f"B{l}_z", [128, 512], BF16, 2)
            y_ring = Ring(nc, f"B{l}_y", [128, 512], BF16, 2)
            scale = 64 ** -0.5
            spi = 0
            for si, Sq in enumerate(seqs):
                t0 = seq_off[si]
                NKT = Sq // 128
                NB = Sq // 512
                for h in range(8):
                    dma(QTz[0][0:64, 0:Sq], qkT[h * 128:h * 128 + 64, t0:t0 + Sq], reads=[R["qkT"]], writes=[RQ])
                    dma(QTz[1][64:128, 0:Sq], qkT[h * 128 + 64:(h + 1) * 128, t0:t0 + Sq], reads=[R["qkT"]], writes=[RQ])
                    dma(KT[:, 0:Sq], qkT[1024 + h * 128: 1024 + (h + 1) * 128, t0:t0 + Sq], reads=[R["qkT"]], writes=[RK])
                    dma(V[:, 0:NKT, :], va[t0:t0 + Sq, h * 128:(h + 1) * 128].rearrange("(k p) f -> p k f", p=128),
                        reads=[R["va"]], writes=[RV])
                    for qb_ in range(NB):
                        qs = slice(qb_ * 512, (qb_ + 1) * 512)
                        zt, Rz = z_ring.next()
                        dma(zt, zaT[h * 128:(h + 1) * 128, t0 + qb_ * 512: t0 + (qb_ + 1) * 512], reads=[R["zaT"]], writes=[Rz])
                        PIPE = 2
                        SB = (0, 1, 7)
                        pend = []

                        def flush_one():
                            kt_, c_, pt_, Rpt_ = pend.pop(0)
                            mm(PS[2 + 2 * c_][:, :], V[:, kt_, :], pt_, kt_ == 0, kt_ == NKT - 1, reads=[RV, Rpt_], writes=[RPS[2 + 2 * c_]])
                            mm(PS[3 + 2 * c_][:, :], ones_b, pt_, kt_ == 0, kt_ == NKT - 1, reads=[Rc, Rpt_], writes=[RPS[3 + 2 * c_]])

                        for kt in range(NKT):
                            for c in range(2):
                                sp, Rsp = PS[SB[spi % 3]], RPS[SB[spi % 3]]
                                spi += 1
                                mm(sp[:, :], KT[:, kt * 128:(kt + 1) * 128], QTz[c][:, qs],
                                   True, True, reads=[RK, RQ], writes=[Rsp])
                                pt, Rpt = pt_ring.next()
                                act(pt, sp[:, :], AF.Exp, reads=[Rsp], writes=[Rpt], scale=scale)
                                pend.append((kt, c, pt, Rpt))
                                if len(pend) > PIPE:
                                    flush_one()
                        while pend:
                            flush_one()
                        r0, Rr0 = f_ring.next()
                        S.op("vector", lambda e, r0=r0: e.reciprocal(out=r0, in_=PS[3][:, :]), reads=[RPS[3]], writes=[Rr0])
                        t0_, Rt0 = f_ring.next()
                        tt("vector", t0_, PS[2][:, :], r0, ALU.mult, reads=[RPS[2], Rr0], writes=[Rt0])
                        r1, Rr1 = f_ring.next()
                        S.op("vector", lambda e, r1=r1: e.reciprocal(out=r1, in_=PS[5][:, :]), reads=[RPS[5]], writes=[Rr1])
                        t1_, Rt1 = f_ring.next()
                        tt("vector", t1_, PS[4][:, :], r1, ALU.mult, reads=[RPS[4], Rr1], writes=[Rt1])
                        o_, Ro = f_ring.next()
                        stt("gpsimd", o_, t1_, neglam, t0_, ALU.mult, ALU.add, reads=[Rt1, Rt0, Rl], writes=[Ro])
                        sq_, Rsq = f_ring.next()
                        tt("gpsimd", sq_, o_, o_, ALU.mult, reads=[Ro], writes=[Rsq])
                        mm(PS[6][:, :], ones_f, sq_, True, True, reads=[Rsq, Rc], writes=[RPS[6]])
                        rn, Rrn = f_ring.next()
                        act(rn, PS[6][:, :], AF.Sqrt, reads=[RPS[6]], writes=[Rrn], scale=1.0 / 128, bias=SUBLN_EPS)
                        S.op("vector", lambda e, rn=rn: e.reciprocal(out=rn, in_=rn), reads=[Rrn], writes=[Rrn])
                        stt("vector", o_, o_, dgt[:, 1:2], rn, ALU.mult, ALU.mult, reads=[Ro, Rrn, Rl], writes=[Ro])
                        yt, Ry = y_ring.next()
                        tt("gpsimd", yt, o_, zt, ALU.mult, reads=[Ro, Rz], writes=[Ry])
                        dma(yaT[h * 128:(h + 1) * 128, t0 + qb_ * 512: t0 + (qb_ + 1) * 512], yt, reads=[Ry], writes=[R["yaT"]])
            S.barrier()

        if "C" in phases:
          with contextlib.ExitStack() as st:
            Ring.stack = st

            def sb(name, shape, dt):
                return st.enter_context(nc.sbuf_tensor(f"C{l}_{name}", shape, dt))
            SMAX = max(seqs)
            NMAX = SMAX // 64
            Rp = Res("gparams")
            alb = sb("alb", [64, 16], F32)
            dtb = sb("dtb", [64, 16], F32)
            dma(alb[:], a_log[l].broadcast_to([64, 16]), writes=[Rp])
            dma(dtb[:], dt_bias[l].broadcast_to([64, 16]), writes=[Rp])
            act(alb[:], alb[:], AF.Exp, reads=[Rp], writes=[Rp])
            ts("vector", alb[:], alb[:], -1.0, ALU.mult, reads=[Rp], writes=[Rp])
            gnb = sb("gnb", [64, 128], F32)
            dma(gnb[:], gng[l].broadcast_to([64, 128]), writes=[Rp])
            ABt = sb("ABt", [64, NMAX, 32], F32)
            G = sb("G", [64, NMAX, 16], F32)
            NBt = sb("NBt", [64, NMAX, 16], F32)
            RG = Res("G")
            O2 = [sb("Of", [64, NMAX, 128], F32), sb("Ob", [64, NMAX, 128], F32)]
            RO = [Res("Of"), Res("Ob")]
            Sst = [sb("Sf", [128, 128], F32), sb("Sb", [128, 128], F32)]
            Sbf = [sb("Sfb", [128, 128], BF16), sb("Sbb", [128, 128], BF16)]
            RS = [Res("Sf"), Res("Sb")]
            RSb = [Res("Sfb"), Res("Sbb")]
            vn_ring = [Ring(nc, f"C{l}_vn{d}", [64, 128], BF16, 2) for d in range(2)]

            def gb(name, shape, dt):
                return [Ring(nc, f"C{l}_{name}{d}", shape, dt, 1) for d in range(2)]
            W8 = 8 * 64
            qT_r, kT_r = gb("qT", [128, W8], BF16), gb("kT", [128, W8], BF16)
            kt_r, vt_r = gb("kt", [64, 8, 128], BF16), gb("vt", [64, 8, 128], BF16)
            GU_r = gb("GU", [64, W8], F32)
            GCc_r = gb("GCc", [64, 8, 4], F32)
            GCr_r = gb("GCr", [128, W8], F32)
            Dm_r = gb("Dm", [64, W8], F32)
            Ei_r = gb("Ei", [64, W8], F32)
            at_r = gb("at", [64, W8], BF16)
            P_r = [gb("Pa", [64, W8], F32), gb("Pb", [64, W8], F32)]
            PT_r = [gb("PTa", [64, W8], F32), gb("PTb", [64, W8], F32)]
            Y_r = gb("Y", [64, W8], F32)
            TT_r = gb("TT", [64, W8], BF16)
            KG_r, KD_r = gb("KG", [64, 8, 128], BF16), gb("KD", [64, 8, 128], BF16)
            Ub_r = gb("Ub", [64, 8, 128], F32)
            WT_r, QG_r = gb("WT", [128, W8], BF16), gb("QG", [128, W8], BF16)
            gl_r = gb("gl", [128, 8], F32)
            yb_r = Ring(nc, f"C{l}_yb", [64, 8, 128], F32, 2)
            ybb_r = Ring(nc, f"C{l}_ybb", [64, 8, 128], BF16, 2)
            zb_r = Ring(nc, f"C{l}_zb", [64, 8, 128], BF16, 2)
            rs_r = Ring(nc, f"C{l}_rs", [64, 8, 2], F32, 2)
            yo_r = Ring(nc, f"C{l}_yo", [128, W8], BF16, 2)

            def bc3(ap2, n):
                return ap2.unsqueeze(2).broadcast_to([ap2.shape[0], 8, n])

            def v3(ap, inner):
                return ap.rearrange("p (n i) -> p n i", n=8)

            for si, Sq in enumerate(seqs):
                t0 = seq_off[si]
                N = Sq // 64
                NG = N // 8
                dma(ABt[:, 0:N, :], ab[t0:t0 + Sq, :].rearrange("(n t) c -> t n c", t=64), reads=[R["ab"]], writes=[RG], q="sync")
                al_b = alb[:, :].unsqueeze(1).broadcast_to([64, N, 16])
                dt_b = dtb[:, :].unsqueeze(1).broadcast_to([64, N, 16])
                tt("vector", G[:, 0:N, :], ABt[:, 0:N, 0:16], dt_b, ALU.add, reads=[RG, Rp], writes=[RG])
                act(G[:, 0:N, :], G[:, 0:N, :], AF.Exp, reads=[RG], writes=[RG])
                act(G[:, 0:N, :], G[:, 0:N, :], AF.Ln, reads=[RG], writes=[RG], bias=1.0)
                tt("vector", G[:, 0:N, :], G[:, 0:N, :], al_b, ALU.mult, reads=[RG, Rp], writes=[RG])
                act(NBt[:, 0:N, :], ABt[:, 0:N, 16:32], AF.Sigmoid, reads=[RG], writes=[RG])
                ts("vector", NBt[:, 0:N, :], NBt[:, 0:N, :], -1.0, ALU.mult, reads=[RG], writes=[RG])
                for h in range(8):
                    for d in range(2):
                        S.op("gpsimd", lambda e, d=d: e.memset(Sst[d][:], 0.0), writes=[RS[d]])
                        S.op("gpsimd", lambda e, d=d: e.memset(Sbf[d][:], 0.0), writes=[RSb[d]])
                    for gi in range(NG):
                        prep = {}
                        for d in range(2):
                            g = gi if d == 0 else NG - 1 - gi
                            col = d * 8 + h
                            n0 = g * 8
                            tok = t0 + g * 512
                            last = 63 if d == 0 else 0
                            qT, RqT = qT_r[d].next()
                            kT, RkT = kT_r[d].next()
                            kt_, Rkt = kt_r[d].next()
                            vt_, Rvt = vt_r[d].next()
                            dma(qT, qbT[h * 128:(h + 1) * 128, tok:tok + 512], reads=[R["qbT"]], writes=[RqT])
                            dma(kT, kbT[h * 128:(h + 1) * 128, tok:tok + 512], reads=[R["kbT"]], writes=[RkT])
                            dma(kt_, ktm[tok:tok + 512, h * 128:(h + 1) * 128].rearrange("(n t) f -> t n f", t=64), reads=[R["ktm"]], writes=[Rkt])
                            dma(vt_, vtm[tok:tok + 512, h * 128:(h + 1) * 128].rearrange("(n t) f -> t n f", t=64), reads=[R["vtm"]], writes=[Rvt])
                            Gc = G[:, n0:n0 + 8, col]
                            NBc = NBt[:, n0:n0 + 8, col]
                            GCc, RGCc = GCc_r[d].next()
                            mm(PS[0][0:64, 0:8], MI[d], Gc, True, True, reads=[RG, Rc], writes=[RPS[0]])
                            cp("vector", GCc[:, :, 0], PS[0][0:64, 0:8], reads=[RPS[0]], writes=[RGCc])
                            GU, RGU = GU_r[d].next()
                            tt("gpsimd", v3(GU, 64), bc3(Gc, 64), MI[d].unsqueeze(1).broadcast_to([64, 8, 64]), ALU.mult,
                               reads=[RG, Rc], writes=[RGU])
                            mm(PS[1][:, :], ones_f[0:64, :], GU, True, True, reads=[RGU, Rc], writes=[RPS[1]])
                            GCr, RGCr = GCr_r[d].next()
                            cp("scalar", GCr, PS[1][:, :], reads=[RPS[1]], writes=[RGCr])
                            Dm, RDm = Dm_r[d].next()
                            tt("vector", v3(Dm, 64), v3(GCr[0:64, :], 64), bc3(GCc[:, :, 0], 64), ALU.subtract, reads=[RGCr, RGCc], writes=[RDm])
                            ts("gpsimd", Dm, Dm, 0.0, ALU.min, reads=[RDm], writes=[RDm])
                            act(Dm, Dm, AF.Exp, reads=[RDm], writes=[RDm])
                            Ei, REi = Ei_r[d].next()
                            tt("gpsimd", v3(Ei, 64), v3(Dm, 64), MI[d].unsqueeze(1).broadcast_to([64, 8, 64]), ALU.mult, reads=[RDm, Rc], writes=[REi])
                            act(GCr, GCr, AF.Exp, reads=[RGCr], writes=[RGCr])
                            gl, Rgl = gl_r[d].next()
                            cp("vector", gl, v3(GCr, 64)[:, :, last], reads=[RGCr], writes=[Rgl])
                            act(GCc[:, :, 1], GCc[:, :, 0], AF.Exp, reads=[RGCc], writes=[RGCc])
                            act(GCc[:, :, 2], GCc[:, :, 0], AF.Identity, reads=[RGCc], writes=[RGCc], scale=-1.0)
                            for n in range(8):
                                cs_ = slice(n * 64, (n + 1) * 64)
                                mm(PS[2][0:64, cs_], kT[:, cs_], kT[:, cs_], True, True, reads=[RkT], writes=[RPS[2]])
                            for n in range(8):
                                cs_ = slice(n * 64, (n + 1) * 64)
                                mm(PS[3][0:64, cs_], kT[:, cs_], qT[:, cs_], True, True, reads=[RkT, RqT], writes=[RPS[3]])
                            at_, Rat = at_r[d].next()
                            tt("vector", at_, PS[3][0:64, :], Ei, ALU.mult, reads=[RPS[3], REi], writes=[Rat])
                            P0, RP0 = P_r[0][d].next()
                            tt("vector", P0, PS[2][0:64, :], Ei, ALU.mult, reads=[RPS[2], REi], writes=[RP0])
                            tt("gpsimd", v3(P0, 64), v3(P0, 64), MS[d].unsqueeze(1).broadcast_to([64, 8, 64]), ALU.mult, reads=[RP0, Rc], writes=[RP0])
                            tt("gpsimd", v3(P0, 64), v3(P0, 64), bc3(NBc, 64), ALU.mult, reads=[RP0, RG], writes=[RP0])
                            prep[d] = dict(g=g, n0=n0, col=col, last=last, qT=(qT, RqT), kT=(kT, RkT), kt=(kt_, Rkt), vt=(vt_, Rvt),
                                           GCc=(GCc, RGCc), GCr=(GCr, RGCr), gl=(gl, Rgl), at=(at_, Rat), P0=(P0, RP0), NBc=NBc)
                        cur = {}
                        for d in range(2):
                            P0, RP0 = prep[d]["P0"]
                            pb = 4 * d
                            for n in range(8):
                                cs_ = slice(n * 64, (n + 1) * 64)
                                tr(PS[pb][0:64, cs_], P0[:, cs_], ident_f[0:64, 0:64], reads=[RP0], writes=[RPS[pb]])
                            PT0, RPT0 = PT_r[0][d].next()
                            cp("scalar", PT0, PS[pb][0:64, :], reads=[RPS[pb]], writes=[RPT0])
                            Y, RY = Y_r[d].next()
                            tt("gpsimd", v3(Y, 64), v3(P0, 64), ident_f[0:64, 0:64].unsqueeze(1).broadcast_to([64, 8, 64]), ALU.add, reads=[RP0, Rc], writes=[RY])
                            cur[d] = [P0, RP0, PT0, RPT0, Y, RY]
                        for k in range(0, 6):
                            for d in range(2):
                                P, RP, PT, RPT, Y, RY = cur[d]
                                pb = 4 * d
                                if k < 5:
                                    for n in range(8):
                                        cs_ = slice(n * 64, (n + 1) * 64)
                                        mm(PS[pb + 1][0:64, cs_], PT[:, cs_], P[:, cs_], True, True, reads=[RP, RPT], writes=[RPS[pb + 1]])
                                    for n in range(8):
                                        cs_ = slice(n * 64, (n + 1) * 64)
                                        mm(PS[pb + 2][0:64, cs_], P[:, cs_], PT[:, cs_], True, True, reads=[RP, RPT], writes=[RPS[pb + 2]])
                                if k >= 1:
                                    for n in range(8):
                                        cs_ = slice(n * 64, (n + 1) * 64)
                                        mm(PS[pb + 3][0:64, cs_], PT[:, cs_], Y[:, cs_], True, True, reads=[RY, RPT], writes=[RPS[pb + 3]])
                                    tt("vector", Y, Y, PS[pb + 3][0:64, :], ALU.add, reads=[RY, RPS[pb + 3]], writes=[RY])
                                if k < 5:
                                    Pn, RPn = P_r[(k + 1) % 2][d].next()
                                    PTn, RPTn = PT_r[(k + 1) % 2][d].next()
                                    cp("scalar", Pn, PS[pb + 1][0:64, :], reads=[RPS[pb + 1]], writes=[RPn])
                                    cp("vector", PTn, PS[pb + 2][0:64, :], reads=[RPS[pb + 2]], writes=[RPTn])
                                    cur[d] = [Pn, RPn, PTn, RPTn, Y, RY]
                        for d in range(2):
                            pr = prep[d]
                            pb = 4 * d
                            Y, RY = cur[d][4], cur[d][5]
                            TT, RTT = TT_r[d].next()
                            cp("gpsimd", TT, Y, reads=[RY], writes=[RTT])
                            kt_, Rkt = pr["kt"]
                            vt_, Rvt = pr["vt"]
                            GCc, RGCc = pr["GCc"]
                            GCr, RGCr = pr["GCr"]
                            gl, Rgl = pr["gl"]
                            qT, RqT = pr["qT"]
                            KG, RKG = KG_r[d].next()
                            tt("gpsimd", KG, kt_, bc3(GCc[:, :, 1], 128), ALU.mult, reads=[Rkt, RGCc], writes=[RKG])
                            pr["KG"] = (KG, RKG)
                            pr["TT"] = (TT, RTT)
                        prep_done = prep
                        for d in range(2):
                            pr = prep_done[d]
                            pb = 4 * d
                            TT, RTT = pr["TT"]
                            KG, RKG = pr["KG"]
                            kt_, Rkt = pr["kt"]
                            vt_, Rvt = pr["vt"]
                            GCc, RGCc = pr["GCc"]
                            GCr, RGCr = pr["GCr"]
                            gl, Rgl = pr["gl"]
                            qT, RqT = pr["qT"]
                            last = pr["last"]
                            NBc = pr["NBc"]
                            KD, RKD = KD_r[d].next()
                            mm(PS[pb][0:64, 0:8], ones_f[0:64, 0:64], G[:, pr["n0"]:pr["n0"] + 8, pr["col"]], True, True, reads=[RG, Rc], writes=[RPS[pb]])
                            tt("vector", GCc[:, :, 3], PS[pb][0:64, 0:8], GCc[:, :, 0], ALU.subtract, reads=[RPS[pb], RGCc], writes=[RGCc])
                            act(GCc[:, :, 3], GCc[:, :, 3], AF.Exp, reads=[RGCc], writes=[RGCc])
                            tt("gpsimd", KD, kt_, bc3(GCc[:, :, 3], 128), ALU.mult, reads=[Rkt, RGCc], writes=[RKD])
                            Ub, RUb = Ub_r[d].next()
                            for half in range(2):
                                for n in range(4):
                                    nn = half * 4 + n
                                    mm(PS[pb + 1][0:64, n * 128:(n + 1) * 128], TT[:, nn * 64:(nn + 1) * 64], vt_[:, nn, :], True, True,
                                       reads=[RTT, Rvt], writes=[RPS[pb + 1]])
                                tt("vector", Ub[:, half * 4:(half + 1) * 4, :], PS[pb + 1][0:64, :].rearrange("p (n f) -> p n f", n=4),
                                   NBc[:, half * 4:(half + 1) * 4].unsqueeze(2).broadcast_to([64, 4, 128]), ALU.mult,
                                   reads=[RPS[pb + 1], RG], writes=[RUb])
                            WT, RWT = WT_r[d].next()
                            for n in range(8):
                                cs_ = slice(n * 64, (n + 1) * 64)
                                mm(PS[pb + 2][:, cs_], KG[:, n, :], TT[:, cs_], True, True, reads=[RKG, RTT], writes=[RPS[pb + 2]])
                            cp("scalar", WT, PS[pb + 2][:, :], reads=[RPS[pb + 2]], writes=[RWT])
                            QG, RQG = QG_r[d].next()
                            tt("vector", QG, qT, GCr, ALU.mult, reads=[RqT, RGCr], writes=[RQG])
                            pr.update(KD=(KD, RKD), Ub=(Ub, RUb), WT=(WT, RWT), QG=(QG, RQG))
                        for step in range(8):
                            for d in range(2):
                                pr = prep_done[d]
                                pb = 4 * d
                                n = step if d == 0 else 7 - step
                                ng = pr["n0"] + n
                                cs_ = slice(n * 64, (n + 1) * 64)
                                WT, RWT = pr["WT"]
                                QG, RQG = pr["QG"]
                                KD, RKD = pr["KD"]
                                Ub, RUb = pr["Ub"]
                                at_, Rat = pr["at"]
                                gl, Rgl = pr["gl"]
                                NBc = pr["NBc"]
                                psn = PS[pb + 3]
                                Rpsn = RPS[pb + 3]
                                mm(psn[0:64, 0:128], WT[:, cs_], Sbf[d][:], True, True, reads=[RWT, RSb[d]], writes=[Rpsn])
                                vn, Rvn = vn_ring[d].next()
                                stt("vector", vn, psn[0:64, 0:128], NBc[:, n:n + 1], Ub[:, n, :], ALU.mult, ALU.subtract,
                                    reads=[Rpsn, RUb, RG], writes=[Rvn])
                                mm(psn[0:64, 128:256], QG[:, cs_], Sbf[d][:], True, False, reads=[RQG, RSb[d]], writes=[Rpsn])
                                mm(psn[0:64, 128:256], at_[:, cs_], vn, False, True, reads=[Rat, Rvn], writes=[Rpsn])
                                mm(psn[:, 256:384], KD[:, n, :], vn, True, True, reads=[RKD, Rvn], writes=[Rpsn])
                                cp("scalar", O2[d][:, ng, :], psn[0:64, 128:256], reads=[Rpsn], writes=[RO[d]])
                                stt("vector", Sst[d][:], Sst[d][:], gl[:, n:n + 1], psn[:, 256:384], ALU.mult, ALU.add,
                                    reads=[RS[d], Rgl, Rpsn], writes=[RS[d]])
                                cp("gpsimd", Sbf[d][:], Sst[d][:], reads=[RS[d]], writes=[RSb[d]])
                    for g in range(NG):
                        n0 = g * 8
                        tok = t0 + g * 512
                        yb_, Ryb = yb_r.next()
                        tt("vector", yb_, O2[0][:, n0:n0 + 8, :], O2[1][:, n0:n0 + 8, :], ALU.add, reads=[RO[0], RO[1]], writes=[Ryb])
                        ybb, Rybb = ybb_r.next()
                        rs, Rrs = rs_r.next()
                        sqf, Rsqf = Ub_r[0].next()
                        tt("gpsimd", sqf, yb_, yb_, ALU.mult, reads=[Ryb], writes=[Rsqf])
                        S.op("vector", lambda e, rs=rs, sqf=sqf: e.tensor_reduce(out=rs[:, :, 0], in_=sqf, axis=AX.X, op=ALU.add),
                             reads=[Rsqf], writes=[Rrs])
                        act(rs[:, :, 1], rs[:, :, 0], AF.Sqrt, reads=[Rrs], writes=[Rrs], scale=1.0 / 128, bias=EPS)
                        S.op("vector", lambda e, rs=rs: e.reciprocal(out=rs[:, :, 1], in_=rs[:, :, 1]), reads=[Rrs], writes=[Rrs])
                        tt("vector", yb_, yb_, bc3(rs[:, :, 1], 128), ALU.mult, reads=[Ryb, Rrs], writes=[Ryb])
                        tt("gpsimd", yb_, yb_, gnb[:, :].unsqueeze(1).broadcast_to([64, 8, 128]), ALU.mult, reads=[Ryb, Rp], writes=[Ryb])
                        zt, Rz = zb_r.next()
                        dma(zt, zb[tok:tok + 512, h * 128:(h + 1) * 128].rearrange("(n t) f -> t n f", t=64), reads=[R["zb"]], writes=[Rz])
                        tt("vector", ybb, yb_, zt, ALU.mult, reads=[Ryb, Rz], writes=[Rybb])
                        psb = PS[0][:, :].bitcast(BF16)
                        for n in range(8):
                            tr(psb[:, n * 64:(n + 1) * 64], ybb[:, n, :], ident_b[0:64, 0:64], reads=[Rybb], writes=[RPS[0]])
                        yo, Ryo = yo_r.next()
                        cp("scalar", yo, psb[:, 0:512], reads=[RPS[0]], writes=[Ryo])
                        dma(ybT[h * 128:(h + 1) * 128, tok:tok + 512], yo, reads=[Ryo], writes=[R["ybT"]])
            S.barrier()

        if "D" in phases:
          with contextlib.ExitStack() as st:
            Ring.stack = st

            def sb(name, shape, dt):
                return st.enter_context(nc.sbuf_tensor(f"D{l}_{name}", shape, dt))
            Wm = [sb(f"W{i}", [128, 8, D], BF16) for i in range(3)]
            RW = Res("Wm")
            wst_ring = Ring(nc, f"D{l}_wst", [128, 8, 512], F32, 2)
            srcs = [w_br[l, 0], w_br[l, 1], w_out[l]]
            for i in range(3):
                wv = srcs[i].rearrange("(c p) n -> p c n", p=128)
                for b in range(2):
                    wst, Rwst = wst_ring.next()
                    dma(wst, wv[:, :, b * 512:(b + 1) * 512], writes=[Rwst])
                    cp("gpsimd", Wm[i][:, :, b * 512:(b + 1) * 512], wst, reads=[Rwst], writes=[RW])
            fgt = sb("fgt", [128, D], F32)
            dma(fgt[:], final_g[0:1, :].broadcast_to([128, D]), writes=[RW])
            ya_r = Ring(nc, f"D{l}_ya", [128, 8, 512], BF16, 2)
            yb_r2 = Ring(nc, f"D{l}_yb", [128, 8, 512], BF16, 2)
            g_r = Ring(nc, f"D{l}_g", [128, 16, 512], BF16, 2)
            mT_r = Ring(nc, f"D{l}_mT", [128, 8, 512], BF16, 2)
            f_r = Ring(nc, f"D{l}_f", [128, 512], F32, 4)
            x_r = Ring(nc, f"D{l}_x", [128, D], F32, 2)
            xo_r = Ring(nc, f"D{l}_xo", [128, D], F32, 2)
            ss_r = Ring(nc, f"D{l}_ss", [128, 2], F32, 2)
            junk = sb("junk", [128, D], BF16)
            Rj = Res("junk")
            pi = 0
            for tb in range(NT // 512):
                tok = tb * 512
                yat, Rya = ya_r.next()
                ybt, Ryb = yb_r2.next()
                gt, Rgt = g_r.next()
                dma(yat, yaT[:, tok:tok + 512].rearrange("(c p) t -> p c t", p=128), reads=[R["yaT"]], writes=[Rya])
                dma(ybt, ybT[:, tok:tok + 512].rearrange("(c p) t -> p c t", p=128), reads=[R["ybT"]], writes=[Ryb])
                dma(gt, gT[:, tok:tok + 512].rearrange("(c p) t -> p c t", p=128), reads=[R["gT"]], writes=[Rgt])
                mT, RmT = mT_r.next()
                for dm in range(8):
                    pa, Rpa = PS[pi % 2], RPS[pi % 2]
                    pb_, Rpb = PS[2 + pi % 2], RPS[2 + pi % 2]
                    pi += 1
                    for c in range(8):
                        mm(pa[:, :], Wm[0][:, c, dm * 128:(dm + 1) * 128], yat[:, c, :], c == 0, c == 7, reads=[RW, Rya], writes=[Rpa])
                    for c in range(8):
                        mm(pb_[:, :], Wm[1][:, c, dm * 128:(dm + 1) * 128], ybt[:, c, :], c == 0, c == 7, reads=[RW, Ryb], writes=[Rpb])
                    f0, Rf0 = f_r.next()
                    f1, Rf1 = f_r.next()
                    tt("vector", f0, pa[:, :], gt[:, dm, :], ALU.mult, reads=[Rpa, Rgt], writes=[Rf0])
                    tt("vector", f1, pb_[:, :], gt[:, 8 + dm, :], ALU.mult, reads=[Rpb, Rgt], writes=[Rf1])
                    tt("gpsimd", mT[:, dm, :], f0, f1, ALU.add, reads=[Rf0, Rf1], writes=[RmT])
                for m in range(4):
                    xt, Rxt = x_r.next()
                    dma(xt, xsrc[tok + m * 128: tok + (m + 1) * 128, :], reads=Rxsrc, writes=[Rxt])
                    xo, Rxo = xo_r.next()
                    for cb0 in (0, 512):
                        po, Rpo = PS[4 + pi % 2], RPS[4 + pi % 2]
                        pi += 1
                        for c in range(8):
                            mm(po[:, :], mT[:, c, m * 128:(m + 1) * 128], Wm[2][:, c, cb0:cb0 + 512], c == 0, c == 7, reads=[RW, RmT], writes=[Rpo])
                        tt("vector", xo[:, cb0:cb0 + 512], po[:, :], xt[:, cb0:cb0 + 512], ALU.add, reads=[Rpo, Rxt], writes=[Rxo])
                    if l < DEPTH - 1:
                        dma(x1[tok + m * 128: tok + (m + 1) * 128, :], xo, reads=[Rxo], writes=[R["x1"]])
                    else:
                        ss, Rss = ss_r.next()
                        act(junk[:], xo, AF.Square, reads=[Rxo], writes=[Rj, Rss], accum_out=ss[:, 0:1])
                        act(ss[:, 1:2], ss[:, 0:1], AF.Sqrt, reads=[Rss], writes=[Rss], scale=1.0 / D, bias=EPS)
                        S.op("vector", lambda e, ss=ss: e.reciprocal(out=ss[:, 1:2], in_=ss[:, 1:2]), reads=[Rss], writes=[Rss])
                        stt("vector", xo, xo, ss[:, 1:2], fgt[:], ALU.mult, ALU.mult, reads=[Rxo, Rss, RW], writes=[Rxo])
                        dma(yout[tok + m * 128: tok + (m + 1) * 128, :], xo, reads=[Rxo], writes=[R["yout"]])
            S.barrier()
    S.emit()
    return nc


def make_consts():
    inv = (10000.0 ** (-np.arange(0, 64, 2, dtype=np.float32) / 64)).astype(np.float32)
    pos = np.arange(4096, dtype=np.float32)
    ang = pos[:, None] * inv[None, :]
    ang = np.concatenate([ang, ang], -1)
    cos = np.cos(ang).astype(np.float32).T
    sin = np.sin(ang).astype(np.float32).T
    sgn = np.concatenate([-np.ones(32, np.float32), np.ones(32, np.float32)])[:, None]
    cosT = np.concatenate([cos, cos], 0)
    sinT = np.concatenate([sin * sgn, sin * sgn], 0)
    idx = np.arange(64)
    cf = np.zeros((128, 6, 128), np.float32)
    cf[:, 0, :] = np.eye(128, dtype=np.float32)
    cf[:, 1, :] = 1.0
    cf[:64, 2, :64] = (idx[:, None] <= idx[None, :])
    cf[:64, 3, :64] = (idx[:, None] < idx[None, :])
    cf[:64, 4, :64] = (idx[:, None] >= idx[None, :])
    cf[:64, 5, :64] = (idx[:, None] > idx[None, :])
    cbm = np.zeros((128, 3, 128), np.float32)
    cbm[:, 0, :] = np.eye(128)
    cbm[:, 1, :] = 1.0
    perm = np.zeros((128, 128), np.float32)
    for f in range(128):
        base = (f // 64) * 64
        dd = f % 64
        k = base + (dd + 32) % 64
        perm[k, f] = 1.0
    cbm[:, 2, :] = perm
    return np.ascontiguousarray(cosT), np.ascontiguousarray(sinT), cf, cbm.astype(ml_dtypes.bfloat16)


def shared_inputs(norm_g, w_in, conv_w, lam_qk, diff_norm_g, a_log, dt_bias, gdn_norm_g, w_branch, w_out, final_g):
    cosT, sinT, cf, cbm = make_consts()
    f = lambda a: np.ascontiguousarray(np.asarray(a, dtype=np.float32))
    convw = f(conv_w).reshape(DEPTH, 4, 24, 128).transpose(0, 3, 2, 1)
    return {
        "w_in": f(w_in), "w_branch": f(w_branch), "w_out": f(w_out), "norm_g": f(norm_g),
        "final_g": f(final_g).reshape(1, D), "convw": np.ascontiguousarray(convw),
        "lam_qk": f(lam_qk).reshape(DEPTH, 1, 256), "dng": f(diff_norm_g).reshape(DEPTH, 128, 1),
        "gng": f(gdn_norm_g).reshape(DEPTH, 1, 128), "a_log": f(a_log).reshape(DEPTH, 1, 16),
        "dt_bias": f(dt_bias).reshape(DEPTH, 1, 16), "cosT": cosT, "sinT": sinT, "constf": cf, "constb": cbm,
    }


_NC_CACHE = {}


def kernel(x_prompt, x_sample, norm_g, w_in, conv_w, lam_qk, diff_norm_g, a_log, dt_bias, gdn_norm_g, w_branch, w_out, final_g):
    x_prompt = np.asarray(x_prompt, dtype=np.float32)
    x_sample = np.asarray(x_sample, dtype=np.float32)
    seqs = SEQS_FULL
    key = tuple(seqs)
    if key not in _NC_CACHE:
        _NC_CACHE[key] = build(seqs)
    nc = _NC_CACHE[key]
    sh = shared_inputs(norm_g, w_in, conv_w, lam_qk, diff_norm_g, a_log, dt_bias, gdn_norm_g, w_branch, w_out, final_g)
    in_maps = []
    for c in range(8):
        xin = np.concatenate([x_prompt[c], x_sample[2 * c], x_sample[2 * c + 1]], axis=0)
        m = dict(sh)
        m["xin"] = np.ascontiguousarray(xin)
        in_maps.append(m)
    res = run_bass_kernel_spmd(nc, in_maps, core_ids=list(range(8)))
    y_prompt = np.empty_like(x_prompt)
    y_sample = np.empty_like(x_sample)
    for c in range(8):
        y = res.results[c]["yout"]
        y_prompt[c] = y[0:2048]
        y_sample[2 * c] = y[2048:2048 + 4096]
        y_sample[2 * c + 1] = y[2048 + 4096:]
    return (y_prompt, y_sample)
```

```python
import math
import os
DBG = int(os.environ.get('MKDBG', '99'))
import contextlib
import numpy as np
import ml_dtypes
import concourse.bass as bass
import concourse.mybir as mybir
from concourse.bass_utils import run_bass_kernel_spmd

F32 = mybir.dt.float32
BF16 = mybir.dt.bfloat16
AF = mybir.ActivationFunctionType
ALU = mybir.AluOpType
AX = mybir.AxisListType

D = 1024
DEPTH = 2
INW = 10272
EPS = 1e-6
SUBLN_EPS = 1e-5
SEQS_FULL = (2048, 4096, 4096)


class Res:
    __slots__ = ("name", "last_w", "readers", "excl")

    def __init__(self, name="", excl=False):
        self.name = name
        self.excl = excl
        self.last_w = None
        self.readers = []


class Op:
    __slots__ = ("eng", "fn", "deps", "needed", "semval", "dma", "dsem")

    def __init__(self, eng, fn, dma):
        self.eng = eng
        self.fn = fn
        self.deps = []
        self.needed = False
        self.semval = None
        self.dma = dma
        self.dsem = None


class Sched:
    ENG = ("tensor", "vector", "scalar", "gpsimd", "sync")
    DQ = ("sync", "gpsimd")
    NDMA = 12

    def __init__(self, nc, same_engine_sync=True):
        self.nc = nc
        self.ops = []
        self.same_engine_sync = same_engine_sync
        self.last = {e: None for e in self.ENG}
        self.lastd = {e: [] for e in self.DQ}

    def op(self, eng, fn, reads=(), writes=(), dma=False):
        o = Op(eng, fn, dma)
        deps = set()
        ex = [r for r in reads if r.excl]
        if ex:
            reads = [r for r in reads if not r.excl]
            writes = list(writes) + [r for r in ex if r not in writes]
        for r in reads:
            if r.last_w is not None:
                deps.add(r.last_w)
        for w in writes:
            if w.last_w is not None:
                deps.add(w.last_w)
            for rd in w.readers:
                deps.add(rd)
        o.deps = list(deps)
        for r in reads:
            r.readers.append(o)
        for w in writes:
            w.last_w = o
            w.readers = []
        self.ops.append(o)
        if dma:
            self.lastd[eng].append(o)
            if len(self.lastd[eng]) > self.NDMA:
                self.lastd[eng].pop(0)
        else:
            self.last[eng] = o
        return o

    def barrier(self):
        deps = [o for o in self.last.values() if o is not None]
        for q in self.DQ:
            deps += self.lastd[q]
        for e in self.ENG:
            o = Op(e, None, False)
            o.deps = list(deps)
            self.ops.append(o)

    def emit(self):
        nc = self.nc
        engs = {"tensor": nc.tensor, "vector": nc.vector, "scalar": nc.scalar,
                "gpsimd": nc.gpsimd, "sync": nc.sync}
        ses = self.same_engine_sync
        for o in self.ops:
            for d in o.deps:
                if d.dma:
                    continue
                if d.eng == o.eng and not o.dma and (o.eng == "tensor" or not ses) and o.fn is not None:
                    continue
                d.needed = True
        tl = {e: nc.alloc_semaphore(name=f"tl_{e}") for e in self.ENG}
        cnt = {e: 0 for e in self.ENG}
        dsems = {e: [nc.alloc_semaphore(name=f"d_{e}_{k}") for k in range(self.NDMA)] for e in self.DQ}
        dcount = {e: [0] * self.NDMA for e in dsems}
        dnext = {e: 0 for e in dsems}
        waited = {e: {} for e in self.ENG}
        nwait = 0
        for o in self.ops:
            E = engs[o.eng]
            w = waited[o.eng]
            reqs = {}
            for d in o.deps:
                if d.dma:
                    key = ("d", d.eng, d.dsem[0])
                    sem, val = d.dsem[1], d.dsem[2]
                else:
                    if d.semval is None:
                        continue
                    if d.eng == "tensor" and o.eng == "tensor" and not o.dma:
                        continue
                    key = ("t", d.eng)
                    sem, val = tl[d.eng], d.semval
                if reqs.get(key, (None, -1))[1] < val:
                    reqs[key] = (sem, val)
            if o.dma:
                k = dnext[o.eng]
                dnext[o.eng] = (k + 1) % self.NDMA
                prev = dcount[o.eng][k]
                if prev > 0:
                    key = ("d", o.eng, k)
                    if reqs.get(key, (None, -1))[1] < prev:
                        reqs[key] = (dsems[o.eng][k], prev)
                dcount[o.eng][k] = prev + 16
                o.dsem = (k, dsems[o.eng][k], prev + 16)
            for key, (sem, val) in reqs.items():
                if w.get(key, 0) < val:
                    E.wait_ge(sem, val)
                    w[key] = val
                    nwait += 1
            if o.fn is None:
                continue
            ins = o.fn(E)
            if o.dma:
                ins.then_inc(o.dsem[1], 16)
            elif o.needed:
                cnt[o.eng] += 1
                o.semval = cnt[o.eng]
                ins.then_inc(tl[o.eng], 1)
        for e in dsems:
            for k in range(self.NDMA):
                if dcount[e][k] > 0:
                    nc.sync.wait_ge(dsems[e][k], dcount[e][k])
        print(f"[sched] ops={len(self.ops)} waits={nwait} incs={cnt}", flush=True)


class Ring:
    def __init__(self, nc, name, shape, dtype, n):
        self.t = Ring.stack.enter_context(nc.sbuf_tensor(name, [shape[0], n] + list(shape[1:]), dtype))
        self.res = [Res(f"{name}{i}") for i in range(n)]
        self.n = n
        self.i = 0

    def next(self):
        k = self.i % self.n
        self.i += 1
        return self.t[:, k], self.res[k]


def build(seqs, debug=False, same_engine_sync=True, phases="ABCD", alim=99, nlayers=DEPTH, pool="gpsimd"):
    NT = sum(seqs)
    nc = bass.Bass("TRN2", target_bir_lowering=False)
    S = Sched(nc, same_engine_sync=same_engine_sync)

    def dram(name, shape, dt, kind):
        return nc.dram_tensor(name, shape, dt, kind=kind).ap()

    okind = "ExternalOutput" if debug else "Internal"
    xin = dram("xin", [NT, D], F32, "ExternalInput")
    w_in = dram("w_in", [DEPTH, D, INW], F32, "ExternalInput")
    w_br = dram("w_branch", [DEPTH, 2, D, D], F32, "ExternalInput")
    w_out = dram("w_out", [DEPTH, D, D], F32, "ExternalInput")
    norm_g = dram("norm_g", [DEPTH, D], F32, "ExternalInput")
    final_g = dram("final_g", [1, D], F32, "ExternalInput")
    convw = dram("convw", [DEPTH, 128, 24, 4], F32, "ExternalInput")
    lam_qk = dram("lam_qk", [DEPTH, 1, 256], F32, "ExternalInput")
    dng = dram("dng", [DEPTH, 128, 1], F32, "ExternalInput")
    gng = dram("gng", [DEPTH, 1, 128], F32, "ExternalInput")
    a_log = dram("a_log", [DEPTH, 1, 16], F32, "ExternalInput")
    dt_bias = dram("dt_bias", [DEPTH, 1, 16], F32, "ExternalInput")
    cosT_d = dram("cosT", [128, 4096], F32, "ExternalInput")
    sinT_d = dram("sinT", [128, 4096], F32, "ExternalInput")
    cf_d = dram("constf", [128, 6, 128], F32, "ExternalInput")
    cb_d = dram("constb", [128, 3, 128], BF16, "ExternalInput")
    yout = dram("yout", [NT, D], F32, "ExternalOutput")
    x1 = dram("x1", [NT, D], F32, okind)
    qkT = dram("qkT", [2048, NT], BF16, okind)
    va = dram("va", [NT, D], BF16, okind)
    zaT = dram("zaT", [D, NT], BF16, okind)
    qbT = dram("qbT", [D, NT], BF16, okind)
    kbT = dram("kbT", [D, NT], BF16, okind)
    ktm = dram("ktm", [NT, D], BF16, okind)
    vtm = dram("vtm", [NT, D], BF16, okind)
    zb = dram("zb", [NT, D], BF16, okind)
    ab = dram("ab", [NT, 32], F32, okind)
    gT = dram("gT", [2048, NT], BF16, okind)
    yaT = dram("yaT", [D, NT], BF16, okind)
    ybT = dram("ybT", [D, NT], BF16, okind)
    R = {n: Res(n) for n in "x1 qkT va zaT qbT kbT ktm vtm zb ab gT yaT ybT yout".split()}

    cf = nc.alloc_sbuf_tensor("cf", [128, 6, 128], F32)
    cb = nc.alloc_sbuf_tensor("cb", [128, 3, 128], BF16)
    Rc = Res("consts")
    S.op("sync", lambda e: e.dma_start(out=cf[:], in_=cf_d[:, :, :]), writes=[Rc], dma=True)
    S.op("sync", lambda e: e.dma_start(out=cb[:], in_=cb_d[:, :, :]), writes=[Rc], dma=True)
    ident_f, ones_f = cf[:, 0, :], cf[:, 1, :]
    ident_b, ones_b, perm_b = cb[:, 0, :], cb[:, 1, :], cb[:, 2, :]
    MI = {0: cf[0:64, 2, 0:64], 1: cf[0:64, 4, 0:64]}
    MS = {0: cf[0:64, 3, 0:64], 1: cf[0:64, 5, 0:64]}

    PS = [nc.alloc_psum_tensor(f"ps{b}", [128, 512], F32) for b in range(8)]
    RPS = [Res(f"ps{b}", excl=True) for b in range(8)]

    DMAQ = ["sync", "gpsimd"]
    dq = [0]

    def dma(out, in_, reads=(), writes=(), q=None):
        if q is None:
            q = DMAQ[dq[0] % 2]
            dq[0] += 1
        return S.op(q, lambda e: e.dma_start(out=out, in_=in_), reads=reads, writes=writes, dma=True)

    def mm(out, lhsT, rhs, start, stop, reads, writes, **kw):
        return S.op("tensor", lambda e: e.matmul(out, lhsT=lhsT, rhs=rhs, start=start, stop=stop, **kw),
                    reads=reads, writes=writes)

    def tr(out, in_, ident, reads, writes):
        return S.op("tensor", lambda e: e.transpose(out, in_, ident), reads=list(reads) + [Rc], writes=writes)

    def act(out, in_, func, reads, writes, eng="scalar", **kw):
        return S.op("scalar", lambda e: e.activation(out=out, in_=in_, func=func, **kw), reads=reads, writes=writes)

    def tt(eng, out, in0, in1, op, reads, writes):
        eng = pool if eng == "gpsimd" else eng
        return S.op(eng, lambda e: e.tensor_tensor(out=out, in0=in0, in1=in1, op=op), reads=reads, writes=writes)

    def ts(eng, out, in0, s1, op0, reads, writes, s2=None, op1=None):
        eng = pool if eng == "gpsimd" else eng
        if op1 is None:
            return S.op(eng, lambda e: e.tensor_scalar(out=out, in0=in0, scalar1=s1, scalar2=None, op0=op0),
                        reads=reads, writes=writes)
        return S.op(eng, lambda e: e.tensor_scalar(out=out, in0=in0, scalar1=s1, scalar2=s2, op0=op0, op1=op1),
                    reads=reads, writes=writes)

    def stt(eng, out, in0, scalar, in1, op0, op1, reads, writes):
        eng = "vector"
        return S.op(eng, lambda e: e.scalar_tensor_tensor(out=out, in0=in0, scalar=scalar, in1=in1, op0=op0, op1=op1),
                    reads=reads, writes=writes)

    def cp(eng, out, in_, reads, writes):
        eng = pool if eng == "gpsimd" else eng
        if eng == "scalar":
            return act(out, in_, AF.Copy, reads, writes)
        return S.op(eng, lambda e: e.tensor_copy(out=out, in_=in_), reads=reads, writes=writes)

    seq_off = [sum(seqs[:i]) for i in range(len(seqs))]

    for l in range(nlayers):
        lam_init = 0.8 - 0.6 * math.exp(-0.3 * l)
        xsrc = xin if l == 0 else x1
        Rxsrc = [] if l == 0 else [R["x1"]]
        if "A" in phases:
          with contextlib.ExitStack() as st:
            Ring.stack = st

            def sb(name, shape, dt):
                return st.enter_context(nc.sbuf_tensor(f"A{l}_{name}", shape, dt))
            SMAX = max(seqs)
            hT = sb("hT", [128, 8, SMAX], BF16)
            RhT = Res("hT")
            gtile = sb("gtile", [128, D], F32)
            Rg = Res("gtile")
            dma(gtile[:], norm_g[l:l + 1, :].broadcast_to([128, D]), writes=[Rg])
            cwt = sb("cwt", [128, 24, 4], F32)
            Rcw = Res("cw")
            dma(cwt[:], convw[l], writes=[Rcw])
            xt_ring = Ring(nc, f"A{l}_xt", [128, D], F32, 2)
            hb_ring = Ring(nc, f"A{l}_hb", [128, D], BF16, 2)
            sq_junk = sb("sqj", [128, D], BF16)
            Rsqj = Res("sqj")
            ss_ring = Ring(nc, f"A{l}_ss", [128, 2], F32, 2)
            wst_ring = Ring(nc, f"A{l}_wst", [128, 8, 512], F32, 2)
            wsec = sb("wsec", [128, 8, 2048], BF16)
            Rw = Res("wsec")
            ev_ring = Ring(nc, f"A{l}_ev", [128, 512], BF16, 4)
            evf_ring = Ring(nc, f"A{l}_evf", [128, 512], F32, 3)
            cs_ring = Ring(nc, f"A{l}_cs", [128, 2, 512], F32, 2)
            xc = sb("xc", [128, SMAX + 4], F32)
            Rxc = Res("xc")
            xacc = sb("xacc", [128, SMAX], F32)
            Rxa = Res("xacc")
            ab_ring = Ring(nc, f"A{l}_ab", [128, 32], F32, 2)
            wv = w_in[l].rearrange("(c p) n -> p c n", p=128)

            def load_w(col0, ncols):
                nb = (ncols + 511) // 512
                for b in range(nb):
                    n = min(512, ncols - b * 512)
                    wst, Rwst = wst_ring.next()
                    dma(wst[:, :, 0:n], wv[:, :, col0 + b * 512: col0 + b * 512 + n], writes=[Rwst])
                    cp("gpsimd", wsec[:, :, b * 512: b * 512 + n], wst[:, :, 0:n], reads=[Rwst], writes=[Rw])

            psi = [0]

            def nextps(k=4):
                b = psi[0] % k
                psi[0] += 1
                return PS[b], RPS[b]

            for si, Sq in enumerate(seqs):
                t0 = seq_off[si]
                NTL = Sq // 128
                NB = Sq // 512
                for m in range(NTL):
                    xt, Rxt = xt_ring.next()
                    dma(xt, xsrc[t0 + m * 128: t0 + (m + 1) * 128, :], reads=Rxsrc, writes=[Rxt])
                    ss, Rss = ss_ring.next()
                    act(sq_junk[:], xt, AF.Square, reads=[Rxt], writes=[Rsqj, Rss], accum_out=ss[:, 0:1])
                    act(ss[:, 1:2], ss[:, 0:1], AF.Sqrt, reads=[Rss], writes=[Rss], scale=1.0 / D, bias=EPS)
                    S.op("vector", lambda e, ss=ss: e.reciprocal(out=ss[:, 1:2], in_=ss[:, 1:2]), reads=[Rss], writes=[Rss])
                    hb, Rhb = hb_ring.next()
                    stt("vector", hb, xt, ss[:, 1:2], gtile[:], ALU.mult, ALU.mult, reads=[Rxt, Rss, Rg], writes=[Rhb])
                    ps, Rps = nextps()
                    psb = ps[:, :].bitcast(BF16)
                    for c in range(8):
                        tr(psb[:, c * 128:(c + 1) * 128], hb[:, c * 128:(c + 1) * 128], ident_b, reads=[Rhb], writes=[Rps])
                    cp("vector", hT[:, :, m * 128:(m + 1) * 128], psb[:, 0:1024].rearrange("p (c t) -> p c t", c=8),
                       reads=[Rps], writes=[RhT])

                def fm_matmuls(ch, tb):
                    ps, Rps = nextps()
                    for c in range(8):
                        mm(ps[:, :], wsec[:, c, ch * 128:(ch + 1) * 128], hT[:, c, tb * 512:(tb + 1) * 512],
                           c == 0, c == 7, reads=[Rw, RhT], writes=[Rps])
                    return ps, Rps

                def tm_matmuls(m, cb0, n=512):
                    ps, Rps = nextps()
                    for c in range(8):
                        mm(ps[:, 0:n], hT[:, c, m * 128:(m + 1) * 128], wsec[:, c, cb0:cb0 + n],
                           c == 0, c == 7, reads=[Rw, RhT], writes=[Rps])
                    return ps, Rps

                if alim >= 2:
                    load_w(0, 2048)
                for tb in range(NB if (alim >= 2 and DBG >= 1) else 0):
                    cs, Rcs = cs_ring.next()
                    dma(cs[:, 0, :], cosT_d[:, tb * 512:(tb + 1) * 512], writes=[Rcs])
                    dma(cs[:, 1, :], sinT_d[:, tb * 512:(tb + 1) * 512], writes=[Rcs])
                    for ch in range(16 if DBG >= 2 else 0):
                        ps, Rps = fm_matmuls(ch, tb)
                        if DBG < 3:
                            continue
                        qb_, Rqb = ev_ring.next()
                        cp("scalar", qb_, ps[:, :], reads=[Rps], writes=[Rqb])
                        if DBG < 4:
                            continue
                        t1, Rt1 = evf_ring.next()
                        if os.environ.get('MKX') == '1':
                            tt("vector", t1, ps[:, :], gtile[:, 0:512], ALU.mult, reads=[Rps, Rg], writes=[Rt1])
                        elif os.environ.get('MKX') == '3':
                            tt("vector", t1, ps[:, :], cs[:, 0, :], ALU.mult, reads=[Rps, Rcs, Rqb], writes=[Rt1])
                        elif os.environ.get('MKX') == '2':
                            cp("vector", t1, ps[:, :], reads=[Rps], writes=[Rt1])
                            tt("vector", t1, t1, cs[:, 0, :], ALU.mult, reads=[Rt1, Rcs], writes=[Rt1])
                        else:
                            tt("vector", t1, ps[:, :], cs[:, 0, :], ALU.mult, reads=[Rps, Rcs], writes=[Rt1])
                        if DBG < 5:
                            continue
                        ps2, Rps2 = PS[4 + (psi[0] % 2)], RPS[4 + (psi[0] % 2)]
                        mm(ps2[:, :], perm_b, qb_, True, True, reads=[Rqb, Rc], writes=[Rps2])
                        t2, Rt2 = evf_ring.next()
                        tt("vector", t2, ps2[:, :], cs[:, 1, :], ALU.mult, reads=[Rps2, Rcs], writes=[Rt2])
                        ob, Rob = ev_ring.next()
                        tt("gpsimd", ob, t1, t2, ALU.add, reads=[Rt1, Rt2], writes=[Rob])
                        dma(qkT[ch * 128:(ch + 1) * 128, t0 + tb * 512: t0 + (tb + 1) * 512], ob, reads=[Rob], writes=[R["qkT"]])
                if alim < 3:
                    continue
                load_w(2048, 2048)
                for m in range(NTL):
                    for cb0 in (0, 512):
                        ps, Rps = tm_matmuls(m, cb0)
                        ob, Rob = ev_ring.next()
                        cp("scalar", ob, ps[:, :], reads=[Rps], writes=[Rob])
                        dma(va[t0 + m * 128: t0 + (m + 1) * 128, cb0:cb0 + 512], ob, reads=[Rob], writes=[R["va"]])
                for tb in range(NB):
                    for ch in range(8):
                        ps, Rps = fm_matmuls(8 + ch, tb)
                        ob, Rob = ev_ring.next()
                        act(ob, ps[:, :], AF.Silu, reads=[Rps], writes=[Rob])
                        dma(zaT[ch * 128:(ch + 1) * 128, t0 + tb * 512: t0 + (tb + 1) * 512], ob, reads=[Rob], writes=[R["zaT"]])
                if alim < 4:
                    continue
                for part in range(3):
                    load_w(4096 + part * 1024, 1024)
                    for ch in range(8):
                        gch = part * 8 + ch
                        S.op("gpsimd", lambda e: e.memset(xc[:, 0:1], 0.0), writes=[Rxc])
                        S.op("gpsimd", lambda e, Sq=Sq: e.memset(xc[:, Sq + 1:Sq + 4], 0.0), writes=[Rxc])
                        for tb in range(NB):
                            ps, Rps = fm_matmuls(ch, tb)
                            cp("scalar", xc[:, 1 + tb * 512: 1 + (tb + 1) * 512], ps[:, :], reads=[Rps], writes=[Rxc])
                        ts("vector", xacc[:, 0:Sq], xc[:, 0:Sq], cwt[:, gch, 0:1], ALU.mult, reads=[Rxc, Rcw], writes=[Rxa])
                        for j in range(1, 4):
                            stt("vector", xacc[:, 0:Sq], xc[:, j:j + Sq], cwt[:, gch, j:j + 1], xacc[:, 0:Sq], ALU.mult, ALU.add,
                                reads=[Rxc, Rxa, Rcw], writes=[Rxa])
                        act(xacc[:, 0:Sq], xacc[:, 0:Sq], AF.Silu, reads=[Rxa], writes=[Rxa])
                        for tb in range(NB):
                            sl = slice(tb * 512, (tb + 1) * 512)
                            ob, Rob = ev_ring.next()
                            if part < 2:
                                sq, Rsq = evf_ring.next()
                                tt("gpsimd", sq, xacc[:, sl], xacc[:, sl], ALU.mult, reads=[Rxa], writes=[Rsq])
                                ps, Rps = nextps()
                                mm(ps[:, :], ones_f, sq, True, True, reads=[Rsq, Rc], writes=[Rps])
                                rn, Rrn = evf_ring.next()
                                act(rn, ps[:, :], AF.Sqrt, reads=[Rps], writes=[Rrn], bias=(128.0 * EPS if part == 0 else EPS),
                                    scale=(128.0 if part == 0 else 1.0))
                                S.op("vector", lambda e, rn=rn: e.reciprocal(out=rn, in_=rn), reads=[Rrn], writes=[Rrn])
                                tt("vector", ob, xacc[:, sl], rn, ALU.mult, reads=[Rxa, Rrn], writes=[Rob])
                                dst = qbT if part == 0 else kbT
                                dma(dst[ch * 128:(ch + 1) * 128, t0 + tb * 512: t0 + (tb + 1) * 512], ob, reads=[Rob],
                                    writes=[R["qbT" if part == 0 else "kbT"]])
                            else:
                                cp("vector", ob, xacc[:, sl], reads=[Rxa], writes=[Rob])
                            if part >= 1:
                                ps, Rps = nextps()
                                psb = ps[:, :].bitcast(BF16)
                                for q4 in range(4):
                                    tr(psb[:, q4 * 128:(q4 + 1) * 128], ob[:, q4 * 128:(q4 + 1) * 128], ident_b, reads=[Rob], writes=[Rps])
                                o2, Ro2 = ev_ring.next()
                                cp("vector", o2, psb[:, 0:512], reads=[Rps], writes=[Ro2])
                                dst = ktm if part == 1 else vtm
                                dma(dst[t0 + tb * 512: t0 + (tb + 1) * 512, ch * 128:(ch + 1) * 128].rearrange("(q p) f -> p q f", p=128),
                                    o2.rearrange("p (q f) -> p q f", q=4), reads=[Ro2], writes=[R["ktm" if part == 1 else "vtm"]])
                if alim < 5:
                    continue
                load_w(7168, 1056)
                for m in range(NTL):
                    for cb0 in (0, 512):
                        ps, Rps = tm_matmuls(m, cb0)
                        ob, Rob = ev_ring.next()
                        act(ob, ps[:, :], AF.Silu, reads=[Rps], writes=[Rob])
                        dma(zb[t0 + m * 128: t0 + (m + 1) * 128, cb0:cb0 + 512], ob, reads=[Rob], writes=[R["zb"]])
                    ps, Rps = tm_matmuls(m, 1024, n=32)
                    abt, Rab = ab_ring.next()
                    cp("vector", abt, ps[:, 0:32], reads=[Rps], writes=[Rab])
                    dma(ab[t0 + m * 128: t0 + (m + 1) * 128, :], abt, reads=[Rab], writes=[R["ab"]])
                if alim < 6:
                    continue
                load_w(8224, 2048)
                for tb in range(NB):
                    for ch in range(16):
                        ps, Rps = fm_matmuls(ch, tb)
                        ob, Rob = ev_ring.next()
                        act(ob, ps[:, :], AF.Sigmoid, reads=[Rps], writes=[Rob])
                        dma(gT[ch * 128:(ch + 1) * 128, t0 + tb * 512: t0 + (tb + 1) * 512], ob, reads=[Rob], writes=[R["gT"]])
            S.barrier()

        if "B" in phases:
          with contextlib.ExitStack() as st:
            Ring.stack = st

            def sb(name, shape, dt):
                return st.enter_context(nc.sbuf_tensor(f"B{l}_{name}", shape, dt))
            SMAX = max(seqs)
            lq = sb("lq", [128, 256], F32)
            lsc = sb("lsc", [128, 8], F32)
            Rl = Res("lam")
            dma(lq[:], lam_qk[l].broadcast_to([128, 256]), writes=[Rl])
            tt("vector", lq[:, 0:64], lq[:, 0:64], lq[:, 64:128], ALU.mult, reads=[Rl], writes=[Rl])
            tt("vector", lq[:, 128:192], lq[:, 128:192], lq[:, 192:256], ALU.mult, reads=[Rl], writes=[Rl])
            S.op("vector", lambda e: e.tensor_reduce(out=lsc[:, 0:1], in_=lq[:, 0:64], axis=AX.X, op=ALU.add), reads=[Rl], writes=[Rl])
            S.op("vector", lambda e: e.tensor_reduce(out=lsc[:, 1:2], in_=lq[:, 128:192], axis=AX.X, op=ALU.add), reads=[Rl], writes=[Rl])
            act(lsc[:, 2:4], lsc[:, 0:2], AF.Exp, reads=[Rl], writes=[Rl])
            tt("vector", lsc[:, 4:5], lsc[:, 3:4], lsc[:, 2:3], ALU.subtract, reads=[Rl], writes=[Rl])
            ts("vector", lsc[:, 5:6], lsc[:, 4:5], -lam_init, ALU.add, reads=[Rl], writes=[Rl])
            neglam = lsc[:, 5:6]
            dgt = sb("dgt", [128, 2], F32)
            dma(dgt[:, 0:1], dng[l], writes=[Rl])
            ts("vector", dgt[:, 1:2], dgt[:, 0:1], (1.0 - lam_init), ALU.mult, reads=[Rl], writes=[Rl])
            QTzz = [[sb(f"QT0{i}", [128, SMAX], BF16), sb(f"QT1{i}", [128, SMAX], BF16)] for i in range(2)]
            for i in range(2):
                S.op("gpsimd", lambda e, i=i: e.memset(QTzz[i][0][64:128, :], 0.0), writes=[])
                S.op("gpsimd", lambda e, i=i: e.memset(QTzz[i][1][0:64, :], 0.0), writes=[])
            KTT = [sb(f"KT{i}", [128, SMAX], BF16) for i in range(2)]
            VV = [sb(f"V{i}", [128, SMAX // 128, 128], BF16) for i in range(2)]
            RQQ, RKK, RVV = [Res("QTa"), Res("QTb")], [Res("KTa"), Res("KTb")], [Res("Va"), Res("Vb")]
            ev_ringB = Ring(nc, f"B{l}_evb", [128, 4, 512], F32, 2)
            hcount = [0]
            deferred = []

            def load_head(si_, h_, slot):
                Sq_ = seqs[si_]
                t0_h = seq_off[si_]
                dma(QTzz[slot][0][0:64, 0:Sq_], qkT[h_ * 128:h_ * 128 + 64, t0_h:t0_h + Sq_], reads=[R["qkT"]], writes=[RQQ[slot]])
                dma(QTzz[slot][1][64:128, 0:Sq_], qkT[h_ * 128 + 64:(h_ + 1) * 128, t0_h:t0_h + Sq_], reads=[R["qkT"]], writes=[RQQ[slot]])
                dma(KTT[slot][:, 0:Sq_], qkT[1024 + h_ * 128: 1024 + (h_ + 1) * 128, t0_h:t0_h + Sq_], reads=[R["qkT"]], writes=[RKK[slot]])
                dma(VV[slot][:, 0:Sq_ // 128, :], va[t0_h:t0_h + Sq_, h_ * 128:(h_ + 1) * 128].rearrange("(k p) f -> p k f", p=128),
                    reads=[R["va"]], writes=[RVV[slot]])

            heads = [(si_, h_) for si_ in range(len(seqs)) for h_ in range(8)]
            load_head(heads[0][0], heads[0][1], 0)
            pt_ring = Ring(nc, f"B{l}_pt", [128, 512], BF16, 6)
            f_ring = Ring(nc, f"B{l}_f", [128, 512], F32, 10)
            z_ring = Ring(nc, f"B{l}_z", [128, 512], BF16, 3)
            y_ring = Ring(nc, f"B{l}_y", [128, 512], BF16, 2)
            scale = 64 ** -0.5
            spi = 0
            for si, Sq in enumerate(seqs):
                t0 = seq_off[si]
                NKT = Sq // 128
                NB = Sq // 512
                for h in range(8):
                    slot = hcount[0] % 2
                    hidx = hcount[0]
                    hcount[0] += 1
                    QTz, KT, V = QTzz[slot], KTT[slot], VV[slot]
                    RQ, RK, RV = RQQ[slot], RKK[slot], RVV[slot]
                    for qb_ in range(NB):
                        if qb_ == min(1, NB - 1) and hidx + 1 < len(heads):
                            load_head(heads[hidx + 1][0], heads[hidx + 1][1], 1 - slot)
                        qs = slice(qb_ * 512, (qb_ + 1) * 512)
                        zt, Rz = z_ring.next()
                        dma(zt, zaT[h * 128:(h + 1) * 128, t0 + qb_ * 512: t0 + (qb_ + 1) * 512], reads=[R["zaT"]], writes=[Rz])
                        PIPE = 2
                        SB = (0, 1, 7)
                        pend = []

                        def flush_one():
                            kt_, c_, pt_, Rpt_ = pend.pop(0)
                            mm(PS[2 + 2 * c_][:, :], V[:, kt_, :], pt_, kt_ == 0, kt_ == NKT - 1, reads=[RV, Rpt_], writes=[RPS[2 + 2 * c_]])
                            mm(PS[3 + 2 * c_][:, :], ones_b, pt_, kt_ == 0, kt_ == NKT - 1, reads=[Rc, Rpt_], writes=[RPS[3 + 2 * c_]])

                        for kt in range(NKT):
                            if kt == NKT // 2:
                                while deferred:
                                    deferred.pop(0)()
                            for c in range(2):
                                sp, Rsp = PS[SB[spi % 3]], RPS[SB[spi % 3]]
                                spi += 1
                                mm(sp[:, :], KT[:, kt * 128:(kt + 1) * 128], QTz[c][:, qs],
                                   True, True, reads=[RK, RQ], writes=[Rsp])
                                pt, Rpt = pt_ring.next()
                                act(pt, sp[:, :], AF.Exp, reads=[Rsp], writes=[Rpt], scale=scale)
                                pend.append((kt, c, pt, Rpt))
                                if len(pend) > PIPE:
                                    flush_one()
                        while pend:
                            flush_one()
                        evb, Revb = ev_ringB.next()
                        cp("vector", evb[:, 0, :], PS[2][:, :], reads=[RPS[2]], writes=[Revb])
                        cp("scalar", evb[:, 1, :], PS[3][:, :], reads=[RPS[3]], writes=[Revb])
                        cp("vector", evb[:, 2, :], PS[4][:, :], reads=[RPS[4]], writes=[Revb])
                        cp("scalar", evb[:, 3, :], PS[5][:, :], reads=[RPS[5]], writes=[Revb])
                        r0, Rr0 = f_ring.next()
                        S.op("vector", lambda e, r0=r0, evb=evb: e.reciprocal(out=r0, in_=evb[:, 1, :]), reads=[Revb], writes=[Rr0])
                        tt("gpsimd", r0, evb[:, 0, :], r0, ALU.mult, reads=[Revb, Rr0], writes=[Rr0])
                        r1, Rr1 = f_ring.next()
                        S.op("vector", lambda e, r1=r1, evb=evb: e.reciprocal(out=r1, in_=evb[:, 3, :]), reads=[Revb], writes=[Rr1])
                        tt("gpsimd", r1, evb[:, 2, :], r1, ALU.mult, reads=[Revb, Rr1], writes=[Rr1])
                        o_, Ro = f_ring.next()
                        stt("vector", o_, r1, neglam, r0, ALU.mult, ALU.add, reads=[Rr1, Rr0, Rl], writes=[Ro])
                        sq_, Rsq = f_ring.next()
                        tt("gpsimd", sq_, o_, o_, ALU.mult, reads=[Ro], writes=[Rsq])

                        def post2(o_=o_, Ro=Ro, sq_=sq_, Rsq=Rsq, zt=zt, Rz=Rz, h=h, col0=t0 + qb_ * 512):
                            mm(PS[6][:, :], ones_f, sq_, True, True, reads=[Rsq, Rc], writes=[RPS[6]])
                            rn, Rrn = f_ring.next()
                            act(rn, PS[6][:, :], AF.Sqrt, reads=[RPS[6]], writes=[Rrn], scale=1.0 / 128, bias=SUBLN_EPS)
                            S.op("vector", lambda e, rn=rn: e.reciprocal(out=rn, in_=rn), reads=[Rrn], writes=[Rrn])
                            stt("vector", o_, o_, dgt[:, 1:2], rn, ALU.mult, ALU.mult, reads=[Ro, Rrn, Rl], writes=[Ro])
                            yt, Ry = y_ring.next()
                            tt("gpsimd", yt, o_, zt, ALU.mult, reads=[Ro, Rz], writes=[Ry])
                            dma(yaT[h * 128:(h + 1) * 128, col0:col0 + 512], yt, reads=[Ry], writes=[R["yaT"]])
                        deferred.append(post2)
            while deferred:
                deferred.pop(0)()
            S.barrier()

        if "C" in phases:
          with contextlib.ExitStack() as st:
            Ring.stack = st

            def sb(name, shape, dt):
                return st.enter_context(nc.sbuf_tensor(f"C{l}_{name}", shape, dt))
            SMAX = max(seqs)
            NMAX = SMAX // 64
            Rp = Res("gparams")
            alb = sb("alb", [64, 16], F32)
            dtb = sb("dtb", [64, 16], F32)
            dma(alb[:], a_log[l].broadcast_to([64, 16]), writes=[Rp])
            dma(dtb[:], dt_bias[l].broadcast_to([64, 16]), writes=[Rp])
            act(alb[:], alb[:], AF.Exp, reads=[Rp], writes=[Rp])
            ts("vector", alb[:], alb[:], -1.0, ALU.mult, reads=[Rp], writes=[Rp])
            gnb = sb("gnb", [64, 128], F32)
            dma(gnb[:], gng[l].broadcast_to([64, 128]), writes=[Rp])
            ABt = sb("ABt", [64, NMAX, 32], F32)
            G = sb("G", [64, NMAX, 16], F32)
            NBt = sb("NBt", [64, NMAX, 16], F32)
            RG = Res("G")
            O2 = [sb("Of", [64, NMAX, 128], F32), sb("Ob", [64, NMAX, 128], F32)]
            RO = [Res("Of"), Res("Ob")]
            Sst = [sb("Sf", [128, 128], F32), sb("Sb", [128, 128], F32)]
            Sbf = [sb("Sfb", [128, 128], BF16), sb("Sbb", [128, 128], BF16)]
            RS = [Res("Sf"), Res("Sb")]
            RSb = [Res("Sfb"), Res("Sbb")]
            vn_ring = [Ring(nc, f"C{l}_vn{d}", [64, 128], BF16, 2) for d in range(2)]

            def gb(name, shape, dt, n=1):
                return [Ring(nc, f"C{l}_{name}{d}", shape, dt, n) for d in range(2)]
            W8 = 8 * 64
            qT_r, kT_r = gb("qT", [128, W8], BF16), gb("kT", [128, W8], BF16)
            kt_r, vt_r = gb("kt", [64, 8, 128], BF16), gb("vt", [64, 8, 128], BF16)
            GU_r = gb("GU", [64, W8], F32)
            GCc_r = gb("GCc", [64, 8, 4], F32)
            GCr_r = gb("GCr", [128, W8], F32)
            Dm_r = gb("Dm", [64, W8], F32)
            Ei_r = gb("Ei", [64, W8], F32)
            at_r = gb("at", [64, W8], BF16, 2)
            P_r = [gb("Pa", [128, W8], F32), gb("Pb", [128, W8], F32)]
            PT_r = [gb("PTa", [128, W8], F32), gb("PTb", [128, W8], F32)]
            P0_r = gb("P0s", [64, W8], F32)
            PT0z = [sb(f"PT0z{d}", [128, W8], F32) for d in range(2)]
            RPT0z = [Res(f"PT0z{d}") for d in range(2)]
            for d in range(2):
                S.op("gpsimd", lambda e, d=d: e.memset(PT0z[d][:], 0.0), writes=[RPT0z[d]])
            Y_r = gb("Y", [128, W8], F32)
            TT_r = gb("TT", [64, W8], BF16)
            KG_r, KD_r = gb("KG", [64, 8, 128], BF16), gb("KD", [64, 8, 128], BF16, 2)
            Ub_r = gb("Ub", [64, 8, 128], F32, 2)
            WT_r, QG_r = gb("WT", [128, W8], BF16, 2), gb("QG", [128, W8], BF16, 2)
            gl_r = gb("gl", [128, 8], F32, 2)
            yb_r = Ring(nc, f"C{l}_yb", [64, 8, 128], F32, 2)
            ybb_r = Ring(nc, f"C{l}_ybb", [64, 8, 128], BF16, 2)
            zb_r = Ring(nc, f"C{l}_zb", [64, 8, 128], BF16, 2)
            rs_r = Ring(nc, f"C{l}_rs", [64, 8, 2], F32, 2)
            yo_r = Ring(nc, f"C{l}_yo", [128, W8], BF16, 2)

            def bc3(ap2, n):
                return ap2.unsqueeze(2).broadcast_to([ap2.shape[0], 8, n])

            def v4(ap):
                return ap.rearrange("p (m c) -> p m c", m=4)

            def v3(ap, inner):
                return ap.rearrange("p (n i) -> p n i", n=8)

            for si, Sq in enumerate(seqs):
                t0 = seq_off[si]
                N = Sq // 64
                NG = N // 8
                dma(ABt[:, 0:N, :], ab[t0:t0 + Sq, :].rearrange("(n t) c -> t n c", t=64), reads=[R["ab"]], writes=[RG], q="sync")
                al_b = alb[:, :].unsqueeze(1).broadcast_to([64, N, 16])
                dt_b = dtb[:, :].unsqueeze(1).broadcast_to([64, N, 16])
                tt("vector", G[:, 0:N, :], ABt[:, 0:N, 0:16], dt_b, ALU.add, reads=[RG, Rp], writes=[RG])
                act(G[:, 0:N, :], G[:, 0:N, :], AF.Exp, reads=[RG], writes=[RG])
                act(G[:, 0:N, :], G[:, 0:N, :], AF.Ln, reads=[RG], writes=[RG], bias=1.0)
                tt("vector", G[:, 0:N, :], G[:, 0:N, :], al_b, ALU.mult, reads=[RG, Rp], writes=[RG])
                act(NBt[:, 0:N, :], ABt[:, 0:N, 16:32], AF.Sigmoid, reads=[RG], writes=[RG])
                ts("vector", NBt[:, 0:N, :], NBt[:, 0:N, :], -1.0, ALU.mult, reads=[RG], writes=[RG])
                for h in range(8):
                    for d in range(2):
                        S.op("gpsimd", lambda e, d=d: e.memset(Sst[d][:], 0.0), writes=[RS[d]])
                        S.op("gpsimd", lambda e, d=d: e.memset(Sbf[d][:], 0.0), writes=[RSb[d]])
                    def prep_group(gi):
                        prep = {}
                        for d in range(2):
                            g = gi if d == 0 else NG - 1 - gi
                            col = d * 8 + h
                            n0 = g * 8
                            tok = t0 + g * 512
                            last = 63 if d == 0 else 0
                            qT, RqT = qT_r[d].next()
                            kT, RkT = kT_r[d].next()
                            kt_, Rkt = kt_r[d].next()
                            vt_, Rvt = vt_r[d].next()
                            dma(qT, qbT[h * 128:(h + 1) * 128, tok:tok + 512], reads=[R["qbT"]], writes=[RqT])
                            dma(kT, kbT[h * 128:(h + 1) * 128, tok:tok + 512], reads=[R["kbT"]], writes=[RkT])
                            dma(kt_, ktm[tok:tok + 512, h * 128:(h + 1) * 128].rearrange("(n t) f -> t n f", t=64), reads=[R["ktm"]], writes=[Rkt])
                            dma(vt_, vtm[tok:tok + 512, h * 128:(h + 1) * 128].rearrange("(n t) f -> t n f", t=64), reads=[R["vtm"]], writes=[Rvt])
                            Gc = G[:, n0:n0 + 8, col]
                            NBc = NBt[:, n0:n0 + 8, col]
                            GCc, RGCc = GCc_r[d].next()
                            pb = 4 * d
                            mm(PS[pb][0:64, 0:8], MI[d], Gc, True, True, reads=[RG, Rc], writes=[RPS[pb]])
                            cp("vector", GCc[:, :, 0], PS[pb][0:64, 0:8], reads=[RPS[pb]], writes=[RGCc])
                            GU, RGU = GU_r[d].next()
                            tt("gpsimd", v3(GU, 64), bc3(Gc, 64), MI[d].unsqueeze(1).broadcast_to([64, 8, 64]), ALU.mult,
                               reads=[RG, Rc], writes=[RGU])
                            mm(PS[pb + 1][:, :], ones_f[0:64, :], GU, True, True, reads=[RGU, Rc], writes=[RPS[pb + 1]])
                            GCr, RGCr = GCr_r[d].next()
                            cp("scalar", GCr, PS[pb + 1][:, :], reads=[RPS[pb + 1]], writes=[RGCr])
                            Dm, RDm = Dm_r[d].next()
                            tt("vector", v3(Dm, 64), v3(GCr[0:64, :], 64), bc3(GCc[:, :, 0], 64), ALU.subtract, reads=[RGCr, RGCc], writes=[RDm])
                            ts("vector", Dm, Dm, 0.0, ALU.min, reads=[RDm], writes=[RDm])
                            act(Dm, Dm, AF.Exp, reads=[RDm], writes=[RDm])
                            Ei, REi = Ei_r[d].next()
                            tt("gpsimd", v3(Ei, 64), v3(Dm, 64), MI[d].unsqueeze(1).broadcast_to([64, 8, 64]), ALU.mult, reads=[RDm, Rc], writes=[REi])
                            act(GCr, GCr, AF.Exp, reads=[RGCr], writes=[RGCr])
                            gl, Rgl = gl_r[d].next()
                            cp("vector", gl, v3(GCr, 64)[:, :, last], reads=[RGCr], writes=[Rgl])
                            act(GCc[:, :, 1], GCc[:, :, 0], AF.Exp, reads=[RGCc], writes=[RGCc])
                            act(GCc[:, :, 2], GCc[:, :, 0], AF.Identity, reads=[RGCc], writes=[RGCc], scale=-1.0)
                            for n in range(8):
                                cs_ = slice(n * 64, (n + 1) * 64)
                                mm(PS[pb + 2][0:64, cs_], kT[:, cs_], kT[:, cs_], True, True, reads=[RkT], writes=[RPS[pb + 2]])
                            for n in range(8):
                                cs_ = slice(n * 64, (n + 1) * 64)
                                mm(PS[pb + 1][0:64, cs_], kT[:, cs_], qT[:, cs_], True, True, reads=[RkT, RqT], writes=[RPS[pb + 1]])
                            at_, Rat = at_r[d].next()
                            tt("vector", at_, PS[pb + 1][0:64, :], Ei, ALU.mult, reads=[RPS[pb + 1], REi], writes=[Rat])
                            P0, RP0 = P0_r[d].next()
                            tt("vector", P0, PS[pb + 2][0:64, :], Ei, ALU.mult, reads=[RPS[pb + 2], REi], writes=[RP0])
                            tt("gpsimd", v3(P0, 64), v3(P0, 64), MS[d].unsqueeze(1).broadcast_to([64, 8, 64]), ALU.mult, reads=[RP0, Rc], writes=[RP0])
                            tt("gpsimd", v3(P0, 64), v3(P0, 64), bc3(NBc, 64), ALU.mult, reads=[RP0, RG], writes=[RP0])
                            prep[d] = dict(g=g, n0=n0, col=col, last=last, qT=(qT, RqT), kT=(kT, RkT), kt=(kt_, Rkt), vt=(vt_, Rvt),
                                           GCc=(GCc, RGCc), GCr=(GCr, RGCr), gl=(gl, Rgl), at=(at_, Rat), P0=(P0, RP0), NBc=NBc)
                            yield
                        cur = {}
                        for d in range(2):
                            P0, RP0 = prep[d]["P0"]
                            pb = 4 * d
                            for m in range(4):
                                tr(PS[pb][:, m * 64:(m + 1) * 64], P0[:, m * 128:(m + 1) * 128], ident_f[0:64, 0:64], reads=[RP0], writes=[RPS[pb]])
                            PT0, RPT0 = PT0z[d], RPT0z[d]
                            cp("scalar", v4(PT0)[0:64, :, 0:64], PS[pb][0:64, 0:256].rearrange("p (m i) -> p m i", m=4), reads=[RPS[pb]], writes=[RPT0])
                            cp("vector", v4(PT0)[64:128, :, 64:128], PS[pb][64:128, 0:256].rearrange("p (m i) -> p m i", m=4), reads=[RPS[pb]], writes=[RPT0])
                            for m in range(4):
                                cs_ = slice(m * 128, (m + 1) * 128)
                                tr(PS[pb + 1][:, cs_], PT0[:, cs_], ident_f, reads=[RPT0], writes=[RPS[pb + 1]])
                            P0b, RP0b = P_r[0][d].next()
                            cp("scalar", P0b, PS[pb + 1][:, :], reads=[RPS[pb + 1]], writes=[RP0b])
                            Y, RY = Y_r[d].next()
                            tt("gpsimd", v4(Y), v4(P0b), ident_f.unsqueeze(1).broadcast_to([128, 4, 128]), ALU.add, reads=[RP0b, Rc], writes=[RY])
                            cur[d] = [P0b, RP0b, PT0, RPT0, Y, RY]
                        yield
                        for k in range(0, 6):
                            for d in range(2):
                                P, RP, PT, RPT, Y, RY = cur[d]
                                pb = 4 * d
                                if k < 5:
                                    for m in range(4):
                                        cs_ = slice(m * 128, (m + 1) * 128)
                                        mm(PS[pb + 1][:, cs_], PT[:, cs_], P[:, cs_], True, True, reads=[RP, RPT], writes=[RPS[pb + 1]])
                                    for m in range(4):
                                        cs_ = slice(m * 128, (m + 1) * 128)
                                        mm(PS[pb + 2][:, cs_], P[:, cs_], PT[:, cs_], True, True, reads=[RP, RPT], writes=[RPS[pb + 2]])
                                if k >= 1:
                                    for m in range(4):
                                        cs_ = slice(m * 128, (m + 1) * 128)
                                        mm(PS[pb][:, cs_], PT[:, cs_], Y[:, cs_], True, True, reads=[RY, RPT], writes=[RPS[pb]])
                                    tt("vector", Y, Y, PS[pb][:, :], ALU.add, reads=[RY, RPS[pb]], writes=[RY])
                                if k < 5:
                                    Pn, RPn = P_r[(k + 1) % 2][d].next()
                                    PTn, RPTn = PT_r[(k + 1) % 2][d].next()
                                    cp("scalar", Pn, PS[pb + 1][:, :], reads=[RPS[pb + 1]], writes=[RPn])
                                    cp("vector", PTn, PS[pb + 2][:, :], reads=[RPS[pb + 2]], writes=[RPTn])
                                    cur[d] = [Pn, RPn, PTn, RPTn, Y, RY]
                            yield
                        for d in range(2):
                            pr = prep[d]
                            pb = 4 * d
                            Y, RY = cur[d][4], cur[d][5]
                            TT, RTT = TT_r[d].next()
                            TT4 = TT.rearrange("p (m two i) -> p m two i", m=4, two=2)
                            cp("gpsimd", TT4[:, :, 0, :], v4(Y)[0:64, :, 0:64], reads=[RY], writes=[RTT])
                            for m in range(4):
                                mm(PS[pb][0:64, m * 64:(m + 1) * 64], ident_f[:, 64:128], Y[:, m * 128 + 64:(m + 1) * 128], True, True,
                                   reads=[RY, Rc], writes=[RPS[pb]])
                            cp("vector", TT4[:, :, 1, :], PS[pb][0:64, 0:256].rearrange("p (m i) -> p m i", m=4), reads=[RPS[pb]], writes=[RTT])
                            kt_, Rkt = pr["kt"]
                            vt_, Rvt = pr["vt"]
                            GCc, RGCc = pr["GCc"]
                            GCr, RGCr = pr["GCr"]
                            gl, Rgl = pr["gl"]
                            qT, RqT = pr["qT"]
                            KG, RKG = KG_r[d].next()
                            tt("gpsimd", KG, kt_, bc3(GCc[:, :, 1], 128), ALU.mult, reads=[Rkt, RGCc], writes=[RKG])
                            pr["KG"] = (KG, RKG)
                            pr["TT"] = (TT, RTT)
                        prep_done = prep
                        yield
                        for d in range(2):
                            pr = prep_done[d]
                            pb = 4 * d
                            TT, RTT = pr["TT"]
                            KG, RKG = pr["KG"]
                            kt_, Rkt = pr["kt"]
                            vt_, Rvt = pr["vt"]
                            GCc, RGCc = pr["GCc"]
                            GCr, RGCr = pr["GCr"]
                            gl, Rgl = pr["gl"]
                            qT, RqT = pr["qT"]
                            last = pr["last"]
                            NBc = pr["NBc"]
                            KD, RKD = KD_r[d].next()
                            mm(PS[pb][0:64, 0:8], ones_f[0:64, 0:64], G[:, pr["n0"]:pr["n0"] + 8, pr["col"]], True, True, reads=[RG, Rc], writes=[RPS[pb]])
                            tt("vector", GCc[:, :, 3], PS[pb][0:64, 0:8], GCc[:, :, 0], ALU.subtract, reads=[RPS[pb], RGCc], writes=[RGCc])
                            act(GCc[:, :, 3], GCc[:, :, 3], AF.Exp, reads=[RGCc], writes=[RGCc])
                            tt("gpsimd", KD, kt_, bc3(GCc[:, :, 3], 128), ALU.mult, reads=[Rkt, RGCc], writes=[RKD])
                            Ub, RUb = Ub_r[d].next()
                            for half in range(2):
                                for n in range(4):
                                    nn = half * 4 + n
                                    mm(PS[pb + 1][0:64, n * 128:(n + 1) * 128], TT[:, nn * 64:(nn + 1) * 64], vt_[:, nn, :], True, True,
                                       reads=[RTT, Rvt], writes=[RPS[pb + 1]])
                                tt("vector", Ub[:, half * 4:(half + 1) * 4, :], PS[pb + 1][0:64, :].rearrange("p (n f) -> p n f", n=4),
                                   NBc[:, half * 4:(half + 1) * 4].unsqueeze(2).broadcast_to([64, 4, 128]), ALU.mult,
                                   reads=[RPS[pb + 1], RG], writes=[RUb])
                            WT, RWT = WT_r[d].next()
                            for n in range(8):
                                cs_ = slice(n * 64, (n + 1) * 64)
                                mm(PS[pb + 2][:, cs_], KG[:, n, :], TT[:, cs_], True, True, reads=[RKG, RTT], writes=[RPS[pb + 2]])
                            cp("scalar", WT, PS[pb + 2][:, :], reads=[RPS[pb + 2]], writes=[RWT])
                            QG, RQG = QG_r[d].next()
                            tt("vector", QG, qT, GCr, ALU.mult, reads=[RqT, RGCr], writes=[RQG])
                            pr.update(KD=(KD, RKD), Ub=(Ub, RUb), WT=(WT, RWT), QG=(QG, RQG))
                            yield
                        return prep_done


                    def scan_step(prep_done, step):
                        for d in range(2):
                            pr = prep_done[d]
                            pb = 4 * d
                            n = step if d == 0 else 7 - step
                            ng = pr["n0"] + n
                            cs_ = slice(n * 64, (n + 1) * 64)
                            WT, RWT = pr["WT"]
                            QG, RQG = pr["QG"]
                            KD, RKD = pr["KD"]
                            Ub, RUb = pr["Ub"]
                            at_, Rat = pr["at"]
                            gl, Rgl = pr["gl"]
                            NBc = pr["NBc"]
                            psn = PS[pb + 3]
                            Rpsn = RPS[pb + 3]
                            mm(psn[0:64, 0:128], WT[:, cs_], Sbf[d][:], True, True, reads=[RWT, RSb[d]], writes=[Rpsn])
                            vn, Rvn = vn_ring[d].next()
                            stt("vector", vn, psn[0:64, 0:128], NBc[:, n:n + 1], Ub[:, n, :], ALU.mult, ALU.subtract,
                                reads=[Rpsn, RUb, RG], writes=[Rvn])
                            mm(psn[0:64, 128:256], QG[:, cs_], Sbf[d][:], True, False, reads=[RQG, RSb[d]], writes=[Rpsn])
                            mm(psn[0:64, 128:256], at_[:, cs_], vn, False, True, reads=[Rat, Rvn], writes=[Rpsn])
                            mm(psn[:, 256:384], KD[:, n, :], vn, True, True, reads=[RKD, Rvn], writes=[Rpsn])
                            cp("scalar", O2[d][:, ng, :], psn[0:64, 128:256], reads=[Rpsn], writes=[RO[d]])
                            stt("vector", Sst[d][:], Sst[d][:], gl[:, n:n + 1], psn[:, 256:384], ALU.mult, ALU.add,
                                reads=[RS[d], Rgl, Rpsn], writes=[RS[d]])
                            cp("gpsimd", Sbf[d][:], Sst[d][:], reads=[RS[d]], writes=[RSb[d]])

                    def run_gen(gen, k):
                        for _ in range(k):
                            try:
                                next(gen)
                            except StopIteration as e_:
                                return e_.value
                        return None

                    pd_cur = run_gen(prep_group(0), 10 ** 6)
                    for gi in range(NG):
                        nxt = prep_group(gi + 1) if gi + 1 < NG else None
                        pd_next = None
                        for step in range(8):
                            scan_step(pd_cur, step)
                            if nxt is not None and pd_next is None:
                                pd_next = run_gen(nxt, 2)
                        if nxt is not None and pd_next is None:
                            pd_next = run_gen(nxt, 10 ** 6)
                        pd_cur = pd_next
                    for g in range(NG):
                        n0 = g * 8
                        tok = t0 + g * 512
                        yb_, Ryb = yb_r.next()
                        tt("vector", yb_, O2[0][:, n0:n0 + 8, :], O2[1][:, n0:n0 + 8, :], ALU.add, reads=[RO[0], RO[1]], writes=[Ryb])
                        ybb, Rybb = ybb_r.next()
                        rs, Rrs = rs_r.next()
                        sqf, Rsqf = Ub_r[0].next()
                        tt("gpsimd", sqf, yb_, yb_, ALU.mult, reads=[Ryb], writes=[Rsqf])
                        S.op("vector", lambda e, rs=rs, sqf=sqf: e.tensor_reduce(out=rs[:, :, 0], in_=sqf, axis=AX.X, op=ALU.add),
                             reads=[Rsqf], writes=[Rrs])
                        act(rs[:, :, 1], rs[:, :, 0], AF.Sqrt, reads=[Rrs], writes=[Rrs], scale=1.0 / 128, bias=EPS)
                        S.op("vector", lambda e, rs=rs: e.reciprocal(out=rs[:, :, 1], in_=rs[:, :, 1]), reads=[Rrs], writes=[Rrs])
                        tt("vector", yb_, yb_, bc3(rs[:, :, 1], 128), ALU.mult, reads=[Ryb, Rrs], writes=[Ryb])
                        tt("gpsimd", yb_, yb_, gnb[:, :].unsqueeze(1).broadcast_to([64, 8, 128]), ALU.mult, reads=[Ryb, Rp], writes=[Ryb])
                        zt, Rz = zb_r.next()
                        dma(zt, zb[tok:tok + 512, h * 128:(h + 1) * 128].rearrange("(n t) f -> t n f", t=64), reads=[R["zb"]], writes=[Rz])
                        tt("vector", ybb, yb_, zt, ALU.mult, reads=[Ryb, Rz], writes=[Rybb])
                        psb = PS[0][:, :].bitcast(BF16)
                        for n in range(8):
                            tr(psb[:, n * 64:(n + 1) * 64], ybb[:, n, :], ident_b[0:64, 0:64], reads=[Rybb], writes=[RPS[0]])
                        yo, Ryo = yo_r.next()
                        cp("scalar", yo, psb[:, 0:512], reads=[RPS[0]], writes=[Ryo])
                        dma(ybT[h * 128:(h + 1) * 128, tok:tok + 512], yo, reads=[Ryo], writes=[R["ybT"]])
            S.barrier()

        if "D" in phases:
          with contextlib.ExitStack() as st:
            Ring.stack = st

            def sb(name, shape, dt):
                return st.enter_context(nc.sbuf_tensor(f"D{l}_{name}", shape, dt))
            Wm = [sb(f"W{i}", [128, 8, D], BF16) for i in range(3)]
            RW = Res("Wm")
            wst_ring = Ring(nc, f"D{l}_wst", [128, 8, 512], F32, 2)
            srcs = [w_br[l, 0], w_br[l, 1], w_out[l]]
            for i in range(3):
                wv = srcs[i].rearrange("(c p) n -> p c n", p=128)
                for b in range(2):
                    wst, Rwst = wst_ring.next()
                    dma(wst, wv[:, :, b * 512:(b + 1) * 512], writes=[Rwst])
                    cp("gpsimd", Wm[i][:, :, b * 512:(b + 1) * 512], wst, reads=[Rwst], writes=[RW])
            fgt = sb("fgt", [128, D], F32)
            dma(fgt[:], final_g[0:1, :].broadcast_to([128, D]), writes=[RW])
            ya_r = Ring(nc, f"D{l}_ya", [128, 8, 512], BF16, 2)
            yb_r2 = Ring(nc, f"D{l}_yb", [128, 8, 512], BF16, 2)
            g_r = Ring(nc, f"D{l}_g", [128, 16, 512], BF16, 2)
            mT_r = Ring(nc, f"D{l}_mT", [128, 8, 512], BF16, 2)
            f_r = Ring(nc, f"D{l}_f", [128, 512], F32, 4)
            x_r = Ring(nc, f"D{l}_x", [128, D], F32, 2)
            xo_r = Ring(nc, f"D{l}_xo", [128, D], F32, 2)
            ss_r = Ring(nc, f"D{l}_ss", [128, 2], F32, 2)
            junk = sb("junk", [128, D], BF16)
            Rj = Res("junk")
            pi = 0
            for tb in range(NT // 512):
                tok = tb * 512
                yat, Rya = ya_r.next()
                ybt, Ryb = yb_r2.next()
                gt, Rgt = g_r.next()
                dma(yat, yaT[:, tok:tok + 512].rearrange("(c p) t -> p c t", p=128), reads=[R["yaT"]], writes=[Rya])
                dma(ybt, ybT[:, tok:tok + 512].rearrange("(c p) t -> p c t", p=128), reads=[R["ybT"]], writes=[Ryb])
                dma(gt, gT[:, tok:tok + 512].rearrange("(c p) t -> p c t", p=128), reads=[R["gT"]], writes=[Rgt])
                mT, RmT = mT_r.next()
                for dm in range(8):
                    pa, Rpa = PS[pi % 2], RPS[pi % 2]
                    pb_, Rpb = PS[2 + pi % 2], RPS[2 + pi % 2]
                    pi += 1
                    for c in range(8):
                        mm(pa[:, :], Wm[0][:, c, dm * 128:(dm + 1) * 128], yat[:, c, :], c == 0, c == 7, reads=[RW, Rya], writes=[Rpa])
                    for c in range(8):
                        mm(pb_[:, :], Wm[1][:, c, dm * 128:(dm + 1) * 128], ybt[:, c, :], c == 0, c == 7, reads=[RW, Ryb], writes=[Rpb])
                    f0, Rf0 = f_r.next()
                    f1, Rf1 = f_r.next()
                    tt("vector", f0, pa[:, :], gt[:, dm, :], ALU.mult, reads=[Rpa, Rgt], writes=[Rf0])
                    tt("vector", f1, pb_[:, :], gt[:, 8 + dm, :], ALU.mult, reads=[Rpb, Rgt], writes=[Rf1])
                    tt("gpsimd", mT[:, dm, :], f0, f1, ALU.add, reads=[Rf0, Rf1], writes=[RmT])
                for m in range(4):
                    xt, Rxt = x_r.next()
                    dma(xt, xsrc[tok + m * 128: tok + (m + 1) * 128, :], reads=Rxsrc, writes=[Rxt])
                    xo, Rxo = xo_r.next()
                    for cb0 in (0, 512):
                        po, Rpo = PS[4 + pi % 2], RPS[4 + pi % 2]
                        pi += 1
                        for c in range(8):
                            mm(po[:, :], mT[:, c, m * 128:(m + 1) * 128], Wm[2][:, c, cb0:cb0 + 512], c == 0, c == 7, reads=[RW, RmT], writes=[Rpo])
                        tt("vector", xo[:, cb0:cb0 + 512], po[:, :], xt[:, cb0:cb0 + 512], ALU.add, reads=[Rpo, Rxt], writes=[Rxo])
                    if l < DEPTH - 1:
                        dma(x1[tok + m * 128: tok + (m + 1) * 128, :], xo, reads=[Rxo], writes=[R["x1"]])
                    else:
                        ss, Rss = ss_r.next()
                        act(junk[:], xo, AF.Square, reads=[Rxo], writes=[Rj, Rss], accum_out=ss[:, 0:1])
                        act(ss[:, 1:2], ss[:, 0:1], AF.Sqrt, reads=[Rss], writes=[Rss], scale=1.0 / D, bias=EPS)
                        S.op("vector", lambda e, ss=ss: e.reciprocal(out=ss[:, 1:2], in_=ss[:, 1:2]), reads=[Rss], writes=[Rss])
                        stt("vector", xo, xo, ss[:, 1:2], fgt[:], ALU.mult, ALU.mult, reads=[Rxo, Rss, RW], writes=[Rxo])
                        dma(yout[tok + m * 128: tok + (m + 1) * 128, :], xo, reads=[Rxo], writes=[R["yout"]])
            S.barrier()
    S.emit()
    return nc


def make_consts():
    inv = (10000.0 ** (-np.arange(0, 64, 2, dtype=np.float32) / 64)).astype(np.float32)
    pos = np.arange(4096, dtype=np.float32)
    ang = pos[:, None] * inv[None, :]
    ang = np.concatenate([ang, ang], -1)
    cos = np.cos(ang).astype(np.float32).T
    sin = np.sin(ang).astype(np.float32).T
    sgn = np.concatenate([-np.ones(32, np.float32), np.ones(32, np.float32)])[:, None]
    cosT = np.concatenate([cos, cos], 0)
    sinT = np.concatenate([sin * sgn, sin * sgn], 0)
    idx = np.arange(64)
    cf = np.zeros((128, 6, 128), np.float32)
    cf[:, 0, :] = np.eye(128, dtype=np.float32)
    cf[:, 1, :] = 1.0
    cf[:64, 2, :64] = (idx[:, None] <= idx[None, :])
    cf[:64, 3, :64] = (idx[:, None] < idx[None, :])
    cf[:64, 4, :64] = (idx[:, None] >= idx[None, :])
    cf[:64, 5, :64] = (idx[:, None] > idx[None, :])
    cbm = np.zeros((128, 3, 128), np.float32)
    cbm[:, 0, :] = np.eye(128)
    cbm[:, 1, :] = 1.0
    perm = np.zeros((128, 128), np.float32)
    for f in range(128):
        base = (f // 64) * 64
        dd = f % 64
        k = base + (dd + 32) % 64
        perm[k, f] = 1.0
    cbm[:, 2, :] = perm
    return np.ascontiguousarray(cosT), np.ascontiguousarray(sinT), cf, cbm.astype(ml_dtypes.bfloat16)


def shared_inputs(norm_g, w_in, conv_w, lam_qk, diff_norm_g, a_log, dt_bias, gdn_norm_g, w_branch, w_out, final_g):
    cosT, sinT, cf, cbm = make_consts()
    f = lambda a: np.ascontiguousarray(np.asarray(a, dtype=np.float32))
    convw = f(conv_w).reshape(DEPTH, 4, 24, 128).transpose(0, 3, 2, 1)
    return {
        "w_in": f(w_in), "w_branch": f(w_branch), "w_out": f(w_out), "norm_g": f(norm_g),
        "final_g": f(final_g).reshape(1, D), "convw": np.ascontiguousarray(convw),
        "lam_qk": f(lam_qk).reshape(DEPTH, 1, 256), "dng": f(diff_norm_g).reshape(DEPTH, 128, 1),
        "gng": f(gdn_norm_g).reshape(DEPTH, 1, 128), "a_log": f(a_log).reshape(DEPTH, 1, 16),
        "dt_bias": f(dt_bias).reshape(DEPTH, 1, 16), "cosT": cosT, "sinT": sinT, "constf": cf, "constb": cbm,
    }


_NC_CACHE = {}


def kernel(x_prompt, x_sample, norm_g, w_in, conv_w, lam_qk, diff_norm_g, a_log, dt_bias, gdn_norm_g, w_branch, w_out, final_g):
    x_prompt = np.asarray(x_prompt, dtype=np.float32)
    x_sample = np.asarray(x_sample, dtype=np.float32)
    seqs = SEQS_FULL
    key = tuple(seqs)
    if key not in _NC_CACHE:
        _NC_CACHE[key] = build(seqs)
    nc = _NC_CACHE[key]
    sh = shared_inputs(norm_g, w_in, conv_w, lam_qk, diff_norm_g, a_log, dt_bias, gdn_norm_g, w_branch, w_out, final_g)
    in_maps = []
    for c in range(8):
        xin = np.concatenate([x_prompt[c], x_sample[2 * c], x_sample[2 * c + 1]], axis=0)
        m = dict(sh)
        m["xin"] = np.ascontiguousarray(xin)
        in_maps.append(m)
    res = run_bass_kernel_spmd(nc, in_maps, core_ids=list(range(8)))
    y_prompt = np.empty_like(x_prompt)
    y_sample = np.empty_like(x_sample)
    for c in range(8):
        y = res.results[c]["yout"]
        y_prompt[c] = y[0:2048]
        y_sample[2 * c] = y[2048:2048 + 4096]
        y_sample[2 * c + 1] = y[2048 + 4096:]
    return (y_prompt, y_sample)
```

```python
import math
import os
DBG = int(os.environ.get('MKDBG', '99'))
import contextlib
import numpy as np
import ml_dtypes
import concourse.bass as bass
import concourse.mybir as mybir
from concourse.bass_utils import run_bass_kernel_spmd

F32 = mybir.dt.float32
BF16 = mybir.dt.bfloat16
AF = mybir.ActivationFunctionType
ALU = mybir.AluOpType
AX = mybir.AxisListType

D = 1024
DEPTH = 2
INW = 10272
EPS = 1e-6
SUBLN_EPS = 1e-5
SEQS_FULL = (2048, 4096, 4096)


class Res:
    __slots__ = ("name", "last_w", "readers", "excl")

    def __init__(self, name="", excl=False):
        self.name = name
        self.excl = excl
        self.last_w = None
        self.readers = []


class Op:
    __slots__ = ("eng", "fn", "deps", "needed", "semval", "dma", "dsem")

    def __init__(self, eng, fn, dma):
        self.eng = eng
        self.fn = fn
        self.deps = []
        self.needed = False
        self.semval = None
        self.dma = dma
        self.dsem = None


class Sched:
    ENG = ("tensor", "vector", "scalar", "gpsimd", "sync")
    DQ = ("sync", "gpsimd")
    NDMA = 12

    def __init__(self, nc, same_engine_sync=True):
        self.nc = nc
        self.ops = []
        self.same_engine_sync = same_engine_sync
        self.last = {e: None for e in self.ENG}
        self.lastd = {e: [] for e in self.DQ}

    def op(self, eng, fn, reads=(), writes=(), dma=False):
        o = Op(eng, fn, dma)
        deps = set()
        ex = [r for r in reads if r.excl]
        if ex:
            reads = [r for r in reads if not r.excl]
            writes = list(writes) + [r for r in ex if r not in writes]
        for r in reads:
            if r.last_w is not None:
                deps.add(r.last_w)
        for w in writes:
            if w.last_w is not None:
                deps.add(w.last_w)
            for rd in w.readers:
                deps.add(rd)
        o.deps = list(deps)
        for r in reads:
            r.readers.append(o)
        for w in writes:
            w.last_w = o
            w.readers = []
        self.ops.append(o)
        if dma:
            self.lastd[eng].append(o)
            if len(self.lastd[eng]) > self.NDMA:
                self.lastd[eng].pop(0)
        else:
            self.last[eng] = o
        return o

    def barrier(self):
        deps = [o for o in self.last.values() if o is not None]
        for q in self.DQ:
            deps += self.lastd[q]
        for e in self.ENG:
            o = Op(e, None, False)
            o.deps = list(deps)
            self.ops.append(o)

    def emit(self):
        nc = self.nc
        engs = {"tensor": nc.tensor, "vector": nc.vector, "scalar": nc.scalar,
                "gpsimd": nc.gpsimd, "sync": nc.sync}
        ses = self.same_engine_sync
        for o in self.ops:
            for d in o.deps:
                if d.dma:
                    continue
                if d.eng == o.eng and not o.dma and (o.eng == "tensor" or not ses) and o.fn is not None:
                    continue
                d.needed = True
        tl = {e: nc.alloc_semaphore(name=f"tl_{e}") for e in self.ENG}
        cnt = {e: 0 for e in self.ENG}
        dsems = {e: [nc.alloc_semaphore(name=f"d_{e}_{k}") for k in range(self.NDMA)] for e in self.DQ}
        dcount = {e: [0] * self.NDMA for e in dsems}
        dnext = {e: 0 for e in dsems}
        waited = {e: {} for e in self.ENG}
        nwait = 0
        for o in self.ops:
            E = engs[o.eng]
            w = waited[o.eng]
            reqs = {}
            for d in o.deps:
                if d.dma:
                    key = ("d", d.eng, d.dsem[0])
                    sem, val = d.dsem[1], d.dsem[2]
                else:
                    if d.semval is None:
                        continue
                    if d.eng == "tensor" and o.eng == "tensor" and not o.dma:
                        continue
                    key = ("t", d.eng)
                    sem, val = tl[d.eng], d.semval
                if reqs.get(key, (None, -1))[1] < val:
                    reqs[key] = (sem, val)
            if o.dma:
                k = dnext[o.eng]
                dnext[o.eng] = (k + 1) % self.NDMA
                prev = dcount[o.eng][k]
                if prev > 0:
                    key = ("d", o.eng, k)
                    if reqs.get(key, (None, -1))[1] < prev:
                        reqs[key] = (dsems[o.eng][k], prev)
                dcount[o.eng][k] = prev + 16
                o.dsem = (k, dsems[o.eng][k], prev + 16)
            for key, (sem, val) in reqs.items():
                if w.get(key, 0) < val:
                    E.wait_ge(sem, val)
                    w[key] = val
                    nwait += 1
            if o.fn is None:
                continue
            ins = o.fn(E)
            if o.dma:
                ins.then_inc(o.dsem[1], 16)
            elif o.needed:
                cnt[o.eng] += 1
                o.semval = cnt[o.eng]
                ins.then_inc(tl[o.eng], 1)
        for e in dsems:
            for k in range(self.NDMA):
                if dcount[e][k] > 0:
                    nc.sync.wait_ge(dsems[e][k], dcount[e][k])
        print(f"[sched] ops={len(self.ops)} waits={nwait} incs={cnt}", flush=True)


class Ring:
    def __init__(self, nc, name, shape, dtype, n):
        self.t = Ring.stack.enter_context(nc.sbuf_tensor(name, [shape[0], n] + list(shape[1:]), dtype))
        self.res = [Res(f"{name}{i}") for i in range(n)]
        self.n = n
        self.i = 0

    def next(self):
        k = self.i % self.n
        self.i += 1
        return self.t[:, k], self.res[k]


def build(seqs, debug=False, same_engine_sync=True, phases="ABCD", alim=99, nlayers=DEPTH, pool="gpsimd"):
    NT = sum(seqs)
    nc = bass.Bass("TRN2", target_bir_lowering=False)
    S = Sched(nc, same_engine_sync=same_engine_sync)

    def dram(name, shape, dt, kind):
        return nc.dram_tensor(name, shape, dt, kind=kind).ap()

    okind = "ExternalOutput" if debug else "Internal"
    xin = dram("xin", [NT, D], F32, "ExternalInput")
    w_in = dram("w_in", [DEPTH, D, INW], F32, "ExternalInput")
    w_br = dram("w_branch", [DEPTH, 2, D, D], F32, "ExternalInput")
    w_out = dram("w_out", [DEPTH, D, D], F32, "ExternalInput")
    norm_g = dram("norm_g", [DEPTH, D], F32, "ExternalInput")
    final_g = dram("final_g", [1, D], F32, "ExternalInput")
    convw = dram("convw", [DEPTH, 128, 24, 4], F32, "ExternalInput")
    lam_qk = dram("lam_qk", [DEPTH, 1, 256], F32, "ExternalInput")
    dng = dram("dng", [DEPTH, 128, 1], F32, "ExternalInput")
    gng = dram("gng", [DEPTH, 1, 128], F32, "ExternalInput")
    a_log = dram("a_log", [DEPTH, 1, 16], F32, "ExternalInput")
    dt_bias = dram("dt_bias", [DEPTH, 1, 16], F32, "ExternalInput")
    cosT_d = dram("cosT", [128, 4096], F32, "ExternalInput")
    sinT_d = dram("sinT", [128, 4096], F32, "ExternalInput")
    cf_d = dram("constf", [128, 6, 128], F32, "ExternalInput")
    cb_d = dram("constb", [128, 3, 128], BF16, "ExternalInput")
    yout = dram("yout", [NT, D], F32, "ExternalOutput")
    x1 = dram("x1", [NT, D], F32, okind)
    qkT = dram("qkT", [2048, NT], BF16, okind)
    va = dram("va", [NT, D], BF16, okind)
    zaT = dram("zaT", [D, NT], BF16, okind)
    qbT = dram("qbT", [D, NT], BF16, okind)
    kbT = dram("kbT", [D, NT], BF16, okind)
    ktm = dram("ktm", [NT, D], BF16, okind)
    vtm = dram("vtm", [NT, D], BF16, okind)
    zb = dram("zb", [NT, D], BF16, okind)
    ab = dram("ab", [NT, 32], F32, okind)
    gT = dram("gT", [2048, NT], BF16, okind)
    yaT = dram("yaT", [D, NT], BF16, okind)
    ybT = dram("ybT", [D, NT], BF16, okind)
    R = {n: Res(n) for n in "x1 qkT va zaT qbT kbT ktm vtm zb ab gT yaT ybT yout".split()}
    Rvals = list(R.values())

    cf = nc.alloc_sbuf_tensor("cf", [128, 6, 128], F32)
    cb = nc.alloc_sbuf_tensor("cb", [128, 3, 128], BF16)
    Rc = Res("consts")
    S.op("sync", lambda e: e.dma_start(out=cf[:], in_=cf_d[:, :, :]), writes=[Rc], dma=True)
    S.op("sync", lambda e: e.dma_start(out=cb[:], in_=cb_d[:, :, :]), writes=[Rc], dma=True)
    ident_f, ones_f = cf[:, 0, :], cf[:, 1, :]
    ident_b, ones_b, perm_b = cb[:, 0, :], cb[:, 1, :], cb[:, 2, :]
    MI = {0: cf[0:64, 2, 0:64], 1: cf[0:64, 4, 0:64]}
    MS = {0: cf[0:64, 3, 0:64], 1: cf[0:64, 5, 0:64]}

    PS = [nc.alloc_psum_tensor(f"ps{b}", [128, 512], F32) for b in range(8)]
    RPS = [Res(f"ps{b}", excl=True) for b in range(8)]

    DMAQ = ["sync", "gpsimd"]
    route = ["alt"]
    dq = [0]

    def dma(out, in_, reads=(), writes=(), q=None):
        if q is None:
            if route[0] == "alt":
                q = DMAQ[dq[0] % 2]
                dq[0] += 1
            else:
                q = "gpsimd" if any(w in Rvals for w in writes) else "sync"
        return S.op(q, lambda e: e.dma_start(out=out, in_=in_), reads=reads, writes=writes, dma=True)

    def mm(out, lhsT, rhs, start, stop, reads, writes, **kw):
        return S.op("tensor", lambda e: e.matmul(out, lhsT=lhsT, rhs=rhs, start=start, stop=stop, **kw),
                    reads=reads, writes=writes)

    def tr(out, in_, ident, reads, writes):
        return S.op("tensor", lambda e: e.transpose(out, in_, ident), reads=list(reads) + [Rc], writes=writes)

    def act(out, in_, func, reads, writes, eng="scalar", **kw):
        return S.op("scalar", lambda e: e.activation(out=out, in_=in_, func=func, **kw), reads=reads, writes=writes)

    def tt(eng, out, in0, in1, op, reads, writes):
        eng = pool if eng == "gpsimd" else eng
        return S.op(eng, lambda e: e.tensor_tensor(out=out, in0=in0, in1=in1, op=op), reads=reads, writes=writes)

    def ts(eng, out, in0, s1, op0, reads, writes, s2=None, op1=None):
        eng = pool if eng == "gpsimd" else eng
        if op1 is None:
            return S.op(eng, lambda e: e.tensor_scalar(out=out, in0=in0, scalar1=s1, scalar2=None, op0=op0),
                        reads=reads, writes=writes)
        return S.op(eng, lambda e: e.tensor_scalar(out=out, in0=in0, scalar1=s1, scalar2=s2, op0=op0, op1=op1),
                    reads=reads, writes=writes)

    def stt(eng, out, in0, scalar, in1, op0, op1, reads, writes):
        eng = "vector"
        return S.op(eng, lambda e: e.scalar_tensor_tensor(out=out, in0=in0, scalar=scalar, in1=in1, op0=op0, op1=op1),
                    reads=reads, writes=writes)

    def cp(eng, out, in_, reads, writes):
        eng = pool if eng == "gpsimd" else eng
        if eng == "scalar":
            return act(out, in_, AF.Copy, reads, writes)
        return S.op(eng, lambda e: e.tensor_copy(out=out, in_=in_), reads=reads, writes=writes)

    seq_off = [sum(seqs[:i]) for i in range(len(seqs))]

    for l in range(nlayers):
        lam_init = 0.8 - 0.6 * math.exp(-0.3 * l)
        xsrc = xin if l == 0 else x1
        Rxsrc = [] if l == 0 else [R["x1"]]
        route[0] = "alt"
        if "A" in phases:
          with contextlib.ExitStack() as st:
            Ring.stack = st

            def sb(name, shape, dt):
                return st.enter_context(nc.sbuf_tensor(f"A{l}_{name}", shape, dt))
            SMAX = max(seqs)
            hT = sb("hT", [128, 8, SMAX], BF16)
            RhT = Res("hT")
            gtile = sb("gtile", [128, D], F32)
            Rg = Res("gtile")
            dma(gtile[:], norm_g[l:l + 1, :].broadcast_to([128, D]), writes=[Rg])
            cwt = sb("cwt", [128, 24, 4], F32)
            Rcw = Res("cw")
            dma(cwt[:], convw[l], writes=[Rcw])
            xt_ring = Ring(nc, f"A{l}_xt", [128, D], F32, 2)
            hb_ring = Ring(nc, f"A{l}_hb", [128, D], BF16, 2)
            sq_junk = sb("sqj", [128, D], BF16)
            Rsqj = Res("sqj")
            ss_ring = Ring(nc, f"A{l}_ss", [128, 2], F32, 2)
            wst_ring = Ring(nc, f"A{l}_wst", [128, 8, 256], F32, 2)
            wsec = sb("wsec", [128, 8, 2048], BF16)
            Rwb = [Res(f"wsec{b}") for b in range(5)]
            ev_ring = Ring(nc, f"A{l}_ev", [128, 512], BF16, 4)
            evf_ring = Ring(nc, f"A{l}_evf", [128, 512], F32, 3)
            cs_ring = Ring(nc, f"A{l}_cs", [128, 2, 512], F32, 2)
            xc = sb("xc", [128, SMAX + 4], F32)
            Rxc = Res("xc")
            xacc_ring = Ring(nc, f"A{l}_xacc", [128, SMAX], F32, 2)
            ab_ring = Ring(nc, f"A{l}_ab", [128, 32], F32, 2)
            wv = w_in[l].rearrange("(c p) n -> p c n", p=128)

            def load_w(col0, ncols, eng="vector"):
                nb = (ncols + 255) // 256
                for b in range(nb):
                    n = min(256, ncols - b * 256)
                    wst, Rwst = wst_ring.next()
                    dma(wst[:, :, 0:n], wv[:, :, col0 + b * 256: col0 + b * 256 + n], writes=[Rwst])
                    cp(eng, wsec[:, :, b * 256: b * 256 + n], wst[:, :, 0:n], reads=[Rwst], writes=[Rwb[(b * 256) // 512]])

            psi = [0]

            def nextps(k=4):
                b = psi[0] % k
                psi[0] += 1
                return PS[b], RPS[b]

            for si, Sq in enumerate(seqs):
                t0 = seq_off[si]
                NTL = Sq // 128
                NB = Sq // 512
                for m in range(NTL):
                    xt, Rxt = xt_ring.next()
                    dma(xt, xsrc[t0 + m * 128: t0 + (m + 1) * 128, :], reads=Rxsrc, writes=[Rxt])
                    ss, Rss = ss_ring.next()
                    act(sq_junk[:], xt, AF.Square, reads=[Rxt], writes=[Rsqj, Rss], accum_out=ss[:, 0:1])
                    act(ss[:, 1:2], ss[:, 0:1], AF.Sqrt, reads=[Rss], writes=[Rss], scale=1.0 / D, bias=EPS)
                    S.op("vector", lambda e, ss=ss: e.reciprocal(out=ss[:, 1:2], in_=ss[:, 1:2]), reads=[Rss], writes=[Rss])
                    hb, Rhb = hb_ring.next()
                    stt("vector", hb, xt, ss[:, 1:2], gtile[:], ALU.mult, ALU.mult, reads=[Rxt, Rss, Rg], writes=[Rhb])
                    ps, Rps = nextps()
                    psb = ps[:, :].bitcast(BF16)
                    for c in range(8):
                        tr(psb[:, c * 128:(c + 1) * 128], hb[:, c * 128:(c + 1) * 128], ident_b, reads=[Rhb], writes=[Rps])
                    cp("vector", hT[:, :, m * 128:(m + 1) * 128], psb[:, 0:1024].rearrange("p (c t) -> p c t", c=8),
                       reads=[Rps], writes=[RhT])

                def fm_matmuls(ch, tb):
                    ps, Rps = nextps()
                    for c in range(8):
                        mm(ps[:, :], wsec[:, c, ch * 128:(ch + 1) * 128], hT[:, c, tb * 512:(tb + 1) * 512],
                           c == 0, c == 7, reads=[Rwb[ch // 4], RhT], writes=[Rps])
                    return ps, Rps

                def tm_matmuls(m, cb0, n=512):
                    ps, Rps = nextps()
                    for c in range(8):
                        mm(ps[:, 0:n], hT[:, c, m * 128:(m + 1) * 128], wsec[:, c, cb0:cb0 + n],
                           c == 0, c == 7, reads=[Rwb[cb0 // 512], RhT], writes=[Rps])
                    return ps, Rps

                if alim >= 2:
                    load_w(0, 2048)
                for tb in range(NB if (alim >= 2 and DBG >= 1) else 0):
                    cs, Rcs = cs_ring.next()
                    dma(cs[:, 0, :], cosT_d[:, tb * 512:(tb + 1) * 512], writes=[Rcs])
                    dma(cs[:, 1, :], sinT_d[:, tb * 512:(tb + 1) * 512], writes=[Rcs])
                    for ch in range(16 if DBG >= 2 else 0):
                        ps, Rps = fm_matmuls(ch, tb)
                        if DBG < 3:
                            continue
                        qb_, Rqb = ev_ring.next()
                        cp("scalar", qb_, ps[:, :], reads=[Rps], writes=[Rqb])
                        if DBG < 4:
                            continue
                        t1, Rt1 = evf_ring.next()
                        if os.environ.get('MKX') == '1':
                            tt("vector", t1, ps[:, :], gtile[:, 0:512], ALU.mult, reads=[Rps, Rg], writes=[Rt1])
                        elif os.environ.get('MKX') == '3':
                            tt("vector", t1, ps[:, :], cs[:, 0, :], ALU.mult, reads=[Rps, Rcs, Rqb], writes=[Rt1])
                        elif os.environ.get('MKX') == '2':
                            cp("vector", t1, ps[:, :], reads=[Rps], writes=[Rt1])
                            tt("vector", t1, t1, cs[:, 0, :], ALU.mult, reads=[Rt1, Rcs], writes=[Rt1])
                        else:
                            tt("vector", t1, ps[:, :], cs[:, 0, :], ALU.mult, reads=[Rps, Rcs], writes=[Rt1])
                        if DBG < 5:
                            continue
                        ps2, Rps2 = PS[4 + (psi[0] % 2)], RPS[4 + (psi[0] % 2)]
                        mm(ps2[:, :], perm_b, qb_, True, True, reads=[Rqb, Rc], writes=[Rps2])
                        t2, Rt2 = evf_ring.next()
                        tt("vector", t2, ps2[:, :], cs[:, 1, :], ALU.mult, reads=[Rps2, Rcs], writes=[Rt2])
                        ob, Rob = ev_ring.next()
                        tt("gpsimd", ob, t1, t2, ALU.add, reads=[Rt1, Rt2], writes=[Rob])
                        dma(qkT[ch * 128:(ch + 1) * 128, t0 + tb * 512: t0 + (tb + 1) * 512], ob, reads=[Rob], writes=[R["qkT"]])
                if alim < 3:
                    continue
                load_w(2048, 2048)
                for cb0 in (0, 512):
                    for m in range(NTL):
                        ps, Rps = tm_matmuls(m, cb0)
                        ob, Rob = ev_ring.next()
                        cp("scalar", ob, ps[:, :], reads=[Rps], writes=[Rob])
                        dma(va[t0 + m * 128: t0 + (m + 1) * 128, cb0:cb0 + 512], ob, reads=[Rob], writes=[R["va"]])
                for ch in range(8):
                    for tb in range(NB):
                        ps, Rps = fm_matmuls(8 + ch, tb)
                        ob, Rob = ev_ring.next()
                        act(ob, ps[:, :], AF.Silu, reads=[Rps], writes=[Rob])
                        dma(zaT[ch * 128:(ch + 1) * 128, t0 + tb * 512: t0 + (tb + 1) * 512], ob, reads=[Rob], writes=[R["zaT"]])
                if alim < 4:
                    continue
                for part in range(3):
                    load_w(4096 + part * 1024, 1024, eng="gpsimd")
                    for ch in range(8):
                        gch = part * 8 + ch
                        S.op("gpsimd", lambda e: e.memset(xc[:, 0:1], 0.0), writes=[Rxc])
                        S.op("gpsimd", lambda e, Sq=Sq: e.memset(xc[:, Sq + 1:Sq + 4], 0.0), writes=[Rxc])
                        for tb in range(NB):
                            ps, Rps = fm_matmuls(ch, tb)
                            cp("scalar", xc[:, 1 + tb * 512: 1 + (tb + 1) * 512], ps[:, :], reads=[Rps], writes=[Rxc])
                        xacc, Rxa = xacc_ring.next()
                        ts("vector", xacc[:, 0:Sq], xc[:, 0:Sq], cwt[:, gch, 0:1], ALU.mult, reads=[Rxc, Rcw], writes=[Rxa])
                        for j in range(1, 4):
                            stt("vector", xacc[:, 0:Sq], xc[:, j:j + Sq], cwt[:, gch, j:j + 1], xacc[:, 0:Sq], ALU.mult, ALU.add,
                                reads=[Rxc, Rxa, Rcw], writes=[Rxa])
                        act(xacc[:, 0:Sq], xacc[:, 0:Sq], AF.Silu, reads=[Rxa], writes=[Rxa])
                        for tb in range(NB):
                            sl = slice(tb * 512, (tb + 1) * 512)
                            ob, Rob = ev_ring.next()
                            if part < 2:
                                sq, Rsq = evf_ring.next()
                                tt("gpsimd", sq, xacc[:, sl], xacc[:, sl], ALU.mult, reads=[Rxa], writes=[Rsq])
                                ps, Rps = nextps()
                                mm(ps[:, :], ones_f, sq, True, True, reads=[Rsq, Rc], writes=[Rps])
                                rn, Rrn = evf_ring.next()
                                act(rn, ps[:, :], AF.Ln, reads=[Rps], writes=[Rrn], bias=(128.0 * EPS if part == 0 else EPS),
                                    scale=(128.0 if part == 0 else 1.0))
                                act(rn, rn, AF.Exp, reads=[Rrn], writes=[Rrn], scale=-0.5)
                                tt("vector", ob, xacc[:, sl], rn, ALU.mult, reads=[Rxa, Rrn], writes=[Rob])
                                dst = qbT if part == 0 else kbT
                                dma(dst[ch * 128:(ch + 1) * 128, t0 + tb * 512: t0 + (tb + 1) * 512], ob, reads=[Rob],
                                    writes=[R["qbT" if part == 0 else "kbT"]])
                            else:
                                cp("vector", ob, xacc[:, sl], reads=[Rxa], writes=[Rob])
                            if part >= 1:
                                ps, Rps = nextps()
                                psb = ps[:, :].bitcast(BF16)
                                for q4 in range(4):
                                    tr(psb[:, q4 * 128:(q4 + 1) * 128], ob[:, q4 * 128:(q4 + 1) * 128], ident_b, reads=[Rob], writes=[Rps])
                                o2, Ro2 = ev_ring.next()
                                cp("vector", o2, psb[:, 0:512], reads=[Rps], writes=[Ro2])
                                dst = ktm if part == 1 else vtm
                                dma(dst[t0 + tb * 512: t0 + (tb + 1) * 512, ch * 128:(ch + 1) * 128].rearrange("(q p) f -> p q f", p=128),
                                    o2.rearrange("p (q f) -> p q f", q=4), reads=[Ro2], writes=[R["ktm" if part == 1 else "vtm"]])
                if alim < 5:
                    continue
                load_w(7168, 1056)
                for cb0 in (0, 512):
                    for m in range(NTL):
                        ps, Rps = tm_matmuls(m, cb0)
                        ob, Rob = ev_ring.next()
                        act(ob, ps[:, :], AF.Silu, reads=[Rps], writes=[Rob])
                        dma(zb[t0 + m * 128: t0 + (m + 1) * 128, cb0:cb0 + 512], ob, reads=[Rob], writes=[R["zb"]])
                for m in range(NTL):
                    ps, Rps = tm_matmuls(m, 1024, n=32)
                    abt, Rab = ab_ring.next()
                    cp("vector", abt, ps[:, 0:32], reads=[Rps], writes=[Rab])
                    dma(ab[t0 + m * 128: t0 + (m + 1) * 128, :], abt, reads=[Rab], writes=[R["ab"]])
                if alim < 6:
                    continue
                load_w(8224, 2048)
                for ch in range(16):
                    for tb in range(NB):
                        ps, Rps = fm_matmuls(ch, tb)
                        ob, Rob = ev_ring.next()
                        act(ob, ps[:, :], AF.Sigmoid, reads=[Rps], writes=[Rob])
                        dma(gT[ch * 128:(ch + 1) * 128, t0 + tb * 512: t0 + (tb + 1) * 512], ob, reads=[Rob], writes=[R["gT"]])
            S.barrier()

        route[0] = "split"
        if "B" in phases:
          with contextlib.ExitStack() as st:
            Ring.stack = st

            def sb(name, shape, dt):
                return st.enter_context(nc.sbuf_tensor(f"B{l}_{name}", shape, dt))
            SMAX = max(seqs)
            lq = sb("lq", [128, 256], F32)
            lsc = sb("lsc", [128, 8], F32)
            Rl = Res("lam")
            dma(lq[:], lam_qk[l].broadcast_to([128, 256]), writes=[Rl])
            tt("vector", lq[:, 0:64], lq[:, 0:64], lq[:, 64:128], ALU.mult, reads=[Rl], writes=[Rl])
            tt("vector", lq[:, 128:192], lq[:, 128:192], lq[:, 192:256], ALU.mult, reads=[Rl], writes=[Rl])
            S.op("vector", lambda e: e.tensor_reduce(out=lsc[:, 0:1], in_=lq[:, 0:64], axis=AX.X, op=ALU.add), reads=[Rl], writes=[Rl])
            S.op("vector", lambda e: e.tensor_reduce(out=lsc[:, 1:2], in_=lq[:, 128:192], axis=AX.X, op=ALU.add), reads=[Rl], writes=[Rl])
            act(lsc[:, 2:4], lsc[:, 0:2], AF.Exp, reads=[Rl], writes=[Rl])
            tt("vector", lsc[:, 4:5], lsc[:, 3:4], lsc[:, 2:3], ALU.subtract, reads=[Rl], writes=[Rl])
            ts("vector", lsc[:, 5:6], lsc[:, 4:5], -lam_init, ALU.add, reads=[Rl], writes=[Rl])
            neglam = lsc[:, 5:6]
            dgt = sb("dgt", [128, 2], F32)
            dma(dgt[:, 0:1], dng[l], writes=[Rl])
            ts("vector", dgt[:, 1:2], dgt[:, 0:1], (1.0 - lam_init), ALU.mult, reads=[Rl], writes=[Rl])
            QTzz = [[sb(f"QT0{i}", [128, SMAX], BF16), sb(f"QT1{i}", [128, SMAX], BF16)] for i in range(2)]
            for i in range(2):
                S.op("gpsimd", lambda e, i=i: e.memset(QTzz[i][0][64:128, :], 0.0), writes=[])
                S.op("gpsimd", lambda e, i=i: e.memset(QTzz[i][1][0:64, :], 0.0), writes=[])
            KTT = [sb(f"KT{i}", [128, SMAX], BF16) for i in range(2)]
            VV = [sb(f"V{i}", [128, SMAX // 128, 128], BF16) for i in range(2)]
            RQQ, RKK, RVV = [Res("QTa"), Res("QTb")], [Res("KTa"), Res("KTb")], [Res("Va"), Res("Vb")]
            ev_ringB = Ring(nc, f"B{l}_evb", [128, 4, 512], F32, 2)
            hcount = [0]
            deferred = []

            def load_head(si_, h_, slot):
                Sq_ = seqs[si_]
                t0_h = seq_off[si_]
                dma(QTzz[slot][0][0:64, 0:Sq_], qkT[h_ * 128:h_ * 128 + 64, t0_h:t0_h + Sq_], reads=[R["qkT"]], writes=[RQQ[slot]])
                dma(QTzz[slot][1][64:128, 0:Sq_], qkT[h_ * 128 + 64:(h_ + 1) * 128, t0_h:t0_h + Sq_], reads=[R["qkT"]], writes=[RQQ[slot]])
                dma(KTT[slot][:, 0:Sq_], qkT[1024 + h_ * 128: 1024 + (h_ + 1) * 128, t0_h:t0_h + Sq_], reads=[R["qkT"]], writes=[RKK[slot]])
                dma(VV[slot][:, 0:Sq_ // 128, :], va[t0_h:t0_h + Sq_, h_ * 128:(h_ + 1) * 128].rearrange("(k p) f -> p k f", p=128),
                    reads=[R["va"]], writes=[RVV[slot]])

            heads = [(si_, h_) for si_ in range(len(seqs)) for h_ in range(8)]
            load_head(heads[0][0], heads[0][1], 0)
            pt_ring = Ring(nc, f"B{l}_pt", [128, 512], BF16, 6)
            f_ring = Ring(nc, f"B{l}_f", [128, 512], F32, 10)
            z_ring = Ring(nc, f"B{l}_z", [128, 512], BF16, 3)
            y_ring = Ring(nc, f"B{l}_y", [128, 512], BF16, 2)
            scale = 64 ** -0.5
            spi = 0
            for si, Sq in enumerate(seqs):
                t0 = seq_off[si]
                NKT = Sq // 128
                NB = Sq // 512
                for h in range(8):
                    slot = hcount[0] % 2
                    hidx = hcount[0]
                    hcount[0] += 1
                    QTz, KT, V = QTzz[slot], KTT[slot], VV[slot]
                    RQ, RK, RV = RQQ[slot], RKK[slot], RVV[slot]
                    for qb_ in range(NB):
                        if qb_ == min(1, NB - 1) and hidx + 1 < len(heads):
                            load_head(heads[hidx + 1][0], heads[hidx + 1][1], 1 - slot)
                        qs = slice(qb_ * 512, (qb_ + 1) * 512)
                        zt, Rz = z_ring.next()
                        dma(zt, zaT[h * 128:(h + 1) * 128, t0 + qb_ * 512: t0 + (qb_ + 1) * 512], reads=[R["zaT"]], writes=[Rz])
                        PIPE = 2
                        SB = (0, 1, 7)
                        pend = []

                        def flush_one():
                            kt_, c_, pt_, Rpt_ = pend.pop(0)
                            mm(PS[2 + 2 * c_][:, :], V[:, kt_, :], pt_, kt_ == 0, kt_ == NKT - 1, reads=[RV, Rpt_], writes=[RPS[2 + 2 * c_]])
                            mm(PS[3 + 2 * c_][:, :], ones_b, pt_, kt_ == 0, kt_ == NKT - 1, reads=[Rc, Rpt_], writes=[RPS[3 + 2 * c_]])

                        for kt in range(NKT):
                            if kt == NKT // 2:
                                while deferred:
                                    deferred.pop(0)()
                            for c in range(2):
                                sp, Rsp = PS[SB[spi % 3]], RPS[SB[spi % 3]]
                                spi += 1
                                mm(sp[:, :], KT[:, kt * 128:(kt + 1) * 128], QTz[c][:, qs],
                                   True, True, reads=[RK, RQ], writes=[Rsp])
                                pt, Rpt = pt_ring.next()
                                act(pt, sp[:, :], AF.Exp, reads=[Rsp], writes=[Rpt], scale=scale)
                                pend.append((kt, c, pt, Rpt))
                                if len(pend) > PIPE:
                                    flush_one()
                        while pend:
                            flush_one()
                        evb, Revb = ev_ringB.next()
                        cp("vector", evb[:, 0, :], PS[2][:, :], reads=[RPS[2]], writes=[Revb])
                        cp("scalar", evb[:, 1, :], PS[3][:, :], reads=[RPS[3]], writes=[Revb])
                        cp("vector", evb[:, 2, :], PS[4][:, :], reads=[RPS[4]], writes=[Revb])
                        cp("scalar", evb[:, 3, :], PS[5][:, :], reads=[RPS[5]], writes=[Revb])
                        r0, Rr0 = f_ring.next()
                        S.op("vector", lambda e, r0=r0, evb=evb: e.reciprocal(out=r0, in_=evb[:, 1, :]), reads=[Revb], writes=[Rr0])
                        tt("gpsimd", r0, evb[:, 0, :], r0, ALU.mult, reads=[Revb, Rr0], writes=[Rr0])
                        r1, Rr1 = f_ring.next()
                        S.op("vector", lambda e, r1=r1, evb=evb: e.reciprocal(out=r1, in_=evb[:, 3, :]), reads=[Revb], writes=[Rr1])
                        tt("gpsimd", r1, evb[:, 2, :], r1, ALU.mult, reads=[Revb, Rr1], writes=[Rr1])
                        o_, Ro = f_ring.next()
                        stt("vector", o_, r1, neglam, r0, ALU.mult, ALU.add, reads=[Rr1, Rr0, Rl], writes=[Ro])
                        sq_, Rsq = f_ring.next()
                        tt("gpsimd", sq_, o_, o_, ALU.mult, reads=[Ro], writes=[Rsq])

                        def post2(o_=o_, Ro=Ro, sq_=sq_, Rsq=Rsq, zt=zt, Rz=Rz, h=h, col0=t0 + qb_ * 512):
                            mm(PS[6][:, :], ones_f, sq_, True, True, reads=[Rsq, Rc], writes=[RPS[6]])
                            rn, Rrn = f_ring.next()
                            act(rn, PS[6][:, :], AF.Sqrt, reads=[RPS[6]], writes=[Rrn], scale=1.0 / 128, bias=SUBLN_EPS)
                            S.op("vector", lambda e, rn=rn: e.reciprocal(out=rn, in_=rn), reads=[Rrn], writes=[Rrn])
                            stt("vector", o_, o_, dgt[:, 1:2], rn, ALU.mult, ALU.mult, reads=[Ro, Rrn, Rl], writes=[Ro])
                            yt, Ry = y_ring.next()
                            tt("gpsimd", yt, o_, zt, ALU.mult, reads=[Ro, Rz], writes=[Ry])
                            dma(yaT[h * 128:(h + 1) * 128, col0:col0 + 512], yt, reads=[Ry], writes=[R["yaT"]])
                        deferred.append(post2)
            while deferred:
                deferred.pop(0)()
            S.barrier()

        if "C" in phases:
          with contextlib.ExitStack() as st:
            Ring.stack = st

            def sb(name, shape, dt):
                return st.enter_context(nc.sbuf_tensor(f"C{l}_{name}", shape, dt))
            SMAX = max(seqs)
            NMAX = SMAX // 64
            Rp = Res("gparams")
            alb = sb("alb", [64, 16], F32)
            dtb = sb("dtb", [64, 16], F32)
            dma(alb[:], a_log[l].broadcast_to([64, 16]), writes=[Rp])
            dma(dtb[:], dt_bias[l].broadcast_to([64, 16]), writes=[Rp])
            act(alb[:], alb[:], AF.Exp, reads=[Rp], writes=[Rp])
            ts("vector", alb[:], alb[:], -1.0, ALU.mult, reads=[Rp], writes=[Rp])
            gnb = sb("gnb", [64, 128], F32)
            dma(gnb[:], gng[l].broadcast_to([64, 128]), writes=[Rp])
            ABt = sb("ABt", [64, NMAX, 32], F32)
            G = sb("G", [64, NMAX, 16], F32)
            NBt = sb("NBt", [64, NMAX, 16], F32)
            RG = Res("G")
            O2 = [sb("Of", [64, NMAX, 128], F32), sb("Ob", [64, NMAX, 128], F32)]
            RO = [Res("Of"), Res("Ob")]
            Sst = [sb("Sf", [128, 128], F32), sb("Sb", [128, 128], F32)]
            Sbf = [sb("Sfb", [128, 128], BF16), sb("Sbb", [128, 128], BF16)]
            RS = [Res("Sf"), Res("Sb")]
            RSb = [Res("Sfb"), Res("Sbb")]
            vn_ring = [Ring(nc, f"C{l}_vn{d}", [64, 128], BF16, 2) for d in range(2)]

            def gb(name, shape, dt, n=1):
                return [Ring(nc, f"C{l}_{name}{d}", shape, dt, n) for d in range(2)]
            W8 = 8 * 64
            qT_r, kT_r = gb("qT", [128, W8], BF16), gb("kT", [128, W8], BF16)
            kt_r, vt_r = gb("kt", [64, 8, 128], BF16), gb("vt", [64, 8, 128], BF16)
            GU_r = gb("GU", [64, W8], F32)
            GCc_r = gb("GCc", [64, 8, 4], F32)
            GCr_r = gb("GCr", [128, W8], F32)
            Dm_r = gb("Dm", [64, W8], F32)
            Ei_r = gb("Ei", [64, W8], F32)
            at_r = gb("at", [64, W8], BF16, 2)
            P_r = [gb("Pa", [128, W8], F32), gb("Pb", [128, W8], F32)]
            PT_r = [gb("PTa", [128, W8], F32), gb("PTb", [128, W8], F32)]
            P0_r = gb("P0s", [64, W8], F32)
            PT0z = [sb(f"PT0z{d}", [128, W8], F32) for d in range(2)]
            RPT0z = [Res(f"PT0z{d}") for d in range(2)]
            for d in range(2):
                S.op("gpsimd", lambda e, d=d: e.memset(PT0z[d][:], 0.0), writes=[RPT0z[d]])
            Y_r = gb("Y", [128, W8], F32)
            TT_r = gb("TT", [64, W8], BF16)
            KG_r, KD_r = gb("KG", [64, 8, 128], BF16), gb("KD", [64, 8, 128], BF16, 2)
            Ub_r = gb("Ub", [64, 8, 128], F32, 2)
            WT_r, QG_r = gb("WT", [128, W8], BF16, 2), gb("QG", [128, W8], BF16, 2)
            gl_r = gb("gl", [128, 8], F32, 2)
            yb_r = Ring(nc, f"C{l}_yb", [64, 8, 128], F32, 2)
            ybb_r = Ring(nc, f"C{l}_ybb", [64, 8, 128], BF16, 2)
            zb_r = Ring(nc, f"C{l}_zb", [64, 8, 128], BF16, 2)
            rs_r = Ring(nc, f"C{l}_rs", [64, 8, 2], F32, 2)
            yo_r = Ring(nc, f"C{l}_yo", [128, W8], BF16, 2)

            def bc3(ap2, n):
                return ap2.unsqueeze(2).broadcast_to([ap2.shape[0], 8, n])

            def v4(ap):
                return ap.rearrange("p (m c) -> p m c", m=4)

            def v3(ap, inner):
                return ap.rearrange("p (n i) -> p n i", n=8)

            for si, Sq in enumerate(seqs):
                t0 = seq_off[si]
                N = Sq // 64
                NG = N // 8
                dma(ABt[:, 0:N, :], ab[t0:t0 + Sq, :].rearrange("(n t) c -> t n c", t=64), reads=[R["ab"]], writes=[RG], q="sync")
                al_b = alb[:, :].unsqueeze(1).broadcast_to([64, N, 16])
                dt_b = dtb[:, :].unsqueeze(1).broadcast_to([64, N, 16])
                tt("vector", G[:, 0:N, :], ABt[:, 0:N, 0:16], dt_b, ALU.add, reads=[RG, Rp], writes=[RG])
                act(G[:, 0:N, :], G[:, 0:N, :], AF.Exp, reads=[RG], writes=[RG])
                act(G[:, 0:N, :], G[:, 0:N, :], AF.Ln, reads=[RG], writes=[RG], bias=1.0)
                tt("vector", G[:, 0:N, :], G[:, 0:N, :], al_b, ALU.mult, reads=[RG, Rp], writes=[RG])
                act(NBt[:, 0:N, :], ABt[:, 0:N, 16:32], AF.Sigmoid, reads=[RG], writes=[RG])
                ts("vector", NBt[:, 0:N, :], NBt[:, 0:N, :], -1.0, ALU.mult, reads=[RG], writes=[RG])
                for h in range(8):
                    for d in range(2):
                        S.op("gpsimd", lambda e, d=d: e.memset(Sst[d][:], 0.0), writes=[RS[d]])
                        S.op("gpsimd", lambda e, d=d: e.memset(Sbf[d][:], 0.0), writes=[RSb[d]])
                    def prep_group(gi):
                        prep = {}
                        for d in range(2):
                            g = gi if d == 0 else NG - 1 - gi
                            col = d * 8 + h
                            n0 = g * 8
                            tok = t0 + g * 512
                            last = 63 if d == 0 else 0
                            qT, RqT = qT_r[d].next()
                            kT, RkT = kT_r[d].next()
                            kt_, Rkt = kt_r[d].next()
                            vt_, Rvt = vt_r[d].next()
                            dma(qT, qbT[h * 128:(h + 1) * 128, tok:tok + 512], reads=[R["qbT"]], writes=[RqT])
                            dma(kT, kbT[h * 128:(h + 1) * 128, tok:tok + 512], reads=[R["kbT"]], writes=[RkT])
                            dma(kt_, ktm[tok:tok + 512, h * 128:(h + 1) * 128].rearrange("(n t) f -> t n f", t=64), reads=[R["ktm"]], writes=[Rkt])
                            dma(vt_, vtm[tok:tok + 512, h * 128:(h + 1) * 128].rearrange("(n t) f -> t n f", t=64), reads=[R["vtm"]], writes=[Rvt])
                            Gc = G[:, n0:n0 + 8, col]
                            NBc = NBt[:, n0:n0 + 8, col]
                            GCc, RGCc = GCc_r[d].next()
                            pb = 4 * d
                            mm(PS[pb][0:64, 0:8], MI[d], Gc, True, True, reads=[RG, Rc], writes=[RPS[pb]])
                            cp("vector", GCc[:, :, 0], PS[pb][0:64, 0:8], reads=[RPS[pb]], writes=[RGCc])
                            GU, RGU = GU_r[d].next()
                            tt("gpsimd", v3(GU, 64), bc3(Gc, 64), MI[d].unsqueeze(1).broadcast_to([64, 8, 64]), ALU.mult,
                               reads=[RG, Rc], writes=[RGU])
                            mm(PS[pb + 1][:, :], ones_f[0:64, :], GU, True, True, reads=[RGU, Rc], writes=[RPS[pb + 1]])
                            GCr, RGCr = GCr_r[d].next()
                            cp("scalar", GCr, PS[pb + 1][:, :], reads=[RPS[pb + 1]], writes=[RGCr])
                            Dm, RDm = Dm_r[d].next()
                            tt("vector", v3(Dm, 64), v3(GCr[0:64, :], 64), bc3(GCc[:, :, 0], 64), ALU.subtract, reads=[RGCr, RGCc], writes=[RDm])
                            ts("vector", Dm, Dm, 0.0, ALU.min, reads=[RDm], writes=[RDm])
                            act(Dm, Dm, AF.Exp, reads=[RDm], writes=[RDm])
                            Ei, REi = Ei_r[d].next()
                            tt("gpsimd", v3(Ei, 64), v3(Dm, 64), MI[d].unsqueeze(1).broadcast_to([64, 8, 64]), ALU.mult, reads=[RDm, Rc], writes=[REi])
                            act(GCr, GCr, AF.Exp, reads=[RGCr], writes=[RGCr])
                            gl, Rgl = gl_r[d].next()
                            cp("vector", gl, v3(GCr, 64)[:, :, last], reads=[RGCr], writes=[Rgl])
                            act(GCc[:, :, 1], GCc[:, :, 0], AF.Exp, reads=[RGCc], writes=[RGCc])
                            act(GCc[:, :, 2], GCc[:, :, 0], AF.Identity, reads=[RGCc], writes=[RGCc], scale=-1.0)
                            for n in range(8):
                                cs_ = slice(n * 64, (n + 1) * 64)
                                mm(PS[pb + 2][0:64, cs_], kT[:, cs_], kT[:, cs_], True, True, reads=[RkT], writes=[RPS[pb + 2]])
                            for n in range(8):
                                cs_ = slice(n * 64, (n + 1) * 64)
                                mm(PS[pb + 1][0:64, cs_], kT[:, cs_], qT[:, cs_], True, True, reads=[RkT, RqT], writes=[RPS[pb + 1]])
                            at_, Rat = at_r[d].next()
                            tt("vector", at_, PS[pb + 1][0:64, :], Ei, ALU.mult, reads=[RPS[pb + 1], REi], writes=[Rat])
                            P0, RP0 = P0_r[d].next()
                            tt("vector", P0, PS[pb + 2][0:64, :], Ei, ALU.mult, reads=[RPS[pb + 2], REi], writes=[RP0])
                            tt("gpsimd", v3(P0, 64), v3(P0, 64), MS[d].unsqueeze(1).broadcast_to([64, 8, 64]), ALU.mult, reads=[RP0, Rc], writes=[RP0])
                            tt("gpsimd", v3(P0, 64), v3(P0, 64), bc3(NBc, 64), ALU.mult, reads=[RP0, RG], writes=[RP0])
                            prep[d] = dict(g=g, n0=n0, col=col, last=last, qT=(qT, RqT), kT=(kT, RkT), kt=(kt_, Rkt), vt=(vt_, Rvt),
                                           GCc=(GCc, RGCc), GCr=(GCr, RGCr), gl=(gl, Rgl), at=(at_, Rat), P0=(P0, RP0), NBc=NBc)
                            yield
                        cur = {}
                        for d in range(2):
                            P0, RP0 = prep[d]["P0"]
                            pb = 4 * d
                            for m in range(4):
                                tr(PS[pb][:, m * 64:(m + 1) * 64], P0[:, m * 128:(m + 1) * 128], ident_f[0:64, 0:64], reads=[RP0], writes=[RPS[pb]])
                            PT0, RPT0 = PT0z[d], RPT0z[d]
                            cp("scalar", v4(PT0)[0:64, :, 0:64], PS[pb][0:64, 0:256].rearrange("p (m i) -> p m i", m=4), reads=[RPS[pb]], writes=[RPT0])
                            cp("vector", v4(PT0)[64:128, :, 64:128], PS[pb][64:128, 0:256].rearrange("p (m i) -> p m i", m=4), reads=[RPS[pb]], writes=[RPT0])
                            for m in range(4):
                                cs_ = slice(m * 128, (m + 1) * 128)
                                tr(PS[pb + 1][:, cs_], PT0[:, cs_], ident_f, reads=[RPT0], writes=[RPS[pb + 1]])
                            P0b, RP0b = P_r[0][d].next()
                            cp("scalar", P0b, PS[pb + 1][:, :], reads=[RPS[pb + 1]], writes=[RP0b])
                            Y, RY = Y_r[d].next()
                            tt("gpsimd", v4(Y), v4(P0b), ident_f.unsqueeze(1).broadcast_to([128, 4, 128]), ALU.add, reads=[RP0b, Rc], writes=[RY])
                            cur[d] = [P0b, RP0b, PT0, RPT0, Y, RY]
                        yield
                        for k in range(0, 6):
                            for d in range(2):
                                P, RP, PT, RPT, Y, RY = cur[d]
                                pb = 4 * d
                                if k < 5:
                                    for m in range(4):
                                        cs_ = slice(m * 128, (m + 1) * 128)
                                        mm(PS[pb + 1][:, cs_], PT[:, cs_], P[:, cs_], True, True, reads=[RP, RPT], writes=[RPS[pb + 1]])
                                    for m in range(4):
                                        cs_ = slice(m * 128, (m + 1) * 128)
                                        mm(PS[pb + 2][:, cs_], P[:, cs_], PT[:, cs_], True, True, reads=[RP, RPT], writes=[RPS[pb + 2]])
                                if k >= 1:
                                    for m in range(4):
                                        cs_ = slice(m * 128, (m + 1) * 128)
                                        mm(PS[pb][:, cs_], PT[:, cs_], Y[:, cs_], True, True, reads=[RY, RPT], writes=[RPS[pb]])
                                    tt("vector", Y, Y, PS[pb][:, :], ALU.add, reads=[RY, RPS[pb]], writes=[RY])
                                if k < 5:
                                    Pn, RPn = P_r[(k + 1) % 2][d].next()
                                    PTn, RPTn = PT_r[(k + 1) % 2][d].next()
                                    cp("scalar", Pn, PS[pb + 1][:, :], reads=[RPS[pb + 1]], writes=[RPn])
                                    cp("vector", PTn, PS[pb + 2][:, :], reads=[RPS[pb + 2]], writes=[RPTn])
                                    cur[d] = [Pn, RPn, PTn, RPTn, Y, RY]
                            yield
                        for d in range(2):
                            pr = prep[d]
                            pb = 4 * d
                            Y, RY = cur[d][4], cur[d][5]
                            TT, RTT = TT_r[d].next()
                            TT4 = TT.rearrange("p (m two i) -> p m two i", m=4, two=2)
                            cp("gpsimd", TT4[:, :, 0, :], v4(Y)[0:64, :, 0:64], reads=[RY], writes=[RTT])
                            for m in range(4):
                                mm(PS[pb][0:64, m * 64:(m + 1) * 64], ident_f[:, 64:128], Y[:, m * 128 + 64:(m + 1) * 128], True, True,
                                   reads=[RY, Rc], writes=[RPS[pb]])
                            cp("vector", TT4[:, :, 1, :], PS[pb][0:64, 0:256].rearrange("p (m i) -> p m i", m=4), reads=[RPS[pb]], writes=[RTT])
                            kt_, Rkt = pr["kt"]
                            vt_, Rvt = pr["vt"]
                            GCc, RGCc = pr["GCc"]
                            GCr, RGCr = pr["GCr"]
                            gl, Rgl = pr["gl"]
                            qT, RqT = pr["qT"]
                            KG, RKG = KG_r[d].next()
                            tt("gpsimd", KG, kt_, bc3(GCc[:, :, 1], 128), ALU.mult, reads=[Rkt, RGCc], writes=[RKG])
                            pr["KG"] = (KG, RKG)
                            pr["TT"] = (TT, RTT)
                        prep_done = prep
                        yield
                        for d in range(2):
                            pr = prep_done[d]
                            pb = 4 * d
                            TT, RTT = pr["TT"]
                            KG, RKG = pr["KG"]
                            kt_, Rkt = pr["kt"]
                            vt_, Rvt = pr["vt"]
                            GCc, RGCc = pr["GCc"]
                            GCr, RGCr = pr["GCr"]
                            gl, Rgl = pr["gl"]
                            qT, RqT = pr["qT"]
                            last = pr["last"]
                            NBc = pr["NBc"]
                            KD, RKD = KD_r[d].next()
                            mm(PS[pb][0:64, 0:8], ones_f[0:64, 0:64], G[:, pr["n0"]:pr["n0"] + 8, pr["col"]], True, True, reads=[RG, Rc], writes=[RPS[pb]])
                            tt("vector", GCc[:, :, 3], PS[pb][0:64, 0:8], GCc[:, :, 0], ALU.subtract, reads=[RPS[pb], RGCc], writes=[RGCc])
                            act(GCc[:, :, 3], GCc[:, :, 3], AF.Exp, reads=[RGCc], writes=[RGCc])
                            tt("gpsimd", KD, kt_, bc3(GCc[:, :, 3], 128), ALU.mult, reads=[Rkt, RGCc], writes=[RKD])
                            Ub, RUb = Ub_r[d].next()
                            for half in range(2):
                                for n in range(4):
                                    nn = half * 4 + n
                                    mm(PS[pb + 1][0:64, n * 128:(n + 1) * 128], TT[:, nn * 64:(nn + 1) * 64], vt_[:, nn, :], True, True,
                                       reads=[RTT, Rvt], writes=[RPS[pb + 1]])
                                tt("vector", Ub[:, half * 4:(half + 1) * 4, :], PS[pb + 1][0:64, :].rearrange("p (n f) -> p n f", n=4),
                                   NBc[:, half * 4:(half + 1) * 4].unsqueeze(2).broadcast_to([64, 4, 128]), ALU.mult,
                                   reads=[RPS[pb + 1], RG], writes=[RUb])
                            WT, RWT = WT_r[d].next()
                            for n in range(8):
                                cs_ = slice(n * 64, (n + 1) * 64)
                                mm(PS[pb + 2][:, cs_], KG[:, n, :], TT[:, cs_], True, True, reads=[RKG, RTT], writes=[RPS[pb + 2]])
                            cp("scalar", WT, PS[pb + 2][:, :], reads=[RPS[pb + 2]], writes=[RWT])
                            QG, RQG = QG_r[d].next()
                            tt("vector", QG, qT, GCr, ALU.mult, reads=[RqT, RGCr], writes=[RQG])
                            pr.update(KD=(KD, RKD), Ub=(Ub, RUb), WT=(WT, RWT), QG=(QG, RQG))
                            yield
                        return prep_done


                    def scan_step(prep_done, step):
                        for d in range(2):
                            pr = prep_done[d]
                            pb = 4 * d
                            n = step if d == 0 else 7 - step
                            ng = pr["n0"] + n
                            cs_ = slice(n * 64, (n + 1) * 64)
                            WT, RWT = pr["WT"]
                            QG, RQG = pr["QG"]
                            KD, RKD = pr["KD"]
                            Ub, RUb = pr["Ub"]
                            at_, Rat = pr["at"]
                            gl, Rgl = pr["gl"]
                            NBc = pr["NBc"]
                            psn = PS[pb + 3]
                            Rpsn = RPS[pb + 3]
                            mm(psn[0:64, 0:128], WT[:, cs_], Sbf[d][:], True, True, reads=[RWT, RSb[d]], writes=[Rpsn])
                            vn, Rvn = vn_ring[d].next()
                            stt("vector", vn, psn[0:64, 0:128], NBc[:, n:n + 1], Ub[:, n, :], ALU.mult, ALU.subtract,
                                reads=[Rpsn, RUb, RG], writes=[Rvn])
                            mm(psn[0:64, 128:256], QG[:, cs_], Sbf[d][:], True, False, reads=[RQG, RSb[d]], writes=[Rpsn])
                            mm(psn[0:64, 128:256], at_[:, cs_], vn, False, True, reads=[Rat, Rvn], writes=[Rpsn])
                            mm(psn[:, 256:384], KD[:, n, :], vn, True, True, reads=[RKD, Rvn], writes=[Rpsn])
                            cp("scalar", O2[d][:, ng, :], psn[0:64, 128:256], reads=[Rpsn], writes=[RO[d]])
                            stt("vector", Sst[d][:], Sst[d][:], gl[:, n:n + 1], psn[:, 256:384], ALU.mult, ALU.add,
                                reads=[RS[d], Rgl, Rpsn], writes=[RS[d]])
                            cp("gpsimd", Sbf[d][:], Sst[d][:], reads=[RS[d]], writes=[RSb[d]])

                    def run_gen(gen, k):
                        for _ in range(k):
                            try:
                                next(gen)
                            except StopIteration as e_:
                                return e_.value
                        return None

                    pd_cur = run_gen(prep_group(0), 10 ** 6)
                    for gi in range(NG):
                        nxt = prep_group(gi + 1) if gi + 1 < NG else None
                        pd_next = None
                        for step in range(8):
                            scan_step(pd_cur, step)
                            if nxt is not None and pd_next is None:
                                pd_next = run_gen(nxt, 2)
                        if nxt is not None and pd_next is None:
                            pd_next = run_gen(nxt, 10 ** 6)
                        pd_cur = pd_next
                    for g in range(NG):
                        n0 = g * 8
                        tok = t0 + g * 512
                        yb_, Ryb = yb_r.next()
                        tt("vector", yb_, O2[0][:, n0:n0 + 8, :], O2[1][:, n0:n0 + 8, :], ALU.add, reads=[RO[0], RO[1]], writes=[Ryb])
                        ybb, Rybb = ybb_r.next()
                        rs, Rrs = rs_r.next()
                        sqf, Rsqf = Ub_r[0].next()
                        tt("gpsimd", sqf, yb_, yb_, ALU.mult, reads=[Ryb], writes=[Rsqf])
                        S.op("vector", lambda e, rs=rs, sqf=sqf: e.tensor_reduce(out=rs[:, :, 0], in_=sqf, axis=AX.X, op=ALU.add),
                             reads=[Rsqf], writes=[Rrs])
                        act(rs[:, :, 1], rs[:, :, 0], AF.Sqrt, reads=[Rrs], writes=[Rrs], scale=1.0 / 128, bias=EPS)
                        S.op("vector", lambda e, rs=rs: e.reciprocal(out=rs[:, :, 1], in_=rs[:, :, 1]), reads=[Rrs], writes=[Rrs])
                        tt("vector", yb_, yb_, bc3(rs[:, :, 1], 128), ALU.mult, reads=[Ryb, Rrs], writes=[Ryb])
                        tt("gpsimd", yb_, yb_, gnb[:, :].unsqueeze(1).broadcast_to([64, 8, 128]), ALU.mult, reads=[Ryb, Rp], writes=[Ryb])
                        zt, Rz = zb_r.next()
                        dma(zt, zb[tok:tok + 512, h * 128:(h + 1) * 128].rearrange("(n t) f -> t n f", t=64), reads=[R["zb"]], writes=[Rz])
                        tt("vector", ybb, yb_, zt, ALU.mult, reads=[Ryb, Rz], writes=[Rybb])
                        psb = PS[0][:, :].bitcast(BF16)
                        for n in range(8):
                            tr(psb[:, n * 64:(n + 1) * 64], ybb[:, n, :], ident_b[0:64, 0:64], reads=[Rybb], writes=[RPS[0]])
                        yo, Ryo = yo_r.next()
                        cp("scalar", yo, psb[:, 0:512], reads=[RPS[0]], writes=[Ryo])
                        dma(ybT[h * 128:(h + 1) * 128, tok:tok + 512], yo, reads=[Ryo], writes=[R["ybT"]])
            S.barrier()

        if "D" in phases:
          with contextlib.ExitStack() as st:
            Ring.stack = st

            def sb(name, shape, dt):
                return st.enter_context(nc.sbuf_tensor(f"D{l}_{name}", shape, dt))
            Wm = [sb(f"W{i}", [128, 8, D], BF16) for i in range(3)]
            RW = Res("Wm")
            wst_ring = Ring(nc, f"D{l}_wst", [128, 8, 512], F32, 2)
            srcs = [w_br[l, 0], w_br[l, 1], w_out[l]]
            for i in range(3):
                wv = srcs[i].rearrange("(c p) n -> p c n", p=128)
                for b in range(2):
                    wst, Rwst = wst_ring.next()
                    dma(wst, wv[:, :, b * 512:(b + 1) * 512], writes=[Rwst])
                    cp("gpsimd", Wm[i][:, :, b * 512:(b + 1) * 512], wst, reads=[Rwst], writes=[RW])
            fgt = sb("fgt", [128, D], F32)
            dma(fgt[:], final_g[0:1, :].broadcast_to([128, D]), writes=[RW])
            ya_r = Ring(nc, f"D{l}_ya", [128, 8, 512], BF16, 2)
            yb_r2 = Ring(nc, f"D{l}_yb", [128, 8, 512], BF16, 2)
            g_r = Ring(nc, f"D{l}_g", [128, 16, 512], BF16, 2)
            mT_r = Ring(nc, f"D{l}_mT", [128, 8, 512], BF16, 2)
            f_r = Ring(nc, f"D{l}_f", [128, 512], F32, 4)
            x_r = Ring(nc, f"D{l}_x", [128, D], F32, 2)
            xo_r = Ring(nc, f"D{l}_xo", [128, D], F32, 2)
            ss_r = Ring(nc, f"D{l}_ss", [128, 2], F32, 2)
            junk = sb("junk", [128, D], BF16)
            Rj = Res("junk")
            pi = 0
            for tb in range(NT // 512):
                tok = tb * 512
                yat, Rya = ya_r.next()
                ybt, Ryb = yb_r2.next()
                gt, Rgt = g_r.next()
                dma(yat, yaT[:, tok:tok + 512].rearrange("(c p) t -> p c t", p=128), reads=[R["yaT"]], writes=[Rya])
                dma(ybt, ybT[:, tok:tok + 512].rearrange("(c p) t -> p c t", p=128), reads=[R["ybT"]], writes=[Ryb])
                dma(gt, gT[:, tok:tok + 512].rearrange("(c p) t -> p c t", p=128), reads=[R["gT"]], writes=[Rgt])
                mT, RmT = mT_r.next()
                for dm in range(8):
                    pa, Rpa = PS[pi % 2], RPS[pi % 2]
                    pb_, Rpb = PS[2 + pi % 2], RPS[2 + pi % 2]
                    pi += 1
                    for c in range(8):
                        mm(pa[:, :], Wm[0][:, c, dm * 128:(dm + 1) * 128], yat[:, c, :], c == 0, c == 7, reads=[RW, Rya], writes=[Rpa])
                    for c in range(8):
                        mm(pb_[:, :], Wm[1][:, c, dm * 128:(dm + 1) * 128], ybt[:, c, :], c == 0, c == 7, reads=[RW, Ryb], writes=[Rpb])
                    f0, Rf0 = f_r.next()
                    f1, Rf1 = f_r.next()
                    tt("vector", f0, pa[:, :], gt[:, dm, :], ALU.mult, reads=[Rpa, Rgt], writes=[Rf0])
                    tt("vector", f1, pb_[:, :], gt[:, 8 + dm, :], ALU.mult, reads=[Rpb, Rgt], writes=[Rf1])
                    tt("gpsimd", mT[:, dm, :], f0, f1, ALU.add, reads=[Rf0, Rf1], writes=[RmT])
                for m in range(4):
                    xt, Rxt = x_r.next()
                    dma(xt, xsrc[tok + m * 128: tok + (m + 1) * 128, :], reads=Rxsrc, writes=[Rxt])
                    xo, Rxo = xo_r.next()
                    for cb0 in (0, 512):
                        po, Rpo = PS[4 + pi % 2], RPS[4 + pi % 2]
                        pi += 1
                        for c in range(8):
                            mm(po[:, :], mT[:, c, m * 128:(m + 1) * 128], Wm[2][:, c, cb0:cb0 + 512], c == 0, c == 7, reads=[RW, RmT], writes=[Rpo])
                        tt("vector", xo[:, cb0:cb0 + 512], po[:, :], xt[:, cb0:cb0 + 512], ALU.add, reads=[Rpo, Rxt], writes=[Rxo])
                    if l < DEPTH - 1:
                        dma(x1[tok + m * 128: tok + (m + 1) * 128, :], xo, reads=[Rxo], writes=[R["x1"]])
                    else:
                        ss, Rss = ss_r.next()
                        act(junk[:], xo, AF.Square, reads=[Rxo], writes=[Rj, Rss], accum_out=ss[:, 0:1])
                        act(ss[:, 1:2], ss[:, 0:1], AF.Sqrt, reads=[Rss], writes=[Rss], scale=1.0 / D, bias=EPS)
                        S.op("vector", lambda e, ss=ss: e.reciprocal(out=ss[:, 1:2], in_=ss[:, 1:2]), reads=[Rss], writes=[Rss])
                        stt("vector", xo, xo, ss[:, 1:2], fgt[:], ALU.mult, ALU.mult, reads=[Rxo, Rss, RW], writes=[Rxo])
                        dma(yout[tok + m * 128: tok + (m + 1) * 128, :], xo, reads=[Rxo], writes=[R["yout"]])
            S.barrier()
    S.emit()
    return nc


def make_consts():
    inv = (10000.0 ** (-np.arange(0, 64, 2, dtype=np.float32) / 64)).astype(np.float32)
    pos = np.arange(4096, dtype=np.float32)
    ang = pos[:, None] * inv[None, :]
    ang = np.concatenate([ang, ang], -1)
    cos = np.cos(ang).astype(np.float32).T
    sin = np.sin(ang).astype(np.float32).T
    sgn = np.concatenate([-np.ones(32, np.float32), np.ones(32, np.float32)])[:, None]
    cosT = np.concatenate([cos, cos], 0)
    sinT = np.concatenate([sin * sgn, sin * sgn], 0)
    idx = np.arange(64)
    cf = np.zeros((128, 6, 128), np.float32)
    cf[:, 0, :] = np.eye(128, dtype=np.float32)
    cf[:, 1, :] = 1.0
    cf[:64, 2, :64] = (idx[:, None] <= idx[None, :])
    cf[:64, 3, :64] = (idx[:, None] < idx[None, :])
    cf[:64, 4, :64] = (idx[:, None] >= idx[None, :])
    cf[:64, 5, :64] = (idx[:, None] > idx[None, :])
    cbm = np.zeros((128, 3, 128), np.float32)
    cbm[:, 0, :] = np.eye(128)
    cbm[:, 1, :] = 1.0
    perm = np.zeros((128, 128), np.float32)
    for f in range(128):
        base = (f // 64) * 64
        dd = f % 64
        k = base + (dd + 32) % 64
        perm[k, f] = 1.0
    cbm[:, 2, :] = perm
    return np.ascontiguousarray(cosT), np.ascontiguousarray(sinT), cf, cbm.astype(ml_dtypes.bfloat16)


def shared_inputs(norm_g, w_in, conv_w, lam_qk, diff_norm_g, a_log, dt_bias, gdn_norm_g, w_branch, w_out, final_g):
    cosT, sinT, cf, cbm = make_consts()
    f = lambda a: np.ascontiguousarray(np.asarray(a, dtype=np.float32))
    convw = f(conv_w).reshape(DEPTH, 4, 24, 128).transpose(0, 3, 2, 1)
    return {
        "w_in": f(w_in), "w_branch": f(w_branch), "w_out": f(w_out), "norm_g": f(norm_g),
        "final_g": f(final_g).reshape(1, D), "convw": np.ascontiguousarray(convw),
        "lam_qk": f(lam_qk).reshape(DEPTH, 1, 256), "dng": f(diff_norm_g).reshape(DEPTH, 128, 1),
        "gng": f(gdn_norm_g).reshape(DEPTH, 1, 128), "a_log": f(a_log).reshape(DEPTH, 1, 16),
        "dt_bias": f(dt_bias).reshape(DEPTH, 1, 16), "cosT": cosT, "sinT": sinT, "constf": cf, "constb": cbm,
    }


_NC_CACHE = {}


def kernel(x_prompt, x_sample, norm_g, w_in, conv_w, lam_qk, diff_norm_g, a_log, dt_bias, gdn_norm_g, w_branch, w_out, final_g):
    x_prompt = np.asarray(x_prompt, dtype=np.float32)
    x_sample = np.asarray(x_sample, dtype=np.float32)
    seqs = SEQS_FULL
    key = tuple(seqs)
    if key not in _NC_CACHE:
        _NC_CACHE[key] = build(seqs)
    nc = _NC_CACHE[key]
    sh = shared_inputs(norm_g, w_in, conv_w, lam_qk, diff_norm_g, a_log, dt_bias, gdn_norm_g, w_branch, w_out, final_g)
    in_maps = []
    for c in range(8):
        xin = np.concatenate([x_prompt[c], x_sample[2 * c], x_sample[2 * c + 1]], axis=0)
        m = dict(sh)
        m["xin"] = np.ascontiguousarray(xin)
        in_maps.append(m)
    res = run_bass_kernel_spmd(nc, in_maps, core_ids=list(range(8)))
    y_prompt = np.empty_like(x_prompt)
    y_sample = np.empty_like(x_sample)
    for c in range(8):
        y = res.results[c]["yout"]
        y_prompt[c] = y[0:2048]
        y_sample[2 * c] = y[2048:2048 + 4096]
        y_sample[2 * c + 1] = y[2048 + 4096:]
    return (y_prompt, y_sample)
```

```python
import math
import os
DBG = int(os.environ.get('MKDBG', '99'))
import contextlib
import numpy as np
import ml_dtypes
import concourse.bass as bass
import concourse.mybir as mybir
from concourse.bass_utils import run_bass_kernel_spmd

F32 = mybir.dt.float32
BF16 = mybir.dt.bfloat16
AF = mybir.ActivationFunctionType
ALU = mybir.AluOpType
AX = mybir.AxisListType

D = 1024
DEPTH = 2
INW = 10272
EPS = 1e-6
SUBLN_EPS = 1e-5
SEQS_FULL = (2048, 4096, 4096)


class Res:
    __slots__ = ("name", "last_w", "readers", "excl")

    def __init__(self, name="", excl=False):
        self.name = name
        self.excl = excl
        self.last_w = None
        self.readers = []


class Op:
    __slots__ = ("eng", "fn", "deps", "needed", "semval", "dma", "dsem")

    def __init__(self, eng, fn, dma):
        self.eng = eng
        self.fn = fn
        self.deps = []
        self.needed = False
        self.semval = None
        self.dma = dma
        self.dsem = None


class Sched:
    ENG = ("tensor", "vector", "scalar", "gpsimd", "sync")
    DQ = ("sync", "gpsimd")
    NDMA = 12

    def __init__(self, nc, same_engine_sync=True):
        self.nc = nc
        self.ops = []
        self.same_engine_sync = same_engine_sync
        self.last = {e: None for e in self.ENG}
        self.lastd = {e: [] for e in self.DQ}

    def op(self, eng, fn, reads=(), writes=(), dma=False):
        o = Op(eng, fn, dma)
        deps = set()
        ex = [r for r in reads if r.excl]
        if ex:
            reads = [r for r in reads if not r.excl]
            writes = list(writes) + [r for r in ex if r not in writes]
        for r in reads:
            if r.last_w is not None:
                deps.add(r.last_w)
        for w in writes:
            if w.last_w is not None:
                deps.add(w.last_w)
            for rd in w.readers:
                deps.add(rd)
        o.deps = list(deps)
        for r in reads:
            r.readers.append(o)
        for w in writes:
            w.last_w = o
            w.readers = []
        self.ops.append(o)
        if dma:
            self.lastd[eng].append(o)
            if len(self.lastd[eng]) > self.NDMA:
                self.lastd[eng].pop(0)
        else:
            self.last[eng] = o
        return o

    def barrier(self):
        deps = [o for o in self.last.values() if o is not None]
        for q in self.DQ:
            deps += self.lastd[q]
        for e in self.ENG:
            o = Op(e, None, False)
            o.deps = list(deps)
            self.ops.append(o)

    def emit(self):
        nc = self.nc
        engs = {"tensor": nc.tensor, "vector": nc.vector, "scalar": nc.scalar,
                "gpsimd": nc.gpsimd, "sync": nc.sync}
        ses = self.same_engine_sync
        for o in self.ops:
            for d in o.deps:
                if d.dma:
                    continue
                if d.eng == o.eng and not o.dma and (o.eng == "tensor" or not ses) and o.fn is not None:
                    continue
                d.needed = True
        tl = {e: nc.alloc_semaphore(name=f"tl_{e}") for e in self.ENG}
        cnt = {e: 0 for e in self.ENG}
        dsems = {e: [nc.alloc_semaphore(name=f"d_{e}_{k}") for k in range(self.NDMA)] for e in self.DQ}
        dcount = {e: [0] * self.NDMA for e in dsems}
        dnext = {e: 0 for e in dsems}
        waited = {e: {} for e in self.ENG}
        nwait = 0
        for o in self.ops:
            E = engs[o.eng]
            w = waited[o.eng]
            reqs = {}
            for d in o.deps:
                if d.dma:
                    key = ("d", d.eng, d.dsem[0])
                    sem, val = d.dsem[1], d.dsem[2]
                else:
                    if d.semval is None:
                        continue
                    if d.eng == "tensor" and o.eng == "tensor" and not o.dma:
                        continue
                    key = ("t", d.eng)
                    sem, val = tl[d.eng], d.semval
                if reqs.get(key, (None, -1))[1] < val:
                    reqs[key] = (sem, val)
            if o.dma:
                k = dnext[o.eng]
                dnext[o.eng] = (k + 1) % self.NDMA
                prev = dcount[o.eng][k]
                if prev > 0:
                    key = ("d", o.eng, k)
                    if reqs.get(key, (None, -1))[1] < prev:
                        reqs[key] = (dsems[o.eng][k], prev)
                dcount[o.eng][k] = prev + 16
                o.dsem = (k, dsems[o.eng][k], prev + 16)
            for key, (sem, val) in reqs.items():
                if w.get(key, 0) < val:
                    E.wait_ge(sem, val)
                    w[key] = val
                    nwait += 1
            if o.fn is None:
                continue
            ins = o.fn(E)
            if o.dma:
                ins.then_inc(o.dsem[1], 16)
            elif o.needed:
                cnt[o.eng] += 1
                o.semval = cnt[o.eng]
                ins.then_inc(tl[o.eng], 1)
        for e in dsems:
            for k in range(self.NDMA):
                if dcount[e][k] > 0:
                    nc.sync.wait_ge(dsems[e][k], dcount[e][k])
        print(f"[sched] ops={len(self.ops)} waits={nwait} incs={cnt}", flush=True)


class Ring:
    def __init__(self, nc, name, shape, dtype, n):
        self.t = Ring.stack.enter_context(nc.sbuf_tensor(name, [shape[0], n] + list(shape[1:]), dtype))
        self.res = [Res(f"{name}{i}") for i in range(n)]
        self.n = n
        self.i = 0

    def next(self):
        k = self.i % self.n
        self.i += 1
        return self.t[:, k], self.res[k]


def build(seqs, debug=False, same_engine_sync=True, phases="ABCD", alim=99, nlayers=DEPTH, pool="gpsimd"):
    NT = sum(seqs)
    nc = bass.Bass("TRN2", target_bir_lowering=False)
    S = Sched(nc, same_engine_sync=same_engine_sync)

    def dram(name, shape, dt, kind):
        return nc.dram_tensor(name, shape, dt, kind=kind).ap()

    okind = "ExternalOutput" if debug else "Internal"
    xin = dram("xin", [NT, D], F32, "ExternalInput")
    w_in = dram("w_in", [DEPTH, D, INW], F32, "ExternalInput")
    w_br = dram("w_branch", [DEPTH, 2, D, D], F32, "ExternalInput")
    w_out = dram("w_out", [DEPTH, D, D], F32, "ExternalInput")
    norm_g = dram("norm_g", [DEPTH, D], F32, "ExternalInput")
    final_g = dram("final_g", [1, D], F32, "ExternalInput")
    convw = dram("convw", [DEPTH, 128, 24, 4], F32, "ExternalInput")
    lam_qk = dram("lam_qk", [DEPTH, 1, 256], F32, "ExternalInput")
    dng = dram("dng", [DEPTH, 128, 1], F32, "ExternalInput")
    gng = dram("gng", [DEPTH, 1, 128], F32, "ExternalInput")
    a_log = dram("a_log", [DEPTH, 1, 16], F32, "ExternalInput")
    dt_bias = dram("dt_bias", [DEPTH, 1, 16], F32, "ExternalInput")
    cosT_d = dram("cosT", [128, 4096], F32, "ExternalInput")
    sinT_d = dram("sinT", [128, 4096], F32, "ExternalInput")
    cf_d = dram("constf", [128, 6, 128], F32, "ExternalInput")
    cb_d = dram("constb", [128, 3, 128], BF16, "ExternalInput")
    yout = dram("yout", [NT, D], F32, "ExternalOutput")
    x1 = dram("x1", [NT, D], F32, okind)
    qkT = dram("qkT", [2048, NT], BF16, okind)
    va = dram("va", [NT, D], BF16, okind)
    zaT = dram("zaT", [D, NT], BF16, okind)
    qbT = dram("qbT", [D, NT], BF16, okind)
    kbT = dram("kbT", [D, NT], BF16, okind)
    ktm = dram("ktm", [NT, D], BF16, okind)
    vtm = dram("vtm", [NT, D], BF16, okind)
    zb = dram("zb", [NT, D], BF16, okind)
    ab = dram("ab", [NT, 32], F32, okind)
    gT = dram("gT", [2048, NT], BF16, okind)
    yaT = dram("yaT", [D, NT], BF16, okind)
    ybT = dram("ybT", [D, NT], BF16, okind)
    R = {n: Res(n) for n in "x1 qkT va zaT qbT kbT ktm vtm zb ab gT yaT ybT yout".split()}
    Rvals = list(R.values())

    cf = nc.alloc_sbuf_tensor("cf", [128, 6, 128], F32)
    cb = nc.alloc_sbuf_tensor("cb", [128, 3, 128], BF16)
    Rc = Res("consts")
    S.op("sync", lambda e: e.dma_start(out=cf[:], in_=cf_d[:, :, :]), writes=[Rc], dma=True)
    S.op("sync", lambda e: e.dma_start(out=cb[:], in_=cb_d[:, :, :]), writes=[Rc], dma=True)
    ident_f, ones_f = cf[:, 0, :], cf[:, 1, :]
    ident_b, ones_b, perm_b = cb[:, 0, :], cb[:, 1, :], cb[:, 2, :]
    MI = {0: cf[0:64, 2, 0:64], 1: cf[0:64, 4, 0:64]}
    MS = {0: cf[0:64, 3, 0:64], 1: cf[0:64, 5, 0:64]}

    PS = [nc.alloc_psum_tensor(f"ps{b}", [128, 512], F32) for b in range(8)]
    RPS = [Res(f"ps{b}", excl=True) for b in range(8)]

    DMAQ = ["sync", "gpsimd"]
    route = ["alt"]
    dq = [0]

    def dma(out, in_, reads=(), writes=(), q=None):
        if q is None:
            if route[0] == "alt":
                q = DMAQ[dq[0] % 2]
                dq[0] += 1
            else:
                q = "gpsimd" if any(w in Rvals for w in writes) else "sync"
        return S.op(q, lambda e: e.dma_start(out=out, in_=in_), reads=reads, writes=writes, dma=True)

    def mm(out, lhsT, rhs, start, stop, reads, writes, **kw):
        return S.op("tensor", lambda e: e.matmul(out, lhsT=lhsT, rhs=rhs, start=start, stop=stop, **kw),
                    reads=reads, writes=writes)

    def tr(out, in_, ident, reads, writes):
        return S.op("tensor", lambda e: e.transpose(out, in_, ident), reads=list(reads) + [Rc], writes=writes)

    def act(out, in_, func, reads, writes, eng="scalar", **kw):
        return S.op("scalar", lambda e: e.activation(out=out, in_=in_, func=func, **kw), reads=reads, writes=writes)

    def tt(eng, out, in0, in1, op, reads, writes):
        eng = pool if eng == "gpsimd" else eng
        return S.op(eng, lambda e: e.tensor_tensor(out=out, in0=in0, in1=in1, op=op), reads=reads, writes=writes)

    def ts(eng, out, in0, s1, op0, reads, writes, s2=None, op1=None):
        eng = pool if eng == "gpsimd" else eng
        if op1 is None:
            return S.op(eng, lambda e: e.tensor_scalar(out=out, in0=in0, scalar1=s1, scalar2=None, op0=op0),
                        reads=reads, writes=writes)
        return S.op(eng, lambda e: e.tensor_scalar(out=out, in0=in0, scalar1=s1, scalar2=s2, op0=op0, op1=op1),
                    reads=reads, writes=writes)

    def stt(eng, out, in0, scalar, in1, op0, op1, reads, writes):
        eng = "vector"
        return S.op(eng, lambda e: e.scalar_tensor_tensor(out=out, in0=in0, scalar=scalar, in1=in1, op0=op0, op1=op1),
                    reads=reads, writes=writes)

    def cp(eng, out, in_, reads, writes):
        eng = pool if eng == "gpsimd" else eng
        if eng == "scalar":
            return act(out, in_, AF.Copy, reads, writes)
        return S.op(eng, lambda e: e.tensor_copy(out=out, in_=in_), reads=reads, writes=writes)

    seq_off = [sum(seqs[:i]) for i in range(len(seqs))]

    for l in range(nlayers):
        lam_init = 0.8 - 0.6 * math.exp(-0.3 * l)
        xsrc = xin if l == 0 else x1
        Rxsrc = [] if l == 0 else [R["x1"]]
        route[0] = "alt"
        if "A" in phases:
          with contextlib.ExitStack() as st:
            Ring.stack = st

            def sb(name, shape, dt):
                return st.enter_context(nc.sbuf_tensor(f"A{l}_{name}", shape, dt))
            SMAX = max(seqs)
            hT = sb("hT", [128, 8, SMAX], BF16)
            RhT = Res("hT")
            gtile = sb("gtile", [128, D], F32)
            Rg = Res("gtile")
            dma(gtile[:], norm_g[l:l + 1, :].broadcast_to([128, D]), writes=[Rg])
            cwt = sb("cwt", [128, 24, 4], F32)
            Rcw = Res("cw")
            dma(cwt[:], convw[l], writes=[Rcw])
            xt_ring = Ring(nc, f"A{l}_xt", [128, D], F32, 2)
            hb_ring = Ring(nc, f"A{l}_hb", [128, D], BF16, 2)
            sq_junk = sb("sqj", [128, D], BF16)
            Rsqj = Res("sqj")
            ss_ring = Ring(nc, f"A{l}_ss", [128, 2], F32, 2)
            wst_ring = Ring(nc, f"A{l}_wst", [128, 8, 256], F32, 2)
            wsec = sb("wsec", [128, 8, 2048], BF16)
            Rwb = [Res(f"wsec{b}") for b in range(5)]
            ev_ring = Ring(nc, f"A{l}_ev", [128, 512], BF16, 4)
            evf_ring = Ring(nc, f"A{l}_evf", [128, 512], F32, 3)
            cs_ring = Ring(nc, f"A{l}_cs", [128, 2, 512], F32, 2)
            xc = sb("xc", [128, SMAX + 4], F32)
            Rxc = Res("xc")
            xacc_ring = Ring(nc, f"A{l}_xacc", [128, SMAX], F32, 2)
            ab_ring = Ring(nc, f"A{l}_ab", [128, 32], F32, 2)
            wv = w_in[l].rearrange("(c p) n -> p c n", p=128)

            def load_w(col0, ncols, eng="vector"):
                nb = (ncols + 255) // 256
                for b in range(nb):
                    n = min(256, ncols - b * 256)
                    wst, Rwst = wst_ring.next()
                    dma(wst[:, :, 0:n], wv[:, :, col0 + b * 256: col0 + b * 256 + n], writes=[Rwst])
                    cp(eng, wsec[:, :, b * 256: b * 256 + n], wst[:, :, 0:n], reads=[Rwst], writes=[Rwb[(b * 256) // 512]])

            psi = [0]

            def nextps(k=4):
                b = psi[0] % k
                psi[0] += 1
                return PS[b], RPS[b]

            for si, Sq in enumerate(seqs):
                t0 = seq_off[si]
                NTL = Sq // 128
                NB = Sq // 512
                for m in range(NTL):
                    xt, Rxt = xt_ring.next()
                    dma(xt, xsrc[t0 + m * 128: t0 + (m + 1) * 128, :], reads=Rxsrc, writes=[Rxt])
                    ss, Rss = ss_ring.next()
                    act(sq_junk[:], xt, AF.Square, reads=[Rxt], writes=[Rsqj, Rss], accum_out=ss[:, 0:1])
                    act(ss[:, 1:2], ss[:, 0:1], AF.Sqrt, reads=[Rss], writes=[Rss], scale=1.0 / D, bias=EPS)
                    S.op("vector", lambda e, ss=ss: e.reciprocal(out=ss[:, 1:2], in_=ss[:, 1:2]), reads=[Rss], writes=[Rss])
                    hb, Rhb = hb_ring.next()
                    stt("vector", hb, xt, ss[:, 1:2], gtile[:], ALU.mult, ALU.mult, reads=[Rxt, Rss, Rg], writes=[Rhb])
                    ps, Rps = nextps()
                    psb = ps[:, :].bitcast(BF16)
                    for c in range(8):
                        tr(psb[:, c * 128:(c + 1) * 128], hb[:, c * 128:(c + 1) * 128], ident_b, reads=[Rhb], writes=[Rps])
                    cp("vector", hT[:, :, m * 128:(m + 1) * 128], psb[:, 0:1024].rearrange("p (c t) -> p c t", c=8),
                       reads=[Rps], writes=[RhT])

                def fm_matmuls(ch, tb):
                    ps, Rps = nextps()
                    for c in range(8):
                        mm(ps[:, :], wsec[:, c, ch * 128:(ch + 1) * 128], hT[:, c, tb * 512:(tb + 1) * 512],
                           c == 0, c == 7, reads=[Rwb[ch // 4], RhT], writes=[Rps])
                    return ps, Rps

                def tm_matmuls(m, cb0, n=512):
                    ps, Rps = nextps()
                    for c in range(8):
                        mm(ps[:, 0:n], hT[:, c, m * 128:(m + 1) * 128], wsec[:, c, cb0:cb0 + n],
                           c == 0, c == 7, reads=[Rwb[cb0 // 512], RhT], writes=[Rps])
                    return ps, Rps

                if alim >= 2:
                    load_w(0, 2048)
                for tb in range(NB if (alim >= 2 and DBG >= 1) else 0):
                    cs, Rcs = cs_ring.next()
                    dma(cs[:, 0, :], cosT_d[:, tb * 512:(tb + 1) * 512], writes=[Rcs])
                    dma(cs[:, 1, :], sinT_d[:, tb * 512:(tb + 1) * 512], writes=[Rcs])
                    for ch in range(16 if DBG >= 2 else 0):
                        ps, Rps = fm_matmuls(ch, tb)
                        if DBG < 3:
                            continue
                        qb_, Rqb = ev_ring.next()
                        cp("scalar", qb_, ps[:, :], reads=[Rps], writes=[Rqb])
                        if DBG < 4:
                            continue
                        t1, Rt1 = evf_ring.next()
                        if os.environ.get('MKX') == '1':
                            tt("vector", t1, ps[:, :], gtile[:, 0:512], ALU.mult, reads=[Rps, Rg], writes=[Rt1])
                        elif os.environ.get('MKX') == '3':
                            tt("vector", t1, ps[:, :], cs[:, 0, :], ALU.mult, reads=[Rps, Rcs, Rqb], writes=[Rt1])
                        elif os.environ.get('MKX') == '2':
                            cp("vector", t1, ps[:, :], reads=[Rps], writes=[Rt1])
                            tt("vector", t1, t1, cs[:, 0, :], ALU.mult, reads=[Rt1, Rcs], writes=[Rt1])
                        else:
                            tt("vector", t1, ps[:, :], cs[:, 0, :], ALU.mult, reads=[Rps, Rcs], writes=[Rt1])
                        if DBG < 5:
                            continue
                        ps2, Rps2 = PS[4 + (psi[0] % 2)], RPS[4 + (psi[0] % 2)]
                        mm(ps2[:, :], perm_b, qb_, True, True, reads=[Rqb, Rc], writes=[Rps2])
                        t2, Rt2 = evf_ring.next()
                        tt("vector", t2, ps2[:, :], cs[:, 1, :], ALU.mult, reads=[Rps2, Rcs], writes=[Rt2])
                        ob, Rob = ev_ring.next()
                        tt("gpsimd", ob, t1, t2, ALU.add, reads=[Rt1, Rt2], writes=[Rob])
                        dma(qkT[ch * 128:(ch + 1) * 128, t0 + tb * 512: t0 + (tb + 1) * 512], ob, reads=[Rob], writes=[R["qkT"]])
                if alim < 3:
                    continue
                load_w(2048, 2048)
                for cb0 in (0, 512):
                    for m in range(NTL):
                        ps, Rps = tm_matmuls(m, cb0)
                        ob, Rob = ev_ring.next()
                        cp("scalar", ob, ps[:, :], reads=[Rps], writes=[Rob])
                        dma(va[t0 + m * 128: t0 + (m + 1) * 128, cb0:cb0 + 512], ob, reads=[Rob], writes=[R["va"]])
                for ch in range(8):
                    for tb in range(NB):
                        ps, Rps = fm_matmuls(8 + ch, tb)
                        ob, Rob = ev_ring.next()
                        act(ob, ps[:, :], AF.Silu, reads=[Rps], writes=[Rob])
                        dma(zaT[ch * 128:(ch + 1) * 128, t0 + tb * 512: t0 + (tb + 1) * 512], ob, reads=[Rob], writes=[R["zaT"]])
                if alim < 4:
                    continue
                def sec3_a(part, ch):
                    S.op("gpsimd", lambda e: e.memset(xc[:, 0:1], 0.0), writes=[Rxc])
                    S.op("gpsimd", lambda e, Sq=Sq: e.memset(xc[:, Sq + 1:Sq + 4], 0.0), writes=[Rxc])
                    for tb in range(NB):
                        ps, Rps = fm_matmuls(ch, tb)
                        cp("scalar", xc[:, 1 + tb * 512: 1 + (tb + 1) * 512], ps[:, :], reads=[Rps], writes=[Rxc])

                def sec3_b(part, ch):
                    gch = part * 8 + ch
                    xacc, Rxa = xacc_ring.next()
                    ts("vector", xacc[:, 0:Sq], xc[:, 0:Sq], cwt[:, gch, 0:1], ALU.mult, reads=[Rxc, Rcw], writes=[Rxa])
                    for j in range(1, 4):
                        stt("vector", xacc[:, 0:Sq], xc[:, j:j + Sq], cwt[:, gch, j:j + 1], xacc[:, 0:Sq], ALU.mult, ALU.add,
                            reads=[Rxc, Rxa, Rcw], writes=[Rxa])
                    act(xacc[:, 0:Sq], xacc[:, 0:Sq], AF.Silu, reads=[Rxa], writes=[Rxa])
                    return xacc, Rxa

                def sec3_c(part, ch, xacc, Rxa):
                    if True:
                        for tb in range(NB):
                            sl = slice(tb * 512, (tb + 1) * 512)
                            ob, Rob = ev_ring.next()
                            if part < 2:
                                sq, Rsq = evf_ring.next()
                                tt("gpsimd", sq, xacc[:, sl], xacc[:, sl], ALU.mult, reads=[Rxa], writes=[Rsq])
                                ps, Rps = nextps()
                                mm(ps[:, :], ones_f, sq, True, True, reads=[Rsq, Rc], writes=[Rps])
                                rn, Rrn = evf_ring.next()
                                act(rn, ps[:, :], AF.Ln, reads=[Rps], writes=[Rrn], bias=(128.0 * EPS if part == 0 else EPS),
                                    scale=(128.0 if part == 0 else 1.0))
                                act(rn, rn, AF.Exp, reads=[Rrn], writes=[Rrn], scale=-0.5)
                                tt("vector", ob, xacc[:, sl], rn, ALU.mult, reads=[Rxa, Rrn], writes=[Rob])
                                dst = qbT if part == 0 else kbT
                                dma(dst[ch * 128:(ch + 1) * 128, t0 + tb * 512: t0 + (tb + 1) * 512], ob, reads=[Rob],
                                    writes=[R["qbT" if part == 0 else "kbT"]])
                            else:
                                cp("vector", ob, xacc[:, sl], reads=[Rxa], writes=[Rob])
                            if part >= 1:
                                ps, Rps = nextps()
                                psb = ps[:, :].bitcast(BF16)
                                for q4 in range(4):
                                    tr(psb[:, q4 * 128:(q4 + 1) * 128], ob[:, q4 * 128:(q4 + 1) * 128], ident_b, reads=[Rob], writes=[Rps])
                                o2, Ro2 = ev_ring.next()
                                cp("vector", o2, psb[:, 0:512], reads=[Rps], writes=[Ro2])
                                dst = ktm if part == 1 else vtm
                                dma(dst[t0 + tb * 512: t0 + (tb + 1) * 512, ch * 128:(ch + 1) * 128].rearrange("(q p) f -> p q f", p=128),
                                    o2.rearrange("p (q f) -> p q f", q=4), reads=[Ro2], writes=[R["ktm" if part == 1 else "vtm"]])

                items = [(p_, c_) for p_ in range(3) for c_ in range(8)]
                load_w(4096, 1024, eng="gpsimd")
                sec3_a(*items[0])
                st_prev = sec3_b(*items[0])
                for ii in range(len(items)):
                    if ii + 1 < len(items):
                        if items[ii + 1][1] == 0:
                            load_w(4096 + items[ii + 1][0] * 1024, 1024, eng="gpsimd")
                        sec3_a(*items[ii + 1])
                    sec3_c(items[ii][0], items[ii][1], *st_prev)
                    if ii + 1 < len(items):
                        st_prev = sec3_b(*items[ii + 1])
                if alim < 5:
                    continue
                load_w(7168, 1056)
                for cb0 in (0, 512):
                    for m in range(NTL):
                        ps, Rps = tm_matmuls(m, cb0)
                        ob, Rob = ev_ring.next()
                        act(ob, ps[:, :], AF.Silu, reads=[Rps], writes=[Rob])
                        dma(zb[t0 + m * 128: t0 + (m + 1) * 128, cb0:cb0 + 512], ob, reads=[Rob], writes=[R["zb"]])
                for m in range(NTL):
                    ps, Rps = tm_matmuls(m, 1024, n=32)
                    abt, Rab = ab_ring.next()
                    cp("vector", abt, ps[:, 0:32], reads=[Rps], writes=[Rab])
                    dma(ab[t0 + m * 128: t0 + (m + 1) * 128, :], abt, reads=[Rab], writes=[R["ab"]])
                if alim < 6:
                    continue
                load_w(8224, 2048)
                for ch in range(16):
                    for tb in range(NB):
                        ps, Rps = fm_matmuls(ch, tb)
                        ob, Rob = ev_ring.next()
                        act(ob, ps[:, :], AF.Sigmoid, reads=[Rps], writes=[Rob])
                        dma(gT[ch * 128:(ch + 1) * 128, t0 + tb * 512: t0 + (tb + 1) * 512], ob, reads=[Rob], writes=[R["gT"]])
            S.barrier()

        route[0] = "split"
        if "B" in phases:
          with contextlib.ExitStack() as st:
            Ring.stack = st

            def sb(name, shape, dt):
                return st.enter_context(nc.sbuf_tensor(f"B{l}_{name}", shape, dt))
            SMAX = max(seqs)
            lq = sb("lq", [128, 256], F32)
            lsc = sb("lsc", [128, 8], F32)
            Rl = Res("lam")
            dma(lq[:], lam_qk[l].broadcast_to([128, 256]), writes=[Rl])
            tt("vector", lq[:, 0:64], lq[:, 0:64], lq[:, 64:128], ALU.mult, reads=[Rl], writes=[Rl])
            tt("vector", lq[:, 128:192], lq[:, 128:192], lq[:, 192:256], ALU.mult, reads=[Rl], writes=[Rl])
            S.op("vector", lambda e: e.tensor_reduce(out=lsc[:, 0:1], in_=lq[:, 0:64], axis=AX.X, op=ALU.add), reads=[Rl], writes=[Rl])
            S.op("vector", lambda e: e.tensor_reduce(out=lsc[:, 1:2], in_=lq[:, 128:192], axis=AX.X, op=ALU.add), reads=[Rl], writes=[Rl])
            act(lsc[:, 2:4], lsc[:, 0:2], AF.Exp, reads=[Rl], writes=[Rl])
            tt("vector", lsc[:, 4:5], lsc[:, 3:4], lsc[:, 2:3], ALU.subtract, reads=[Rl], writes=[Rl])
            ts("vector", lsc[:, 5:6], lsc[:, 4:5], -lam_init, ALU.add, reads=[Rl], writes=[Rl])
            neglam = lsc[:, 5:6]
            dgt = sb("dgt", [128, 2], F32)
            dma(dgt[:, 0:1], dng[l], writes=[Rl])
            ts("vector", dgt[:, 1:2], dgt[:, 0:1], (1.0 - lam_init), ALU.mult, reads=[Rl], writes=[Rl])
            QTzz = [[sb(f"QT0{i}", [128, SMAX], BF16), sb(f"QT1{i}", [128, SMAX], BF16)] for i in range(2)]
            for i in range(2):
                S.op("gpsimd", lambda e, i=i: e.memset(QTzz[i][0][64:128, :], 0.0), writes=[])
                S.op("gpsimd", lambda e, i=i: e.memset(QTzz[i][1][0:64, :], 0.0), writes=[])
            KTT = [sb(f"KT{i}", [128, SMAX], BF16) for i in range(2)]
            VV = [sb(f"V{i}", [128, SMAX // 128, 128], BF16) for i in range(2)]
            RQQ, RKK, RVV = [Res("QTa"), Res("QTb")], [Res("KTa"), Res("KTb")], [Res("Va"), Res("Vb")]
            ev_ringB = Ring(nc, f"B{l}_evb", [128, 4, 512], F32, 2)
            hcount = [0]
            deferred = []

            def load_head(si_, h_, slot):
                Sq_ = seqs[si_]
                t0_h = seq_off[si_]
                dma(QTzz[slot][0][0:64, 0:Sq_], qkT[h_ * 128:h_ * 128 + 64, t0_h:t0_h + Sq_], reads=[R["qkT"]], writes=[RQQ[slot]])
                dma(QTzz[slot][1][64:128, 0:Sq_], qkT[h_ * 128 + 64:(h_ + 1) * 128, t0_h:t0_h + Sq_], reads=[R["qkT"]], writes=[RQQ[slot]])
                dma(KTT[slot][:, 0:Sq_], qkT[1024 + h_ * 128: 1024 + (h_ + 1) * 128, t0_h:t0_h + Sq_], reads=[R["qkT"]], writes=[RKK[slot]])
                dma(VV[slot][:, 0:Sq_ // 128, :], va[t0_h:t0_h + Sq_, h_ * 128:(h_ + 1) * 128].rearrange("(k p) f -> p k f", p=128),
                    reads=[R["va"]], writes=[RVV[slot]])

            heads = [(si_, h_) for si_ in range(len(seqs)) for h_ in range(8)]
            load_head(heads[0][0], heads[0][1], 0)
            pt_ring = Ring(nc, f"B{l}_pt", [128, 512], BF16, 6)
            f_ring = Ring(nc, f"B{l}_f", [128, 512], F32, 10)
            z_ring = Ring(nc, f"B{l}_z", [128, 512], BF16, 3)
            y_ring = Ring(nc, f"B{l}_y", [128, 512], BF16, 2)
            scale = 64 ** -0.5
            spi = 0
            for si, Sq in enumerate(seqs):
                t0 = seq_off[si]
                NKT = Sq // 128
                NB = Sq // 512
                for h in range(8):
                    slot = hcount[0] % 2
                    hidx = hcount[0]
                    hcount[0] += 1
                    QTz, KT, V = QTzz[slot], KTT[slot], VV[slot]
                    RQ, RK, RV = RQQ[slot], RKK[slot], RVV[slot]
                    for qb_ in range(NB):
                        if qb_ == min(1, NB - 1) and hidx + 1 < len(heads):
                            load_head(heads[hidx + 1][0], heads[hidx + 1][1], 1 - slot)
                        qs = slice(qb_ * 512, (qb_ + 1) * 512)
                        zt, Rz = z_ring.next()
                        dma(zt, zaT[h * 128:(h + 1) * 128, t0 + qb_ * 512: t0 + (qb_ + 1) * 512], reads=[R["zaT"]], writes=[Rz])
                        PIPE = 2
                        SB = (0, 1, 7)
                        pend = []

                        def flush_one():
                            kt_, c_, pt_, Rpt_ = pend.pop(0)
                            mm(PS[2 + 2 * c_][:, :], V[:, kt_, :], pt_, kt_ == 0, kt_ == NKT - 1, reads=[RV, Rpt_], writes=[RPS[2 + 2 * c_]])
                            mm(PS[3 + 2 * c_][:, :], ones_b, pt_, kt_ == 0, kt_ == NKT - 1, reads=[Rc, Rpt_], writes=[RPS[3 + 2 * c_]])

                        for kt in range(NKT):
                            if kt == NKT // 2:
                                while deferred:
                                    deferred.pop(0)()
                            for c in range(2):
                                sp, Rsp = PS[SB[spi % 3]], RPS[SB[spi % 3]]
                                spi += 1
                                mm(sp[:, :], KT[:, kt * 128:(kt + 1) * 128], QTz[c][:, qs],
                                   True, True, reads=[RK, RQ], writes=[Rsp])
                                pt, Rpt = pt_ring.next()
                                act(pt, sp[:, :], AF.Exp, reads=[Rsp], writes=[Rpt], scale=scale)
                                pend.append((kt, c, pt, Rpt))
                                if len(pend) > PIPE:
                                    flush_one()
                        while pend:
                            flush_one()
                        evb, Revb = ev_ringB.next()
                        cp("vector", evb[:, 0, :], PS[2][:, :], reads=[RPS[2]], writes=[Revb])
                        cp("scalar", evb[:, 1, :], PS[3][:, :], reads=[RPS[3]], writes=[Revb])
                        cp("vector", evb[:, 2, :], PS[4][:, :], reads=[RPS[4]], writes=[Revb])
                        cp("scalar", evb[:, 3, :], PS[5][:, :], reads=[RPS[5]], writes=[Revb])
                        r0, Rr0 = f_ring.next()
                        S.op("vector", lambda e, r0=r0, evb=evb: e.reciprocal(out=r0, in_=evb[:, 1, :]), reads=[Revb], writes=[Rr0])
                        tt("gpsimd", r0, evb[:, 0, :], r0, ALU.mult, reads=[Revb, Rr0], writes=[Rr0])
                        r1, Rr1 = f_ring.next()
                        S.op("vector", lambda e, r1=r1, evb=evb: e.reciprocal(out=r1, in_=evb[:, 3, :]), reads=[Revb], writes=[Rr1])
                        tt("gpsimd", r1, evb[:, 2, :], r1, ALU.mult, reads=[Revb, Rr1], writes=[Rr1])
                        o_, Ro = f_ring.next()
                        stt("vector", o_, r1, neglam, r0, ALU.mult, ALU.add, reads=[Rr1, Rr0, Rl], writes=[Ro])
                        sq_, Rsq = f_ring.next()
                        tt("gpsimd", sq_, o_, o_, ALU.mult, reads=[Ro], writes=[Rsq])

                        def post2(o_=o_, Ro=Ro, sq_=sq_, Rsq=Rsq, zt=zt, Rz=Rz, h=h, col0=t0 + qb_ * 512):
                            mm(PS[6][:, :], ones_f, sq_, True, True, reads=[Rsq, Rc], writes=[RPS[6]])
                            rn, Rrn = f_ring.next()
                            act(rn, PS[6][:, :], AF.Sqrt, reads=[RPS[6]], writes=[Rrn], scale=1.0 / 128, bias=SUBLN_EPS)
                            S.op("vector", lambda e, rn=rn: e.reciprocal(out=rn, in_=rn), reads=[Rrn], writes=[Rrn])
                            stt("vector", o_, o_, dgt[:, 1:2], rn, ALU.mult, ALU.mult, reads=[Ro, Rrn, Rl], writes=[Ro])
                            yt, Ry = y_ring.next()
                            tt("gpsimd", yt, o_, zt, ALU.mult, reads=[Ro, Rz], writes=[Ry])
                            dma(yaT[h * 128:(h + 1) * 128, col0:col0 + 512], yt, reads=[Ry], writes=[R["yaT"]])
                        deferred.append(post2)
            while deferred:
                deferred.pop(0)()
            S.barrier()

        if "C" in phases:
          with contextlib.ExitStack() as st:
            Ring.stack = st

            def sb(name, shape, dt):
                return st.enter_context(nc.sbuf_tensor(f"C{l}_{name}", shape, dt))
            SMAX = max(seqs)
            NMAX = SMAX // 64
            Rp = Res("gparams")
            alb = sb("alb", [64, 16], F32)
            dtb = sb("dtb", [64, 16], F32)
            dma(alb[:], a_log[l].broadcast_to([64, 16]), writes=[Rp])
            dma(dtb[:], dt_bias[l].broadcast_to([64, 16]), writes=[Rp])
            act(alb[:], alb[:], AF.Exp, reads=[Rp], writes=[Rp])
            ts("vector", alb[:], alb[:], -1.0, ALU.mult, reads=[Rp], writes=[Rp])
            gnb = sb("gnb", [64, 128], F32)
            dma(gnb[:], gng[l].broadcast_to([64, 128]), writes=[Rp])
            ABt = sb("ABt", [64, NMAX, 32], F32)
            G = sb("G", [64, NMAX, 16], F32)
            NBt = sb("NBt", [64, NMAX, 16], F32)
            RG = Res("G")
            O2 = [sb("Of", [64, NMAX, 128], F32), sb("Ob", [64, NMAX, 128], F32)]
            RO = [Res("Of"), Res("Ob")]
            Sst = [sb("Sf", [128, 128], F32), sb("Sb", [128, 128], F32)]
            Sbf = [sb("Sfb", [128, 128], BF16), sb("Sbb", [128, 128], BF16)]
            RS = [Res("Sf"), Res("Sb")]
            RSb = [Res("Sfb"), Res("Sbb")]
            vn_ring = [Ring(nc, f"C{l}_vn{d}", [64, 128], BF16, 2) for d in range(2)]

            def gb(name, shape, dt, n=1):
                return [Ring(nc, f"C{l}_{name}{d}", shape, dt, n) for d in range(2)]
            W8 = 8 * 64
            qT_r, kT_r = gb("qT", [128, W8], BF16), gb("kT", [128, W8], BF16)
            kt_r, vt_r = gb("kt", [64, 8, 128], BF16), gb("vt", [64, 8, 128], BF16)
            GU_r = gb("GU", [64, W8], F32)
            GCc_r = gb("GCc", [64, 8, 4], F32)
            GCr_r = gb("GCr", [128, W8], F32)
            Dm_r = gb("Dm", [64, W8], F32)
            Ei_r = gb("Ei", [64, W8], F32)
            at_r = gb("at", [64, W8], BF16, 2)
            P_r = [gb("Pa", [128, W8], F32), gb("Pb", [128, W8], F32)]
            PT_r = [gb("PTa", [128, W8], F32), gb("PTb", [128, W8], F32)]
            P0_r = gb("P0s", [64, W8], F32)
            PT0z = [sb(f"PT0z{d}", [128, W8], F32) for d in range(2)]
            RPT0z = [Res(f"PT0z{d}") for d in range(2)]
            for d in range(2):
                S.op("gpsimd", lambda e, d=d: e.memset(PT0z[d][:], 0.0), writes=[RPT0z[d]])
            Y_r = gb("Y", [128, W8], F32)
            TT_r = gb("TT", [64, W8], BF16)
            KG_r, KD_r = gb("KG", [64, 8, 128], BF16), gb("KD", [64, 8, 128], BF16, 2)
            Ub_r = gb("Ub", [64, 8, 128], F32, 2)
            WT_r, QG_r = gb("WT", [128, W8], BF16, 2), gb("QG", [128, W8], BF16, 2)
            gl_r = gb("gl", [128, 8], F32, 2)
            yb_r = Ring(nc, f"C{l}_yb", [64, 8, 128], F32, 2)
            ybb_r = Ring(nc, f"C{l}_ybb", [64, 8, 128], BF16, 2)
            zb_r = Ring(nc, f"C{l}_zb", [64, 8, 128], BF16, 2)
            rs_r = Ring(nc, f"C{l}_rs", [64, 8, 2], F32, 2)
            yo_r = Ring(nc, f"C{l}_yo", [128, W8], BF16, 2)

            def bc3(ap2, n):
                return ap2.unsqueeze(2).broadcast_to([ap2.shape[0], 8, n])

            def v4(ap):
                return ap.rearrange("p (m c) -> p m c", m=4)

            def v3(ap, inner):
                return ap.rearrange("p (n i) -> p n i", n=8)

            for si, Sq in enumerate(seqs):
                t0 = seq_off[si]
                N = Sq // 64
                NG = N // 8
                dma(ABt[:, 0:N, :], ab[t0:t0 + Sq, :].rearrange("(n t) c -> t n c", t=64), reads=[R["ab"]], writes=[RG], q="sync")
                al_b = alb[:, :].unsqueeze(1).broadcast_to([64, N, 16])
                dt_b = dtb[:, :].unsqueeze(1).broadcast_to([64, N, 16])
                tt("vector", G[:, 0:N, :], ABt[:, 0:N, 0:16], dt_b, ALU.add, reads=[RG, Rp], writes=[RG])
                act(G[:, 0:N, :], G[:, 0:N, :], AF.Exp, reads=[RG], writes=[RG])
                act(G[:, 0:N, :], G[:, 0:N, :], AF.Ln, reads=[RG], writes=[RG], bias=1.0)
                tt("vector", G[:, 0:N, :], G[:, 0:N, :], al_b, ALU.mult, reads=[RG, Rp], writes=[RG])
                act(NBt[:, 0:N, :], ABt[:, 0:N, 16:32], AF.Sigmoid, reads=[RG], writes=[RG])
                ts("vector", NBt[:, 0:N, :], NBt[:, 0:N, :], -1.0, ALU.mult, reads=[RG], writes=[RG])
                for h in range(8):
                    for d in range(2):
                        S.op("gpsimd", lambda e, d=d: e.memset(Sst[d][:], 0.0), writes=[RS[d]])
                        S.op("gpsimd", lambda e, d=d: e.memset(Sbf[d][:], 0.0), writes=[RSb[d]])
                    def prep_group(gi):
                        prep = {}
                        for d in range(2):
                            g = gi if d == 0 else NG - 1 - gi
                            col = d * 8 + h
                            n0 = g * 8
                            tok = t0 + g * 512
                            last = 63 if d == 0 else 0
                            qT, RqT = qT_r[d].next()
                            kT, RkT = kT_r[d].next()
                            kt_, Rkt = kt_r[d].next()
                            vt_, Rvt = vt_r[d].next()
                            dma(qT, qbT[h * 128:(h + 1) * 128, tok:tok + 512], reads=[R["qbT"]], writes=[RqT])
                            dma(kT, kbT[h * 128:(h + 1) * 128, tok:tok + 512], reads=[R["kbT"]], writes=[RkT])
                            dma(kt_, ktm[tok:tok + 512, h * 128:(h + 1) * 128].rearrange("(n t) f -> t n f", t=64), reads=[R["ktm"]], writes=[Rkt])
                            dma(vt_, vtm[tok:tok + 512, h * 128:(h + 1) * 128].rearrange("(n t) f -> t n f", t=64), reads=[R["vtm"]], writes=[Rvt])
                            Gc = G[:, n0:n0 + 8, col]
                            NBc = NBt[:, n0:n0 + 8, col]
                            GCc, RGCc = GCc_r[d].next()
                            pb = 4 * d
                            mm(PS[pb][0:64, 0:8], MI[d], Gc, True, True, reads=[RG, Rc], writes=[RPS[pb]])
                            cp("vector", GCc[:, :, 0], PS[pb][0:64, 0:8], reads=[RPS[pb]], writes=[RGCc])
                            GU, RGU = GU_r[d].next()
                            tt("gpsimd", v3(GU, 64), bc3(Gc, 64), MI[d].unsqueeze(1).broadcast_to([64, 8, 64]), ALU.mult,
                               reads=[RG, Rc], writes=[RGU])
                            mm(PS[pb + 1][:, :], ones_f[0:64, :], GU, True, True, reads=[RGU, Rc], writes=[RPS[pb + 1]])
                            GCr, RGCr = GCr_r[d].next()
                            cp("scalar", GCr, PS[pb + 1][:, :], reads=[RPS[pb + 1]], writes=[RGCr])
                            Dm, RDm = Dm_r[d].next()
                            tt("vector", v3(Dm, 64), v3(GCr[0:64, :], 64), bc3(GCc[:, :, 0], 64), ALU.subtract, reads=[RGCr, RGCc], writes=[RDm])
                            ts("vector", Dm, Dm, 0.0, ALU.min, reads=[RDm], writes=[RDm])
                            act(Dm, Dm, AF.Exp, reads=[RDm], writes=[RDm])
                            Ei, REi = Ei_r[d].next()
                            tt("gpsimd", v3(Ei, 64), v3(Dm, 64), MI[d].unsqueeze(1).broadcast_to([64, 8, 64]), ALU.mult, reads=[RDm, Rc], writes=[REi])
                            act(GCr, GCr, AF.Exp, reads=[RGCr], writes=[RGCr])
                            gl, Rgl = gl_r[d].next()
                            cp("vector", gl, v3(GCr, 64)[:, :, last], reads=[RGCr], writes=[Rgl])
                            act(GCc[:, :, 1], GCc[:, :, 0], AF.Exp, reads=[RGCc], writes=[RGCc])
                            act(GCc[:, :, 2], GCc[:, :, 0], AF.Identity, reads=[RGCc], writes=[RGCc], scale=-1.0)
                            for n in range(8):
                                cs_ = slice(n * 64, (n + 1) * 64)
                                mm(PS[pb + 2][0:64, cs_], kT[:, cs_], kT[:, cs_], True, True, reads=[RkT], writes=[RPS[pb + 2]])
                            for n in range(8):
                                cs_ = slice(n * 64, (n + 1) * 64)
                                mm(PS[pb + 1][0:64, cs_], kT[:, cs_], qT[:, cs_], True, True, reads=[RkT, RqT], writes=[RPS[pb + 1]])
                            at_, Rat = at_r[d].next()
                            tt("vector", at_, PS[pb + 1][0:64, :], Ei, ALU.mult, reads=[RPS[pb + 1], REi], writes=[Rat])
                            P0, RP0 = P0_r[d].next()
                            tt("vector", P0, PS[pb + 2][0:64, :], Ei, ALU.mult, reads=[RPS[pb + 2], REi], writes=[RP0])
                            tt("gpsimd", v3(P0, 64), v3(P0, 64), MS[d].unsqueeze(1).broadcast_to([64, 8, 64]), ALU.mult, reads=[RP0, Rc], writes=[RP0])
                            tt("gpsimd", v3(P0, 64), v3(P0, 64), bc3(NBc, 64), ALU.mult, reads=[RP0, RG], writes=[RP0])
                            prep[d] = dict(g=g, n0=n0, col=col, last=last, qT=(qT, RqT), kT=(kT, RkT), kt=(kt_, Rkt), vt=(vt_, Rvt),
                                           GCc=(GCc, RGCc), GCr=(GCr, RGCr), gl=(gl, Rgl), at=(at_, Rat), P0=(P0, RP0), NBc=NBc)
                            yield
                        cur = {}
                        for d in range(2):
                            P0, RP0 = prep[d]["P0"]
                            pb = 4 * d
                            for m in range(4):
                                tr(PS[pb][:, m * 64:(m + 1) * 64], P0[:, m * 128:(m + 1) * 128], ident_f[0:64, 0:64], reads=[RP0], writes=[RPS[pb]])
                            PT0, RPT0 = PT0z[d], RPT0z[d]
                            cp("scalar", v4(PT0)[0:64, :, 0:64], PS[pb][0:64, 0:256].rearrange("p (m i) -> p m i", m=4), reads=[RPS[pb]], writes=[RPT0])
                            cp("vector", v4(PT0)[64:128, :, 64:128], PS[pb][64:128, 0:256].rearrange("p (m i) -> p m i", m=4), reads=[RPS[pb]], writes=[RPT0])
                            for m in range(4):
                                cs_ = slice(m * 128, (m + 1) * 128)
                                tr(PS[pb + 1][:, cs_], PT0[:, cs_], ident_f, reads=[RPT0], writes=[RPS[pb + 1]])
                            P0b, RP0b = P_r[0][d].next()
                            cp("scalar", P0b, PS[pb + 1][:, :], reads=[RPS[pb + 1]], writes=[RP0b])
                            Y, RY = Y_r[d].next()
                            tt("gpsimd", v4(Y), v4(P0b), ident_f.unsqueeze(1).broadcast_to([128, 4, 128]), ALU.add, reads=[RP0b, Rc], writes=[RY])
                            cur[d] = [P0b, RP0b, PT0, RPT0, Y, RY]
                        yield
                        for k in range(0, 6):
                            for d in range(2):
                                P, RP, PT, RPT, Y, RY = cur[d]
                                pb = 4 * d
                                if k < 5:
                                    for m in range(4):
                                        cs_ = slice(m * 128, (m + 1) * 128)
                                        mm(PS[pb + 1][:, cs_], PT[:, cs_], P[:, cs_], True, True, reads=[RP, RPT], writes=[RPS[pb + 1]])
                                    for m in range(4):
                                        cs_ = slice(m * 128, (m + 1) * 128)
                                        mm(PS[pb + 2][:, cs_], P[:, cs_], PT[:, cs_], True, True, reads=[RP, RPT], writes=[RPS[pb + 2]])
                                if k >= 1:
                                    for m in range(4):
                                        cs_ = slice(m * 128, (m + 1) * 128)
                                        mm(PS[pb][:, cs_], PT[:, cs_], Y[:, cs_], True, True, reads=[RY, RPT], writes=[RPS[pb]])
                                    tt("vector", Y, Y, PS[pb][:, :], ALU.add, reads=[RY, RPS[pb]], writes=[RY])
                                if k < 5:
                                    Pn, RPn = P_r[(k + 1) % 2][d].next()
                                    PTn, RPTn = PT_r[(k + 1) % 2][d].next()
                                    cp("scalar", Pn, PS[pb + 1][:, :], reads=[RPS[pb + 1]], writes=[RPn])
                                    cp("vector", PTn, PS[pb + 2][:, :], reads=[RPS[pb + 2]], writes=[RPTn])
                                    cur[d] = [Pn, RPn, PTn, RPTn, Y, RY]
                            yield
                        for d in range(2):
                            pr = prep[d]
                            pb = 4 * d
                            Y, RY = cur[d][4], cur[d][5]
                            TT, RTT = TT_r[d].next()
                            TT4 = TT.rearrange("p (m two i) -> p m two i", m=4, two=2)
                            cp("gpsimd", TT4[:, :, 0, :], v4(Y)[0:64, :, 0:64], reads=[RY], writes=[RTT])
                            for m in range(4):
                                mm(PS[pb][0:64, m * 64:(m + 1) * 64], ident_f[:, 64:128], Y[:, m * 128 + 64:(m + 1) * 128], True, True,
                                   reads=[RY, Rc], writes=[RPS[pb]])
                            cp("vector", TT4[:, :, 1, :], PS[pb][0:64, 0:256].rearrange("p (m i) -> p m i", m=4), reads=[RPS[pb]], writes=[RTT])
                            kt_, Rkt = pr["kt"]
                            vt_, Rvt = pr["vt"]
                            GCc, RGCc = pr["GCc"]
                            GCr, RGCr = pr["GCr"]
                            gl, Rgl = pr["gl"]
                            qT, RqT = pr["qT"]
                            KG, RKG = KG_r[d].next()
                            tt("gpsimd", KG, kt_, bc3(GCc[:, :, 1], 128), ALU.mult, reads=[Rkt, RGCc], writes=[RKG])
                            pr["KG"] = (KG, RKG)
                            pr["TT"] = (TT, RTT)
                        prep_done = prep
                        yield
                        for d in range(2):
                            pr = prep_done[d]
                            pb = 4 * d
                            TT, RTT = pr["TT"]
                            KG, RKG = pr["KG"]
                            kt_, Rkt = pr["kt"]
                            vt_, Rvt = pr["vt"]
                            GCc, RGCc = pr["GCc"]
                            GCr, RGCr = pr["GCr"]
                            gl, Rgl = pr["gl"]
                            qT, RqT = pr["qT"]
                            last = pr["last"]
                            NBc = pr["NBc"]
                            KD, RKD = KD_r[d].next()
                            mm(PS[pb][0:64, 0:8], ones_f[0:64, 0:64], G[:, pr["n0"]:pr["n0"] + 8, pr["col"]], True, True, reads=[RG, Rc], writes=[RPS[pb]])
                            tt("vector", GCc[:, :, 3], PS[pb][0:64, 0:8], GCc[:, :, 0], ALU.subtract, reads=[RPS[pb], RGCc], writes=[RGCc])
                            act(GCc[:, :, 3], GCc[:, :, 3], AF.Exp, reads=[RGCc], writes=[RGCc])
                            tt("gpsimd", KD, kt_, bc3(GCc[:, :, 3], 128), ALU.mult, reads=[Rkt, RGCc], writes=[RKD])
                            Ub, RUb = Ub_r[d].next()
                            for half in range(2):
                                for n in range(4):
                                    nn = half * 4 + n
                                    mm(PS[pb + 1][0:64, n * 128:(n + 1) * 128], TT[:, nn * 64:(nn + 1) * 64], vt_[:, nn, :], True, True,
                                       reads=[RTT, Rvt], writes=[RPS[pb + 1]])
                                tt("vector", Ub[:, half * 4:(half + 1) * 4, :], PS[pb + 1][0:64, :].rearrange("p (n f) -> p n f", n=4),
                                   NBc[:, half * 4:(half + 1) * 4].unsqueeze(2).broadcast_to([64, 4, 128]), ALU.mult,
                                   reads=[RPS[pb + 1], RG], writes=[RUb])
                            WT, RWT = WT_r[d].next()
                            for n in range(8):
                                cs_ = slice(n * 64, (n + 1) * 64)
                                mm(PS[pb + 2][:, cs_], KG[:, n, :], TT[:, cs_], True, True, reads=[RKG, RTT], writes=[RPS[pb + 2]])
                            cp("scalar", WT, PS[pb + 2][:, :], reads=[RPS[pb + 2]], writes=[RWT])
                            QG, RQG = QG_r[d].next()
                            tt("vector", QG, qT, GCr, ALU.mult, reads=[RqT, RGCr], writes=[RQG])
                            pr.update(KD=(KD, RKD), Ub=(Ub, RUb), WT=(WT, RWT), QG=(QG, RQG))
                            yield
                        return prep_done


                    def scan_step(prep_done, step):
                        for d in range(2):
                            pr = prep_done[d]
                            pb = 4 * d
                            n = step if d == 0 else 7 - step
                            ng = pr["n0"] + n
                            cs_ = slice(n * 64, (n + 1) * 64)
                            WT, RWT = pr["WT"]
                            QG, RQG = pr["QG"]
                            KD, RKD = pr["KD"]
                            Ub, RUb = pr["Ub"]
                            at_, Rat = pr["at"]
                            gl, Rgl = pr["gl"]
                            NBc = pr["NBc"]
                            psn = PS[pb + 3]
                            Rpsn = RPS[pb + 3]
                            mm(psn[0:64, 0:128], WT[:, cs_], Sbf[d][:], True, True, reads=[RWT, RSb[d]], writes=[Rpsn])
                            vn, Rvn = vn_ring[d].next()
                            stt("vector", vn, psn[0:64, 0:128], NBc[:, n:n + 1], Ub[:, n, :], ALU.mult, ALU.subtract,
                                reads=[Rpsn, RUb, RG], writes=[Rvn])
                            mm(psn[0:64, 128:256], QG[:, cs_], Sbf[d][:], True, False, reads=[RQG, RSb[d]], writes=[Rpsn])
                            mm(psn[0:64, 128:256], at_[:, cs_], vn, False, True, reads=[Rat, Rvn], writes=[Rpsn])
                            mm(psn[:, 256:384], KD[:, n, :], vn, True, True, reads=[RKD, Rvn], writes=[Rpsn])
                            cp("scalar", O2[d][:, ng, :], psn[0:64, 128:256], reads=[Rpsn], writes=[RO[d]])
                            stt("vector", Sst[d][:], Sst[d][:], gl[:, n:n + 1], psn[:, 256:384], ALU.mult, ALU.add,
                                reads=[RS[d], Rgl, Rpsn], writes=[RS[d]])
                            cp("scalar", Sbf[d][:], Sst[d][:], reads=[RS[d]], writes=[RSb[d]])

                    def run_gen(gen, k):
                        for _ in range(k):
                            try:
                                next(gen)
                            except StopIteration as e_:
                                return e_.value
                        return None

                    pd_cur = run_gen(prep_group(0), 10 ** 6)
                    for gi in range(NG):
                        nxt = prep_group(gi + 1) if gi + 1 < NG else None
                        pd_next = None
                        for step in range(8):
                            scan_step(pd_cur, step)
                            if nxt is not None and pd_next is None:
                                pd_next = run_gen(nxt, 4)
                        if nxt is not None and pd_next is None:
                            pd_next = run_gen(nxt, 10 ** 6)
                        pd_cur = pd_next
                    for g in range(NG):
                        n0 = g * 8
                        tok = t0 + g * 512
                        yb_, Ryb = yb_r.next()
                        tt("vector", yb_, O2[0][:, n0:n0 + 8, :], O2[1][:, n0:n0 + 8, :], ALU.add, reads=[RO[0], RO[1]], writes=[Ryb])
                        ybb, Rybb = ybb_r.next()
                        rs, Rrs = rs_r.next()
                        sqf, Rsqf = Ub_r[0].next()
                        tt("gpsimd", sqf, yb_, yb_, ALU.mult, reads=[Ryb], writes=[Rsqf])
                        S.op("vector", lambda e, rs=rs, sqf=sqf: e.tensor_reduce(out=rs[:, :, 0], in_=sqf, axis=AX.X, op=ALU.add),
                             reads=[Rsqf], writes=[Rrs])
                        act(rs[:, :, 1], rs[:, :, 0], AF.Sqrt, reads=[Rrs], writes=[Rrs], scale=1.0 / 128, bias=EPS)
                        S.op("vector", lambda e, rs=rs: e.reciprocal(out=rs[:, :, 1], in_=rs[:, :, 1]), reads=[Rrs], writes=[Rrs])
                        tt("vector", yb_, yb_, bc3(rs[:, :, 1], 128), ALU.mult, reads=[Ryb, Rrs], writes=[Ryb])
                        tt("gpsimd", yb_, yb_, gnb[:, :].unsqueeze(1).broadcast_to([64, 8, 128]), ALU.mult, reads=[Ryb, Rp], writes=[Ryb])
                        zt, Rz = zb_r.next()
                        dma(zt, zb[tok:tok + 512, h * 128:(h + 1) * 128].rearrange("(n t) f -> t n f", t=64), reads=[R["zb"]], writes=[Rz])
                        tt("vector", ybb, yb_, zt, ALU.mult, reads=[Ryb, Rz], writes=[Rybb])
                        psb = PS[0][:, :].bitcast(BF16)
                        for n in range(8):
                            tr(psb[:, n * 64:(n + 1) * 64], ybb[:, n, :], ident_b[0:64, 0:64], reads=[Rybb], writes=[RPS[0]])
                        yo, Ryo = yo_r.next()
                        cp("scalar", yo, psb[:, 0:512], reads=[RPS[0]], writes=[Ryo])
                        dma(ybT[h * 128:(h + 1) * 128, tok:tok + 512], yo, reads=[Ryo], writes=[R["ybT"]])
            S.barrier()

        if "D" in phases:
          with contextlib.ExitStack() as st:
            Ring.stack = st

            def sb(name, shape, dt):
                return st.enter_context(nc.sbuf_tensor(f"D{l}_{name}", shape, dt))
            Wm = [sb(f"W{i}", [128, 8, D], BF16) for i in range(3)]
            RW = Res("Wm")
            wst_ring = Ring(nc, f"D{l}_wst", [128, 8, 512], F32, 2)
            srcs = [w_br[l, 0], w_br[l, 1], w_out[l]]
            for i in range(3):
                wv = srcs[i].rearrange("(c p) n -> p c n", p=128)
                for b in range(2):
                    wst, Rwst = wst_ring.next()
                    dma(wst, wv[:, :, b * 512:(b + 1) * 512], writes=[Rwst])
                    cp("gpsimd", Wm[i][:, :, b * 512:(b + 1) * 512], wst, reads=[Rwst], writes=[RW])
            fgt = sb("fgt", [128, D], F32)
            dma(fgt[:], final_g[0:1, :].broadcast_to([128, D]), writes=[RW])
            ya_r = Ring(nc, f"D{l}_ya", [128, 8, 512], BF16, 2)
            yb_r2 = Ring(nc, f"D{l}_yb", [128, 8, 512], BF16, 2)
            g_r = Ring(nc, f"D{l}_g", [128, 16, 512], BF16, 2)
            mT_r = Ring(nc, f"D{l}_mT", [128, 8, 512], BF16, 2)
            f_r = Ring(nc, f"D{l}_f", [128, 512], F32, 4)
            x_r = Ring(nc, f"D{l}_x", [128, D], F32, 2)
            xo_r = Ring(nc, f"D{l}_xo", [128, D], F32, 2)
            ss_r = Ring(nc, f"D{l}_ss", [128, 2], F32, 2)
            junk = sb("junk", [128, D], BF16)
            Rj = Res("junk")
            pi = 0
            for tb in range(NT // 512):
                tok = tb * 512
                yat, Rya = ya_r.next()
                ybt, Ryb = yb_r2.next()
                gt, Rgt = g_r.next()
                dma(yat, yaT[:, tok:tok + 512].rearrange("(c p) t -> p c t", p=128), reads=[R["yaT"]], writes=[Rya])
                dma(ybt, ybT[:, tok:tok + 512].rearrange("(c p) t -> p c t", p=128), reads=[R["ybT"]], writes=[Ryb])
                dma(gt, gT[:, tok:tok + 512].rearrange("(c p) t -> p c t", p=128), reads=[R["gT"]], writes=[Rgt])
                mT, RmT = mT_r.next()
                for dm in range(8):
                    pa, Rpa = PS[pi % 2], RPS[pi % 2]
                    pb_, Rpb = PS[2 + pi % 2], RPS[2 + pi % 2]
                    pi += 1
                    for c in range(8):
                        mm(pa[:, :], Wm[0][:, c, dm * 128:(dm + 1) * 128], yat[:, c, :], c == 0, c == 7, reads=[RW, Rya], writes=[Rpa])
                    for c in range(8):
                        mm(pb_[:, :], Wm[1][:, c, dm * 128:(dm + 1) * 128], ybt[:, c, :], c == 0, c == 7, reads=[RW, Ryb], writes=[Rpb])
                    f0, Rf0 = f_r.next()
                    f1, Rf1 = f_r.next()
                    tt("vector", f0, pa[:, :], gt[:, dm, :], ALU.mult, reads=[Rpa, Rgt], writes=[Rf0])
                    tt("vector", f1, pb_[:, :], gt[:, 8 + dm, :], ALU.mult, reads=[Rpb, Rgt], writes=[Rf1])
                    tt("gpsimd", mT[:, dm, :], f0, f1, ALU.add, reads=[Rf0, Rf1], writes=[RmT])
                for m in range(4):
                    xt, Rxt = x_r.next()
                    dma(xt, xsrc[tok + m * 128: tok + (m + 1) * 128, :], reads=Rxsrc, writes=[Rxt])
                    xo, Rxo = xo_r.next()
                    for cb0 in (0, 512):
                        po, Rpo = PS[4 + pi % 2], RPS[4 + pi % 2]
                        pi += 1
                        for c in range(8):
                            mm(po[:, :], mT[:, c, m * 128:(m + 1) * 128], Wm[2][:, c, cb0:cb0 + 512], c == 0, c == 7, reads=[RW, RmT], writes=[Rpo])
                        tt("vector", xo[:, cb0:cb0 + 512], po[:, :], xt[:, cb0:cb0 + 512], ALU.add, reads=[Rpo, Rxt], writes=[Rxo])
                    if l < DEPTH - 1:
                        dma(x1[tok + m * 128: tok + (m + 1) * 128, :], xo, reads=[Rxo], writes=[R["x1"]])
                    else:
                        ss, Rss = ss_r.next()
                        act(junk[:], xo, AF.Square, reads=[Rxo], writes=[Rj, Rss], accum_out=ss[:, 0:1])
                        act(ss[:, 1:2], ss[:, 0:1], AF.Sqrt, reads=[Rss], writes=[Rss], scale=1.0 / D, bias=EPS)
                        S.op("vector", lambda e, ss=ss: e.reciprocal(out=ss[:, 1:2], in_=ss[:, 1:2]), reads=[Rss], writes=[Rss])
                        stt("vector", xo, xo, ss[:, 1:2], fgt[:], ALU.mult, ALU.mult, reads=[Rxo, Rss, RW], writes=[Rxo])
                        dma(yout[tok + m * 128: tok + (m + 1) * 128, :], xo, reads=[Rxo], writes=[R["yout"]])
            S.barrier()
    S.emit()
    return nc


def make_consts():
    inv = (10000.0 ** (-np.arange(0, 64, 2, dtype=np.float32) / 64)).astype(np.float32)
    pos = np.arange(4096, dtype=np.float32)
    ang = pos[:, None] * inv[None, :]
    ang = np.concatenate([ang, ang], -1)
    cos = np.cos(ang).astype(np.float32).T
    sin = np.sin(ang).astype(np.float32).T
    sgn = np.concatenate([-np.ones(32, np.float32), np.ones(32, np.float32)])[:, None]
    cosT = np.concatenate([cos, cos], 0)
    sinT = np.concatenate([sin * sgn, sin * sgn], 0)
    idx = np.arange(64)
    cf = np.zeros((128, 6, 128), np.float32)
    cf[:, 0, :] = np.eye(128, dtype=np.float32)
    cf[:, 1, :] = 1.0
    cf[:64, 2, :64] = (idx[:, None] <= idx[None, :])
    cf[:64, 3, :64] = (idx[:, None] < idx[None, :])
    cf[:64, 4, :64] = (idx[:, None] >= idx[None, :])
    cf[:64, 5, :64] = (idx[:, None] > idx[None, :])
    cbm = np.zeros((128, 3, 128), np.float32)
    cbm[:, 0, :] = np.eye(128)
    cbm[:, 1, :] = 1.0
    perm = np.zeros((128, 128), np.float32)
    for f in range(128):
        base = (f // 64) * 64
        dd = f % 64
        k = base + (dd + 32) % 64
        perm[k, f] = 1.0
    cbm[:, 2, :] = perm
    return np.ascontiguousarray(cosT), np.ascontiguousarray(sinT), cf, cbm.astype(ml_dtypes.bfloat16)


def shared_inputs(norm_g, w_in, conv_w, lam_qk, diff_norm_g, a_log, dt_bias, gdn_norm_g, w_branch, w_out, final_g):
    cosT, sinT, cf, cbm = make_consts()
    f = lambda a: np.ascontiguousarray(np.asarray(a, dtype=np.float32))
    convw = f(conv_w).reshape(DEPTH, 4, 24, 128).transpose(0, 3, 2, 1)
    return {
        "w_in": f(w_in), "w_branch": f(w_branch), "w_out": f(w_out), "norm_g": f(norm_g),
        "final_g": f(final_g).reshape(1, D), "convw": np.ascontiguousarray(convw),
        "lam_qk": f(lam_qk).reshape(DEPTH, 1, 256), "dng": f(diff_norm_g).reshape(DEPTH, 128, 1),
        "gng": f(gdn_norm_g).reshape(DEPTH, 1, 128), "a_log": f(a_log).reshape(DEPTH, 1, 16),
        "dt_bias": f(dt_bias).reshape(DEPTH, 1, 16), "cosT": cosT, "sinT": sinT, "constf": cf, "constb": cbm,
    }


_NC_CACHE = {}


def kernel(x_prompt, x_sample, norm_g, w_in, conv_w, lam_qk, diff_norm_g, a_log, dt_bias, gdn_norm_g, w_branch, w_out, final_g):
    x_prompt = np.asarray(x_prompt, dtype=np.float32)
    x_sample = np.asarray(x_sample, dtype=np.float32)
    seqs = SEQS_FULL
    key = tuple(seqs)
    if key not in _NC_CACHE:
        _NC_CACHE[key] = build(seqs)
    nc = _NC_CACHE[key]
    sh = shared_inputs(norm_g, w_in, conv_w, lam_qk, diff_norm_g, a_log, dt_bias, gdn_norm_g, w_branch, w_out, final_g)
    in_maps = []
    for c in range(8):
        xin = np.concatenate([x_prompt[c], x_sample[2 * c], x_sample[2 * c + 1]], axis=0)
        m = dict(sh)
        m["xin"] = np.ascontiguousarray(xin)
        in_maps.append(m)
    res = run_bass_kernel_spmd(nc, in_maps, core_ids=list(range(8)))
    y_prompt = np.empty_like(x_prompt)
    y_sample = np.empty_like(x_sample)
    for c in range(8):
        y = res.results[c]["yout"]
        y_prompt[c] = y[0:2048]
        y_sample[2 * c] = y[2048:2048 + 4096]
        y_sample[2 * c + 1] = y[2048 + 4096:]
    return (y_prompt, y_sample)
```

```python
import math
import os
DBG = int(os.environ.get('MKDBG', '99'))
import contextlib
import numpy as np
import ml_dtypes
import concourse.bass as bass
import concourse.mybir as mybir
from concourse.bass_utils import run_bass_kernel_spmd

F32 = mybir.dt.float32
BF16 = mybir.dt.bfloat16
AF = mybir.ActivationFunctionType
ALU = mybir.AluOpType
AX = mybir.AxisListType

D = 1024
DEPTH = 2
INW = 10272
EPS = 1e-6
SUBLN_EPS = 1e-5
SEQS_FULL = (2048, 4096, 4096)


class Res:
    __slots__ = ("name", "last_w", "readers", "excl")

    def __init__(self, name="", excl=False):
        self.name = name
        self.excl = excl
        self.last_w = None
        self.readers = []


class Op:
    __slots__ = ("eng", "fn", "deps", "needed", "semval", "dma", "dsem")

    def __init__(self, eng, fn, dma):
        self.eng = eng
        self.fn = fn
        self.deps = []
        self.needed = False
        self.semval = None
        self.dma = dma
        self.dsem = None


class Sched:
    ENG = ("tensor", "vector", "scalar", "gpsimd", "sync")
    DQ = ("sync", "gpsimd")
    NDMA = 12

    def __init__(self, nc, same_engine_sync=True):
        self.nc = nc
        self.ops = []
        self.same_engine_sync = same_engine_sync
        self.last = {e: None for e in self.ENG}
        self.lastd = {e: [] for e in self.DQ}

    def op(self, eng, fn, reads=(), writes=(), dma=False):
        o = Op(eng, fn, dma)
        deps = set()
        ex = [r for r in reads if r.excl]
        if ex:
            reads = [r for r in reads if not r.excl]
            writes = list(writes) + [r for r in ex if r not in writes]
        for r in reads:
            if r.last_w is not None:
                deps.add(r.last_w)
        for w in writes:
            if w.last_w is not None:
                deps.add(w.last_w)
            for rd in w.readers:
                deps.add(rd)
        o.deps = list(deps)
        for r in reads:
            r.readers.append(o)
        for w in writes:
            w.last_w = o
            w.readers = []
        self.ops.append(o)
        if dma:
            self.lastd[eng].append(o)
            if len(self.lastd[eng]) > self.NDMA:
                self.lastd[eng].pop(0)
        else:
            self.last[eng] = o
        return o

    def barrier(self):
        deps = [o for o in self.last.values() if o is not None]
        for q in self.DQ:
            deps += self.lastd[q]
        for e in self.ENG:
            o = Op(e, None, False)
            o.deps = list(deps)
            self.ops.append(o)

    def emit(self):
        nc = self.nc
        engs = {"tensor": nc.tensor, "vector": nc.vector, "scalar": nc.scalar,
                "gpsimd": nc.gpsimd, "sync": nc.sync}
        ses = self.same_engine_sync
        for o in self.ops:
            for d in o.deps:
                if d.dma:
                    continue
                if d.eng == o.eng and not o.dma and (o.eng == "tensor" or not ses) and o.fn is not None:
                    continue
                d.needed = True
        tl = {e: nc.alloc_semaphore(name=f"tl_{e}") for e in self.ENG}
        cnt = {e: 0 for e in self.ENG}
        dsems = {e: [nc.alloc_semaphore(name=f"d_{e}_{k}") for k in range(self.NDMA)] for e in self.DQ}
        dcount = {e: [0] * self.NDMA for e in dsems}
        dnext = {e: 0 for e in dsems}
        waited = {e: {} for e in self.ENG}
        nwait = 0
        for o in self.ops:
            E = engs[o.eng]
            w = waited[o.eng]
            reqs = {}
            for d in o.deps:
                if d.dma:
                    key = ("d", d.eng, d.dsem[0])
                    sem, val = d.dsem[1], d.dsem[2]
                else:
                    if d.semval is None:
                        continue
                    if d.eng == "tensor" and o.eng == "tensor" and not o.dma:
                        continue
                    key = ("t", d.eng)
                    sem, val = tl[d.eng], d.semval
                if reqs.get(key, (None, -1))[1] < val:
                    reqs[key] = (sem, val)
            if o.dma:
                k = dnext[o.eng]
                dnext[o.eng] = (k + 1) % self.NDMA
                prev = dcount[o.eng][k]
                if prev > 0:
                    key = ("d", o.eng, k)
                    if reqs.get(key, (None, -1))[1] < prev:
                        reqs[key] = (dsems[o.eng][k], prev)
                dcount[o.eng][k] = prev + 16
                o.dsem = (k, dsems[o.eng][k], prev + 16)
            for key, (sem, val) in reqs.items():
                if w.get(key, 0) < val:
                    E.wait_ge(sem, val)
                    w[key] = val
                    nwait += 1
            if o.fn is None:
                continue
            ins = o.fn(E)
            if o.dma:
                ins.then_inc(o.dsem[1], 16)
            elif o.needed:
                cnt[o.eng] += 1
                o.semval = cnt[o.eng]
                ins.then_inc(tl[o.eng], 1)
        for e in dsems:
            for k in range(self.NDMA):
                if dcount[e][k] > 0:
                    nc.sync.wait_ge(dsems[e][k], dcount[e][k])
        print(f"[sched] ops={len(self.ops)} waits={nwait} incs={cnt}", flush=True)


class Ring:
    def __init__(self, nc, name, shape, dtype, n):
        self.t = Ring.stack.enter_context(nc.sbuf_tensor(name, [shape[0], n] + list(shape[1:]), dtype))
        self.res = [Res(f"{name}{i}") for i in range(n)]
        self.n = n
        self.i = 0

    def next(self):
        k = self.i % self.n
        self.i += 1
        return self.t[:, k], self.res[k]


def build(seqs, debug=False, same_engine_sync=True, phases="ABCD", alim=99, nlayers=DEPTH, pool="gpsimd"):
    NT = sum(seqs)
    nc = bass.Bass("TRN2", target_bir_lowering=False)
    S = Sched(nc, same_engine_sync=same_engine_sync)

    def dram(name, shape, dt, kind):
        return nc.dram_tensor(name, shape, dt, kind=kind).ap()

    okind = "ExternalOutput" if debug else "Internal"
    xin = dram("xin", [NT, D], F32, "ExternalInput")
    w_in = dram("w_in", [DEPTH, D, INW], F32, "ExternalInput")
    w_br = dram("w_branch", [DEPTH, 2, D, D], F32, "ExternalInput")
    w_out = dram("w_out", [DEPTH, D, D], F32, "ExternalInput")
    norm_g = dram("norm_g", [DEPTH, D], F32, "ExternalInput")
    final_g = dram("final_g", [1, D], F32, "ExternalInput")
    convw = dram("convw", [DEPTH, 128, 24, 4], F32, "ExternalInput")
    lam_qk = dram("lam_qk", [DEPTH, 1, 256], F32, "ExternalInput")
    dng = dram("dng", [DEPTH, 128, 1], F32, "ExternalInput")
    gng = dram("gng", [DEPTH, 1, 128], F32, "ExternalInput")
    a_log = dram("a_log", [DEPTH, 1, 16], F32, "ExternalInput")
    dt_bias = dram("dt_bias", [DEPTH, 1, 16], F32, "ExternalInput")
    cosT_d = dram("cosT", [128, 4096], F32, "ExternalInput")
    sinT_d = dram("sinT", [128, 4096], F32, "ExternalInput")
    cf_d = dram("constf", [128, 6, 128], F32, "ExternalInput")
    cb_d = dram("constb", [128, 3, 128], BF16, "ExternalInput")
    yout = dram("yout", [NT, D], F32, "ExternalOutput")
    x1 = dram("x1", [NT, D], F32, okind)
    qkT = dram("qkT", [2048, NT], BF16, okind)
    va = dram("va", [NT, D], BF16, okind)
    zaT = dram("zaT", [D, NT], BF16, okind)
    qbT = dram("qbT", [D, NT], BF16, okind)
    kbT = dram("kbT", [D, NT], BF16, okind)
    ktm = dram("ktm", [NT, D], BF16, okind)
    vtm = dram("vtm", [NT, D], BF16, okind)
    zb = dram("zb", [NT, D], BF16, okind)
    ab = dram("ab", [NT, 32], F32, okind)
    gT = dram("gT", [2048, NT], BF16, okind)
    yaT = dram("yaT", [D, NT], BF16, okind)
    ybT = dram("ybT", [D, NT], BF16, okind)
    R = {n: Res(n) for n in "x1 qkT va zaT qbT kbT ktm vtm zb ab gT yaT ybT yout".split()}
    Rvals = list(R.values())

    cf = nc.alloc_sbuf_tensor("cf", [128, 6, 128], F32)
    cb = nc.alloc_sbuf_tensor("cb", [128, 3, 128], BF16)
    Rc = Res("consts")
    S.op("sync", lambda e: e.dma_start(out=cf[:], in_=cf_d[:, :, :]), writes=[Rc], dma=True)
    S.op("sync", lambda e: e.dma_start(out=cb[:], in_=cb_d[:, :, :]), writes=[Rc], dma=True)
    ident_f, ones_f = cf[:, 0, :], cf[:, 1, :]
    ident_b, ones_b, perm_b = cb[:, 0, :], cb[:, 1, :], cb[:, 2, :]
    MI = {0: cf[0:64, 2, 0:64], 1: cf[0:64, 4, 0:64]}
    MS = {0: cf[0:64, 3, 0:64], 1: cf[0:64, 5, 0:64]}

    PS = [nc.alloc_psum_tensor(f"ps{b}", [128, 512], F32) for b in range(8)]
    RPS = [Res(f"ps{b}", excl=True) for b in range(8)]

    DMAQ = ["sync", "gpsimd"]
    route = ["alt"]
    dq = [0]

    def dma(out, in_, reads=(), writes=(), q=None):
        if q is None:
            if route[0] == "alt":
                q = DMAQ[dq[0] % 2]
                dq[0] += 1
            else:
                q = "gpsimd" if any(w in Rvals for w in writes) else "sync"
        return S.op(q, lambda e: e.dma_start(out=out, in_=in_), reads=reads, writes=writes, dma=True)

    def mm(out, lhsT, rhs, start, stop, reads, writes, **kw):
        return S.op("tensor", lambda e: e.matmul(out, lhsT=lhsT, rhs=rhs, start=start, stop=stop, **kw),
                    reads=reads, writes=writes)

    def tr(out, in_, ident, reads, writes):
        return S.op("tensor", lambda e: e.transpose(out, in_, ident), reads=list(reads) + [Rc], writes=writes)

    def act(out, in_, func, reads, writes, eng="scalar", **kw):
        return S.op("scalar", lambda e: e.activation(out=out, in_=in_, func=func, **kw), reads=reads, writes=writes)

    def tt(eng, out, in0, in1, op, reads, writes):
        eng = pool if eng == "gpsimd" else eng
        return S.op(eng, lambda e: e.tensor_tensor(out=out, in0=in0, in1=in1, op=op), reads=reads, writes=writes)

    def ts(eng, out, in0, s1, op0, reads, writes, s2=None, op1=None):
        eng = pool if eng == "gpsimd" else eng
        if op1 is None:
            return S.op(eng, lambda e: e.tensor_scalar(out=out, in0=in0, scalar1=s1, scalar2=None, op0=op0),
                        reads=reads, writes=writes)
        return S.op(eng, lambda e: e.tensor_scalar(out=out, in0=in0, scalar1=s1, scalar2=s2, op0=op0, op1=op1),
                    reads=reads, writes=writes)

    def stt(eng, out, in0, scalar, in1, op0, op1, reads, writes):
        eng = "vector"
        return S.op(eng, lambda e: e.scalar_tensor_tensor(out=out, in0=in0, scalar=scalar, in1=in1, op0=op0, op1=op1),
                    reads=reads, writes=writes)

    def cp(eng, out, in_, reads, writes):
        eng = pool if eng == "gpsimd" else eng
        if eng == "scalar":
            return act(out, in_, AF.Copy, reads, writes)
        return S.op(eng, lambda e: e.tensor_copy(out=out, in_=in_), reads=reads, writes=writes)

    seq_off = [sum(seqs[:i]) for i in range(len(seqs))]

    for l in range(nlayers):
        lam_init = 0.8 - 0.6 * math.exp(-0.3 * l)
        xsrc = xin if l == 0 else x1
        Rxsrc = [] if l == 0 else [R["x1"]]
        route[0] = "alt"
        if "A" in phases:
          with contextlib.ExitStack() as st:
            Ring.stack = st

            def sb(name, shape, dt):
                return st.enter_context(nc.sbuf_tensor(f"A{l}_{name}", shape, dt))
            SMAX = max(seqs)
            hT = sb("hT", [128, 8, SMAX], BF16)
            RhT = Res("hT")
            gtile = sb("gtile", [128, D], F32)
            Rg = Res("gtile")
            dma(gtile[:], norm_g[l:l + 1, :].broadcast_to([128, D]), writes=[Rg])
            cwt = sb("cwt", [128, 24, 4], F32)
            Rcw = Res("cw")
            dma(cwt[:], convw[l], writes=[Rcw])
            xt_ring = Ring(nc, f"A{l}_xt", [128, D], F32, 2)
            hb_ring = Ring(nc, f"A{l}_hb", [128, D], BF16, 2)
            sq_junk = sb("sqj", [128, D], BF16)
            Rsqj = Res("sqj")
            ss_ring = Ring(nc, f"A{l}_ss", [128, 2], F32, 2)
            wst_ring = Ring(nc, f"A{l}_wst", [128, 8, 256], F32, 2)
            wsec = sb("wsec", [128, 8, 2048], BF16)
            Rwb = [Res(f"wsec{b}") for b in range(5)]
            ev_ring = Ring(nc, f"A{l}_ev", [128, 512], BF16, 4)
            evf_ring = Ring(nc, f"A{l}_evf", [128, 512], F32, 3)
            cs_ring = Ring(nc, f"A{l}_cs", [128, 2, 512], F32, 2)
            xc = sb("xc", [128, SMAX + 4], F32)
            Rxc = Res("xc")
            xacc_ring = Ring(nc, f"A{l}_xacc", [128, SMAX], F32, 2)
            ab_ring = Ring(nc, f"A{l}_ab", [128, 32], F32, 2)
            wv = w_in[l].rearrange("(c p) n -> p c n", p=128)

            def load_w(col0, ncols, eng="vector"):
                nb = (ncols + 255) // 256
                for b in range(nb):
                    n = min(256, ncols - b * 256)
                    wst, Rwst = wst_ring.next()
                    dma(wst[:, :, 0:n], wv[:, :, col0 + b * 256: col0 + b * 256 + n], writes=[Rwst])
                    cp(eng, wsec[:, :, b * 256: b * 256 + n], wst[:, :, 0:n], reads=[Rwst], writes=[Rwb[(b * 256) // 512]])

            psi = [0]

            def nextps(k=4):
                b = psi[0] % k
                psi[0] += 1
                return PS[b], RPS[b]

            for si, Sq in enumerate(seqs):
                t0 = seq_off[si]
                NTL = Sq // 128
                NB = Sq // 512
                for m in range(NTL):
                    xt, Rxt = xt_ring.next()
                    dma(xt, xsrc[t0 + m * 128: t0 + (m + 1) * 128, :], reads=Rxsrc, writes=[Rxt])
                    ss, Rss = ss_ring.next()
                    act(sq_junk[:], xt, AF.Square, reads=[Rxt], writes=[Rsqj, Rss], accum_out=ss[:, 0:1])
                    act(ss[:, 1:2], ss[:, 0:1], AF.Sqrt, reads=[Rss], writes=[Rss], scale=1.0 / D, bias=EPS)
                    S.op("vector", lambda e, ss=ss: e.reciprocal(out=ss[:, 1:2], in_=ss[:, 1:2]), reads=[Rss], writes=[Rss])
                    hb, Rhb = hb_ring.next()
                    stt("vector", hb, xt, ss[:, 1:2], gtile[:], ALU.mult, ALU.mult, reads=[Rxt, Rss, Rg], writes=[Rhb])
                    ps, Rps = nextps()
                    psb = ps[:, :].bitcast(BF16)
                    for c in range(8):
                        tr(psb[:, c * 128:(c + 1) * 128], hb[:, c * 128:(c + 1) * 128], ident_b, reads=[Rhb], writes=[Rps])
                    cp("vector", hT[:, :, m * 128:(m + 1) * 128], psb[:, 0:1024].rearrange("p (c t) -> p c t", c=8),
                       reads=[Rps], writes=[RhT])

                def fm_matmuls(ch, tb):
                    ps, Rps = nextps()
                    for c in range(8):
                        mm(ps[:, :], wsec[:, c, ch * 128:(ch + 1) * 128], hT[:, c, tb * 512:(tb + 1) * 512],
                           c == 0, c == 7, reads=[Rwb[ch // 4], RhT], writes=[Rps])
                    return ps, Rps

                def tm_matmuls(m, cb0, n=512):
                    ps, Rps = nextps()
                    for c in range(8):
                        mm(ps[:, 0:n], hT[:, c, m * 128:(m + 1) * 128], wsec[:, c, cb0:cb0 + n],
                           c == 0, c == 7, reads=[Rwb[cb0 // 512], RhT], writes=[Rps])
                    return ps, Rps

                if alim >= 2:
                    load_w(0, 2048)
                for tb in range(NB if (alim >= 2 and DBG >= 1) else 0):
                    cs, Rcs = cs_ring.next()
                    dma(cs[:, 0, :], cosT_d[:, tb * 512:(tb + 1) * 512], writes=[Rcs])
                    dma(cs[:, 1, :], sinT_d[:, tb * 512:(tb + 1) * 512], writes=[Rcs])
                    for ch in range(16 if DBG >= 2 else 0):
                        ps, Rps = fm_matmuls(ch, tb)
                        if DBG < 3:
                            continue
                        qb_, Rqb = ev_ring.next()
                        cp("scalar", qb_, ps[:, :], reads=[Rps], writes=[Rqb])
                        if DBG < 4:
                            continue
                        t1, Rt1 = evf_ring.next()
                        if os.environ.get('MKX') == '1':
                            tt("vector", t1, ps[:, :], gtile[:, 0:512], ALU.mult, reads=[Rps, Rg], writes=[Rt1])
                        elif os.environ.get('MKX') == '3':
                            tt("vector", t1, ps[:, :], cs[:, 0, :], ALU.mult, reads=[Rps, Rcs, Rqb], writes=[Rt1])
                        elif os.environ.get('MKX') == '2':
                            cp("vector", t1, ps[:, :], reads=[Rps], writes=[Rt1])
                            tt("vector", t1, t1, cs[:, 0, :], ALU.mult, reads=[Rt1, Rcs], writes=[Rt1])
                        else:
                            tt("vector", t1, ps[:, :], cs[:, 0, :], ALU.mult, reads=[Rps, Rcs], writes=[Rt1])
                        if DBG < 5:
                            continue
                        ps2, Rps2 = PS[4 + (psi[0] % 2)], RPS[4 + (psi[0] % 2)]
                        mm(ps2[:, :], perm_b, qb_, True, True, reads=[Rqb, Rc], writes=[Rps2])
                        t2, Rt2 = evf_ring.next()
                        tt("vector", t2, ps2[:, :], cs[:, 1, :], ALU.mult, reads=[Rps2, Rcs], writes=[Rt2])
                        ob, Rob = ev_ring.next()
                        tt("gpsimd", ob, t1, t2, ALU.add, reads=[Rt1, Rt2], writes=[Rob])
                        dma(qkT[ch * 128:(ch + 1) * 128, t0 + tb * 512: t0 + (tb + 1) * 512], ob, reads=[Rob], writes=[R["qkT"]])
                if alim < 3:
                    continue
                load_w(2048, 2048)
                for cb0 in (0, 512):
                    for m in range(NTL):
                        ps, Rps = tm_matmuls(m, cb0)
                        ob, Rob = ev_ring.next()
                        cp("scalar", ob, ps[:, :], reads=[Rps], writes=[Rob])
                        dma(va[t0 + m * 128: t0 + (m + 1) * 128, cb0:cb0 + 512], ob, reads=[Rob], writes=[R["va"]])
                for ch in range(8):
                    for tb in range(NB):
                        ps, Rps = fm_matmuls(8 + ch, tb)
                        ob, Rob = ev_ring.next()
                        act(ob, ps[:, :], AF.Silu, reads=[Rps], writes=[Rob])
                        dma(zaT[ch * 128:(ch + 1) * 128, t0 + tb * 512: t0 + (tb + 1) * 512], ob, reads=[Rob], writes=[R["zaT"]])
                if alim < 4:
                    continue
                def sec3_a(part, ch):
                    S.op("gpsimd", lambda e: e.memset(xc[:, 0:1], 0.0), writes=[Rxc])
                    S.op("gpsimd", lambda e, Sq=Sq: e.memset(xc[:, Sq + 1:Sq + 4], 0.0), writes=[Rxc])
                    for tb in range(NB):
                        ps, Rps = fm_matmuls(ch, tb)
                        cp("scalar", xc[:, 1 + tb * 512: 1 + (tb + 1) * 512], ps[:, :], reads=[Rps], writes=[Rxc])

                def sec3_b(part, ch):
                    gch = part * 8 + ch
                    xacc, Rxa = xacc_ring.next()
                    ts("vector", xacc[:, 0:Sq], xc[:, 0:Sq], cwt[:, gch, 0:1], ALU.mult, reads=[Rxc, Rcw], writes=[Rxa])
                    for j in range(1, 4):
                        stt("vector", xacc[:, 0:Sq], xc[:, j:j + Sq], cwt[:, gch, j:j + 1], xacc[:, 0:Sq], ALU.mult, ALU.add,
                            reads=[Rxc, Rxa, Rcw], writes=[Rxa])
                    act(xacc[:, 0:Sq], xacc[:, 0:Sq], AF.Silu, reads=[Rxa], writes=[Rxa])
                    return xacc, Rxa

                def sec3_c(part, ch, xacc, Rxa):
                    if True:
                        for tb in range(NB):
                            sl = slice(tb * 512, (tb + 1) * 512)
                            ob, Rob = ev_ring.next()
                            if part < 2:
                                sq, Rsq = evf_ring.next()
                                tt("gpsimd", sq, xacc[:, sl], xacc[:, sl], ALU.mult, reads=[Rxa], writes=[Rsq])
                                ps, Rps = nextps()
                                mm(ps[:, :], ones_f, sq, True, True, reads=[Rsq, Rc], writes=[Rps])
                                rn, Rrn = evf_ring.next()
                                act(rn, ps[:, :], AF.Ln, reads=[Rps], writes=[Rrn], bias=(128.0 * EPS if part == 0 else EPS),
                                    scale=(128.0 if part == 0 else 1.0))
                                act(rn, rn, AF.Exp, reads=[Rrn], writes=[Rrn], scale=-0.5)
                                tt("vector", ob, xacc[:, sl], rn, ALU.mult, reads=[Rxa, Rrn], writes=[Rob])
                                dst = qbT if part == 0 else kbT
                                dma(dst[ch * 128:(ch + 1) * 128, t0 + tb * 512: t0 + (tb + 1) * 512], ob, reads=[Rob],
                                    writes=[R["qbT" if part == 0 else "kbT"]])
                            else:
                                cp("vector", ob, xacc[:, sl], reads=[Rxa], writes=[Rob])
                            if part >= 1:
                                ps, Rps = nextps()
                                psb = ps[:, :].bitcast(BF16)
                                for q4 in range(4):
                                    tr(psb[:, q4 * 128:(q4 + 1) * 128], ob[:, q4 * 128:(q4 + 1) * 128], ident_b, reads=[Rob], writes=[Rps])
                                o2, Ro2 = ev_ring.next()
                                cp("vector", o2, psb[:, 0:512], reads=[Rps], writes=[Ro2])
                                dst = ktm if part == 1 else vtm
                                dma(dst[t0 + tb * 512: t0 + (tb + 1) * 512, ch * 128:(ch + 1) * 128].rearrange("(q p) f -> p q f", p=128),
                                    o2.rearrange("p (q f) -> p q f", q=4), reads=[Ro2], writes=[R["ktm" if part == 1 else "vtm"]])

                items = [(p_, c_) for p_ in range(3) for c_ in range(8)]
                load_w(4096, 1024, eng="gpsimd")
                sec3_a(*items[0])
                st_prev = sec3_b(*items[0])
                for ii in range(len(items)):
                    if ii + 1 < len(items):
                        if items[ii + 1][1] == 0:
                            load_w(4096 + items[ii + 1][0] * 1024, 1024, eng="gpsimd")
                        sec3_a(*items[ii + 1])
                    sec3_c(items[ii][0], items[ii][1], *st_prev)
                    if ii + 1 < len(items):
                        st_prev = sec3_b(*items[ii + 1])
                if alim < 5:
                    continue
                load_w(7168, 1056)
                for cb0 in (0, 512):
                    for m in range(NTL):
                        ps, Rps = tm_matmuls(m, cb0)
                        ob, Rob = ev_ring.next()
                        act(ob, ps[:, :], AF.Silu, reads=[Rps], writes=[Rob])
                        dma(zb[t0 + m * 128: t0 + (m + 1) * 128, cb0:cb0 + 512], ob, reads=[Rob], writes=[R["zb"]])
                for m in range(NTL):
                    ps, Rps = tm_matmuls(m, 1024, n=32)
                    abt, Rab = ab_ring.next()
                    cp("vector", abt, ps[:, 0:32], reads=[Rps], writes=[Rab])
                    dma(ab[t0 + m * 128: t0 + (m + 1) * 128, :], abt, reads=[Rab], writes=[R["ab"]])
                if alim < 6:
                    continue
                load_w(8224, 2048)
                for ch in range(16):
                    for tb in range(NB):
                        ps, Rps = fm_matmuls(ch, tb)
                        ob, Rob = ev_ring.next()
                        act(ob, ps[:, :], AF.Sigmoid, reads=[Rps], writes=[Rob])
                        dma(gT[ch * 128:(ch + 1) * 128, t0 + tb * 512: t0 + (tb + 1) * 512], ob, reads=[Rob], writes=[R["gT"]])
            S.barrier()

        route[0] = "split"
        if "B" in phases:
          with contextlib.ExitStack() as st:
            Ring.stack = st

            def sb(name, shape, dt):
                return st.enter_context(nc.sbuf_tensor(f"B{l}_{name}", shape, dt))
            SMAX = max(seqs)
            lq = sb("lq", [128, 256], F32)
            lsc = sb("lsc", [128, 8], F32)
            Rl = Res("lam")
            dma(lq[:], lam_qk[l].broadcast_to([128, 256]), writes=[Rl])
            tt("vector", lq[:, 0:64], lq[:, 0:64], lq[:, 64:128], ALU.mult, reads=[Rl], writes=[Rl])
            tt("vector", lq[:, 128:192], lq[:, 128:192], lq[:, 192:256], ALU.mult, reads=[Rl], writes=[Rl])
            S.op("vector", lambda e: e.tensor_reduce(out=lsc[:, 0:1], in_=lq[:, 0:64], axis=AX.X, op=ALU.add), reads=[Rl], writes=[Rl])
            S.op("vector", lambda e: e.tensor_reduce(out=lsc[:, 1:2], in_=lq[:, 128:192], axis=AX.X, op=ALU.add), reads=[Rl], writes=[Rl])
            act(lsc[:, 2:4], lsc[:, 0:2], AF.Exp, reads=[Rl], writes=[Rl])
            tt("vector", lsc[:, 4:5], lsc[:, 3:4], lsc[:, 2:3], ALU.subtract, reads=[Rl], writes=[Rl])
            ts("vector", lsc[:, 5:6], lsc[:, 4:5], -lam_init, ALU.add, reads=[Rl], writes=[Rl])
            neglam = lsc[:, 5:6]
            dgt = sb("dgt", [128, 2], F32)
            dma(dgt[:, 0:1], dng[l], writes=[Rl])
            ts("vector", dgt[:, 1:2], dgt[:, 0:1], (1.0 - lam_init), ALU.mult, reads=[Rl], writes=[Rl])
            QTzz = [[sb(f"QT0{i}", [128, SMAX], BF16), sb(f"QT1{i}", [128, SMAX], BF16)] for i in range(2)]
            for i in range(2):
                S.op("gpsimd", lambda e, i=i: e.memset(QTzz[i][0][64:128, :], 0.0), writes=[])
                S.op("gpsimd", lambda e, i=i: e.memset(QTzz[i][1][0:64, :], 0.0), writes=[])
            KTT = [sb(f"KT{i}", [128, SMAX], BF16) for i in range(2)]
            VV = [sb(f"V{i}", [128, SMAX // 128, 128], BF16) for i in range(2)]
            RQQ, RKK, RVV = [Res("QTa"), Res("QTb")], [Res("KTa"), Res("KTb")], [Res("Va"), Res("Vb")]
            ev_ringB = Ring(nc, f"B{l}_evb", [128, 4, 512], F32, 2)
            hcount = [0]
            deferred = []

            def load_head(si_, h_, slot):
                Sq_ = seqs[si_]
                t0_h = seq_off[si_]
                dma(QTzz[slot][0][0:64, 0:Sq_], qkT[h_ * 128:h_ * 128 + 64, t0_h:t0_h + Sq_], reads=[R["qkT"]], writes=[RQQ[slot]])
                dma(QTzz[slot][1][64:128, 0:Sq_], qkT[h_ * 128 + 64:(h_ + 1) * 128, t0_h:t0_h + Sq_], reads=[R["qkT"]], writes=[RQQ[slot]])
                dma(KTT[slot][:, 0:Sq_], qkT[1024 + h_ * 128: 1024 + (h_ + 1) * 128, t0_h:t0_h + Sq_], reads=[R["qkT"]], writes=[RKK[slot]])
                dma(VV[slot][:, 0:Sq_ // 128, :], va[t0_h:t0_h + Sq_, h_ * 128:(h_ + 1) * 128].rearrange("(k p) f -> p k f", p=128),
                    reads=[R["va"]], writes=[RVV[slot]])

            heads = [(si_, h_) for si_ in range(len(seqs)) for h_ in range(8)]
            load_head(heads[0][0], heads[0][1], 0)
            pt_ring = Ring(nc, f"B{l}_pt", [128, 512], BF16, 6)
            f_ring = Ring(nc, f"B{l}_f", [128, 512], F32, 10)
            z_ring = Ring(nc, f"B{l}_z", [128, 512], BF16, 3)
            y_ring = Ring(nc, f"B{l}_y", [128, 512], BF16, 2)
            scale = 64 ** -0.5
            spi = 0
            for si, Sq in enumerate(seqs):
                t0 = seq_off[si]
                NKT = Sq // 128
                NB = Sq // 512
                for h in range(8):
                    slot = hcount[0] % 2
                    hidx = hcount[0]
                    hcount[0] += 1
                    QTz, KT, V = QTzz[slot], KTT[slot], VV[slot]
                    RQ, RK, RV = RQQ[slot], RKK[slot], RVV[slot]
                    for qb_ in range(NB):
                        if qb_ == min(1, NB - 1) and hidx + 1 < len(heads):
                            load_head(heads[hidx + 1][0], heads[hidx + 1][1], 1 - slot)
                        qs = slice(qb_ * 512, (qb_ + 1) * 512)
                        zt, Rz = z_ring.next()
                        dma(zt, zaT[h * 128:(h + 1) * 128, t0 + qb_ * 512: t0 + (qb_ + 1) * 512], reads=[R["zaT"]], writes=[Rz])
                        PIPE = 2
                        SB = (0, 1, 7)
                        pend = []

                        def flush_one():
                            kt_, c_, pt_, Rpt_ = pend.pop(0)
                            mm(PS[2 + 2 * c_][:, :], V[:, kt_, :], pt_, kt_ == 0, kt_ == NKT - 1, reads=[RV, Rpt_], writes=[RPS[2 + 2 * c_]])
                            mm(PS[3 + 2 * c_][:, :], ones_b, pt_, kt_ == 0, kt_ == NKT - 1, reads=[Rc, Rpt_], writes=[RPS[3 + 2 * c_]])

                        for kt in range(NKT):
                            if kt == NKT // 2:
                                while deferred:
                                    deferred.pop(0)()
                            for c in range(2):
                                sp, Rsp = PS[SB[spi % 3]], RPS[SB[spi % 3]]
                                spi += 1
                                mm(sp[:, :], KT[:, kt * 128:(kt + 1) * 128], QTz[c][:, qs],
                                   True, True, reads=[RK, RQ], writes=[Rsp])
                                pt, Rpt = pt_ring.next()
                                act(pt, sp[:, :], AF.Exp, reads=[Rsp], writes=[Rpt], scale=scale)
                                pend.append((kt, c, pt, Rpt))
                                if len(pend) > PIPE:
                                    flush_one()
                        while pend:
                            flush_one()
                        evb, Revb = ev_ringB.next()
                        cp("vector", evb[:, 0, :], PS[2][:, :], reads=[RPS[2]], writes=[Revb])
                        cp("scalar", evb[:, 1, :], PS[3][:, :], reads=[RPS[3]], writes=[Revb])
                        cp("vector", evb[:, 2, :], PS[4][:, :], reads=[RPS[4]], writes=[Revb])
                        cp("scalar", evb[:, 3, :], PS[5][:, :], reads=[RPS[5]], writes=[Revb])
                        r0, Rr0 = f_ring.next()
                        S.op("vector", lambda e, r0=r0, evb=evb: e.reciprocal(out=r0, in_=evb[:, 1, :]), reads=[Revb], writes=[Rr0])
                        tt("gpsimd", r0, evb[:, 0, :], r0, ALU.mult, reads=[Revb, Rr0], writes=[Rr0])
                        r1, Rr1 = f_ring.next()
                        S.op("vector", lambda e, r1=r1, evb=evb: e.reciprocal(out=r1, in_=evb[:, 3, :]), reads=[Revb], writes=[Rr1])
                        tt("gpsimd", r1, evb[:, 2, :], r1, ALU.mult, reads=[Revb, Rr1], writes=[Rr1])
                        o_, Ro = f_ring.next()
                        stt("vector", o_, r1, neglam, r0, ALU.mult, ALU.add, reads=[Rr1, Rr0, Rl], writes=[Ro])
                        sq_, Rsq = f_ring.next()
                        tt("gpsimd", sq_, o_, o_, ALU.mult, reads=[Ro], writes=[Rsq])

                        def post2(o_=o_, Ro=Ro, sq_=sq_, Rsq=Rsq, zt=zt, Rz=Rz, h=h, col0=t0 + qb_ * 512):
                            mm(PS[6][:, :], ones_f, sq_, True, True, reads=[Rsq, Rc], writes=[RPS[6]])
                            rn, Rrn = f_ring.next()
                            act(rn, PS[6][:, :], AF.Sqrt, reads=[RPS[6]], writes=[Rrn], scale=1.0 / 128, bias=SUBLN_EPS)
                            S.op("vector", lambda e, rn=rn: e.reciprocal(out=rn, in_=rn), reads=[Rrn], writes=[Rrn])
                            stt("vector", o_, o_, dgt[:, 1:2], rn, ALU.mult, ALU.mult, reads=[Ro, Rrn, Rl], writes=[Ro])
                            yt, Ry = y_ring.next()
                            tt("gpsimd", yt, o_, zt, ALU.mult, reads=[Ro, Rz], writes=[Ry])
                            dma(yaT[h * 128:(h + 1) * 128, col0:col0 + 512], yt, reads=[Ry], writes=[R["yaT"]])
                        deferred.append(post2)
            while deferred:
                deferred.pop(0)()
            S.barrier()

        if "C" in phases:
          with contextlib.ExitStack() as st:
            Ring.stack = st

            def sb(name, shape, dt):
                return st.enter_context(nc.sbuf_tensor(f"C{l}_{name}", shape, dt))
            SMAX = max(seqs)
            NMAX = SMAX // 64
            Rp = Res("gparams")
            alb = sb("alb", [64, 16], F32)
            dtb = sb("dtb", [64, 16], F32)
            dma(alb[:], a_log[l].broadcast_to([64, 16]), writes=[Rp])
            dma(dtb[:], dt_bias[l].broadcast_to([64, 16]), writes=[Rp])
            act(alb[:], alb[:], AF.Exp, reads=[Rp], writes=[Rp])
            ts("vector", alb[:], alb[:], -1.0, ALU.mult, reads=[Rp], writes=[Rp])
            gnb = sb("gnb", [64, 128], F32)
            dma(gnb[:], gng[l].broadcast_to([64, 128]), writes=[Rp])
            ABt = sb("ABt", [64, NMAX, 32], F32)
            G = sb("G", [64, NMAX, 16], F32)
            NBt = sb("NBt", [64, NMAX, 16], F32)
            RG = Res("G")
            O2 = [sb("Of", [64, NMAX, 128], F32), sb("Ob", [64, NMAX, 128], F32)]
            RO = [Res("Of"), Res("Ob")]
            Sst = [sb("Sf", [128, 128], F32), sb("Sb", [128, 128], F32)]
            Sbf = [sb("Sfb", [128, 128], BF16), sb("Sbb", [128, 128], BF16)]
            RS = [Res("Sf"), Res("Sb")]
            RSb = [Res("Sfb"), Res("Sbb")]
            vn_ring = [Ring(nc, f"C{l}_vn{d}", [64, 128], BF16, 2) for d in range(2)]

            def gb(name, shape, dt, n=1):
                return [Ring(nc, f"C{l}_{name}{d}", shape, dt, n) for d in range(2)]
            W8 = 8 * 64
            qT_r, kT_r = gb("qT", [128, W8], BF16), gb("kT", [128, W8], BF16)
            kt_r, vt_r = gb("kt", [64, 8, 128], BF16), gb("vt", [64, 8, 128], BF16)
            GU_r = gb("GU", [64, W8], F32)
            GCc_r = gb("GCc", [64, 8, 4], F32)
            GCr_r = gb("GCr", [128, W8], F32)
            Dm_r = gb("Dm", [64, W8], F32)
            Ei_r = gb("Ei", [64, W8], F32)
            at_r = gb("at", [64, W8], BF16, 2)
            P_r = [gb("Pa", [128, W8], F32), gb("Pb", [128, W8], F32)]
            PT_r = [gb("PTa", [128, W8], F32), gb("PTb", [128, W8], F32)]
            P0_r = gb("P0s", [64, W8], F32)
            PT0z = [sb(f"PT0z{d}", [128, W8], F32) for d in range(2)]
            RPT0z = [Res(f"PT0z{d}") for d in range(2)]
            for d in range(2):
                S.op("gpsimd", lambda e, d=d: e.memset(PT0z[d][:], 0.0), writes=[RPT0z[d]])
            Y_r = gb("Y", [128, W8], F32)
            TT_r = gb("TT", [64, W8], BF16)
            KG_r, KD_r = gb("KG", [64, 8, 128], BF16), gb("KD", [64, 8, 128], BF16, 2)
            Ub_r = gb("Ub", [64, 8, 128], F32, 2)
            WT_r, QG_r = gb("WT", [128, W8], BF16, 2), gb("QG", [128, W8], BF16, 2)
            gl_r = gb("gl", [128, 8], F32, 2)
            yb_r = Ring(nc, f"C{l}_yb", [64, 8, 128], F32, 2)
            ybb_r = Ring(nc, f"C{l}_ybb", [64, 8, 128], BF16, 2)
            zb_r = Ring(nc, f"C{l}_zb", [64, 8, 128], BF16, 2)
            rs_r = Ring(nc, f"C{l}_rs", [64, 8, 2], F32, 2)
            yo_r = Ring(nc, f"C{l}_yo", [128, W8], BF16, 2)

            def bc3(ap2, n):
                return ap2.unsqueeze(2).broadcast_to([ap2.shape[0], 8, n])

            def v4(ap):
                return ap.rearrange("p (m c) -> p m c", m=4)

            def v3(ap, inner):
                return ap.rearrange("p (n i) -> p n i", n=8)

            for si, Sq in enumerate(seqs):
                t0 = seq_off[si]
                N = Sq // 64
                NG = N // 8
                dma(ABt[:, 0:N, :], ab[t0:t0 + Sq, :].rearrange("(n t) c -> t n c", t=64), reads=[R["ab"]], writes=[RG], q="sync")
                al_b = alb[:, :].unsqueeze(1).broadcast_to([64, N, 16])
                dt_b = dtb[:, :].unsqueeze(1).broadcast_to([64, N, 16])
                tt("vector", G[:, 0:N, :], ABt[:, 0:N, 0:16], dt_b, ALU.add, reads=[RG, Rp], writes=[RG])
                act(G[:, 0:N, :], G[:, 0:N, :], AF.Exp, reads=[RG], writes=[RG])
                act(G[:, 0:N, :], G[:, 0:N, :], AF.Ln, reads=[RG], writes=[RG], bias=1.0)
                tt("vector", G[:, 0:N, :], G[:, 0:N, :], al_b, ALU.mult, reads=[RG, Rp], writes=[RG])
                act(NBt[:, 0:N, :], ABt[:, 0:N, 16:32], AF.Sigmoid, reads=[RG], writes=[RG])
                ts("vector", NBt[:, 0:N, :], NBt[:, 0:N, :], -1.0, ALU.mult, reads=[RG], writes=[RG])
                for h in range(8):
                    for d in range(2):
                        S.op("gpsimd", lambda e, d=d: e.memset(Sst[d][:], 0.0), writes=[RS[d]])
                        S.op("gpsimd", lambda e, d=d: e.memset(Sbf[d][:], 0.0), writes=[RSb[d]])
                    def prep_group(gi):
                        prep = {}
                        for d in range(2):
                            g = gi if d == 0 else NG - 1 - gi
                            col = d * 8 + h
                            n0 = g * 8
                            tok = t0 + g * 512
                            last = 63 if d == 0 else 0
                            qT, RqT = qT_r[d].next()
                            kT, RkT = kT_r[d].next()
                            kt_, Rkt = kt_r[d].next()
                            vt_, Rvt = vt_r[d].next()
                            dma(qT, qbT[h * 128:(h + 1) * 128, tok:tok + 512], reads=[R["qbT"]], writes=[RqT])
                            dma(kT, kbT[h * 128:(h + 1) * 128, tok:tok + 512], reads=[R["kbT"]], writes=[RkT])
                            dma(kt_, ktm[tok:tok + 512, h * 128:(h + 1) * 128].rearrange("(n t) f -> t n f", t=64), reads=[R["ktm"]], writes=[Rkt])
                            dma(vt_, vtm[tok:tok + 512, h * 128:(h + 1) * 128].rearrange("(n t) f -> t n f", t=64), reads=[R["vtm"]], writes=[Rvt])
                            Gc = G[:, n0:n0 + 8, col]
                            NBc = NBt[:, n0:n0 + 8, col]
                            GCc, RGCc = GCc_r[d].next()
                            pb = 4 * d
                            mm(PS[pb][0:64, 0:8], MI[d], Gc, True, True, reads=[RG, Rc], writes=[RPS[pb]])
                            cp("vector", GCc[:, :, 0], PS[pb][0:64, 0:8], reads=[RPS[pb]], writes=[RGCc])
                            GU, RGU = GU_r[d].next()
                            tt("gpsimd", v3(GU, 64), bc3(Gc, 64), MI[d].unsqueeze(1).broadcast_to([64, 8, 64]), ALU.mult,
                               reads=[RG, Rc], writes=[RGU])
                            mm(PS[pb + 1][:, :], ones_f[0:64, :], GU, True, True, reads=[RGU, Rc], writes=[RPS[pb + 1]])
                            GCr, RGCr = GCr_r[d].next()
                            cp("scalar", GCr, PS[pb + 1][:, :], reads=[RPS[pb + 1]], writes=[RGCr])
                            Dm, RDm = Dm_r[d].next()
                            tt("vector", v3(Dm, 64), v3(GCr[0:64, :], 64), bc3(GCc[:, :, 0], 64), ALU.subtract, reads=[RGCr, RGCc], writes=[RDm])
                            ts("vector", Dm, Dm, 0.0, ALU.min, reads=[RDm], writes=[RDm])
                            act(Dm, Dm, AF.Exp, reads=[RDm], writes=[RDm])
                            Ei, REi = Ei_r[d].next()
                            tt("gpsimd", v3(Ei, 64), v3(Dm, 64), MI[d].unsqueeze(1).broadcast_to([64, 8, 64]), ALU.mult, reads=[RDm, Rc], writes=[REi])
                            act(GCr, GCr, AF.Exp, reads=[RGCr], writes=[RGCr])
                            gl, Rgl = gl_r[d].next()
                            cp("vector", gl, v3(GCr, 64)[:, :, last], reads=[RGCr], writes=[Rgl])
                            act(GCc[:, :, 1], GCc[:, :, 0], AF.Exp, reads=[RGCc], writes=[RGCc])
                            act(GCc[:, :, 2], GCc[:, :, 0], AF.Identity, reads=[RGCc], writes=[RGCc], scale=-1.0)
                            for n in range(8):
                                cs_ = slice(n * 64, (n + 1) * 64)
                                mm(PS[pb + 2][0:64, cs_], kT[:, cs_], kT[:, cs_], True, True, reads=[RkT], writes=[RPS[pb + 2]])
                            for n in range(8):
                                cs_ = slice(n * 64, (n + 1) * 64)
                                mm(PS[pb + 1][0:64, cs_], kT[:, cs_], qT[:, cs_], True, True, reads=[RkT, RqT], writes=[RPS[pb + 1]])
                            at_, Rat = at_r[d].next()
                            tt("vector", at_, PS[pb + 1][0:64, :], Ei, ALU.mult, reads=[RPS[pb + 1], REi], writes=[Rat])
                            P0, RP0 = P0_r[d].next()
                            tt("vector", P0, PS[pb + 2][0:64, :], Ei, ALU.mult, reads=[RPS[pb + 2], REi], writes=[RP0])
                            tt("gpsimd", v3(P0, 64), v3(P0, 64), MS[d].unsqueeze(1).broadcast_to([64, 8, 64]), ALU.mult, reads=[RP0, Rc], writes=[RP0])
                            tt("gpsimd", v3(P0, 64), v3(P0, 64), bc3(NBc, 64), ALU.mult, reads=[RP0, RG], writes=[RP0])
                            prep[d] = dict(g=g, n0=n0, col=col, last=last, qT=(qT, RqT), kT=(kT, RkT), kt=(kt_, Rkt), vt=(vt_, Rvt),
                                           GCc=(GCc, RGCc), GCr=(GCr, RGCr), gl=(gl, Rgl), at=(at_, Rat), P0=(P0, RP0), NBc=NBc)
                            yield
                        cur = {}
                        for d in range(2):
                            P0, RP0 = prep[d]["P0"]
                            pb = 4 * d
                            for m in range(4):
                                tr(PS[pb][:, m * 64:(m + 1) * 64], P0[:, m * 128:(m + 1) * 128], ident_f[0:64, 0:64], reads=[RP0], writes=[RPS[pb]])
                            PT0, RPT0 = PT0z[d], RPT0z[d]
                            cp("scalar", v4(PT0)[0:64, :, 0:64], PS[pb][0:64, 0:256].rearrange("p (m i) -> p m i", m=4), reads=[RPS[pb]], writes=[RPT0])
                            cp("vector", v4(PT0)[64:128, :, 64:128], PS[pb][64:128, 0:256].rearrange("p (m i) -> p m i", m=4), reads=[RPS[pb]], writes=[RPT0])
                            for m in range(4):
                                cs_ = slice(m * 128, (m + 1) * 128)
                                tr(PS[pb + 1][:, cs_], PT0[:, cs_], ident_f, reads=[RPT0], writes=[RPS[pb + 1]])
                            P0b, RP0b = P_r[0][d].next()
                            cp("scalar", P0b, PS[pb + 1][:, :], reads=[RPS[pb + 1]], writes=[RP0b])
                            Y, RY = Y_r[d].next()
                            tt("gpsimd", v4(Y), v4(P0b), ident_f.unsqueeze(1).broadcast_to([128, 4, 128]), ALU.add, reads=[RP0b, Rc], writes=[RY])
                            cur[d] = [P0b, RP0b, PT0, RPT0, Y, RY]
                        yield
                        for k in range(0, 6):
                            for d in range(2):
                                P, RP, PT, RPT, Y, RY = cur[d]
                                pb = 4 * d
                                if k < 5:
                                    for m in range(4):
                                        cs_ = slice(m * 128, (m + 1) * 128)
                                        mm(PS[pb + 1][:, cs_], PT[:, cs_], P[:, cs_], True, True, reads=[RP, RPT], writes=[RPS[pb + 1]])
                                    for m in range(4):
                                        cs_ = slice(m * 128, (m + 1) * 128)
                                        mm(PS[pb + 2][:, cs_], P[:, cs_], PT[:, cs_], True, True, reads=[RP, RPT], writes=[RPS[pb + 2]])
                                if k >= 1:
                                    for m in range(4):
                                        cs_ = slice(m * 128, (m + 1) * 128)
                                        mm(PS[pb][:, cs_], PT[:, cs_], Y[:, cs_], True, True, reads=[RY, RPT], writes=[RPS[pb]])
                                    tt("vector", Y, Y, PS[pb][:, :], ALU.add, reads=[RY, RPS[pb]], writes=[RY])
                                if k < 5:
                                    Pn, RPn = P_r[(k + 1) % 2][d].next()
                                    PTn, RPTn = PT_r[(k + 1) % 2][d].next()
                                    cp("scalar", Pn, PS[pb + 1][:, :], reads=[RPS[pb + 1]], writes=[RPn])
                                    cp("vector", PTn, PS[pb + 2][:, :], reads=[RPS[pb + 2]], writes=[RPTn])
                                    cur[d] = [Pn, RPn, PTn, RPTn, Y, RY]
                            yield
                        for d in range(2):
                            pr = prep[d]
                            pb = 4 * d
                            Y, RY = cur[d][4], cur[d][5]
                            TT, RTT = TT_r[d].next()
                            TT4 = TT.rearrange("p (m two i) -> p m two i", m=4, two=2)
                            cp("scalar", TT4[:, :, 0, :], v4(Y)[0:64, :, 0:64], reads=[RY], writes=[RTT])
                            for m in range(4):
                                mm(PS[pb][0:64, m * 64:(m + 1) * 64], ident_f[:, 64:128], Y[:, m * 128 + 64:(m + 1) * 128], True, True,
                                   reads=[RY, Rc], writes=[RPS[pb]])
                            cp("vector", TT4[:, :, 1, :], PS[pb][0:64, 0:256].rearrange("p (m i) -> p m i", m=4), reads=[RPS[pb]], writes=[RTT])
                            kt_, Rkt = pr["kt"]
                            vt_, Rvt = pr["vt"]
                            GCc, RGCc = pr["GCc"]
                            GCr, RGCr = pr["GCr"]
                            gl, Rgl = pr["gl"]
                            qT, RqT = pr["qT"]
                            KG, RKG = KG_r[d].next()
                            tt("gpsimd", KG, kt_, bc3(GCc[:, :, 1], 128), ALU.mult, reads=[Rkt, RGCc], writes=[RKG])
                            pr["KG"] = (KG, RKG)
                            pr["TT"] = (TT, RTT)
                        prep_done = prep
                        yield
                        for d in range(2):
                            pr = prep_done[d]
                            pb = 4 * d
                            TT, RTT = pr["TT"]
                            KG, RKG = pr["KG"]
                            kt_, Rkt = pr["kt"]
                            vt_, Rvt = pr["vt"]
                            GCc, RGCc = pr["GCc"]
                            GCr, RGCr = pr["GCr"]
                            gl, Rgl = pr["gl"]
                            qT, RqT = pr["qT"]
                            last = pr["last"]
                            NBc = pr["NBc"]
                            KD, RKD = KD_r[d].next()
                            mm(PS[pb][0:64, 0:8], ones_f[0:64, 0:64], G[:, pr["n0"]:pr["n0"] + 8, pr["col"]], True, True, reads=[RG, Rc], writes=[RPS[pb]])
                            tt("vector", GCc[:, :, 3], PS[pb][0:64, 0:8], GCc[:, :, 0], ALU.subtract, reads=[RPS[pb], RGCc], writes=[RGCc])
                            act(GCc[:, :, 3], GCc[:, :, 3], AF.Exp, reads=[RGCc], writes=[RGCc])
                            tt("gpsimd", KD, kt_, bc3(GCc[:, :, 3], 128), ALU.mult, reads=[Rkt, RGCc], writes=[RKD])
                            Ub, RUb = Ub_r[d].next()
                            for half in range(2):
                                for n in range(4):
                                    nn = half * 4 + n
                                    mm(PS[pb + 1][0:64, n * 128:(n + 1) * 128], TT[:, nn * 64:(nn + 1) * 64], vt_[:, nn, :], True, True,
                                       reads=[RTT, Rvt], writes=[RPS[pb + 1]])
                                tt("vector", Ub[:, half * 4:(half + 1) * 4, :], PS[pb + 1][0:64, :].rearrange("p (n f) -> p n f", n=4),
                                   NBc[:, half * 4:(half + 1) * 4].unsqueeze(2).broadcast_to([64, 4, 128]), ALU.mult,
                                   reads=[RPS[pb + 1], RG], writes=[RUb])
                            WT, RWT = WT_r[d].next()
                            for n in range(8):
                                cs_ = slice(n * 64, (n + 1) * 64)
                                mm(PS[pb + 2][:, cs_], KG[:, n, :], TT[:, cs_], True, True, reads=[RKG, RTT], writes=[RPS[pb + 2]])
                            cp("scalar", WT, PS[pb + 2][:, :], reads=[RPS[pb + 2]], writes=[RWT])
                            QG, RQG = QG_r[d].next()
                            tt("vector", QG, qT, GCr, ALU.mult, reads=[RqT, RGCr], writes=[RQG])
                            pr.update(KD=(KD, RKD), Ub=(Ub, RUb), WT=(WT, RWT), QG=(QG, RQG))
                            yield
                        return prep_done


                    def scan_step(prep_done, step):
                        for d in range(2):
                            pr = prep_done[d]
                            pb = 4 * d
                            n = step if d == 0 else 7 - step
                            ng = pr["n0"] + n
                            cs_ = slice(n * 64, (n + 1) * 64)
                            WT, RWT = pr["WT"]
                            QG, RQG = pr["QG"]
                            KD, RKD = pr["KD"]
                            Ub, RUb = pr["Ub"]
                            at_, Rat = pr["at"]
                            gl, Rgl = pr["gl"]
                            NBc = pr["NBc"]
                            psn = PS[pb + 3]
                            Rpsn = RPS[pb + 3]
                            mm(psn[0:64, 0:128], WT[:, cs_], Sbf[d][:], True, True, reads=[RWT, RSb[d]], writes=[Rpsn])
                            vn, Rvn = vn_ring[d].next()
                            stt("vector", vn, psn[0:64, 0:128], NBc[:, n:n + 1], Ub[:, n, :], ALU.mult, ALU.subtract,
                                reads=[Rpsn, RUb, RG], writes=[Rvn])
                            mm(psn[0:64, 128:256], QG[:, cs_], Sbf[d][:], True, False, reads=[RQG, RSb[d]], writes=[Rpsn])
                            mm(psn[0:64, 128:256], at_[:, cs_], vn, False, True, reads=[Rat, Rvn], writes=[Rpsn])
                            mm(psn[:, 256:384], KD[:, n, :], vn, True, True, reads=[RKD, Rvn], writes=[Rpsn])
                            cp("scalar", O2[d][:, ng, :], psn[0:64, 128:256], reads=[Rpsn], writes=[RO[d]])
                            stt("vector", Sst[d][:], Sst[d][:], gl[:, n:n + 1], psn[:, 256:384], ALU.mult, ALU.add,
                                reads=[RS[d], Rgl, Rpsn], writes=[RS[d]])
                            cp("scalar", Sbf[d][:], Sst[d][:], reads=[RS[d]], writes=[RSb[d]])

                    def run_gen(gen, k):
                        for _ in range(k):
                            try:
                                next(gen)
                            except StopIteration as e_:
                                return e_.value
                        return None

                    pd_cur = run_gen(prep_group(0), 10 ** 6)
                    for gi in range(NG):
                        nxt = prep_group(gi + 1) if gi + 1 < NG else None
                        pd_next = None
                        for step in range(8):
                            scan_step(pd_cur, step)
                            if nxt is not None and pd_next is None:
                                pd_next = run_gen(nxt, 4)
                        if nxt is not None and pd_next is None:
                            pd_next = run_gen(nxt, 10 ** 6)
                        pd_cur = pd_next
                    for g in range(NG):
                        n0 = g * 8
                        tok = t0 + g * 512
                        yb_, Ryb = yb_r.next()
                        tt("vector", yb_, O2[0][:, n0:n0 + 8, :], O2[1][:, n0:n0 + 8, :], ALU.add, reads=[RO[0], RO[1]], writes=[Ryb])
                        ybb, Rybb = ybb_r.next()
                        rs, Rrs = rs_r.next()
                        sqf, Rsqf = Ub_r[0].next()
                        tt("gpsimd", sqf, yb_, yb_, ALU.mult, reads=[Ryb], writes=[Rsqf])
                        S.op("vector", lambda e, rs=rs, sqf=sqf: e.tensor_reduce(out=rs[:, :, 0], in_=sqf, axis=AX.X, op=ALU.add),
                             reads=[Rsqf], writes=[Rrs])
                        act(rs[:, :, 1], rs[:, :, 0], AF.Sqrt, reads=[Rrs], writes=[Rrs], scale=1.0 / 128, bias=EPS)
                        S.op("vector", lambda e, rs=rs: e.reciprocal(out=rs[:, :, 1], in_=rs[:, :, 1]), reads=[Rrs], writes=[Rrs])
                        tt("vector", yb_, yb_, bc3(rs[:, :, 1], 128), ALU.mult, reads=[Ryb, Rrs], writes=[Ryb])
                        tt("gpsimd", yb_, yb_, gnb[:, :].unsqueeze(1).broadcast_to([64, 8, 128]), ALU.mult, reads=[Ryb, Rp], writes=[Ryb])
                        zt, Rz = zb_r.next()
                        dma(zt, zb[tok:tok + 512, h * 128:(h + 1) * 128].rearrange("(n t) f -> t n f", t=64), reads=[R["zb"]], writes=[Rz])
                        tt("vector", ybb, yb_, zt, ALU.mult, reads=[Ryb, Rz], writes=[Rybb])
                        psb = PS[0][:, :].bitcast(BF16)
                        for n in range(8):
                            tr(psb[:, n * 64:(n + 1) * 64], ybb[:, n, :], ident_b[0:64, 0:64], reads=[Rybb], writes=[RPS[0]])
                        yo, Ryo = yo_r.next()
                        cp("scalar", yo, psb[:, 0:512], reads=[RPS[0]], writes=[Ryo])
                        dma(ybT[h * 128:(h + 1) * 128, tok:tok + 512], yo, reads=[Ryo], writes=[R["ybT"]])
            S.barrier()

        if "D" in phases:
          with contextlib.ExitStack() as st:
            Ring.stack = st

            def sb(name, shape, dt):
                return st.enter_context(nc.sbuf_tensor(f"D{l}_{name}", shape, dt))
            Wm = [sb(f"W{i}", [128, 8, D], BF16) for i in range(3)]
            RW = Res("Wm")
            wst_ring = Ring(nc, f"D{l}_wst", [128, 8, 512], F32, 2)
            srcs = [w_br[l, 0], w_br[l, 1], w_out[l]]
            for i in range(3):
                wv = srcs[i].rearrange("(c p) n -> p c n", p=128)
                for b in range(2):
                    wst, Rwst = wst_ring.next()
                    dma(wst, wv[:, :, b * 512:(b + 1) * 512], writes=[Rwst])
                    cp("gpsimd", Wm[i][:, :, b * 512:(b + 1) * 512], wst, reads=[Rwst], writes=[RW])
            fgt = sb("fgt", [128, D], F32)
            dma(fgt[:], final_g[0:1, :].broadcast_to([128, D]), writes=[RW])
            ya_r = Ring(nc, f"D{l}_ya", [128, 8, 512], BF16, 2)
            yb_r2 = Ring(nc, f"D{l}_yb", [128, 8, 512], BF16, 2)
            g_r = Ring(nc, f"D{l}_g", [128, 16, 512], BF16, 2)
            mT_r = Ring(nc, f"D{l}_mT", [128, 8, 512], BF16, 2)
            f_r = Ring(nc, f"D{l}_f", [128, 512], F32, 4)
            x_r = Ring(nc, f"D{l}_x", [128, D], F32, 2)
            xo_r = Ring(nc, f"D{l}_xo", [128, D], F32, 2)
            ss_r = Ring(nc, f"D{l}_ss", [128, 2], F32, 2)
            junk = sb("junk", [128, D], BF16)
            Rj = Res("junk")
            pi = 0
            for tb in range(NT // 512):
                tok = tb * 512
                yat, Rya = ya_r.next()
                ybt, Ryb = yb_r2.next()
                gt, Rgt = g_r.next()
                dma(yat, yaT[:, tok:tok + 512].rearrange("(c p) t -> p c t", p=128), reads=[R["yaT"]], writes=[Rya])
                dma(ybt, ybT[:, tok:tok + 512].rearrange("(c p) t -> p c t", p=128), reads=[R["ybT"]], writes=[Ryb])
                dma(gt, gT[:, tok:tok + 512].rearrange("(c p) t -> p c t", p=128), reads=[R["gT"]], writes=[Rgt])
                mT, RmT = mT_r.next()
                for dm in range(8):
                    pa, Rpa = PS[pi % 2], RPS[pi % 2]
                    pb_, Rpb = PS[2 + pi % 2], RPS[2 + pi % 2]
                    pi += 1
                    for c in range(8):
                        mm(pa[:, :], Wm[0][:, c, dm * 128:(dm + 1) * 128], yat[:, c, :], c == 0, c == 7, reads=[RW, Rya], writes=[Rpa])
                    for c in range(8):
                        mm(pb_[:, :], Wm[1][:, c, dm * 128:(dm + 1) * 128], ybt[:, c, :], c == 0, c == 7, reads=[RW, Ryb], writes=[Rpb])
                    f0, Rf0 = f_r.next()
                    f1, Rf1 = f_r.next()
                    tt("vector", f0, pa[:, :], gt[:, dm, :], ALU.mult, reads=[Rpa, Rgt], writes=[Rf0])
                    tt("vector", f1, pb_[:, :], gt[:, 8 + dm, :], ALU.mult, reads=[Rpb, Rgt], writes=[Rf1])
                    tt("gpsimd", mT[:, dm, :], f0, f1, ALU.add, reads=[Rf0, Rf1], writes=[RmT])
                for m in range(4):
                    xt, Rxt = x_r.next()
                    dma(xt, xsrc[tok + m * 128: tok + (m + 1) * 128, :], reads=Rxsrc, writes=[Rxt])
                    xo, Rxo = xo_r.next()
                    for cb0 in (0, 512):
                        po, Rpo = PS[4 + pi % 2], RPS[4 + pi % 2]
                        pi += 1
                        for c in range(8):
                            mm(po[:, :], mT[:, c, m * 128:(m + 1) * 128], Wm[2][:, c, cb0:cb0 + 512], c == 0, c == 7, reads=[RW, RmT], writes=[Rpo])
                        tt("vector", xo[:, cb0:cb0 + 512], po[:, :], xt[:, cb0:cb0 + 512], ALU.add, reads=[Rpo, Rxt], writes=[Rxo])
                    if l < DEPTH - 1:
                        dma(x1[tok + m * 128: tok + (m + 1) * 128, :], xo, reads=[Rxo], writes=[R["x1"]])
                    else:
                        ss, Rss = ss_r.next()
                        act(junk[:], xo, AF.Square, reads=[Rxo], writes=[Rj, Rss], accum_out=ss[:, 0:1])
                        act(ss[:, 1:2], ss[:, 0:1], AF.Sqrt, reads=[Rss], writes=[Rss], scale=1.0 / D, bias=EPS)
                        S.op("vector", lambda e, ss=ss: e.reciprocal(out=ss[:, 1:2], in_=ss[:, 1:2]), reads=[Rss], writes=[Rss])
                        stt("vector", xo, xo, ss[:, 1:2], fgt[:], ALU.mult, ALU.mult, reads=[Rxo, Rss, RW], writes=[Rxo])
                        dma(yout[tok + m * 128: tok + (m + 1) * 128, :], xo, reads=[Rxo], writes=[R["yout"]])
            S.barrier()
    S.emit()
    return nc


def make_consts():
    inv = (10000.0 ** (-np.arange(0, 64, 2, dtype=np.float32) / 64)).astype(np.float32)
    pos = np.arange(4096, dtype=np.float32)
    ang = pos[:, None] * inv[None, :]
    ang = np.concatenate([ang, ang], -1)
    cos = np.cos(ang).astype(np.float32).T
    sin = np.sin(ang).astype(np.float32).T
    sgn = np.concatenate([-np.ones(32, np.float32), np.ones(32, np.float32)])[:, None]
    cosT = np.concatenate([cos, cos], 0)
    sinT = np.concatenate([sin * sgn, sin * sgn], 0)
    idx = np.arange(64)
    cf = np.zeros((128, 6, 128), np.float32)
    cf[:, 0, :] = np.eye(128, dtype=np.float32)
    cf[:, 1, :] = 1.0
    cf[:64, 2, :64] = (idx[:, None] <= idx[None, :])
    cf[:64, 3, :64] = (idx[:, None] < idx[None, :])
    cf[:64, 4, :64] = (idx[:, None] >= idx[None, :])
    cf[:64, 5, :64] = (idx[:, None] > idx[None, :])
    cbm = np.zeros((128, 3, 128), np.float32)
    cbm[:, 0, :] = np.eye(128)
    cbm[:, 1, :] = 1.0
    perm = np.zeros((128, 128), np.float32)
    for f in range(128):
        base = (f // 64) * 64
        dd = f % 64
        k = base + (dd + 32) % 64
        perm[k, f] = 1.0
    cbm[:, 2, :] = perm
    return np.ascontiguousarray(cosT), np.ascontiguousarray(sinT), cf, cbm.astype(ml_dtypes.bfloat16)


def shared_inputs(norm_g, w_in, conv_w, lam_qk, diff_norm_g, a_log, dt_bias, gdn_norm_g, w_branch, w_out, final_g):
    cosT, sinT, cf, cbm = make_consts()
    f = lambda a: np.ascontiguousarray(np.asarray(a, dtype=np.float32))
    convw = f(conv_w).reshape(DEPTH, 4, 24, 128).transpose(0, 3, 2, 1)
    return {
        "w_in": f(w_in), "w_branch": f(w_branch), "w_out": f(w_out), "norm_g": f(norm_g),
        "final_g": f(final_g).reshape(1, D), "convw": np.ascontiguousarray(convw),
        "lam_qk": f(lam_qk).reshape(DEPTH, 1, 256), "dng": f(diff_norm_g).reshape(DEPTH, 128, 1),
        "gng": f(gdn_norm_g).reshape(DEPTH, 1, 128), "a_log": f(a_log).reshape(DEPTH, 1, 16),
        "dt_bias": f(dt_bias).reshape(DEPTH, 1, 16), "cosT": cosT, "sinT": sinT, "constf": cf, "constb": cbm,
    }


_NC_CACHE = {}


def kernel(x_prompt, x_sample, norm_g, w_in, conv_w, lam_qk, diff_norm_g, a_log, dt_bias, gdn_norm_g, w_branch, w_out, final_g):
    x_prompt = np.asarray(x_prompt, dtype=np.float32)
    x_sample = np.asarray(x_sample, dtype=np.float32)
    seqs = SEQS_FULL
    key = tuple(seqs)
    if key not in _NC_CACHE:
        _NC_CACHE[key] = build(seqs)
    nc = _NC_CACHE[key]
    sh = shared_inputs(norm_g, w_in, conv_w, lam_qk, diff_norm_g, a_log, dt_bias, gdn_norm_g, w_branch, w_out, final_g)
    in_maps = []
    for c in range(8):
        xin = np.concatenate([x_prompt[c], x_sample[2 * c], x_sample[2 * c + 1]], axis=0)
        m = dict(sh)
        m["xin"] = np.ascontiguousarray(xin)
        in_maps.append(m)
    res = run_bass_kernel_spmd(nc, in_maps, core_ids=list(range(8)))
    y_prompt = np.empty_like(x_prompt)
    y_sample = np.empty_like(x_sample)
    for c in range(8):
        y = res.results[c]["yout"]
        y_prompt[c] = y[0:2048]
        y_sample[2 * c] = y[2048:2048 + 4096]
        y_sample[2 * c + 1] = y[2048 + 4096:]
    return (y_prompt, y_sample)
```
